# Optimizing a Trainium2 kernel written in Bass

```python
import math, functools
import jax, jax.numpy as jnp
from jax import lax
import numpy as np


D_MODEL = 2048
BATCH = 2
SEQ = 8192
DEPTH = 4

GRID_W = 64
CTX_LEN = 256
N_MIXERS = 2
N_MOD = 6
MLA_HEADS = 16
MLA_Q_RANK = 512
MLA_KV_RANK = 512
MLA_NOPE = 128
MLA_ROPE = 64
MLA_V = 128
GQA_HEADS = 16
GQA_KV_HEADS = 4
GQA_HEAD_DIM = 128
D_FF = 5632
CONV_W = 3

ROPE_BASE = 10000.0
EPS = 1e-6
Q_BLOCK = 128
N_MLA_LAYERS = (DEPTH + N_MIXERS - 1) // N_MIXERS
N_GQA_LAYERS = DEPTH // N_MIXERS
MLA_SCALE = 1.0 / math.sqrt(MLA_NOPE + MLA_ROPE)
GQA_SCALE = 1.0 / math.sqrt(GQA_HEAD_DIM)

kernel_name = "hybrid_mla_gqa_convffn_dit"


def rms_norm(x, g):
    xf = x.astype(jnp.float32)
    y = xf * lax.rsqrt(jnp.mean(xf * xf, axis=-1, keepdims=True) + EPS)
    return (y * g.astype(jnp.float32)).astype(x.dtype)


def modulate(x, g, shift, scale):
    return rms_norm(x, g) * (1.0 + scale) + shift


def axial_rope_tables(rows, cols, rot_dim):
    axis_dim = rot_dim // 2
    inv = jnp.power(ROPE_BASE, -jnp.arange(0, axis_dim, 2, dtype=jnp.float32) / axis_dim)
    ang_r = rows.astype(jnp.float32)[:, None] * inv
    ang_c = cols.astype(jnp.float32)[:, None] * inv
    ang = jnp.concatenate([ang_r, ang_r, ang_c, ang_c], axis=-1)
    return jnp.cos(ang), jnp.sin(ang)


def rotate_half(x):
    x1, x2 = jnp.split(x, 2, axis=-1)
    return jnp.concatenate([-x2, x1], axis=-1)


def apply_axial_rope(x, cos, sin):
    half = x.shape[-1] // 2
    rot = jnp.concatenate([rotate_half(x[..., :half]), rotate_half(x[..., half:])], axis=-1)
    return (x * cos[:, None, :] + rot * sin[:, None, :]).astype(x.dtype)


def attend(q, k, v, scale):
    B, Q, H, Dq = q.shape
    Hk = k.shape[2]
    qg = q.reshape(B, Q, Hk, H // Hk, Dq)
    s = jnp.einsum("bqkgd,btkd->bkgqt", qg, k, preferred_element_type=jnp.float32) * scale
    p = jax.nn.softmax(s, axis=-1)
    o = jnp.einsum("bkgqt,btkd->bqkgd", p.astype(v.dtype), v, preferred_element_type=jnp.float32)
    return o.reshape(B, Q, H, v.shape[-1]).astype(q.dtype)


def blocked_attention(q, k, v, scale):
    B, S, H, Dq = q.shape
    nb = S // Q_BLOCK
    qs = q.reshape(B, nb, Q_BLOCK, H, Dq).swapaxes(0, 1)
    o = lax.map(lambda qb: attend(qb, k, v, scale), qs)
    return o.swapaxes(0, 1).reshape(B, S, H, v.shape[-1])


def mla_queries(h, rope, w_dq, g_dq, w_uq, g_q_nope, g_q_pe):
    B, S, _ = h.shape
    cq = rms_norm(h @ w_dq, g_dq)
    q = (cq @ w_uq).reshape(B, S, MLA_HEADS, MLA_NOPE + MLA_ROPE)
    q_nope = rms_norm(q[..., :MLA_NOPE], g_q_nope)
    q_pe = rms_norm(q[..., MLA_NOPE:], g_q_pe)
    if rope is not None:
        q_pe = apply_axial_rope(q_pe, *rope)
    return jnp.concatenate([q_nope, q_pe], axis=-1)


def mla_keys_values(h, rope, w_dkv, g_dkv, g_k_pe, w_ukv, g_k_nope):
    B, S, _ = h.shape
    kv_a = h @ w_dkv
    c_kv = rms_norm(kv_a[..., :MLA_KV_RANK], g_dkv)
    k_pe = rms_norm(kv_a[..., MLA_KV_RANK:], g_k_pe)[:, :, None, :]
    if rope is not None:
        k_pe = apply_axial_rope(k_pe, *rope)
    kv = (c_kv @ w_ukv).reshape(B, S, MLA_HEADS, MLA_NOPE + MLA_V)
    k_nope = rms_norm(kv[..., :MLA_NOPE], g_k_nope)
    v = kv[..., MLA_NOPE:]
    k = jnp.concatenate([k_nope, jnp.broadcast_to(k_pe, (B, S, MLA_HEADS, MLA_ROPE))], axis=-1)
    return k, v


def gqa_queries(h, rope, w_q, g_q):
    B, S, _ = h.shape
    q = rms_norm((h @ w_q).reshape(B, S, GQA_HEADS, GQA_HEAD_DIM), g_q)
    if rope is not None:
        q = apply_axial_rope(q, *rope)
    return q


def gqa_keys_values(h, rope, w_kv, g_k):
    B, S, _ = h.shape
    kv = (h @ w_kv).reshape(B, S, 2, GQA_KV_HEADS, GQA_HEAD_DIM)
    k = rms_norm(kv[:, :, 0], g_k)
    v = kv[:, :, 1]
    if rope is not None:
        k = apply_axial_rope(k, *rope)
    return k, v


def depthwise_conv_centred(u, w, b):
    S = u.shape[1]
    pad = CONV_W // 2
    up = jnp.pad(u, ((0, 0), (pad, pad), (0, 0)))
    return sum(up[:, k:k + S] * w[k] for k in range(CONV_W)) + b


def conv_ffn(h, w_up, conv_w, conv_b, w_down):
    u = h @ w_up
    gate, val = u[..., :D_FF], u[..., D_FF:]
    gate = depthwise_conv_centred(gate, conv_w, conv_b)
    return (jax.nn.silu(gate) * val) @ w_down


def setup_inputs(seed: int = 0) -> dict:
    key = jax.random.key(seed)
    ks = iter(jax.random.split(key, 40))
    D, L, LA, LB = D_MODEL, DEPTH, N_MLA_LAYERS, N_GQA_LAYERS

    def nrm(shape, scale):
        return jax.random.normal(next(ks), shape, jnp.float32) * scale

    def gain(shape):
        return 1.0 + nrm(shape, 0.02)

    return {
        "x": nrm((BATCH, SEQ, D), 1.0),
        "c": nrm((BATCH, D), 1.0),
        "ctx": nrm((BATCH, CTX_LEN, D), 1.0),
        "c_ctx": nrm((D,), 1.0),
        "w_mod": nrm((L, D, N_MOD * D), 0.5 * D ** -0.5),
        "b_mod": nrm((L, N_MOD * D), 0.01),
        "norm_mix": gain((L, D)),
        "norm_ffn": gain((L, D)),
        "mla_w_dq": nrm((LA, D, MLA_Q_RANK), D ** -0.5),
        "mla_g_dq": gain((LA, MLA_Q_RANK)),
        "mla_w_uq": nrm((LA, MLA_Q_RANK, MLA_HEADS * (MLA_NOPE + MLA_ROPE)), MLA_Q_RANK ** -0.5),
        "mla_g_q_nope": gain((LA, MLA_NOPE)),
        "mla_g_q_pe": gain((LA, MLA_ROPE)),
        "mla_w_dkv": nrm((LA, D, MLA_KV_RANK + MLA_ROPE), D ** -0.5),
        "mla_g_dkv": gain((LA, MLA_KV_RANK)),
        "mla_g_k_pe": gain((LA, MLA_ROPE)),
        "mla_w_ukv": nrm((LA, MLA_KV_RANK, MLA_HEADS * (MLA_NOPE + MLA_V)), MLA_KV_RANK ** -0.5),
        "mla_g_k_nope": gain((LA, MLA_NOPE)),
        "mla_w_o": nrm((LA, MLA_HEADS * MLA_V, D), (MLA_HEADS * MLA_V) ** -0.5),
        "gqa_w_q": nrm((LB, D, GQA_HEADS * GQA_HEAD_DIM), D ** -0.5),
        "gqa_g_q": gain((LB, GQA_HEAD_DIM)),
        "gqa_w_kv": nrm((LB, D, 2 * GQA_KV_HEADS * GQA_HEAD_DIM), D ** -0.5),
        "gqa_g_k": gain((LB, GQA_HEAD_DIM)),
        "gqa_w_o": nrm((LB, GQA_HEADS * GQA_HEAD_DIM, D), (GQA_HEADS * GQA_HEAD_DIM) ** -0.5),
        "ffn_w_up": nrm((L, D, 2 * D_FF), D ** -0.5),
        "ffn_conv_w": nrm((L, CONV_W, D_FF), CONV_W ** -0.5),
        "ffn_conv_b": nrm((L, D_FF), 0.01),
        "ffn_w_down": nrm((L, D_FF, D), D_FF ** -0.5),
    }


def reference(x, c, ctx, c_ctx, w_mod, b_mod, norm_mix, norm_ffn,
              mla_w_dq, mla_g_dq, mla_w_uq, mla_g_q_nope, mla_g_q_pe,
              mla_w_dkv, mla_g_dkv, mla_g_k_pe, mla_w_ukv, mla_g_k_nope, mla_w_o,
              gqa_w_q, gqa_g_q, gqa_w_kv, gqa_g_k, gqa_w_o,
              ffn_w_up, ffn_conv_w, ffn_conv_b, ffn_w_down):
    B, S, _ = x.shape
    C = ctx.shape[1]
    ROWS = S // GRID_W
    rows = jnp.repeat(jnp.arange(ROWS, dtype=jnp.int32), GRID_W)
    cols = jnp.tile(jnp.arange(GRID_W, dtype=jnp.int32), ROWS)
    rope_mla = axial_rope_tables(rows, cols, MLA_ROPE)
    rope_gqa = axial_rope_tables(rows, cols, GQA_HEAD_DIM)
    silu_c = jax.nn.silu(c)
    silu_cc = jax.nn.silu(c_ctx)

    for i in range(DEPTH):
        last = i == DEPTH - 1
        j = i // N_MIXERS
        mod = (silu_c @ w_mod[i] + b_mod[i])[:, None, :]
        mod_c = silu_cc @ w_mod[i] + b_mod[i]
        sh1, sc1, g1, sh2, sc2, g2 = jnp.split(mod, N_MOD, axis=-1)
        csh1, csc1, cg1, csh2, csc2, cg2 = jnp.split(mod_c, N_MOD, axis=-1)

        h = modulate(x, norm_mix[i], sh1, sc1)
        hc = modulate(ctx, norm_mix[i], csh1, csc1)
        if i % N_MIXERS == 0:
            q_fn = functools.partial(mla_queries, w_dq=mla_w_dq[j], g_dq=mla_g_dq[j], w_uq=mla_w_uq[j],
                                     g_q_nope=mla_g_q_nope[j], g_q_pe=mla_g_q_pe[j])
            kv_fn = functools.partial(mla_keys_values, w_dkv=mla_w_dkv[j], g_dkv=mla_g_dkv[j],
                                      g_k_pe=mla_g_k_pe[j], w_ukv=mla_w_ukv[j], g_k_nope=mla_g_k_nope[j])
            w_o, rope, scale = mla_w_o[j], rope_mla, MLA_SCALE
        else:
            q_fn = functools.partial(gqa_queries, w_q=gqa_w_q[j], g_q=gqa_g_q[j])
            kv_fn = functools.partial(gqa_keys_values, w_kv=gqa_w_kv[j], g_k=gqa_g_k[j])
            w_o, rope, scale = gqa_w_o[j], rope_gqa, GQA_SCALE

        k_lat, v_lat = kv_fn(h, rope)
        k_ctx, v_ctx = kv_fn(hc, None)
        o = blocked_attention(q_fn(h, rope),
                              jnp.concatenate([k_lat, k_ctx], axis=1),
                              jnp.concatenate([v_lat, v_ctx], axis=1), scale)
        x = x + g1 * (o.reshape(B, S, -1) @ w_o)
        if not last:
            oc = attend(q_fn(hc, None), k_ctx, v_ctx, scale)
            ctx = ctx + cg1 * (oc.reshape(B, C, -1) @ w_o)

        x = x + g2 * conv_ffn(modulate(x, norm_ffn[i], sh2, sc2),
                              ffn_w_up[i], ffn_conv_w[i], ffn_conv_b[i], ffn_w_down[i])
        if not last:
            ctx = ctx + cg2 * conv_ffn(modulate(ctx, norm_ffn[i], csh2, csc2),
                                       ffn_w_up[i], ffn_conv_w[i], ffn_conv_b[i], ffn_w_down[i])
    return x
```

```python
import math
import numpy as np
import concourse.bass as bass
import concourse.mybir as mybir
from concourse.bass_utils import run_bass_kernel_spmd

F32 = mybir.dt.float32
BF16 = mybir.dt.bfloat16
AF = mybir.ActivationFunctionType
ALU = mybir.AluOpType
EPS = 1e-6
ROPE_BASE = 10000.0


class Cfg:
    def __init__(s, **kw):
        s.D = 2048; s.S = 8192; s.B = 2; s.C = 256; s.FF = 5632; s.depth = 4; s.GRID_W = 64
        s.MH = 16; s.QR = 512; s.KVR = 512; s.GH = 16; s.GKV = 4
        s.W1 = 512; s.W4 = 410; s.NR = 4
        s.NWS = 3; s.NWU = 4; s.NWD = 2; s.WDSPLIT = 1
        for k, v in kw.items():
            setattr(s, k, v)
        s.TL = s.S // s.NR
        s.DC = s.D // 128; s.FC = s.FF // 128
        s.NTOK = s.TL + s.C


class Op:
    __slots__ = ("eng", "fn", "deps", "marked", "cum", "is_dma", "sem", "val", "inc", "idx", "epoch")


class Buf:
    def __init__(s, ap=None, sem=None):
        s.ap = ap; s.w = []; s.r = []; s.gen = []; s.sem = sem; s.cnt = 0

    def new_gen(s):
        s.gen = s.w + s.r; s.w = []; s.r = []


class Prog:
    def __init__(s, nc):
        s.nc = nc
        s.ops = {k: [] for k in ("pe", "act", "dve", "pool", "sp")}
        s.engsem = {}
        s.epoch = 0; s.nidx = 0
        s.ccsem = nc.alloc_semaphore("ccsem"); s.cccnt = 0

    def new_epoch(s):
        s.epoch += 1

    def esem(s, eng, epoch):
        k = (eng, epoch)
        if k not in s.engsem:
            s.engsem[k] = s.nc.alloc_semaphore("es_%s_%d" % (eng, epoch))
        return s.engsem[k]

    def emit(s, eng, fn, reads=(), writes=(), pwrites=(), extra=()):
        deps = list(extra)
        for b in reads:
            deps += b.w
        for b in writes:
            b.new_gen(); deps += b.gen
        for b in pwrites:
            deps += b.gen
        best = {}
        for d in deps:
            if d.is_dma:
                k = ("s", d.sem.num); v = d.val
            else:
                k = ("e", d.eng); v = d.idx
            o = best.get(k)
            if o is None or v > o[0]:
                best[k] = (v, d)
        deps = [o[1] for o in best.values()]
        op = Op(); op.eng = eng; op.fn = fn; op.deps = deps; op.marked = False; op.cum = 0
        op.is_dma = False; op.sem = None; op.val = 0; op.inc = 0
        op.idx = s.nidx; s.nidx += 1; op.epoch = s.epoch
        for b in reads:
            b.r.append(op)
        for b in writes:
            b.w.append(op)
        for b in pwrites:
            b.w.append(op)
        s.ops[eng].append(op)
        return op

    def dma(s, eng, out, in_, sb, reads=(), writes=(), pwrites=(), extra=()):
        op = s.emit(eng, lambda e: e.dma_start(out=out, in_=in_), reads, writes, pwrites, extra)
        sb.cnt += 16
        op.is_dma = True; op.sem = sb.sem; op.val = sb.cnt; op.inc = 16
        return op

    def cc(s, kind, groups, in_ap, out_ap, reads=(), writes=(), pwrites=()):
        op = s.emit("pool", lambda e: e.collective_compute(kind, ALU.bypass, replica_groups=groups,
                                                            ins=[in_ap], outs=[out_ap]), reads, writes, pwrites)
        s.cccnt += 1
        op.is_dma = True; op.sem = s.ccsem; op.val = s.cccnt; op.inc = 1
        return op

    def barrier(s):
        deps = []
        latest = {}
        for eng, ops in s.ops.items():
            lastc = None
            for op in ops:
                if op.is_dma:
                    latest[op.sem.num] = op
                else:
                    lastc = op
            if lastc is not None:
                deps.append(lastc)
        deps += list(latest.values())
        for eng in s.ops:
            s.emit(eng, lambda e: e.nop(), extra=deps)

    def finalize(s):
        nc = s.nc
        for eng, ops in s.ops.items():
            for op in ops:
                for d in op.deps:
                    if not d.is_dma:
                        d.marked = True
        for eng, ops in s.ops.items():
            cnt = {}
            for op in ops:
                if (not op.is_dma) and op.marked:
                    cnt[op.epoch] = cnt.get(op.epoch, 0) + 1
                    s.esem(eng, op.epoch)
                op.cum = cnt.get(op.epoch, 0)

        def run(engname, e):
            waited = {}
            for op in s.ops[engname]:
                for d in op.deps:
                    if d.is_dma:
                        key = ("s", d.sem.num); val = d.val; sem = d.sem
                    else:
                        if d.eng == "pe" and engname == "pe":
                            continue
                        key = ("e", d.eng, d.epoch); val = d.cum; sem = s.engsem[(d.eng, d.epoch)]
                    if waited.get(key, 0) >= val:
                        continue
                    waited[key] = val
                    e.wait_ge(sem, val)
                ins = op.fn(e)
                if op.is_dma:
                    if op.inc == 16:
                        ins.then_inc(op.sem, 16)
                    else:
                        ins.then_inc(op.sem)
                elif op.marked:
                    ins.then_inc(s.engsem[(op.eng, op.epoch)], 1)

        with nc.Block() as block:
            @block.sync
            def _(e):
                run("sp", e)

            @block.gpsimd
            def _(e):
                run("pool", e)

            @block.scalar
            def _(e):
                run("act", e)

            @block.vector
            def _(e):
                run("dve", e)

            @block.tensor
            def _(e):
                run("pe", e)


def _fm(v):
    v = np.asarray(v, np.float32)
    return np.ascontiguousarray(v.reshape(-1, 128).T)


def rope_tables(cfg, tok0, n, rot_dim):
    axis_dim = rot_dim // 2
    inv = np.power(np.float32(ROPE_BASE), -np.arange(0, axis_dim, 2, dtype=np.float32) / np.float32(axis_dim)).astype(np.float32)
    t = np.arange(tok0, tok0 + n)
    rows = (t // cfg.GRID_W).astype(np.float32); cols = (t % cfg.GRID_W).astype(np.float32)
    ar = rows[:, None] * inv; ac = cols[:, None] * inv
    ang = np.concatenate([ar, ar, ac, ac], axis=-1).astype(np.float32)
    return np.cos(ang).T.astype(np.float32), np.sin(ang).T.astype(np.float32)


def rot_matrix_T(rot_dim):
    half = rot_dim // 2; q = half // 2
    R = np.zeros((rot_dim, rot_dim), np.float32)
    for base in (0, half):
        for i in range(q):
            R[base + i, base + i + q] = -1.0
            R[base + q + i, base + i] = 1.0
    return np.ascontiguousarray(R.T)


def build(cfg, debug=False):
    nc = bass.Bass("TRN2", target_bir_lowering=False)
    P = Prog(nc)
    D, TL, C, FF, DC, FC, NTOK, NR = cfg.D, cfg.TL, cfg.C, cfg.FF, cfg.DC, cfg.FC, cfg.NTOK, cfg.NR
    L = cfg.depth
    LA = (L + 1) // 2; LB = max(L // 2, 1)
    MH, QR, KVR, GH, GKV = cfg.MH, cfg.QR, cfg.KVR, cfg.GH, cfg.GKV
    QC = QR // 128; KC2 = KVR // 128
    T = NR * TL + C
    NCH = T // 128
    W1 = cfg.W1; W4 = cfg.W4

    def din(name, shape, dt=F32):
        return nc.dram_tensor(name, list(shape), dt, kind="ExternalInput").ap()

    def dscr(name, shape, dt):
        return nc.dram_tensor(name, list(shape), dt).ap()

    xT_in = din("xT", [D, TL]); ctxT_in = din("ctxT", [D, C]); cvec = din("cvec", [128, DC * 2])
    cosg = din("cosg", [128, TL]); sing = din("sing", [128, TL]); cosm = din("cosm", [128, TL]); sinm = din("sinm", [128, TL])
    RgT = din("RgT", [128, 128]); RmT = din("RmT", [128, 128]); bones = din("bones", [128, 128]); sel = din("sel", [128, 2 * NR])
    NV = 2 * DC + 6 * DC + 4 * FC + QC + KC2 + 6
    vecs = din("vecs", [L, 128, NV])
    w_mod = din("w_mod", [L, D, 6 * D])
    mla_w_dq = din("mla_w_dq", [LA, D, QR]); mla_w_uq = din("mla_w_uq", [LA, QR, MH * 192])
    mla_w_dkv = din("mla_w_dkv", [LA, D, KVR + 64]); mla_w_ukv = din("mla_w_ukv", [LA, KVR, MH * 256])
    mla_w_o = din("mla_w_o", [LA, MH * 128, D])
    gqa_w_q = din("gqa_w_q", [LB, D, GH * 128]); gqa_w_kv = din("gqa_w_kv", [LB, D, 2 * GKV * 128])
    gqa_w_o = din("gqa_w_o", [LB, GH * 128, D])
    ffn_w_up = din("ffn_w_up", [L, D, 2 * FF]); ffn_w_down = din("ffn_w_down", [L, FF, D])
    yT = nc.dram_tensor("yT", [D, TL], F32, kind="ExternalOutput").ap()

    xT_s = dscr("xT_s", [D, TL], F32); ctxT_s = dscr("ctxT_s", [D, C], F32)
    qT_d = dscr("qT_d", [max(MH, GH) * 128, NTOK], BF16); qpeT_d = dscr("qpeT_d", [MH // 2 * 128, NTOK], BF16)
    kvs = {}
    for nm, nkv in (("m", MH), ("g", GKV)):
        HB = 2 if nkv % 2 == 0 else 1
        while HB > 1 and HB * 128 * TL * 2 > (1 << 20):
            HB //= 2
        nb = nkv // HB
        kvs[nm] = dict(HB=HB, NB=nb,
                       kT_src=[dscr("kT_src%s%d" % (nm, i), [HB * 128, TL], BF16) for i in range(nb)],
                       kT_all=[dscr("kT_all%s%d" % (nm, i), [NR * HB * 128, TL], BF16) for i in range(nb)],
                       v_src=[dscr("v_src%s%d" % (nm, i), [TL, HB * 128], BF16) for i in range(nb)],
                       v_all=[dscr("v_all%s%d" % (nm, i), [NR * TL, HB * 128], BF16) for i in range(nb)],
                       kT_ctx=dscr("kT_ctx" + nm, [nkv * 128, C], BF16), v_ctx=dscr("v_ctx" + nm, [C, nkv * 128], BF16))
    kpe_src = dscr("kpe_src", [128, TL], BF16); kpe_all = dscr("kpe_all", [NR * 128, TL], BF16); kpe_ctx = dscr("kpe_ctx", [128, C], BF16)
    h2T_d = dscr("h2T_d", [D, TL + 2], BF16); h2cT_d = dscr("h2cT_d", [D, C + 2], BF16)
    hsrc = dscr("hsrc", [128, DC * 2], BF16); hall = dscr("hall", [NR * 128, DC * 2], BF16)
    NUT = (FC + 1) // 2; NDT = (DC + 1) // 2
    wupT = dscr("wupT", [NUT * 2, 128, DC * 256], BF16); wdnT = dscr("wdnT", [NDT, 128, FC * 256], BF16)
    Dwc = Buf()
    Dx = Buf(); Dctx = Buf(); Dq = Buf(); Dqpe = Buf(); Dksrc = Buf(); Dkall = Buf(); Dvsrc = Buf(); Dvall = Buf()
    Dkpesrc = Buf(); Dkpeall = Buf(); Dkctx = Buf(); Dvctx = Buf(); Dkpectx = Buf(); Dh2 = Buf(); Dh2c = Buf(); Dhsrc = Buf(); Dhall = Buf()
    Dy = Buf()

    ARENA_BYTES = 207 * 1024
    arena = nc.alloc_sbuf_tensor("arena", [128, ARENA_BYTES // 4], F32)
    apos = [0]
    sempool = {}
    semidx = [0]
    persist_sems = [0]

    def phase_reset(base):
        apos[0] = base; semidx[0] = persist_sems[0]

    def carve(shape, dt, sem=True):
        esz = 4 if dt == F32 else 2
        nb = int(np.prod(shape)) * esz
        nb_al = (nb + 63) // 64 * 64
        off = apos[0]; apos[0] += nb_al
        assert apos[0] <= ARENA_BYTES, ("SBUF arena overflow", apos[0])
        a = arena[:, off // 4:(off + nb) // 4]
        if dt != F32:
            a = a.bitcast(dt)
        if len(shape) == 2:
            a = a.rearrange("p (a b) -> p a b", a=shape[0])
        elif len(shape) == 3:
            a = a.rearrange("p (a b c) -> p a b c", a=shape[0], b=shape[1])
        sm = None
        if sem:
            k = semidx[0]; semidx[0] += 1
            if k not in sempool:
                sempool[k] = [nc.alloc_semaphore("bs%d" % k), 0]
            sm = sempool[k]
        b = Buf(a, sm[0] if sm else None)
        if sm:
            b.cnt = sm[1]; b.semrec = sm
        return b

    def mm(out, lhsT, rhs, start, stop, reads, writes=(), pwrites=()):
        return P.emit("pe", lambda e: e.matmul(out, lhsT, rhs, start=start, stop=stop), reads=reads, writes=writes, pwrites=pwrites)

    def act(out, in_, func, reads, writes=(), pwrites=(), bias=None, scale=None, eng="act"):
        kw = {}
        if bias is not None:
            kw["bias"] = bias
        if scale is not None:
            kw["scale"] = scale
        return P.emit(eng, lambda e: e.activation(out, in_, func, **kw), reads=reads, writes=writes, pwrites=pwrites)

    def stt(out, in0, scalar, in1, op0, op1, reads, writes=(), pwrites=(), eng="dve"):
        return P.emit(eng, lambda e: e.scalar_tensor_tensor(out, in0, scalar, in1, op0, op1), reads=reads, writes=writes, pwrites=pwrites)

    def tt(out, in0, in1, op, reads, writes=(), pwrites=(), eng="dve"):
        return P.emit(eng, lambda e: e.tensor_tensor(out, in0, in1, op), reads=reads, writes=writes, pwrites=pwrites)

    def tsm(out, in0, s1, reads, writes=(), pwrites=(), eng="dve"):
        return P.emit(eng, lambda e: e.tensor_scalar(out, in0, s1, None, ALU.mult), reads=reads, writes=writes, pwrites=pwrites)

    def recip(out, in_, reads, writes=(), pwrites=()):
        return P.emit("dve", lambda e: e.reciprocal(out, in_), reads=reads, writes=writes, pwrites=pwrites)

    def cpy(out, in_, reads, writes=(), pwrites=(), eng="dve"):
        return P.emit(eng, lambda e: e.tensor_copy(out, in_), reads=reads, writes=writes, pwrites=pwrites)

    def mset(out, val, writes=(), pwrites=(), reads=(), eng="dve"):
        return P.emit(eng, lambda e: e.memset(out, val), reads=reads, writes=writes, pwrites=pwrites)

    def dma(eng, out, in_, sb, reads=(), writes=(), pwrites=()):
        sb.cnt = sb.semrec[1]
        op = P.dma(eng, out, in_, sb, reads=reads, writes=writes, pwrites=pwrites)
        sb.semrec[1] = sb.cnt
        return op

    ones_f = carve([128], F32, False); ones_b = carve([128], BF16, False); bones_f = carve([128], F32)
    Rg_f = carve([128], F32); Rm_f = carve([128], F32); sel_f = carve([2 * NR], F32)
    sv_f = carve([DC * 2], F32); sv_b = carve([DC, 2], BF16, False)
    vec = carve([NV], F32)
    mod = carve([6 * DC, 2], F32, False)
    A1 = carve([DC, 2], F32, False); A2 = carve([DC, 2], F32, False)
    gsc = carve([8], F32, False)
    bnd = carve([DC, 2], BF16); halo = carve([DC, 2], BF16, False); hs = carve([NR, DC, 2], BF16)
    halo_f = carve([DC, 2], F32, False)
    eps_c = carve([1], F32, False)
    cvl = [carve([1], F32) for _ in range(3)]
    psum = [Buf(nc.alloc_psum_tensor("ps%d" % i, [128, 512], F32)[:]) for i in range(8)]
    PERSIST = apos[0]
    persist_sems[0] = semidx[0]

    o = 0
    V_NMIX = o; o += DC
    V_NFFN = o; o += DC
    V_BMOD = o; o += 6 * DC
    V_CW = o; o += 3 * FC
    V_CB = o; o += FC
    V_GDQ = o; o += QC
    V_GDKV = o; o += KC2
    V_GQN = o; o += 1
    V_GQP = o; o += 1
    V_GKP = o; o += 1
    V_GKN = o; o += 1
    V_GQ = o; o += 1
    V_GK = o; o += 1
    assert o == NV

    def vcol(i):
        return vec.ap[:, i:i + 1]

    def modcol(m, c, jj):
        return mod.ap[:, m * DC + c, jj:jj + 1]

    NWS = cfg.NWS

    mset(ones_f.ap, 1.0, writes=[ones_f]); mset(ones_b.ap, 1.0, writes=[ones_b]); mset(eps_c.ap, EPS, writes=[eps_c])
    dma("sp", bones_f.ap, bones[:, :], bones_f, writes=[bones_f])
    dma("sp", Rg_f.ap, RgT[:, :], Rg_f, writes=[Rg_f])
    dma("sp", Rm_f.ap, RmT[:, :], Rm_f, writes=[Rm_f])
    dma("sp", sel_f.ap, sel[:, :], sel_f, writes=[sel_f])
    dma("sp", sv_f.ap, cvec[:, :], sv_f, writes=[sv_f])
    act(sv_b.ap.rearrange("p a b -> p (a b)"), sv_f.ap, AF.Silu, reads=[sv_f], writes=[sv_b])

    lat_tiles = [("lat", t0, min(W1, TL - t0)) for t0 in range(0, TL, W1)]
    ctx_tiles = [("ctx", t0, min(W1, C - t0)) for t0 in range(0, C, W1)]
    groups = [list(range(g * NR, (g + 1) * NR)) for g in range(8 // NR)]

    def xsrc(kind, l):
        if kind == "lat":
            return (xT_in if l == 0 else xT_s), Dx
        return (ctxT_in if l == 0 else ctxT_s), Dctx

    def fm3(dram2d, c0, n):
        return dram2d.rearrange("(c p) t -> p c t", p=128)[:, :, c0:c0 + n]

    def rstd_from_ss(ssps, n, nd, rs_sq, rs):
        act(rs_sq.ap[:, :n], ssps.ap[:, :n], AF.Sqrt, reads=[ssps, eps_c], writes=[rs_sq], bias=eps_c.ap[:, 0:1], scale=1.0 / nd)
        recip(rs.ap[:, :n], rs_sq.ap[:, :n], reads=[rs_sq], writes=[rs])

    for l in range(L):
        last = (l == L - 1)
        is_mla = (l % 2 == 0)
        j = l // 2
        P.new_epoch()
        NH = MH if is_mla else GH
        NKV = MH if is_mla else GKV
        GRP = NH // NKV
        KV = kvs["m" if is_mla else "g"]
        kT_src, kT_all, v_src, v_all, kT_ctx, v_ctx = KV["kT_src"], KV["kT_all"], KV["v_src"], KV["v_all"], KV["kT_ctx"], KV["v_ctx"]
        HB, NB = KV["HB"], KV["NB"]

        def ksrc_rows(hd):
            return kT_src[hd // HB][(hd % HB) * 128:(hd % HB + 1) * 128, :]

        def vsrc_cols(tok0, ntok, h0, nh):
            out = []
            hd = h0
            while hd < h0 + nh:
                k = min(HB - hd % HB, h0 + nh - hd)
                out.append((v_src[hd // HB][tok0:tok0 + ntok, (hd % HB) * 128:(hd % HB + k) * 128], (hd - h0) * 128, k * 128))
                hd += k
            return out

        phase_reset(PERSIST)
        dma("sp", vec.ap, vecs[l], vec, writes=[vec])
        wsm = [carve([DC, 512], BF16) for _ in range(3)]
        modps = psum[0]
        modps.new_gen()
        nfg = (6 * DC + 3) // 4
        wm3 = w_mod[l].rearrange("(kc p) f -> p kc f", p=128)
        for fg in range(nfg):
            nf = min(4, 6 * DC - fg * 4)
            wb = wsm[fg % 3]
            dma("pool", wb.ap[:, :, :nf * 128], wm3[:, :, fg * 512:fg * 512 + nf * 128], wb, writes=[wb])
            for fc in range(nf):
                col = (fg * 4 + fc) * 2
                for kc in range(DC):
                    mm(modps.ap[:, col:col + 2], wb.ap[:, kc, fc * 128:(fc + 1) * 128], sv_b.ap[:, kc, :], kc == 0, kc == DC - 1,
                       reads=[wb, sv_b], pwrites=[modps])
        mp3 = modps.ap[:, 0:12 * DC].rearrange("p (a b) -> p a b", b=2)
        for jj in range(2):
            tt(mod.ap[:, :, jj], mp3[:, :, jj], vec.ap[:, V_BMOD:V_BMOD + 6 * DC], ALU.add, reads=[modps, vec],
               writes=[mod] if jj == 0 else (), pwrites=[mod] if jj else ())
        for (Ab, voff, m) in ((A1, V_NMIX, 1), (A2, V_NFFN, 4)):
            for jj in range(2):
                stt(Ab.ap[:, :, jj], mod.ap[:, m * DC:(m + 1) * DC, jj], 1.0, vec.ap[:, voff:voff + DC], ALU.add, ALU.mult,
                    reads=[mod, vec], writes=[Ab] if jj == 0 else (), pwrites=[Ab] if jj else ())
        if is_mla:
            sc = 1.0 / math.sqrt(192.0)
            glist = [(0, V_GQN, sc), (1, V_GQP, sc), (2, V_GKP, 1.0), (3, V_GKN, 1.0)]
        else:
            sc = 1.0 / math.sqrt(128.0)
            glist = [(0, V_GQ, sc), (1, V_GK, 1.0)]
        for gi, (gidx, vo, mul) in enumerate(glist):
            tsm(gsc.ap[:, gidx:gidx + 1], vec.ap[:, vo:vo + 1], mul, reads=[vec], writes=[gsc] if gi == 0 else (), pwrites=[gsc] if gi else ())
        P.barrier()

        phase_reset(PERSIST)
        xt = [carve([DC, W1], F32) for _ in range(1)]
        hT = [carve([DC, W1], BF16, False) for _ in range(2)]
        sq = [carve([W1], F32, False) for _ in range(2)]
        rs_sq = carve([W1], F32, False); rs = carve([W1], F32, False)
        tmpf = [carve([W1], F32, False) for _ in range(2)]
        cs = [carve([W1], F32) for _ in range(2)]; sn = [carve([W1], F32) for _ in range(2)]
        NSET = 3
        nsets = [dict(sq=carve([W1], F32, False), rs_sq=carve([W1], F32, False), rs=carve([W1], F32, False),
                      qn=carve([W1], F32, False), t1=carve([W1], F32, False), t2=carve([W1], F32, False)) for _ in range(NSET)]
        stg = [carve([W1], BF16) for _ in range(3)]
        vstg = [carve([512], BF16) for _ in range(2)]
        ws = [carve([max(DC, QC, KC2), 512], BF16) for _ in range(NWS)]
        if is_mla:
            cq_f = carve([max(QC, KC2), W1], F32, False); cqn = carve([QC, W1], BF16, False); ckvn = carve([KC2, W1], BF16, False)
        cnt = dict(ws=0, stg=0, vstg=0, pi=0)

        def wload(ring, src3, kc, m, split=1):
            b = ring[cnt["ws"] % len(ring)]; cnt["ws"] += 1
            view = b.ap.rearrange("p a b -> p (a b)")[:, :kc * m].rearrange("p (a b) -> p a b", a=kc)
            step = (kc + split - 1) // split
            for i, k0 in enumerate(range(0, kc, step)):
                k1 = min(kc, k0 + step)
                dma("pool", view[:, k0:k1, :], src3[:, k0:k1, :], b, writes=[b] if i == 0 else (), pwrites=[b] if i else ())
            return b, view

        def wload4(ring, src4, kc, nh):
            b = ring[cnt["ws"] % len(ring)]; cnt["ws"] += 1
            view = b.ap.rearrange("p a b -> p (a b)")[:, :kc * nh * 128].rearrange("p (a h e) -> p a h e", a=kc, h=nh)
            for hq in range(nh):
                dma("pool", view[:, :, hq, :], src4[:, :, hq, :], b, writes=[b] if hq == 0 else (), pwrites=[b] if hq else ())
            return b, view

        def wload2(ring, srcA, srcB, kc):
            b = ring[cnt["ws"] % len(ring)]; cnt["ws"] += 1
            view = b.ap.rearrange("p a b -> p (a b)")[:, :kc * 128].rearrange("p (a b) -> p a b", a=kc)
            dma("pool", view[:, :, 0:64], srcA, b, writes=[b])
            dma("pool", view[:, :, 64:128], srcB, b, pwrites=[b])
            return b, view

        def modulate(xb, n, Ab, shm, jj, hb, sq, rs_sq, rs, tmpf):
            ssps = psum[1]
            for c in range(DC):
                s_ = sq[c % 2]
                act(s_.ap[:, :n], xb.ap[:, c, :n], AF.Square, reads=[xb], writes=[s_])
                mm(ssps.ap[:, :n], ones_f.ap, s_.ap[:, :n], c == 0, c == DC - 1, reads=[s_, ones_f],
                   writes=[ssps] if c == 0 else (), pwrites=() if c == 0 else [ssps])
            rstd_from_ss(ssps, n, float(D), rs_sq, rs)
            hb.new_gen()
            for c in range(DC):
                tf = tmpf[c % 2]
                stt(tf.ap[:, :n], xb.ap[:, c, :n], Ab.ap[:, c, jj:jj + 1], rs.ap[:, :n], ALU.mult, ALU.mult, reads=[xb, Ab, rs], writes=[tf])
                act(hb.ap[:, c, :n], tf.ap[:, :n], AF.Identity, reads=[tf, mod], pwrites=[hb], bias=modcol(shm, c, jj), scale=1.0)

        def norm_rope_store(ps, n, nd, onesb, gcol, rope, Rb, cb, sb_, dst_ap, Ddst):
            k_ = cnt["stg"]
            S_ = nsets[k_ % NSET]
            s_ = S_["sq"]; rs_sq = S_["rs_sq"]; rs = S_["rs"]; qn = S_["qn"]; t1 = S_["t1"]; t2 = S_["t2"]
            act(s_.ap[:, :n], ps.ap[:, :n], AF.Square, reads=[ps], writes=[s_])
            ssps = psum[1 + k_ % 2]
            mm(ssps.ap[:, :n], onesb.ap, s_.ap[:, :n], True, True, reads=[s_, onesb], writes=[ssps])
            rstd_from_ss(ssps, n, float(nd), rs_sq, rs)
            st = stg[cnt["stg"] % 3]; cnt["stg"] += 1
            if not rope:
                stt(st.ap[:, :n], ps.ap[:, :n], gcol, rs.ap[:, :n], ALU.mult, ALU.mult, reads=[ps, rs, gsc], writes=[st])
            else:
                stt(qn.ap[:, :n], ps.ap[:, :n], gcol, rs.ap[:, :n], ALU.mult, ALU.mult, reads=[ps, rs, gsc], writes=[qn])
                rps = psum[7] if k_ % 2 else psum[0]
                mm(rps.ap[:, :n], Rb.ap, qn.ap[:, :n], True, True, reads=[Rb, qn], writes=[rps])
                tt(t1.ap[:, :n], qn.ap[:, :n], cb.ap[:, :n], ALU.mult, reads=[qn, cb], writes=[t1])
                tt(t2.ap[:, :n], rps.ap[:, :n], sb_.ap[:, :n], ALU.mult, reads=[rps, sb_], writes=[t2])
                tt(st.ap[:, :n], t1.ap[:, :n], t2.ap[:, :n], ALU.add, reads=[t1, t2], writes=[st])
            dma("sp", dst_ap, st.ap[:, :n], st, reads=[st], pwrites=[Ddst])

        def proj(ps, wview, wb, src, kcs, n):
            for kc in range(kcs):
                mm(ps.ap[:, :n], wview[:, kc, :], src.ap[:, kc, :n], kc == 0, kc == kcs - 1, reads=[wb, src],
                   writes=[ps] if kc == 0 else (), pwrites=() if kc == 0 else [ps])

        def nextps():
            p_ = psum[3 + cnt["pi"] % 2]; cnt["pi"] += 1
            return p_

        def vproj(src, kcs, n, wbs, dst_fn, Ddst):
            ncol = len(wbs) * 128
            for s0 in range(0, n, 128):
                vps = psum[5 + (cnt["vstg"] % 2)]
                vps.new_gen()
                for qi, (wb, wv_) in enumerate(wbs):
                    for kc in range(kcs):
                        mm(vps.ap[:, qi * 128:(qi + 1) * 128], src.ap[:, kc, s0:s0 + 128], wv_[:, kc, :], kc == 0, kc == kcs - 1,
                           reads=[src, wb], pwrites=[vps])
                vs = vstg[cnt["vstg"] % 2]; cnt["vstg"] += 1
                act(vs.ap[:, :ncol], vps.ap[:, :ncol], AF.Copy, reads=[vps], writes=[vs])
                for (dap, coff, ncl) in dst_fn(s0):
                    dma("sp", dap, vs.ap[:, coff:coff + ncl], vs, reads=[vs], pwrites=[Ddst])

        for D_ in (Dq, Dqpe, Dksrc, Dvsrc, Dkpesrc, Dkctx, Dvctx, Dkpectx):
            D_.new_gen()
        tiles = lat_tiles + ctx_tiles
        for ti, (kind, t0, n) in enumerate(tiles):
            jj = 0 if kind == "lat" else 1
            tokq = t0 if kind == "lat" else TL + t0
            xb = xt[0]; hb = hT[ti % 2]
            src_d, Dsrc = xsrc(kind, l)
            dma("sp", xb.ap[:, :, :n], fm3(src_d, t0, n), xb, reads=[Dsrc], writes=[xb])
            modulate(xb, n, A1, 0, jj, hb, sq, rs_sq, rs, tmpf)
            rope = (kind == "lat")
            need_q = not (kind == "ctx" and last)
            cb = sb_ = None
            if rope:
                cb = cs[ti % 2]; sb_ = sn[ti % 2]
                dma("sp", cb.ap[:, :n], (cosm if is_mla else cosg)[:, t0:t0 + n], cb, writes=[cb])
                dma("sp", sb_.ap[:, :n], (sinm if is_mla else sing)[:, t0:t0 + n], sb_, writes=[sb_])
            kdst = (lambda r0, r1: ksrc_rows(r0 // 128)[:, t0:t0 + n]) if kind == "lat" else (lambda r0, r1: kT_ctx[r0:r1, t0:t0 + n])
            Dk = Dksrc if kind == "lat" else Dkctx
            Dv = Dvsrc if kind == "lat" else Dvctx
            if not is_mla:
                wq = gqa_w_q[j].rearrange("(kc p) m -> p kc m", p=128)
                wkv = gqa_w_kv[j].rearrange("(kc p) m -> p kc m", p=128)
                if need_q:
                    for h0 in range(0, GH, 4):
                        nh = min(4, GH - h0)
                        wb, wv_ = wload(ws, wq[:, :, h0 * 128:(h0 + nh) * 128], DC, nh * 128)
                        for hq in range(nh):
                            h = h0 + hq
                            ps = nextps()
                            proj(ps, wv_[:, :, hq * 128:(hq + 1) * 128], wb, hb, DC, n)
                            norm_rope_store(ps, n, 128, ones_f, gsc.ap[:, 0:1], rope, Rg_f, cb, sb_, qT_d[h * 128:(h + 1) * 128, tokq:tokq + n], Dq)
                for g0 in range(0, GKV, 4):
                    ng = min(4, GKV - g0)
                    wb, wv_ = wload(ws, wkv[:, :, g0 * 128:(g0 + ng) * 128], DC, ng * 128)
                    for gq in range(ng):
                        g = g0 + gq
                        ps = nextps()
                        proj(ps, wv_[:, :, gq * 128:(gq + 1) * 128], wb, hb, DC, n)
                        norm_rope_store(ps, n, 128, ones_f, gsc.ap[:, 1:2], rope, Rg_f, cb, sb_, kdst(g * 128, (g + 1) * 128), Dk)
                for c0 in range(0, GKV, 4):
                    ng = min(4, GKV - c0)
                    wb, wv_ = wload(ws, wkv[:, :, (GKV + c0) * 128:(GKV + c0 + ng) * 128], DC, ng * 128)
                    wbs = [(wb, wv_[:, :, q4 * 128:(q4 + 1) * 128]) for q4 in range(ng)]
                    if kind == "lat":
                        vproj(hb, DC, n, wbs, (lambda s0, c0=c0, ng=ng: vsrc_cols(t0 + s0, 128, c0, ng)), Dv)
                    else:
                        vproj(hb, DC, n, wbs, (lambda s0, c0=c0, ng=ng: [(v_ctx[t0 + s0:t0 + s0 + 128, c0 * 128:(c0 + ng) * 128], 0, ng * 128)]), Dv)
            else:
                wdq = mla_w_dq[j].rearrange("(kc p) m -> p kc m", p=128)
                wuq = mla_w_uq[j].rearrange("(kc p) m -> p kc m", p=128)
                wdkv = mla_w_dkv[j].rearrange("(kc p) m -> p kc m", p=128)
                wukv = mla_w_ukv[j].rearrange("(kc p) m -> p kc m", p=128)

                def compress(wsrc, nchunk, gv_off, outn):
                    ssps = psum[1]
                    outn.new_gen(); cq_f.new_gen()
                    for oc in range(nchunk):
                        if oc % 4 == 0:
                            nq = min(4, nchunk - oc)
                            wb, wvw = wload(ws, wsrc[:, :, oc * 128:(oc + nq) * 128], DC, nq * 128)
                        wv_ = wvw[:, :, (oc % 4) * 128:(oc % 4 + 1) * 128]
                        ps = nextps()
                        proj(ps, wv_, wb, hb, DC, n)
                        act(cq_f.ap[:, oc, :n], ps.ap[:, :n], AF.Copy, reads=[ps], pwrites=[cq_f])
                        s_ = sq[oc % 2]
                        act(s_.ap[:, :n], ps.ap[:, :n], AF.Square, reads=[ps], writes=[s_])
                        mm(ssps.ap[:, :n], ones_f.ap, s_.ap[:, :n], oc == 0, oc == nchunk - 1, reads=[s_, ones_f],
                           writes=[ssps] if oc == 0 else (), pwrites=() if oc == 0 else [ssps])
                    rstd_from_ss(ssps, n, float(nchunk * 128), rs_sq, rs)
                    for oc in range(nchunk):
                        stt(outn.ap[:, oc, :n], cq_f.ap[:, oc, :n], vcol(gv_off + oc), rs.ap[:, :n], ALU.mult, ALU.mult,
                            reads=[cq_f, rs, vec], pwrites=[outn])

                if need_q:
                    compress(wdq, QC, V_GDQ, cqn)
                    wuq4 = wuq.rearrange("p k (h e) -> p k h e", e=192)
                    for h in range(MH):
                        if h % 4 == 0:
                            nq = min(4, MH - h)
                            wb, wvw = wload4(ws, wuq4[:, :, h:h + nq, 0:128], QC, nq)
                        wv_ = wvw[:, :, h % 4, :]
                        ps = nextps()
                        proj(ps, wv_, wb, cqn, QC, n)
                        norm_rope_store(ps, n, 128, ones_f, gsc.ap[:, 0:1], False, None, None, None, qT_d[h * 128:(h + 1) * 128, tokq:tokq + n], Dq)
                    for hp in range(MH // 2):
                        h0 = 2 * hp
                        wb, wv_ = wload2(ws, wuq[:, :, h0 * 192 + 128:h0 * 192 + 192], wuq[:, :, (h0 + 1) * 192 + 128:(h0 + 1) * 192 + 192], QC)
                        ps = nextps()
                        proj(ps, wv_, wb, cqn, QC, n)
                        norm_rope_store(ps, n, 64, bones_f, gsc.ap[:, 1:2], rope, Rm_f, cb, sb_, qpeT_d[hp * 128:(hp + 1) * 128, tokq:tokq + n], Dqpe)
                compress(wdkv, KC2, V_GDKV, ckvn)
                wb, wv_ = wload2(ws, wdkv[:, :, KVR:KVR + 64], wdkv[:, :, KVR:KVR + 64], DC)
                ps = nextps()
                proj(ps, wv_, wb, hb, DC, n)
                if kind == "lat":
                    norm_rope_store(ps, n, 64, bones_f, gsc.ap[:, 2:3], True, Rm_f, cb, sb_, kpe_src[:, t0:t0 + n], Dkpesrc)
                else:
                    norm_rope_store(ps, n, 64, bones_f, gsc.ap[:, 2:3], False, None, None, None, kpe_ctx[:, t0:t0 + n], Dkpectx)
                wukv4 = wukv.rearrange("p k (h e) -> p k h e", e=256)
                for h in range(MH):
                    if h % 4 == 0:
                        nq = min(4, MH - h)
                        wb, wvw = wload4(ws, wukv4[:, :, h:h + nq, 0:128], KC2, nq)
                    wv_ = wvw[:, :, h % 4, :]
                    ps = nextps()
                    proj(ps, wv_, wb, ckvn, KC2, n)
                    norm_rope_store(ps, n, 128, ones_f, gsc.ap[:, 3:4], False, None, None, None, kdst(h * 128, (h + 1) * 128), Dk)
                for h0 in range(0, MH, 4):
                    nh = min(4, MH - h0)
                    wb, wvw = wload4(ws, wukv4[:, :, h0:h0 + nh, 128:256], KC2, nh)
                    wbs = [(wb, wvw[:, :, q4, :]) for q4 in range(nh)]
                    if kind == "lat":
                        vproj(ckvn, KC2, n, wbs, (lambda s0, h0=h0, nh=nh: vsrc_cols(t0 + s0, 128, h0, nh)), Dv)
                    else:
                        vproj(ckvn, KC2, n, wbs, (lambda s0, h0=h0, nh=nh: [(v_ctx[t0 + s0:t0 + s0 + 128, h0 * 128:(h0 + nh) * 128], 0, nh * 128)]), Dv)
        P.barrier()

        Dkall.new_gen(); Dvall.new_gen()
        for bi in range(NB):
            P.cc("AllGather", groups, kT_src[bi][:, :], kT_all[bi][:, :], reads=[Dksrc], pwrites=[Dkall])
            P.cc("AllGather", groups, v_src[bi][:, :], v_all[bi][:, :], reads=[Dvsrc], pwrites=[Dvall])
        if is_mla:
            P.cc("AllGather", groups, kpe_src[:, :], kpe_all[:, :], reads=[Dkpesrc], writes=[Dkpeall])
        P.barrier()

        phase_reset(PERSIST)
        oT = carve([NH, NTOK], BF16, False)
        P3BASE = apos[0]
        KA = [carve([T], BF16) for _ in range(2)]
        VV = [carve([NCH, 128], BF16) for _ in range(2)]
        KBm = [carve([T], BF16) for _ in range(2)] if is_mla else None
        QA = [carve([W1], BF16) for _ in range(2)]
        QB = [carve([W1], BF16) for _ in range(2)] if is_mla else None
        pT = [carve([W1], BF16, False) for _ in range(4)]
        rec = carve([W1], F32, False)
        accD = [carve([W1], F32, False) for _ in range(2)]
        if is_mla:
            kpa = kpe_all.rearrange("(r d) t -> d r t", d=128)
            for hf in range(2):
                lo, hi = hf * 64, hf * 64 + 64
                zl, zh = (64, 128) if hf == 0 else (0, 64)
                mset(KBm[hf].ap[zl:zh, :], 0.0, writes=[KBm[hf]])
                dma("sp", KBm[hf].ap[lo:hi, 0:NR * TL].rearrange("p (r t) -> p r t", r=NR), kpa[lo:hi], KBm[hf], reads=[Dkpeall], pwrites=[KBm[hf]])
                dma("sp", KBm[hf].ap[lo:hi, NR * TL:T], kpe_ctx[lo:hi, :], KBm[hf], reads=[Dkpectx], pwrites=[KBm[hf]])
        wup = ffn_w_up[l].rearrange("(kc p) m -> p kc m", p=128)
        wdn = ffn_w_down[l].rearrange("(kc p) m -> p kc m", p=128)
        Dwc.new_gen()
        ncv = 0
        for g in range(NUT):
            nq = min(2, FC - 2 * g)
            for part in range(2):
                ln = cvl[ncv % len(cvl)]; ncv += 1
                dma("pool", wupT[2 * g + part][:, :DC * nq * 128].rearrange("p (kc m) -> p kc m", kc=DC),
                    wup[:, :, part * FF + g * 256:part * FF + g * 256 + nq * 128], ln, writes=[ln], pwrites=[Dwc])
        for g in range(NDT):
            nq = min(2, DC - 2 * g)
            ln = cvl[ncv % len(cvl)]; ncv += 1
            dma("pool", wdnT[g][:, :FC * nq * 128].rearrange("p (kc m) -> p kc m", kc=FC),
                wdn[:, :, g * 256:g * 256 + nq * 128], ln, writes=[ln], pwrites=[Dwc])
        qtiles = lat_tiles + ([] if last else ctx_tiles)
        oT.new_gen()
        it = 0
        kall4 = [a.rearrange("(r g d) t -> d g r t", r=NR, g=HB) for a in kT_all]
        vall4 = [a.rearrange("(ch p) (g d) -> p ch g d", p=128, d=128) for a in v_all]
        vctx4 = v_ctx.rearrange("(ch p) (g d) -> p ch g d", p=128, d=128)
        for g in range(NKV):
            ka = KA[g % 2]; vv = VV[g % 2]
            dma("sp", ka.ap[:, 0:NR * TL].rearrange("p (r t) -> p r t", r=NR), kall4[g // HB][:, g % HB], ka, reads=[Dkall], writes=[ka])
            dma("sp", ka.ap[:, NR * TL:T], kT_ctx[g * 128:(g + 1) * 128, :], ka, reads=[Dkctx], pwrites=[ka])
            dma("sp", vv.ap[:, 0:NR * TL // 128, :], vall4[g // HB][:, :, g % HB, :], vv, reads=[Dvall], writes=[vv])
            dma("sp", vv.ap[:, NR * TL // 128:NCH, :], vctx4[:, :, g, :], vv, reads=[Dvctx], pwrites=[vv])
            for hh in range(GRP):
                h = g * GRP + hh
                hp = (h % 2) * 64
                for (kind, t0, n) in qtiles:
                    tokq = t0 if kind == "lat" else TL + t0
                    qa = QA[it % 2]; qb = QB[it % 2] if is_mla else None
                    ops_ = psum[3 + it % 2]; sums = psum[5 + it % 2]
                    it += 1
                    dma("sp", qa.ap[:, :n], qT_d[h * 128:(h + 1) * 128, tokq:tokq + n], qa, reads=[Dq], writes=[qa])
                    if is_mla:
                        dma("sp", qb.ap[:, :n], qpeT_d[(h // 2) * 128:(h // 2 + 1) * 128, tokq:tokq + n], qb, reads=[Dqpe], writes=[qb])
                    chunks = list(range(NCH)) if kind == "lat" else list(range(NR * TL // 128, NCH))
                    nchk = len(chunks)

                    aD = accD[it % 2]
                    kbm = KBm[h % 2] if is_mla else None
                    pe_chunks = [ci for ci in range(nchk) if (not is_mla) and ci % 4 == 3]
                    dve_chunks = [ci for ci in range(nchk) if ci not in pe_chunks]

                    def emit_S(ci):
                        ch = chunks[ci]
                        sp_ = psum[ci % 3]
                        mm(sp_.ap[:, :n], ka.ap[:, ch * 128:(ch + 1) * 128], qa.ap[:, :n], True, not is_mla, reads=[ka, qa], writes=[sp_])
                        if is_mla:
                            mm(sp_.ap[:, :n], kbm.ap[:, ch * 128:(ch + 1) * 128], qb.ap[:, :n], False, True, reads=[kbm, qb], pwrites=[sp_])
                        pt = pT[ci % 4]
                        act(pt.ap[:, :n], sp_.ap[:, :n], AF.Exp, reads=[sp_], writes=[pt])

                    def emit_PV(ci):
                        ch = chunks[ci]
                        pt = pT[ci % 4]
                        mm(ops_.ap[:, :n], vv.ap[:, ch, :], pt.ap[:, :n], ci == 0, ci == nchk - 1, reads=[vv, pt],
                           writes=[ops_] if ci == 0 else (), pwrites=() if ci == 0 else [ops_])
                        if ci in pe_chunks:
                            mm(sums.ap[:, :n], ones_b.ap, pt.ap[:, :n], ci == pe_chunks[0], False, reads=[ones_b, pt],
                               writes=[sums] if ci == pe_chunks[0] else (), pwrites=() if ci == pe_chunks[0] else [sums])
                        elif ci == dve_chunks[0]:
                            cpy(aD.ap[:, :n], pt.ap[:, :n], reads=[pt], writes=[aD])
                        else:
                            tt(aD.ap[:, :n], aD.ap[:, :n], pt.ap[:, :n], ALU.add, reads=[pt, aD], pwrites=[aD])

                    emit_S(0)
                    if nchk > 1:
                        emit_S(1)
                    for ci in range(nchk):
                        if ci + 2 < nchk:
                            emit_S(ci + 2)
                        emit_PV(ci)
                    mm(sums.ap[:, :n], ones_f.ap, aD.ap[:, :n], not pe_chunks, True, reads=[ones_f, aD],
                       writes=[sums] if not pe_chunks else (), pwrites=[sums] if pe_chunks else ())
                    recip(rec.ap[:, :n], sums.ap[:, :n], reads=[sums], writes=[rec])
                    tt(oT.ap[:, h, tokq:tokq + n], ops_.ap[:, :n], rec.ap[:, :n], ALU.mult, reads=[ops_, rec], pwrites=[oT])
        P.barrier()

        phase_reset(P3BASE)
        xt3 = [carve([DC, W1], F32) for _ in range(2)]
        h3 = [carve([DC, W1], BF16) for _ in range(1)]
        sq3 = [carve([W1], F32, False) for _ in range(2)]
        rs_sq3 = carve([W1], F32, False); rs3 = carve([W1], F32, False)
        tmpf3 = [carve([W1], F32, False) for _ in range(2)]
        ws3 = [carve([NH, 256], BF16) for _ in range(NWS)]
        w_o = (mla_w_o if is_mla else gqa_w_o)[j].rearrange("(kc p) m -> p kc m", p=128)
        tiles3 = lat_tiles + ([] if last else ctx_tiles)
        Dh2.new_gen(); Dh2c.new_gen(); bnd.new_gen()
        for ti, (kind, t0, n) in enumerate(tiles3):
            jj = 0 if kind == "lat" else 1
            tokq = t0 if kind == "lat" else TL + t0
            xb = xt3[ti % 2]; hb = h3[0]
            src_d, Dsrc = xsrc(kind, l)
            dst_d = xT_s if kind == "lat" else ctxT_s
            dma("sp", xb.ap[:, :, :n], fm3(src_d, t0, n), xb, reads=[Dsrc], writes=[xb])
            for dc in range(DC):
                if dc % 2 == 0:
                    nq = min(2, DC - dc)
                    wb, wvw = wload(ws3, w_o[:, :, dc * 128:(dc + nq) * 128], NH, nq * 128)
                wv_ = wvw[:, :, (dc % 2) * 128:(dc % 2 + 1) * 128]
                ps = nextps()
                for kc in range(NH):
                    mm(ps.ap[:, :n], wv_[:, kc, :], oT.ap[:, kc, tokq:tokq + n], kc == 0, kc == NH - 1, reads=[wb, oT],
                       writes=[ps] if kc == 0 else (), pwrites=() if kc == 0 else [ps])
                stt(xb.ap[:, dc, :n], ps.ap[:, :n], modcol(2, dc, jj), xb.ap[:, dc, :n], ALU.mult, ALU.add, reads=[ps, mod, xb], pwrites=[xb])
            dma("sp", fm3(dst_d, t0, n), xb.ap[:, :, :n], xb, reads=[xb], pwrites=[Dsrc])
            modulate(xb, n, A2, 3, jj, hb, sq3, rs_sq3, rs3, tmpf3)
            if kind == "lat":
                dma("sp", fm3(h2T_d, 1 + t0, n), hb.ap[:, :, :n], hb, reads=[hb], pwrites=[Dh2])
                if t0 == 0:
                    cpy(bnd.ap[:, :, 0], hb.ap[:, :, 0], reads=[hb], pwrites=[bnd])
                if t0 + n == TL:
                    cpy(bnd.ap[:, :, 1], hb.ap[:, :, n - 1], reads=[hb], pwrites=[bnd])
            else:
                dma("sp", fm3(h2cT_d, 1 + t0, n), hb.ap[:, :, :n], hb, reads=[hb], pwrites=[Dh2c])
        dma("sp", hsrc[:, :], bnd.ap.rearrange("p a b -> p (a b)"), bnd, reads=[bnd], writes=[Dhsrc])
        P.barrier()
        P.cc("AllGather", groups, hsrc[:, :], hall[:, :], reads=[Dhsrc], writes=[Dhall])
        dma("sp", hs.ap, hall.rearrange("(r p) (c t) -> p r c t", p=128, t=2), hs, reads=[Dhall], writes=[hs])
        for side in range(2):
            for r in range(NR):
                srcc = hs.ap[:, r, :, 1 - side]
                scol = sel_f.ap[:, side * NR + r:side * NR + r + 1]
                if r == 0:
                    tsm(halo_f.ap[:, :, side], srcc, scol, reads=[hs, sel_f], writes=[halo_f] if side == 0 else (), pwrites=() if side == 0 else [halo_f])
                else:
                    stt(halo_f.ap[:, :, side], srcc, scol, halo_f.ap[:, :, side], ALU.mult, ALU.add, reads=[hs, sel_f, halo_f], pwrites=[halo_f])
        cpy(halo.ap, halo_f.ap, reads=[halo_f], writes=[halo])
        P.barrier()

        phase_reset(PERSIST)
        WIN = W4 + 2
        h2w = [carve([DC, WIN], BF16) for _ in range(2)]
        aT = carve([FC, W4], BF16, False)
        xw = [carve([DC, W4], F32) for _ in range(1)]
        c1 = [carve([W4], F32, False) for _ in range(2)]; c2 = [carve([W4], F32, False) for _ in range(2)]
        c3 = [carve([W4], F32, False) for _ in range(2)]; sg = [carve([W4], F32, False) for _ in range(2)]
        wsu = [carve([DC, 256], BF16) for _ in range(cfg.NWU)]
        wsd = [carve([FC, 256], BF16) for _ in range(cfg.NWD)]
        def wload_bf(ring, src2, kc, m):
            b = ring[cnt["ws"] % len(ring)]; cnt["ws"] += 1
            v2 = b.ap.rearrange("p a b -> p (a b)")[:, :kc * m]
            dma("pool", v2, src2, b, reads=[Dwc], writes=[b])
            return b, v2.rearrange("p (a b) -> p a b", a=kc)

        wins = [("lat", s, min(W4, TL - s)) for s in range(0, TL, W4)]
        if not last:
            wins += [("ctx", s, min(W4, C - s)) for s in range(0, C, W4)]
        pi = 0
        for wi, (kind, s0, nout) in enumerate(wins):
            jj = 0 if kind == "lat" else 1
            nin = nout + 2
            hw = h2w[wi % 2]; xb = xw[0]
            tot = TL if kind == "lat" else C
            if kind == "lat":
                dma("sp", hw.ap[:, :, :nin], fm3(h2T_d, s0, nin), hw, reads=[Dh2], writes=[hw])
            else:
                dma("sp", hw.ap[:, :, :nin], fm3(h2cT_d, s0, nin), hw, reads=[Dh2c], writes=[hw])
            if s0 == 0:
                if kind == "lat":
                    cpy(hw.ap[:, :, 0], halo.ap[:, :, 0], reads=[halo, hw], pwrites=[hw])
                else:
                    mset(hw.ap[:, :, 0], 0.0, reads=[hw], pwrites=[hw])
            if s0 + nout == tot:
                if kind == "lat":
                    cpy(hw.ap[:, :, nin - 1], halo.ap[:, :, 1], reads=[halo, hw], pwrites=[hw])
                else:
                    mset(hw.ap[:, :, nin - 1], 0.0, reads=[hw], pwrites=[hw])
            src_d = xT_s if kind == "lat" else ctxT_s
            Dsrc = Dx if kind == "lat" else Dctx
            dma("sp", xb.ap[:, :, :nout], fm3(src_d, s0, nout), xb, reads=[Dsrc], writes=[xb])
            aT.new_gen()
            for fc in range(FC):
                if fc % 2 == 0:
                    nq = min(2, FC - fc)
                    wgb, wgw = wload_bf(wsu, wupT[fc + 0][:, :DC * nq * 128], DC, nq * 128)
                    wvb, wvw = wload_bf(wsu, wupT[fc + 1][:, :DC * nq * 128], DC, nq * 128)
                wgv = wgw[:, :, (fc % 2) * 128:(fc % 2 + 1) * 128]
                wvv = wvw[:, :, (fc % 2) * 128:(fc % 2 + 1) * 128]
                gps = psum[1 + pi % 2]; vps = psum[3 + pi % 2]; pi += 1
                for kc in range(DC):
                    mm(gps.ap[:, :nin], wgv[:, kc, :], hw.ap[:, kc, :nin], kc == 0, kc == DC - 1, reads=[wgb, hw],
                       writes=[gps] if kc == 0 else (), pwrites=() if kc == 0 else [gps])
                for kc in range(DC):
                    mm(vps.ap[:, :nin], wvv[:, kc, :], hw.ap[:, kc, :nin], kc == 0, kc == DC - 1, reads=[wvb, hw],
                       writes=[vps] if kc == 0 else (), pwrites=() if kc == 0 else [vps])
                a1 = c1[fc % 2]; a2 = c2[fc % 2]; a3 = c3[fc % 2]; sgb = sg[fc % 2]
                act(a1.ap[:, :nout], gps.ap[:, 1:1 + nout], AF.Identity, reads=[gps, vec], writes=[a1], bias=vcol(V_CB + fc), scale=vcol(V_CW + FC + fc))
                stt(a2.ap[:, :nout], gps.ap[:, 0:nout], vcol(V_CW + fc), a1.ap[:, :nout], ALU.mult, ALU.add, reads=[gps, vec, a1], writes=[a2])
                stt(a3.ap[:, :nout], gps.ap[:, 2:2 + nout], vcol(V_CW + 2 * FC + fc), a2.ap[:, :nout], ALU.mult, ALU.add, reads=[gps, vec, a2], writes=[a3])
                act(sgb.ap[:, :nout], a3.ap[:, :nout], AF.Silu, reads=[a3], writes=[sgb])
                tt(aT.ap[:, fc, :nout], sgb.ap[:, :nout], vps.ap[:, 1:1 + nout], ALU.mult, reads=[sgb, vps], pwrites=[aT])
            for dc in range(DC):
                if dc % 2 == 0:
                    nq = min(2, DC - dc)
                    wb, wdw = wload_bf(wsd, wdnT[dc // 2][:, :FC * nq * 128], FC, nq * 128)
                wv_ = wdw[:, :, (dc % 2) * 128:(dc % 2 + 1) * 128]
                ps = psum[5 + pi % 2]; pi += 1
                for kc in range(FC):
                    mm(ps.ap[:, :nout], wv_[:, kc, :], aT.ap[:, kc, :nout], kc == 0, kc == FC - 1, reads=[wb, aT],
                       writes=[ps] if kc == 0 else (), pwrites=() if kc == 0 else [ps])
                stt(xb.ap[:, dc, :nout], ps.ap[:, :nout], modcol(5, dc, jj), xb.ap[:, dc, :nout], ALU.mult, ALU.add, reads=[ps, mod, xb], pwrites=[xb])
            if kind == "lat" and last:
                dma("sp", fm3(yT, s0, nout), xb.ap[:, :, :nout], xb, reads=[xb], pwrites=[Dy])
            else:
                dst_d = xT_s if kind == "lat" else ctxT_s
                dma("sp", fm3(dst_d, s0, nout), xb.ap[:, :, :nout], xb, reads=[xb], pwrites=[Dsrc])
        P.barrier()

    P.emit("sp", lambda e: e.nop(), reads=[Dy])
    P.finalize()
    global LASTP
    LASTP = P
    return nc


def pack_vecs(cfg, inp, l):
    DC, FC = cfg.DC, cfg.FC
    j = l // 2
    cols = [_fm(inp["norm_mix"][l]), _fm(inp["norm_ffn"][l]), _fm(inp["b_mod"][l])]
    cw = np.asarray(inp["ffn_conv_w"][l], np.float32)
    cols += [_fm(cw[0]), _fm(cw[1]), _fm(cw[2]), _fm(inp["ffn_conv_b"][l])]
    QC = cfg.QR // 128; KC2 = cfg.KVR // 128
    z1 = np.zeros((128, 1), np.float32)
    if l % 2 == 0:
        dup = lambda v: np.concatenate([np.asarray(v, np.float32)] * 2)[:, None]
        cols += [_fm(inp["mla_g_dq"][j]), _fm(inp["mla_g_dkv"][j]), np.asarray(inp["mla_g_q_nope"][j], np.float32)[:, None],
                 dup(inp["mla_g_q_pe"][j]), dup(inp["mla_g_k_pe"][j]), np.asarray(inp["mla_g_k_nope"][j], np.float32)[:, None], z1, z1]
    else:
        cols += [np.zeros((128, QC), np.float32), np.zeros((128, KC2), np.float32), z1, z1, z1, z1,
                 np.asarray(inp["gqa_g_q"][j], np.float32)[:, None], np.asarray(inp["gqa_g_k"][j], np.float32)[:, None]]
    return np.concatenate(cols, axis=1).astype(np.float32)


def prep_inputs(cfg, inp):
    NR, TL, D, C = cfg.NR, cfg.TL, cfg.D, cfg.C
    L = cfg.depth
    f = lambda a: np.ascontiguousarray(np.asarray(a, np.float32))
    vecs = np.stack([pack_vecs(cfg, inp, l) for l in range(L)])
    RgT = rot_matrix_T(128)
    Rm = rot_matrix_T(64)
    RmT = np.zeros((128, 128), np.float32); RmT[:64, :64] = Rm; RmT[64:, 64:] = Rm
    bones = np.zeros((128, 128), np.float32); bones[:64, :64] = 1; bones[64:, 64:] = 1
    shared = dict(RgT=RgT, RmT=RmT, bones=bones, vecs=vecs, w_mod=f(inp["w_mod"]),
                  mla_w_dq=f(inp["mla_w_dq"]), mla_w_uq=f(inp["mla_w_uq"]), mla_w_dkv=f(inp["mla_w_dkv"]),
                  mla_w_ukv=f(inp["mla_w_ukv"]), mla_w_o=f(inp["mla_w_o"]),
                  gqa_w_q=f(inp["gqa_w_q"]), gqa_w_kv=f(inp["gqa_w_kv"]), gqa_w_o=f(inp["gqa_w_o"]),
                  ffn_w_up=f(inp["ffn_w_up"]), ffn_w_down=f(inp["ffn_w_down"]))
    x = np.asarray(inp["x"], np.float32); ctx = np.asarray(inp["ctx"], np.float32)
    c = np.asarray(inp["c"], np.float32); c_ctx = np.asarray(inp["c_ctx"], np.float32)
    maps = []
    for core in range(8):
        b = core // NR; r = core % NR
        t0 = r * TL
        m = dict(shared)
        m["xT"] = np.ascontiguousarray(x[b, t0:t0 + TL].T)
        m["ctxT"] = np.ascontiguousarray(ctx[b].T)
        cv = np.stack([_fm(c[b]), _fm(c_ctx)], axis=-1)
        m["cvec"] = np.ascontiguousarray(cv.reshape(128, -1))
        cg, sg = rope_tables(cfg, t0, TL, 128)
        cm, sm = rope_tables(cfg, t0, TL, 64)
        m["cosg"] = cg; m["sing"] = sg
        m["cosm"] = np.concatenate([cm, cm], 0); m["sinm"] = np.concatenate([sm, sm], 0)
        s = np.zeros((128, 2 * NR), np.float32)
        if r > 0:
            s[:, r - 1] = 1.0
        if r < NR - 1:
            s[:, NR + r + 1] = 1.0
        m["sel"] = s
        maps.append(m)
    return maps


_NC_CACHE = {}


def run(cfg, inp, debug=False):
    key = (cfg.D, cfg.S, cfg.C, cfg.FF, cfg.depth)
    if key not in _NC_CACHE:
        _NC_CACHE[key] = build(cfg, debug)
    nc = _NC_CACHE[key]
    maps = prep_inputs(cfg, inp)
    res = run_bass_kernel_spmd(nc, maps, core_ids=list(range(8)))
    out = np.zeros((cfg.B, cfg.S, cfg.D), np.float32)
    for core in range(8):
        b = core // cfg.NR; r = core % cfg.NR
        out[b, r * cfg.TL:(r + 1) * cfg.TL] = res.results[core]["yT"].T
    return out


def kernel(**inputs):
    return run(Cfg(), inputs)
```

```python
import math
import numpy as np
import concourse.bass as bass
import concourse.mybir as mybir
from concourse.bass_utils import run_bass_kernel_spmd

F32 = mybir.dt.float32
BF16 = mybir.dt.bfloat16
AF = mybir.ActivationFunctionType
ALU = mybir.AluOpType
EPS = 1e-6
ROPE_BASE = 10000.0


class Cfg:
    def __init__(s, **kw):
        s.D = 2048; s.S = 8192; s.B = 2; s.C = 256; s.FF = 5632; s.depth = 4; s.GRID_W = 64
        s.MH = 16; s.QR = 512; s.KVR = 512; s.GH = 16; s.GKV = 4
        s.W1 = 512; s.W4 = 410; s.NR = 4
        s.NWS = 3; s.NWU = 4; s.NWD = 2; s.WDSPLIT = 1
        for k, v in kw.items():
            setattr(s, k, v)
        s.TL = s.S // s.NR
        s.DC = s.D // 128; s.FC = s.FF // 128
        s.NTOK = s.TL + s.C


class Op:
    __slots__ = ("eng", "fn", "deps", "marked", "cum", "is_dma", "sem", "val", "inc", "idx", "epoch")


class Buf:
    def __init__(s, ap=None, sem=None):
        s.ap = ap; s.w = []; s.r = []; s.gen = []; s.sem = sem; s.cnt = 0

    def new_gen(s):
        s.gen = s.w + s.r; s.w = []; s.r = []


class Prog:
    def __init__(s, nc):
        s.nc = nc
        s.ops = {k: [] for k in ("pe", "act", "dve", "pool", "sp")}
        s.engsem = {}
        s.epoch = 0; s.nidx = 0
        s.ccsem = nc.alloc_semaphore("ccsem"); s.cccnt = 0

    def new_epoch(s):
        s.epoch += 1

    def esem(s, eng, epoch):
        k = (eng, epoch)
        if k not in s.engsem:
            s.engsem[k] = s.nc.alloc_semaphore("es_%s_%d" % (eng, epoch))
        return s.engsem[k]

    def emit(s, eng, fn, reads=(), writes=(), pwrites=(), extra=()):
        deps = list(extra)
        for b in reads:
            deps += b.w
        for b in writes:
            b.new_gen(); deps += b.gen
        for b in pwrites:
            deps += b.gen
        best = {}
        for d in deps:
            if d.is_dma:
                k = ("s", d.sem.num); v = d.val
            else:
                k = ("e", d.eng); v = d.idx
            o = best.get(k)
            if o is None or v > o[0]:
                best[k] = (v, d)
        deps = [o[1] for o in best.values()]
        op = Op(); op.eng = eng; op.fn = fn; op.deps = deps; op.marked = False; op.cum = 0
        op.is_dma = False; op.sem = None; op.val = 0; op.inc = 0
        op.idx = s.nidx; s.nidx += 1; op.epoch = s.epoch
        for b in reads:
            b.r.append(op)
        for b in writes:
            b.w.append(op)
        for b in pwrites:
            b.w.append(op)
        s.ops[eng].append(op)
        return op

    def dma(s, eng, out, in_, sb, reads=(), writes=(), pwrites=(), extra=()):
        op = s.emit(eng, lambda e: e.dma_start(out=out, in_=in_), reads, writes, pwrites, extra)
        sb.cnt += 16
        op.is_dma = True; op.sem = sb.sem; op.val = sb.cnt; op.inc = 16
        return op

    def cc(s, kind, groups, in_ap, out_ap, reads=(), writes=(), pwrites=()):
        op = s.emit("pool", lambda e: e.collective_compute(kind, ALU.bypass, replica_groups=groups,
                                                            ins=[in_ap], outs=[out_ap]), reads, writes, pwrites)
        s.cccnt += 1
        op.is_dma = True; op.sem = s.ccsem; op.val = s.cccnt; op.inc = 1
        return op

    def barrier(s):
        deps = []
        latest = {}
        for eng, ops in s.ops.items():
            lastc = None
            for op in ops:
                if op.is_dma:
                    latest[op.sem.num] = op
                else:
                    lastc = op
            if lastc is not None:
                deps.append(lastc)
        deps += list(latest.values())
        for eng in s.ops:
            s.emit(eng, lambda e: e.nop(), extra=deps)

    def finalize(s):
        nc = s.nc
        for eng, ops in s.ops.items():
            for op in ops:
                for d in op.deps:
                    if not d.is_dma:
                        d.marked = True
        for eng, ops in s.ops.items():
            cnt = {}
            for op in ops:
                if (not op.is_dma) and op.marked:
                    cnt[op.epoch] = cnt.get(op.epoch, 0) + 1
                    s.esem(eng, op.epoch)
                op.cum = cnt.get(op.epoch, 0)

        def run(engname, e):
            waited = {}
            for op in s.ops[engname]:
                for d in op.deps:
                    if d.is_dma:
                        key = ("s", d.sem.num); val = d.val; sem = d.sem
                    else:
                        if d.eng == "pe" and engname == "pe":
                            continue
                        key = ("e", d.eng, d.epoch); val = d.cum; sem = s.engsem[(d.eng, d.epoch)]
                    if waited.get(key, 0) >= val:
                        continue
                    waited[key] = val
                    e.wait_ge(sem, val)
                ins = op.fn(e)
                if op.is_dma:
                    if op.inc == 16:
                        ins.then_inc(op.sem, 16)
                    else:
                        ins.then_inc(op.sem)
                elif op.marked:
                    ins.then_inc(s.engsem[(op.eng, op.epoch)], 1)

        with nc.Block() as block:
            @block.sync
            def _(e):
                run("sp", e)

            @block.gpsimd
            def _(e):
                run("pool", e)

            @block.scalar
            def _(e):
                run("act", e)

            @block.vector
            def _(e):
                run("dve", e)

            @block.tensor
            def _(e):
                run("pe", e)


def _fm(v):
    v = np.asarray(v, np.float32)
    return np.ascontiguousarray(v.reshape(-1, 128).T)


def rope_tables(cfg, tok0, n, rot_dim):
    axis_dim = rot_dim // 2
    inv = np.power(np.float32(ROPE_BASE), -np.arange(0, axis_dim, 2, dtype=np.float32) / np.float32(axis_dim)).astype(np.float32)
    t = np.arange(tok0, tok0 + n)
    rows = (t // cfg.GRID_W).astype(np.float32); cols = (t % cfg.GRID_W).astype(np.float32)
    ar = rows[:, None] * inv; ac = cols[:, None] * inv
    ang = np.concatenate([ar, ar, ac, ac], axis=-1).astype(np.float32)
    return np.cos(ang).T.astype(np.float32), np.sin(ang).T.astype(np.float32)


def rot_matrix_T(rot_dim):
    half = rot_dim // 2; q = half // 2
    R = np.zeros((rot_dim, rot_dim), np.float32)
    for base in (0, half):
        for i in range(q):
            R[base + i, base + i + q] = -1.0
            R[base + q + i, base + i] = 1.0
    return np.ascontiguousarray(R.T)


def build(cfg, debug=False):
    nc = bass.Bass("TRN2", target_bir_lowering=False)
    P = Prog(nc)
    D, TL, C, FF, DC, FC, NTOK, NR = cfg.D, cfg.TL, cfg.C, cfg.FF, cfg.DC, cfg.FC, cfg.NTOK, cfg.NR
    L = cfg.depth
    LA = (L + 1) // 2; LB = max(L // 2, 1)
    MH, QR, KVR, GH, GKV = cfg.MH, cfg.QR, cfg.KVR, cfg.GH, cfg.GKV
    QC = QR // 128; KC2 = KVR // 128
    T = NR * TL + C
    NCH = T // 128
    W1 = cfg.W1; W4 = cfg.W4

    def din(name, shape, dt=F32):
        return nc.dram_tensor(name, list(shape), dt, kind="ExternalInput").ap()

    def dscr(name, shape, dt):
        return nc.dram_tensor(name, list(shape), dt).ap()

    xT_in = din("xT", [D, TL]); ctxT_in = din("ctxT", [D, C]); cvec = din("cvec", [128, DC * 2])
    cosg = din("cosg", [128, TL]); sing = din("sing", [128, TL]); cosm = din("cosm", [128, TL]); sinm = din("sinm", [128, TL])
    RgT = din("RgT", [128, 128]); RmT = din("RmT", [128, 128]); bones = din("bones", [128, 128]); sel = din("sel", [128, 2 * NR])
    NV = 2 * DC + 6 * DC + 4 * FC + QC + KC2 + 6
    vecs = din("vecs", [L, 128, NV])
    w_mod = din("w_mod", [L, D, 6 * D])
    mla_w_dq = din("mla_w_dq", [LA, D, QR]); mla_w_uq = din("mla_w_uq", [LA, QR, MH * 192])
    mla_w_dkv = din("mla_w_dkv", [LA, D, KVR + 64]); mla_w_ukv = din("mla_w_ukv", [LA, KVR, MH * 256])
    mla_w_o = din("mla_w_o", [LA, MH * 128, D])
    gqa_w_q = din("gqa_w_q", [LB, D, GH * 128]); gqa_w_kv = din("gqa_w_kv", [LB, D, 2 * GKV * 128])
    gqa_w_o = din("gqa_w_o", [LB, GH * 128, D])
    ffn_w_up = din("ffn_w_up", [L, D, 2 * FF]); ffn_w_down = din("ffn_w_down", [L, FF, D])
    yT = nc.dram_tensor("yT", [D, TL], F32, kind="ExternalOutput").ap()

    xT_s = dscr("xT_s", [D, TL], F32); ctxT_s = dscr("ctxT_s", [D, C], F32)
    qT_d = dscr("qT_d", [max(MH, GH) * 128, NTOK], BF16); qpeT_d = dscr("qpeT_d", [MH // 2 * 128, NTOK], BF16)
    kvs = {}
    for nm, nkv in (("m", MH), ("g", GKV)):
        HB = 2 if nkv % 2 == 0 else 1
        while HB > 1 and HB * 128 * TL * 2 > (1 << 20):
            HB //= 2
        nb = nkv // HB
        kvs[nm] = dict(HB=HB, NB=nb,
                       kT_src=[dscr("kT_src%s%d" % (nm, i), [HB * 128, TL], BF16) for i in range(nb)],
                       kT_all=[dscr("kT_all%s%d" % (nm, i), [NR * HB * 128, TL], BF16) for i in range(nb)],
                       v_src=[dscr("v_src%s%d" % (nm, i), [TL, HB * 128], BF16) for i in range(nb)],
                       v_all=[dscr("v_all%s%d" % (nm, i), [NR * TL, HB * 128], BF16) for i in range(nb)],
                       kT_ctx=dscr("kT_ctx" + nm, [nkv * 128, C], BF16), v_ctx=dscr("v_ctx" + nm, [C, nkv * 128], BF16))
    kpe_src = dscr("kpe_src", [128, TL], BF16); kpe_all = dscr("kpe_all", [NR * 128, TL], BF16); kpe_ctx = dscr("kpe_ctx", [128, C], BF16)
    h2T_d = dscr("h2T_d", [D, TL + 2], BF16); h2cT_d = dscr("h2cT_d", [D, C + 2], BF16)
    hsrc = dscr("hsrc", [128, DC * 2], BF16); hall = dscr("hall", [NR * 128, DC * 2], BF16)
    NUT = (FC + 1) // 2; NDT = (DC + 1) // 2
    wupT = dscr("wupT", [NUT * 2, 128, DC * 256], BF16); wdnT = dscr("wdnT", [NDT, 128, FC * 256], BF16)
    Dwc = Buf()
    Dx = Buf(); Dctx = Buf(); Dq = Buf(); Dqpe = Buf(); Dksrc = Buf(); Dkall = Buf(); Dvsrc = Buf(); Dvall = Buf()
    Dkpesrc = Buf(); Dkpeall = Buf(); Dkctx = Buf(); Dvctx = Buf(); Dkpectx = Buf(); Dh2 = Buf(); Dh2c = Buf(); Dhsrc = Buf(); Dhall = Buf()
    Dy = Buf()

    ARENA_BYTES = 207 * 1024
    arena = nc.alloc_sbuf_tensor("arena", [128, ARENA_BYTES // 4], F32)
    apos = [0]
    sempool = {}
    semidx = [0]
    persist_sems = [0]

    def phase_reset(base):
        apos[0] = base; semidx[0] = persist_sems[0]

    def carve(shape, dt, sem=True):
        esz = 4 if dt == F32 else 2
        nb = int(np.prod(shape)) * esz
        nb_al = (nb + 63) // 64 * 64
        off = apos[0]; apos[0] += nb_al
        assert apos[0] <= ARENA_BYTES, ("SBUF arena overflow", apos[0])
        a = arena[:, off // 4:(off + nb) // 4]
        if dt != F32:
            a = a.bitcast(dt)
        if len(shape) == 2:
            a = a.rearrange("p (a b) -> p a b", a=shape[0])
        elif len(shape) == 3:
            a = a.rearrange("p (a b c) -> p a b c", a=shape[0], b=shape[1])
        sm = None
        if sem:
            k = semidx[0]; semidx[0] += 1
            if k not in sempool:
                sempool[k] = [nc.alloc_semaphore("bs%d" % k), 0]
            sm = sempool[k]
        b = Buf(a, sm[0] if sm else None)
        if sm:
            b.cnt = sm[1]; b.semrec = sm
        return b

    def mm(out, lhsT, rhs, start, stop, reads, writes=(), pwrites=()):
        return P.emit("pe", lambda e: e.matmul(out, lhsT, rhs, start=start, stop=stop), reads=reads, writes=writes, pwrites=pwrites)

    def act(out, in_, func, reads, writes=(), pwrites=(), bias=None, scale=None, eng="act"):
        kw = {}
        if bias is not None:
            kw["bias"] = bias
        if scale is not None:
            kw["scale"] = scale
        return P.emit(eng, lambda e: e.activation(out, in_, func, **kw), reads=reads, writes=writes, pwrites=pwrites)

    def stt(out, in0, scalar, in1, op0, op1, reads, writes=(), pwrites=(), eng="dve"):
        return P.emit(eng, lambda e: e.scalar_tensor_tensor(out, in0, scalar, in1, op0, op1), reads=reads, writes=writes, pwrites=pwrites)

    def tt(out, in0, in1, op, reads, writes=(), pwrites=(), eng="dve"):
        return P.emit(eng, lambda e: e.tensor_tensor(out, in0, in1, op), reads=reads, writes=writes, pwrites=pwrites)

    def tsm(out, in0, s1, reads, writes=(), pwrites=(), eng="dve"):
        return P.emit(eng, lambda e: e.tensor_scalar(out, in0, s1, None, ALU.mult), reads=reads, writes=writes, pwrites=pwrites)

    def recip(out, in_, reads, writes=(), pwrites=()):
        return P.emit("dve", lambda e: e.reciprocal(out, in_), reads=reads, writes=writes, pwrites=pwrites)

    def cpy(out, in_, reads, writes=(), pwrites=(), eng="dve"):
        return P.emit(eng, lambda e: e.tensor_copy(out, in_), reads=reads, writes=writes, pwrites=pwrites)

    def mset(out, val, writes=(), pwrites=(), reads=(), eng="dve"):
        return P.emit(eng, lambda e: e.memset(out, val), reads=reads, writes=writes, pwrites=pwrites)

    def dma(eng, out, in_, sb, reads=(), writes=(), pwrites=()):
        sb.cnt = sb.semrec[1]
        op = P.dma(eng, out, in_, sb, reads=reads, writes=writes, pwrites=pwrites)
        sb.semrec[1] = sb.cnt
        return op

    ones_f = carve([128], F32, False); ones_b = carve([128], BF16, False); bones_f = carve([128], F32)
    Rg_f = carve([128], F32); Rm_f = carve([128], F32); sel_f = carve([2 * NR], F32)
    sv_f = carve([DC * 2], F32); sv_b = carve([DC, 2], BF16, False)
    vec = carve([NV], F32)
    mod = carve([6 * DC, 2], F32, False)
    A1 = carve([DC, 2], F32, False); A2 = carve([DC, 2], F32, False)
    gsc = carve([8], F32, False)
    bnd = carve([DC, 2], BF16); halo = carve([DC, 2], BF16, False); hs = carve([NR, DC, 2], BF16)
    halo_f = carve([DC, 2], F32, False)
    eps_c = carve([1], F32, False)
    cvl = [carve([1], F32) for _ in range(3)]
    psum = [Buf(nc.alloc_psum_tensor("ps%d" % i, [128, 512], F32)[:]) for i in range(8)]
    PERSIST = apos[0]
    persist_sems[0] = semidx[0]

    o = 0
    V_NMIX = o; o += DC
    V_NFFN = o; o += DC
    V_BMOD = o; o += 6 * DC
    V_CW = o; o += 3 * FC
    V_CB = o; o += FC
    V_GDQ = o; o += QC
    V_GDKV = o; o += KC2
    V_GQN = o; o += 1
    V_GQP = o; o += 1
    V_GKP = o; o += 1
    V_GKN = o; o += 1
    V_GQ = o; o += 1
    V_GK = o; o += 1
    assert o == NV

    def vcol(i):
        return vec.ap[:, i:i + 1]

    def modcol(m, c, jj):
        return mod.ap[:, m * DC + c, jj:jj + 1]

    NWS = cfg.NWS

    mset(ones_f.ap, 1.0, writes=[ones_f]); mset(ones_b.ap, 1.0, writes=[ones_b]); mset(eps_c.ap, EPS, writes=[eps_c])
    dma("sp", bones_f.ap, bones[:, :], bones_f, writes=[bones_f])
    dma("sp", Rg_f.ap, RgT[:, :], Rg_f, writes=[Rg_f])
    dma("sp", Rm_f.ap, RmT[:, :], Rm_f, writes=[Rm_f])
    dma("sp", sel_f.ap, sel[:, :], sel_f, writes=[sel_f])
    dma("sp", sv_f.ap, cvec[:, :], sv_f, writes=[sv_f])
    act(sv_b.ap.rearrange("p a b -> p (a b)"), sv_f.ap, AF.Silu, reads=[sv_f], writes=[sv_b])

    lat_tiles = [("lat", t0, min(W1, TL - t0)) for t0 in range(0, TL, W1)]
    ctx_tiles = [("ctx", t0, min(W1, C - t0)) for t0 in range(0, C, W1)]
    groups = [list(range(g * NR, (g + 1) * NR)) for g in range(8 // NR)]

    def xsrc(kind, l):
        if kind == "lat":
            return (xT_in if l == 0 else xT_s), Dx
        return (ctxT_in if l == 0 else ctxT_s), Dctx

    def fm3(dram2d, c0, n):
        return dram2d.rearrange("(c p) t -> p c t", p=128)[:, :, c0:c0 + n]

    def rstd_from_ss(ssps, n, nd, rs_sq, rs):
        act(rs_sq.ap[:, :n], ssps.ap[:, :n], AF.Sqrt, reads=[ssps, eps_c], writes=[rs_sq], bias=eps_c.ap[:, 0:1], scale=1.0 / nd)
        recip(rs.ap[:, :n], rs_sq.ap[:, :n], reads=[rs_sq], writes=[rs])

    for l in range(L):
        last = (l == L - 1)
        is_mla = (l % 2 == 0)
        j = l // 2
        P.new_epoch()
        NH = MH if is_mla else GH
        NKV = MH if is_mla else GKV
        GRP = NH // NKV
        KV = kvs["m" if is_mla else "g"]
        kT_src, kT_all, v_src, v_all, kT_ctx, v_ctx = KV["kT_src"], KV["kT_all"], KV["v_src"], KV["v_all"], KV["kT_ctx"], KV["v_ctx"]
        HB, NB = KV["HB"], KV["NB"]

        def ksrc_rows(hd):
            return kT_src[hd // HB][(hd % HB) * 128:(hd % HB + 1) * 128, :]

        def vsrc_cols(tok0, ntok, h0, nh):
            out = []
            hd = h0
            while hd < h0 + nh:
                k = min(HB - hd % HB, h0 + nh - hd)
                out.append((v_src[hd // HB][tok0:tok0 + ntok, (hd % HB) * 128:(hd % HB + k) * 128], (hd - h0) * 128, k * 128))
                hd += k
            return out

        phase_reset(PERSIST)
        dma("sp", vec.ap, vecs[l], vec, writes=[vec])
        wsm = [carve([DC, 512], BF16) for _ in range(3)]
        modps = psum[0]
        modps.new_gen()
        nfg = (6 * DC + 3) // 4
        wm3 = w_mod[l].rearrange("(kc p) f -> p kc f", p=128)
        for fg in range(nfg):
            nf = min(4, 6 * DC - fg * 4)
            wb = wsm[fg % 3]
            dma("pool", wb.ap[:, :, :nf * 128], wm3[:, :, fg * 512:fg * 512 + nf * 128], wb, writes=[wb])
            for fc in range(nf):
                col = (fg * 4 + fc) * 2
                for kc in range(DC):
                    mm(modps.ap[:, col:col + 2], wb.ap[:, kc, fc * 128:(fc + 1) * 128], sv_b.ap[:, kc, :], kc == 0, kc == DC - 1,
                       reads=[wb, sv_b], pwrites=[modps])
        mp3 = modps.ap[:, 0:12 * DC].rearrange("p (a b) -> p a b", b=2)
        for jj in range(2):
            tt(mod.ap[:, :, jj], mp3[:, :, jj], vec.ap[:, V_BMOD:V_BMOD + 6 * DC], ALU.add, reads=[modps, vec],
               writes=[mod] if jj == 0 else (), pwrites=[mod] if jj else ())
        for (Ab, voff, m) in ((A1, V_NMIX, 1), (A2, V_NFFN, 4)):
            for jj in range(2):
                stt(Ab.ap[:, :, jj], mod.ap[:, m * DC:(m + 1) * DC, jj], 1.0, vec.ap[:, voff:voff + DC], ALU.add, ALU.mult,
                    reads=[mod, vec], writes=[Ab] if jj == 0 else (), pwrites=[Ab] if jj else ())
        if is_mla:
            sc = 1.0 / math.sqrt(192.0)
            glist = [(0, V_GQN, sc), (1, V_GQP, sc), (2, V_GKP, 1.0), (3, V_GKN, 1.0)]
        else:
            sc = 1.0 / math.sqrt(128.0)
            glist = [(0, V_GQ, sc), (1, V_GK, 1.0)]
        for gi, (gidx, vo, mul) in enumerate(glist):
            tsm(gsc.ap[:, gidx:gidx + 1], vec.ap[:, vo:vo + 1], mul, reads=[vec], writes=[gsc] if gi == 0 else (), pwrites=[gsc] if gi else ())
        P.barrier()

        phase_reset(PERSIST)
        xt = [carve([DC, W1], F32) for _ in range(1)]
        hT = [carve([DC, W1], BF16, False) for _ in range(2)]
        sq = [carve([W1], F32, False) for _ in range(2)]
        rs_sq = carve([W1], F32, False); rs = carve([W1], F32, False)
        tmpf = [carve([W1], F32, False) for _ in range(2)]
        cs = [carve([W1], F32) for _ in range(2)]; sn = [carve([W1], F32) for _ in range(2)]
        NSET = 3
        nsets = [dict(sq=carve([W1], F32, False), rs_sq=carve([W1], F32, False), rs=carve([W1], F32, False),
                      qn=carve([W1], F32, False), t1=carve([W1], F32, False), t2=carve([W1], F32, False)) for _ in range(NSET)]
        stg = [carve([W1], BF16) for _ in range(3)]
        vstg = [carve([512], BF16) for _ in range(2)]
        ws = [carve([max(DC, QC, KC2), 512], BF16) for _ in range(NWS)]
        if is_mla:
            cq_f = carve([max(QC, KC2), W1], F32, False); cqn = carve([QC, W1], BF16, False); ckvn = carve([KC2, W1], BF16, False)
        cnt = dict(ws=0, stg=0, vstg=0, pi=0)

        def wload(ring, src3, kc, m, split=1):
            b = ring[cnt["ws"] % len(ring)]; cnt["ws"] += 1
            view = b.ap.rearrange("p a b -> p (a b)")[:, :kc * m].rearrange("p (a b) -> p a b", a=kc)
            step = (kc + split - 1) // split
            for i, k0 in enumerate(range(0, kc, step)):
                k1 = min(kc, k0 + step)
                dma("pool", view[:, k0:k1, :], src3[:, k0:k1, :], b, writes=[b] if i == 0 else (), pwrites=[b] if i else ())
            return b, view

        def wload4(ring, src4, kc, nh):
            b = ring[cnt["ws"] % len(ring)]; cnt["ws"] += 1
            view = b.ap.rearrange("p a b -> p (a b)")[:, :kc * nh * 128].rearrange("p (a h e) -> p a h e", a=kc, h=nh)
            for hq in range(nh):
                dma("pool", view[:, :, hq, :], src4[:, :, hq, :], b, writes=[b] if hq == 0 else (), pwrites=[b] if hq else ())
            return b, view

        def wload2(ring, srcA, srcB, kc):
            b = ring[cnt["ws"] % len(ring)]; cnt["ws"] += 1
            view = b.ap.rearrange("p a b -> p (a b)")[:, :kc * 128].rearrange("p (a b) -> p a b", a=kc)
            dma("pool", view[:, :, 0:64], srcA, b, writes=[b])
            dma("pool", view[:, :, 64:128], srcB, b, pwrites=[b])
            return b, view

        def modulate(xb, n, Ab, shm, jj, hb, sq, rs_sq, rs, tmpf):
            ssps = psum[1]
            for c in range(DC):
                s_ = sq[c % 2]
                act(s_.ap[:, :n], xb.ap[:, c, :n], AF.Square, reads=[xb], writes=[s_])
                mm(ssps.ap[:, :n], ones_f.ap, s_.ap[:, :n], c == 0, c == DC - 1, reads=[s_, ones_f],
                   writes=[ssps] if c == 0 else (), pwrites=() if c == 0 else [ssps])
            rstd_from_ss(ssps, n, float(D), rs_sq, rs)
            hb.new_gen()
            for c in range(DC):
                tf = tmpf[c % 2]
                stt(tf.ap[:, :n], xb.ap[:, c, :n], Ab.ap[:, c, jj:jj + 1], rs.ap[:, :n], ALU.mult, ALU.mult, reads=[xb, Ab, rs], writes=[tf])
                act(hb.ap[:, c, :n], tf.ap[:, :n], AF.Identity, reads=[tf, mod], pwrites=[hb], bias=modcol(shm, c, jj), scale=1.0)

        def run_jobs(jobs, n, cb, sb_):
            N = len(jobs)
            pbuf = [psum[3], psum[4], psum[5]]
            st_of = {}

            def A(i):
                J = jobs[i]
                wb, wv_ = J["w"]()
                ps = pbuf[i % 3]
                proj(ps, wv_, wb, J["src"], J["kcs"], n)
                S_ = nsets[i % NSET]
                act(S_["sq"].ap[:, :n], ps.ap[:, :n], AF.Square, reads=[ps], writes=[S_["sq"]])

            def B(i):
                J = jobs[i]
                ps = pbuf[i % 3]
                S_ = nsets[i % NSET]
                ssps = psum[1 + i % 2]
                mm(ssps.ap[:, :n], J["ones"].ap, S_["sq"].ap[:, :n], True, True, reads=[S_["sq"], J["ones"]], writes=[ssps])
                rstd_from_ss(ssps, n, float(J["nd"]), S_["rs_sq"], S_["rs"])
                if not J["rope"]:
                    st = stg[cnt["stg"] % 3]; cnt["stg"] += 1
                    stt(st.ap[:, :n], ps.ap[:, :n], J["gcol"], S_["rs"].ap[:, :n], ALU.mult, ALU.mult, reads=[ps, S_["rs"], gsc], writes=[st])
                    dma("sp", J["dst"], st.ap[:, :n], st, reads=[st], pwrites=[J["Ddst"]])
                else:
                    stt(S_["qn"].ap[:, :n], ps.ap[:, :n], J["gcol"], S_["rs"].ap[:, :n], ALU.mult, ALU.mult, reads=[ps, S_["rs"], gsc], writes=[S_["qn"]])

            def C(i):
                J = jobs[i]
                if not J["rope"]:
                    return
                S_ = nsets[i % NSET]
                qn = S_["qn"]; t1 = S_["t1"]; t2 = S_["t2"]
                rps = psum[7] if i % 2 else psum[0]
                mm(rps.ap[:, :n], J["R"].ap, qn.ap[:, :n], True, True, reads=[J["R"], qn], writes=[rps])
                tt(t1.ap[:, :n], qn.ap[:, :n], cb.ap[:, :n], ALU.mult, reads=[qn, cb], writes=[t1])
                tt(t2.ap[:, :n], rps.ap[:, :n], sb_.ap[:, :n], ALU.mult, reads=[rps, sb_], writes=[t2])
                st = stg[cnt["stg"] % 3]; cnt["stg"] += 1
                tt(st.ap[:, :n], t1.ap[:, :n], t2.ap[:, :n], ALU.add, reads=[t1, t2], writes=[st])
                dma("sp", J["dst"], st.ap[:, :n], st, reads=[st], pwrites=[J["Ddst"]])

            for i in range(N + 2):
                if i < N:
                    A(i)
                if 0 <= i - 1 < N:
                    B(i - 1)
                if 0 <= i - 2 < N:
                    C(i - 2)

        def job(wfn, src, kcs, nd, onesb, gcol, rope, Rb, dst_ap, Ddst):
            return dict(w=wfn, src=src, kcs=kcs, nd=nd, ones=onesb, gcol=gcol, rope=rope, R=Rb, dst=dst_ap, Ddst=Ddst)

        def shared_w(loader):
            box = []

            def get():
                if not box:
                    box.append(loader())
                return box[0]
            return get

        def proj(ps, wview, wb, src, kcs, n):
            for kc in range(kcs):
                mm(ps.ap[:, :n], wview[:, kc, :], src.ap[:, kc, :n], kc == 0, kc == kcs - 1, reads=[wb, src],
                   writes=[ps] if kc == 0 else (), pwrites=() if kc == 0 else [ps])

        def nextps():
            p_ = psum[3 + cnt["pi"] % 2]; cnt["pi"] += 1
            return p_

        def vproj(src, kcs, n, wbs, dst_fn, Ddst):
            ncol = len(wbs) * 128
            for s0 in range(0, n, 128):
                vps = psum[6]
                vps.new_gen()
                for qi, (wb, wv_) in enumerate(wbs):
                    for kc in range(kcs):
                        mm(vps.ap[:, qi * 128:(qi + 1) * 128], src.ap[:, kc, s0:s0 + 128], wv_[:, kc, :], kc == 0, kc == kcs - 1,
                           reads=[src, wb], pwrites=[vps])
                vs = vstg[cnt["vstg"] % 2]; cnt["vstg"] += 1
                act(vs.ap[:, :ncol], vps.ap[:, :ncol], AF.Copy, reads=[vps], writes=[vs])
                for (dap, coff, ncl) in dst_fn(s0):
                    dma("sp", dap, vs.ap[:, coff:coff + ncl], vs, reads=[vs], pwrites=[Ddst])

        for D_ in (Dq, Dqpe, Dksrc, Dvsrc, Dkpesrc, Dkctx, Dvctx, Dkpectx):
            D_.new_gen()
        tiles = lat_tiles + ctx_tiles
        for ti, (kind, t0, n) in enumerate(tiles):
            jj = 0 if kind == "lat" else 1
            tokq = t0 if kind == "lat" else TL + t0
            xb = xt[0]; hb = hT[ti % 2]
            src_d, Dsrc = xsrc(kind, l)
            dma("sp", xb.ap[:, :, :n], fm3(src_d, t0, n), xb, reads=[Dsrc], writes=[xb])
            modulate(xb, n, A1, 0, jj, hb, sq, rs_sq, rs, tmpf)
            rope = (kind == "lat")
            need_q = not (kind == "ctx" and last)
            cb = sb_ = None
            if rope:
                cb = cs[ti % 2]; sb_ = sn[ti % 2]
                dma("sp", cb.ap[:, :n], (cosm if is_mla else cosg)[:, t0:t0 + n], cb, writes=[cb])
                dma("sp", sb_.ap[:, :n], (sinm if is_mla else sing)[:, t0:t0 + n], sb_, writes=[sb_])
            kdst = (lambda r0, r1: ksrc_rows(r0 // 128)[:, t0:t0 + n]) if kind == "lat" else (lambda r0, r1: kT_ctx[r0:r1, t0:t0 + n])
            Dk = Dksrc if kind == "lat" else Dkctx
            Dv = Dvsrc if kind == "lat" else Dvctx
            if not is_mla:
                wq = gqa_w_q[j].rearrange("(kc p) m -> p kc m", p=128)
                wkv = gqa_w_kv[j].rearrange("(kc p) m -> p kc m", p=128)
                jobs = []
                if need_q:
                    for h0 in range(0, GH, 4):
                        nh = min(4, GH - h0)
                        sw = shared_w(lambda h0=h0, nh=nh: wload(ws, wq[:, :, h0 * 128:(h0 + nh) * 128], DC, nh * 128))
                        for hq in range(nh):
                            h = h0 + hq
                            jobs.append(job((lambda sw=sw, hq=hq: (sw()[0], sw()[1][:, :, hq * 128:(hq + 1) * 128])), hb, DC, 128, ones_f,
                                            gsc.ap[:, 0:1], rope, Rg_f, qT_d[h * 128:(h + 1) * 128, tokq:tokq + n], Dq))
                for g0 in range(0, GKV, 4):
                    ng = min(4, GKV - g0)
                    sw = shared_w(lambda g0=g0, ng=ng: wload(ws, wkv[:, :, g0 * 128:(g0 + ng) * 128], DC, ng * 128))
                    for gq in range(ng):
                        g = g0 + gq
                        jobs.append(job((lambda sw=sw, gq=gq: (sw()[0], sw()[1][:, :, gq * 128:(gq + 1) * 128])), hb, DC, 128, ones_f,
                                        gsc.ap[:, 1:2], rope, Rg_f, kdst(g * 128, (g + 1) * 128), Dk))
                run_jobs(jobs, n, cb, sb_)
                for c0 in range(0, GKV, 4):
                    ng = min(4, GKV - c0)
                    wb, wv_ = wload(ws, wkv[:, :, (GKV + c0) * 128:(GKV + c0 + ng) * 128], DC, ng * 128)
                    wbs = [(wb, wv_[:, :, q4 * 128:(q4 + 1) * 128]) for q4 in range(ng)]
                    if kind == "lat":
                        vproj(hb, DC, n, wbs, (lambda s0, c0=c0, ng=ng: vsrc_cols(t0 + s0, 128, c0, ng)), Dv)
                    else:
                        vproj(hb, DC, n, wbs, (lambda s0, c0=c0, ng=ng: [(v_ctx[t0 + s0:t0 + s0 + 128, c0 * 128:(c0 + ng) * 128], 0, ng * 128)]), Dv)
            else:
                wdq = mla_w_dq[j].rearrange("(kc p) m -> p kc m", p=128)
                wuq = mla_w_uq[j].rearrange("(kc p) m -> p kc m", p=128)
                wdkv = mla_w_dkv[j].rearrange("(kc p) m -> p kc m", p=128)
                wukv = mla_w_ukv[j].rearrange("(kc p) m -> p kc m", p=128)

                def compress(wsrc, nchunk, gv_off, outn):
                    ssps = psum[1]
                    outn.new_gen(); cq_f.new_gen()
                    for oc in range(nchunk):
                        if oc % 4 == 0:
                            nq = min(4, nchunk - oc)
                            wb, wvw = wload(ws, wsrc[:, :, oc * 128:(oc + nq) * 128], DC, nq * 128)
                        wv_ = wvw[:, :, (oc % 4) * 128:(oc % 4 + 1) * 128]
                        ps = nextps()
                        proj(ps, wv_, wb, hb, DC, n)
                        act(cq_f.ap[:, oc, :n], ps.ap[:, :n], AF.Copy, reads=[ps], pwrites=[cq_f])
                        s_ = sq[oc % 2]
                        act(s_.ap[:, :n], ps.ap[:, :n], AF.Square, reads=[ps], writes=[s_])
                        mm(ssps.ap[:, :n], ones_f.ap, s_.ap[:, :n], oc == 0, oc == nchunk - 1, reads=[s_, ones_f],
                           writes=[ssps] if oc == 0 else (), pwrites=() if oc == 0 else [ssps])
                    rstd_from_ss(ssps, n, float(nchunk * 128), rs_sq, rs)
                    for oc in range(nchunk):
                        stt(outn.ap[:, oc, :n], cq_f.ap[:, oc, :n], vcol(gv_off + oc), rs.ap[:, :n], ALU.mult, ALU.mult,
                            reads=[cq_f, rs, vec], pwrites=[outn])

                if need_q:
                    compress(wdq, QC, V_GDQ, cqn)
                    wuq4 = wuq.rearrange("p k (h e) -> p k h e", e=192)
                    jobs = []
                    for h0 in range(0, MH, 4):
                        nq = min(4, MH - h0)
                        sw = shared_w(lambda h0=h0, nq=nq: wload4(ws, wuq4[:, :, h0:h0 + nq, 0:128], QC, nq))
                        for hq in range(nq):
                            h = h0 + hq
                            jobs.append(job((lambda sw=sw, hq=hq: (sw()[0], sw()[1][:, :, hq, :])), cqn, QC, 128, ones_f,
                                            gsc.ap[:, 0:1], False, None, qT_d[h * 128:(h + 1) * 128, tokq:tokq + n], Dq))
                    for hp in range(MH // 2):
                        h0 = 2 * hp
                        jobs.append(job((lambda h0=h0: wload2(ws, wuq[:, :, h0 * 192 + 128:h0 * 192 + 192], wuq[:, :, (h0 + 1) * 192 + 128:(h0 + 1) * 192 + 192], QC)),
                                        cqn, QC, 64, bones_f, gsc.ap[:, 1:2], rope, Rm_f, qpeT_d[hp * 128:(hp + 1) * 128, tokq:tokq + n], Dqpe))
                    run_jobs(jobs, n, cb, sb_)
                compress(wdkv, KC2, V_GDKV, ckvn)
                jobs = []
                jobs.append(job((lambda: wload2(ws, wdkv[:, :, KVR:KVR + 64], wdkv[:, :, KVR:KVR + 64], DC)), hb, DC, 64, bones_f, gsc.ap[:, 2:3],
                                rope, Rm_f, (kpe_src[:, t0:t0 + n] if kind == "lat" else kpe_ctx[:, t0:t0 + n]), (Dkpesrc if kind == "lat" else Dkpectx)))
                wukv4 = wukv.rearrange("p k (h e) -> p k h e", e=256)
                for h0 in range(0, MH, 4):
                    nq = min(4, MH - h0)
                    sw = shared_w(lambda h0=h0, nq=nq: wload4(ws, wukv4[:, :, h0:h0 + nq, 0:128], KC2, nq))
                    for hq in range(nq):
                        h = h0 + hq
                        jobs.append(job((lambda sw=sw, hq=hq: (sw()[0], sw()[1][:, :, hq, :])), ckvn, KC2, 128, ones_f,
                                        gsc.ap[:, 3:4], False, None, kdst(h * 128, (h + 1) * 128), Dk))
                run_jobs(jobs, n, cb, sb_)
                for h0 in range(0, MH, 4):
                    nh = min(4, MH - h0)
                    wb, wvw = wload4(ws, wukv4[:, :, h0:h0 + nh, 128:256], KC2, nh)
                    wbs = [(wb, wvw[:, :, q4, :]) for q4 in range(nh)]
                    if kind == "lat":
                        vproj(ckvn, KC2, n, wbs, (lambda s0, h0=h0, nh=nh: vsrc_cols(t0 + s0, 128, h0, nh)), Dv)
                    else:
                        vproj(ckvn, KC2, n, wbs, (lambda s0, h0=h0, nh=nh: [(v_ctx[t0 + s0:t0 + s0 + 128, h0 * 128:(h0 + nh) * 128], 0, nh * 128)]), Dv)
        P.barrier()

        Dkall.new_gen(); Dvall.new_gen()
        for bi in range(NB):
            P.cc("AllGather", groups, kT_src[bi][:, :], kT_all[bi][:, :], reads=[Dksrc], pwrites=[Dkall])
            P.cc("AllGather", groups, v_src[bi][:, :], v_all[bi][:, :], reads=[Dvsrc], pwrites=[Dvall])
        if is_mla:
            P.cc("AllGather", groups, kpe_src[:, :], kpe_all[:, :], reads=[Dkpesrc], writes=[Dkpeall])
        P.barrier()

        phase_reset(PERSIST)
        oT = carve([NH, NTOK], BF16, False)
        P3BASE = apos[0]
        KA = [carve([T], BF16) for _ in range(2)]
        VV = [carve([NCH, 128], BF16) for _ in range(2)]
        KBm = [carve([T], BF16) for _ in range(2)] if is_mla else None
        QA = [carve([W1], BF16) for _ in range(2)]
        QB = [carve([W1], BF16) for _ in range(2)] if is_mla else None
        pT = [carve([W1], BF16, False) for _ in range(4)]
        rec = carve([W1], F32, False)
        accD = [carve([W1], F32, False) for _ in range(2)]
        if is_mla:
            kpa = kpe_all.rearrange("(r d) t -> d r t", d=128)
            for hf in range(2):
                lo, hi = hf * 64, hf * 64 + 64
                zl, zh = (64, 128) if hf == 0 else (0, 64)
                mset(KBm[hf].ap[zl:zh, :], 0.0, writes=[KBm[hf]])
                dma("sp", KBm[hf].ap[lo:hi, 0:NR * TL].rearrange("p (r t) -> p r t", r=NR), kpa[lo:hi], KBm[hf], reads=[Dkpeall], pwrites=[KBm[hf]])
                dma("sp", KBm[hf].ap[lo:hi, NR * TL:T], kpe_ctx[lo:hi, :], KBm[hf], reads=[Dkpectx], pwrites=[KBm[hf]])
        wup = ffn_w_up[l].rearrange("(kc p) m -> p kc m", p=128)
        wdn = ffn_w_down[l].rearrange("(kc p) m -> p kc m", p=128)
        Dwc.new_gen()
        ncv = 0
        for g in range(NUT):
            nq = min(2, FC - 2 * g)
            for part in range(2):
                ln = cvl[ncv % len(cvl)]; ncv += 1
                dma("pool", wupT[2 * g + part][:, :DC * nq * 128].rearrange("p (kc m) -> p kc m", kc=DC),
                    wup[:, :, part * FF + g * 256:part * FF + g * 256 + nq * 128], ln, writes=[ln], pwrites=[Dwc])
        for g in range(NDT):
            nq = min(2, DC - 2 * g)
            ln = cvl[ncv % len(cvl)]; ncv += 1
            dma("pool", wdnT[g][:, :FC * nq * 128].rearrange("p (kc m) -> p kc m", kc=FC),
                wdn[:, :, g * 256:g * 256 + nq * 128], ln, writes=[ln], pwrites=[Dwc])
        qtiles = lat_tiles + ([] if last else ctx_tiles)
        oT.new_gen()
        it = 0
        kall4 = [a.rearrange("(r g d) t -> d g r t", r=NR, g=HB) for a in kT_all]
        vall4 = [a.rearrange("(ch p) (g d) -> p ch g d", p=128, d=128) for a in v_all]
        vctx4 = v_ctx.rearrange("(ch p) (g d) -> p ch g d", p=128, d=128)
        for g in range(NKV):
            ka = KA[g % 2]; vv = VV[g % 2]
            dma("sp", ka.ap[:, 0:NR * TL].rearrange("p (r t) -> p r t", r=NR), kall4[g // HB][:, g % HB], ka, reads=[Dkall], writes=[ka])
            dma("sp", ka.ap[:, NR * TL:T], kT_ctx[g * 128:(g + 1) * 128, :], ka, reads=[Dkctx], pwrites=[ka])
            dma("sp", vv.ap[:, 0:NR * TL // 128, :], vall4[g // HB][:, :, g % HB, :], vv, reads=[Dvall], writes=[vv])
            dma("sp", vv.ap[:, NR * TL // 128:NCH, :], vctx4[:, :, g, :], vv, reads=[Dvctx], pwrites=[vv])
            for hh in range(GRP):
                h = g * GRP + hh
                hp = (h % 2) * 64
                for (kind, t0, n) in qtiles:
                    tokq = t0 if kind == "lat" else TL + t0
                    qa = QA[it % 2]; qb = QB[it % 2] if is_mla else None
                    ops_ = psum[3 + it % 2]; sums = psum[5 + it % 2]
                    it += 1
                    dma("sp", qa.ap[:, :n], qT_d[h * 128:(h + 1) * 128, tokq:tokq + n], qa, reads=[Dq], writes=[qa])
                    if is_mla:
                        dma("sp", qb.ap[:, :n], qpeT_d[(h // 2) * 128:(h // 2 + 1) * 128, tokq:tokq + n], qb, reads=[Dqpe], writes=[qb])
                    chunks = list(range(NCH)) if kind == "lat" else list(range(NR * TL // 128, NCH))
                    nchk = len(chunks)

                    aD = accD[it % 2]
                    kbm = KBm[h % 2] if is_mla else None
                    pe_chunks = [ci for ci in range(nchk) if (not is_mla) and ci % 4 == 3]
                    dve_chunks = [ci for ci in range(nchk) if ci not in pe_chunks]

                    def emit_S(ci):
                        ch = chunks[ci]
                        sp_ = psum[ci % 3]
                        mm(sp_.ap[:, :n], ka.ap[:, ch * 128:(ch + 1) * 128], qa.ap[:, :n], True, not is_mla, reads=[ka, qa], writes=[sp_])
                        if is_mla:
                            mm(sp_.ap[:, :n], kbm.ap[:, ch * 128:(ch + 1) * 128], qb.ap[:, :n], False, True, reads=[kbm, qb], pwrites=[sp_])
                        pt = pT[ci % 4]
                        act(pt.ap[:, :n], sp_.ap[:, :n], AF.Exp, reads=[sp_], writes=[pt])

                    def emit_PV(ci):
                        ch = chunks[ci]
                        pt = pT[ci % 4]
                        mm(ops_.ap[:, :n], vv.ap[:, ch, :], pt.ap[:, :n], ci == 0, ci == nchk - 1, reads=[vv, pt],
                           writes=[ops_] if ci == 0 else (), pwrites=() if ci == 0 else [ops_])
                        if ci in pe_chunks:
                            mm(sums.ap[:, :n], ones_b.ap, pt.ap[:, :n], ci == pe_chunks[0], False, reads=[ones_b, pt],
                               writes=[sums] if ci == pe_chunks[0] else (), pwrites=() if ci == pe_chunks[0] else [sums])
                        elif ci == dve_chunks[0]:
                            cpy(aD.ap[:, :n], pt.ap[:, :n], reads=[pt], writes=[aD])
                        else:
                            tt(aD.ap[:, :n], aD.ap[:, :n], pt.ap[:, :n], ALU.add, reads=[pt, aD], pwrites=[aD])

                    emit_S(0)
                    if nchk > 1:
                        emit_S(1)
                    for ci in range(nchk):
                        if ci + 2 < nchk:
                            emit_S(ci + 2)
                        emit_PV(ci)
                    mm(sums.ap[:, :n], ones_f.ap, aD.ap[:, :n], not pe_chunks, True, reads=[ones_f, aD],
                       writes=[sums] if not pe_chunks else (), pwrites=[sums] if pe_chunks else ())
                    recip(rec.ap[:, :n], sums.ap[:, :n], reads=[sums], writes=[rec])
                    tt(oT.ap[:, h, tokq:tokq + n], ops_.ap[:, :n], rec.ap[:, :n], ALU.mult, reads=[ops_, rec], pwrites=[oT])
        P.barrier()

        phase_reset(P3BASE)
        xt3 = [carve([DC, W1], F32) for _ in range(2)]
        h3 = [carve([DC, W1], BF16) for _ in range(1)]
        sq3 = [carve([W1], F32, False) for _ in range(2)]
        rs_sq3 = carve([W1], F32, False); rs3 = carve([W1], F32, False)
        tmpf3 = [carve([W1], F32, False) for _ in range(2)]
        ws3 = [carve([NH, 256], BF16) for _ in range(NWS)]
        w_o = (mla_w_o if is_mla else gqa_w_o)[j].rearrange("(kc p) m -> p kc m", p=128)
        tiles3 = lat_tiles + ([] if last else ctx_tiles)
        Dh2.new_gen(); Dh2c.new_gen(); bnd.new_gen()
        for ti, (kind, t0, n) in enumerate(tiles3):
            jj = 0 if kind == "lat" else 1
            tokq = t0 if kind == "lat" else TL + t0
            xb = xt3[ti % 2]; hb = h3[0]
            src_d, Dsrc = xsrc(kind, l)
            dst_d = xT_s if kind == "lat" else ctxT_s
            dma("sp", xb.ap[:, :, :n], fm3(src_d, t0, n), xb, reads=[Dsrc], writes=[xb])
            for dc in range(DC):
                if dc % 2 == 0:
                    nq = min(2, DC - dc)
                    wb, wvw = wload(ws3, w_o[:, :, dc * 128:(dc + nq) * 128], NH, nq * 128)
                wv_ = wvw[:, :, (dc % 2) * 128:(dc % 2 + 1) * 128]
                ps = nextps()
                for kc in range(NH):
                    mm(ps.ap[:, :n], wv_[:, kc, :], oT.ap[:, kc, tokq:tokq + n], kc == 0, kc == NH - 1, reads=[wb, oT],
                       writes=[ps] if kc == 0 else (), pwrites=() if kc == 0 else [ps])
                stt(xb.ap[:, dc, :n], ps.ap[:, :n], modcol(2, dc, jj), xb.ap[:, dc, :n], ALU.mult, ALU.add, reads=[ps, mod, xb], pwrites=[xb])
            dma("sp", fm3(dst_d, t0, n), xb.ap[:, :, :n], xb, reads=[xb], pwrites=[Dsrc])
            modulate(xb, n, A2, 3, jj, hb, sq3, rs_sq3, rs3, tmpf3)
            if kind == "lat":
                dma("sp", fm3(h2T_d, 1 + t0, n), hb.ap[:, :, :n], hb, reads=[hb], pwrites=[Dh2])
                if t0 == 0:
                    cpy(bnd.ap[:, :, 0], hb.ap[:, :, 0], reads=[hb], pwrites=[bnd])
                if t0 + n == TL:
                    cpy(bnd.ap[:, :, 1], hb.ap[:, :, n - 1], reads=[hb], pwrites=[bnd])
            else:
                dma("sp", fm3(h2cT_d, 1 + t0, n), hb.ap[:, :, :n], hb, reads=[hb], pwrites=[Dh2c])
        dma("sp", hsrc[:, :], bnd.ap.rearrange("p a b -> p (a b)"), bnd, reads=[bnd], writes=[Dhsrc])
        P.barrier()
        P.cc("AllGather", groups, hsrc[:, :], hall[:, :], reads=[Dhsrc], writes=[Dhall])
        dma("sp", hs.ap, hall.rearrange("(r p) (c t) -> p r c t", p=128, t=2), hs, reads=[Dhall], writes=[hs])
        for side in range(2):
            for r in range(NR):
                srcc = hs.ap[:, r, :, 1 - side]
                scol = sel_f.ap[:, side * NR + r:side * NR + r + 1]
                if r == 0:
                    tsm(halo_f.ap[:, :, side], srcc, scol, reads=[hs, sel_f], writes=[halo_f] if side == 0 else (), pwrites=() if side == 0 else [halo_f])
                else:
                    stt(halo_f.ap[:, :, side], srcc, scol, halo_f.ap[:, :, side], ALU.mult, ALU.add, reads=[hs, sel_f, halo_f], pwrites=[halo_f])
        cpy(halo.ap, halo_f.ap, reads=[halo_f], writes=[halo])
        P.barrier()

        phase_reset(PERSIST)
        WIN = W4 + 2
        h2w = [carve([DC, WIN], BF16) for _ in range(2)]
        aT = carve([FC, W4], BF16, False)
        xw = [carve([DC, W4], F32) for _ in range(1)]
        c1 = [carve([W4], F32, False) for _ in range(2)]; c2 = [carve([W4], F32, False) for _ in range(2)]
        c3 = [carve([W4], F32, False) for _ in range(2)]; sg = [carve([W4], F32, False) for _ in range(2)]
        wsu = [carve([DC, 256], BF16) for _ in range(cfg.NWU)]
        wsd = [carve([FC, 256], BF16) for _ in range(cfg.NWD)]
        def wload_bf(ring, src2, kc, m):
            b = ring[cnt["ws"] % len(ring)]; cnt["ws"] += 1
            v2 = b.ap.rearrange("p a b -> p (a b)")[:, :kc * m]
            dma("pool", v2, src2, b, reads=[Dwc], writes=[b])
            return b, v2.rearrange("p (a b) -> p a b", a=kc)

        wins = [("lat", s, min(W4, TL - s)) for s in range(0, TL, W4)]
        if not last:
            wins += [("ctx", s, min(W4, C - s)) for s in range(0, C, W4)]
        pi = 0
        for wi, (kind, s0, nout) in enumerate(wins):
            jj = 0 if kind == "lat" else 1
            nin = nout + 2
            hw = h2w[wi % 2]; xb = xw[0]
            tot = TL if kind == "lat" else C
            if kind == "lat":
                dma("sp", hw.ap[:, :, :nin], fm3(h2T_d, s0, nin), hw, reads=[Dh2], writes=[hw])
            else:
                dma("sp", hw.ap[:, :, :nin], fm3(h2cT_d, s0, nin), hw, reads=[Dh2c], writes=[hw])
            if s0 == 0:
                if kind == "lat":
                    cpy(hw.ap[:, :, 0], halo.ap[:, :, 0], reads=[halo, hw], pwrites=[hw])
                else:
                    mset(hw.ap[:, :, 0], 0.0, reads=[hw], pwrites=[hw])
            if s0 + nout == tot:
                if kind == "lat":
                    cpy(hw.ap[:, :, nin - 1], halo.ap[:, :, 1], reads=[halo, hw], pwrites=[hw])
                else:
                    mset(hw.ap[:, :, nin - 1], 0.0, reads=[hw], pwrites=[hw])
            src_d = xT_s if kind == "lat" else ctxT_s
            Dsrc = Dx if kind == "lat" else Dctx
            dma("sp", xb.ap[:, :, :nout], fm3(src_d, s0, nout), xb, reads=[Dsrc], writes=[xb])
            aT.new_gen()
            for fc in range(FC):
                if fc % 2 == 0:
                    nq = min(2, FC - fc)
                    wgb, wgw = wload_bf(wsu, wupT[fc + 0][:, :DC * nq * 128], DC, nq * 128)
                    wvb, wvw = wload_bf(wsu, wupT[fc + 1][:, :DC * nq * 128], DC, nq * 128)
                wgv = wgw[:, :, (fc % 2) * 128:(fc % 2 + 1) * 128]
                wvv = wvw[:, :, (fc % 2) * 128:(fc % 2 + 1) * 128]
                gps = psum[1 + pi % 2]; vps = psum[3 + pi % 2]; pi += 1
                for kc in range(DC):
                    mm(gps.ap[:, :nin], wgv[:, kc, :], hw.ap[:, kc, :nin], kc == 0, kc == DC - 1, reads=[wgb, hw],
                       writes=[gps] if kc == 0 else (), pwrites=() if kc == 0 else [gps])
                for kc in range(DC):
                    mm(vps.ap[:, :nin], wvv[:, kc, :], hw.ap[:, kc, :nin], kc == 0, kc == DC - 1, reads=[wvb, hw],
                       writes=[vps] if kc == 0 else (), pwrites=() if kc == 0 else [vps])
                a1 = c1[fc % 2]; a2 = c2[fc % 2]; a3 = c3[fc % 2]; sgb = sg[fc % 2]
                act(a1.ap[:, :nout], gps.ap[:, 1:1 + nout], AF.Identity, reads=[gps, vec], writes=[a1], bias=vcol(V_CB + fc), scale=vcol(V_CW + FC + fc))
                stt(a2.ap[:, :nout], gps.ap[:, 0:nout], vcol(V_CW + fc), a1.ap[:, :nout], ALU.mult, ALU.add, reads=[gps, vec, a1], writes=[a2])
                stt(a3.ap[:, :nout], gps.ap[:, 2:2 + nout], vcol(V_CW + 2 * FC + fc), a2.ap[:, :nout], ALU.mult, ALU.add, reads=[gps, vec, a2], writes=[a3])
                act(sgb.ap[:, :nout], a3.ap[:, :nout], AF.Silu, reads=[a3], writes=[sgb])
                tt(aT.ap[:, fc, :nout], sgb.ap[:, :nout], vps.ap[:, 1:1 + nout], ALU.mult, reads=[sgb, vps], pwrites=[aT])
            for dc in range(DC):
                if dc % 2 == 0:
                    nq = min(2, DC - dc)
                    wb, wdw = wload_bf(wsd, wdnT[dc // 2][:, :FC * nq * 128], FC, nq * 128)
                wv_ = wdw[:, :, (dc % 2) * 128:(dc % 2 + 1) * 128]
                ps = psum[5 + pi % 2]; pi += 1
                for kc in range(FC):
                    mm(ps.ap[:, :nout], wv_[:, kc, :], aT.ap[:, kc, :nout], kc == 0, kc == FC - 1, reads=[wb, aT],
                       writes=[ps] if kc == 0 else (), pwrites=() if kc == 0 else [ps])
                stt(xb.ap[:, dc, :nout], ps.ap[:, :nout], modcol(5, dc, jj), xb.ap[:, dc, :nout], ALU.mult, ALU.add, reads=[ps, mod, xb], pwrites=[xb])
            if kind == "lat" and last:
                dma("sp", fm3(yT, s0, nout), xb.ap[:, :, :nout], xb, reads=[xb], pwrites=[Dy])
            else:
                dst_d = xT_s if kind == "lat" else ctxT_s
                dma("sp", fm3(dst_d, s0, nout), xb.ap[:, :, :nout], xb, reads=[xb], pwrites=[Dsrc])
        P.barrier()

    P.emit("sp", lambda e: e.nop(), reads=[Dy])
    P.finalize()
    global LASTP
    LASTP = P
    return nc


def pack_vecs(cfg, inp, l):
    DC, FC = cfg.DC, cfg.FC
    j = l // 2
    cols = [_fm(inp["norm_mix"][l]), _fm(inp["norm_ffn"][l]), _fm(inp["b_mod"][l])]
    cw = np.asarray(inp["ffn_conv_w"][l], np.float32)
    cols += [_fm(cw[0]), _fm(cw[1]), _fm(cw[2]), _fm(inp["ffn_conv_b"][l])]
    QC = cfg.QR // 128; KC2 = cfg.KVR // 128
    z1 = np.zeros((128, 1), np.float32)
    if l % 2 == 0:
        dup = lambda v: np.concatenate([np.asarray(v, np.float32)] * 2)[:, None]
        cols += [_fm(inp["mla_g_dq"][j]), _fm(inp["mla_g_dkv"][j]), np.asarray(inp["mla_g_q_nope"][j], np.float32)[:, None],
                 dup(inp["mla_g_q_pe"][j]), dup(inp["mla_g_k_pe"][j]), np.asarray(inp["mla_g_k_nope"][j], np.float32)[:, None], z1, z1]
    else:
        cols += [np.zeros((128, QC), np.float32), np.zeros((128, KC2), np.float32), z1, z1, z1, z1,
                 np.asarray(inp["gqa_g_q"][j], np.float32)[:, None], np.asarray(inp["gqa_g_k"][j], np.float32)[:, None]]
    return np.concatenate(cols, axis=1).astype(np.float32)


def prep_inputs(cfg, inp):
    NR, TL, D, C = cfg.NR, cfg.TL, cfg.D, cfg.C
    L = cfg.depth
    f = lambda a: np.ascontiguousarray(np.asarray(a, np.float32))
    vecs = np.stack([pack_vecs(cfg, inp, l) for l in range(L)])
    RgT = rot_matrix_T(128)
    Rm = rot_matrix_T(64)
    RmT = np.zeros((128, 128), np.float32); RmT[:64, :64] = Rm; RmT[64:, 64:] = Rm
    bones = np.zeros((128, 128), np.float32); bones[:64, :64] = 1; bones[64:, 64:] = 1
    shared = dict(RgT=RgT, RmT=RmT, bones=bones, vecs=vecs, w_mod=f(inp["w_mod"]),
                  mla_w_dq=f(inp["mla_w_dq"]), mla_w_uq=f(inp["mla_w_uq"]), mla_w_dkv=f(inp["mla_w_dkv"]),
                  mla_w_ukv=f(inp["mla_w_ukv"]), mla_w_o=f(inp["mla_w_o"]),
                  gqa_w_q=f(inp["gqa_w_q"]), gqa_w_kv=f(inp["gqa_w_kv"]), gqa_w_o=f(inp["gqa_w_o"]),
                  ffn_w_up=f(inp["ffn_w_up"]), ffn_w_down=f(inp["ffn_w_down"]))
    x = np.asarray(inp["x"], np.float32); ctx = np.asarray(inp["ctx"], np.float32)
    c = np.asarray(inp["c"], np.float32); c_ctx = np.asarray(inp["c_ctx"], np.float32)
    maps = []
    for core in range(8):
        b = core // NR; r = core % NR
        t0 = r * TL
        m = dict(shared)
        m["xT"] = np.ascontiguousarray(x[b, t0:t0 + TL].T)
        m["ctxT"] = np.ascontiguousarray(ctx[b].T)
        cv = np.stack([_fm(c[b]), _fm(c_ctx)], axis=-1)
        m["cvec"] = np.ascontiguousarray(cv.reshape(128, -1))
        cg, sg = rope_tables(cfg, t0, TL, 128)
        cm, sm = rope_tables(cfg, t0, TL, 64)
        m["cosg"] = cg; m["sing"] = sg
        m["cosm"] = np.concatenate([cm, cm], 0); m["sinm"] = np.concatenate([sm, sm], 0)
        s = np.zeros((128, 2 * NR), np.float32)
        if r > 0:
            s[:, r - 1] = 1.0
        if r < NR - 1:
            s[:, NR + r + 1] = 1.0
        m["sel"] = s
        maps.append(m)
    return maps


_NC_CACHE = {}


def run(cfg, inp, debug=False):
    key = (cfg.D, cfg.S, cfg.C, cfg.FF, cfg.depth)
    if key not in _NC_CACHE:
        _NC_CACHE[key] = build(cfg, debug)
    nc = _NC_CACHE[key]
    maps = prep_inputs(cfg, inp)
    res = run_bass_kernel_spmd(nc, maps, core_ids=list(range(8)))
    out = np.zeros((cfg.B, cfg.S, cfg.D), np.float32)
    for core in range(8):
        b = core // cfg.NR; r = core % cfg.NR
        out[b, r * cfg.TL:(r + 1) * cfg.TL] = res.results[core]["yT"].T
    return out


def kernel(**inputs):
    return run(Cfg(), inputs)
```

```python
import math
import numpy as np
import concourse.bass as bass
import concourse.mybir as mybir
from concourse.bass_utils import run_bass_kernel_spmd

F32 = mybir.dt.float32
BF16 = mybir.dt.bfloat16
AF = mybir.ActivationFunctionType
ALU = mybir.AluOpType
EPS = 1e-6
ROPE_BASE = 10000.0


class Cfg:
    def __init__(s, **kw):
        s.D = 2048; s.S = 8192; s.B = 2; s.C = 256; s.FF = 5632; s.depth = 4; s.GRID_W = 64
        s.MH = 16; s.QR = 512; s.KVR = 512; s.GH = 16; s.GKV = 4
        s.W1 = 512; s.W4 = 410; s.NR = 4
        s.NWS = 3; s.NWU = 4; s.NWD = 2; s.WDSPLIT = 1
        for k, v in kw.items():
            setattr(s, k, v)
        s.TL = s.S // s.NR
        s.DC = s.D // 128; s.FC = s.FF // 128
        s.NTOK = s.TL + s.C


class Op:
    __slots__ = ("eng", "fn", "deps", "marked", "cum", "is_dma", "sem", "val", "inc", "idx", "epoch")


class Buf:
    def __init__(s, ap=None, sem=None):
        s.ap = ap; s.w = []; s.r = []; s.gen = []; s.sem = sem; s.cnt = 0

    def new_gen(s):
        s.gen = s.w + s.r; s.w = []; s.r = []


class Prog:
    def __init__(s, nc):
        s.nc = nc
        s.ops = {k: [] for k in ("pe", "act", "dve", "pool", "sp")}
        s.engsem = {}
        s.epoch = 0; s.nidx = 0
        s.ccsem = nc.alloc_semaphore("ccsem"); s.cccnt = 0

    def new_epoch(s):
        s.epoch += 1

    def esem(s, eng, epoch):
        k = (eng, epoch)
        if k not in s.engsem:
            s.engsem[k] = s.nc.alloc_semaphore("es_%s_%d" % (eng, epoch))
        return s.engsem[k]

    def emit(s, eng, fn, reads=(), writes=(), pwrites=(), extra=()):
        deps = list(extra)
        for b in reads:
            deps += b.w
        for b in writes:
            b.new_gen(); deps += b.gen
        for b in pwrites:
            deps += b.gen
        best = {}
        for d in deps:
            if d.is_dma:
                k = ("s", d.sem.num); v = d.val
            else:
                k = ("e", d.eng); v = d.idx
            o = best.get(k)
            if o is None or v > o[0]:
                best[k] = (v, d)
        deps = [o[1] for o in best.values()]
        op = Op(); op.eng = eng; op.fn = fn; op.deps = deps; op.marked = False; op.cum = 0
        op.is_dma = False; op.sem = None; op.val = 0; op.inc = 0
        op.idx = s.nidx; s.nidx += 1; op.epoch = s.epoch
        for b in reads:
            b.r.append(op)
        for b in writes:
            b.w.append(op)
        for b in pwrites:
            b.w.append(op)
        s.ops[eng].append(op)
        return op

    def dma(s, eng, out, in_, sb, reads=(), writes=(), pwrites=(), extra=()):
        op = s.emit(eng, lambda e: e.dma_start(out=out, in_=in_), reads, writes, pwrites, extra)
        sb.cnt += 16
        op.is_dma = True; op.sem = sb.sem; op.val = sb.cnt; op.inc = 16
        return op

    def cc(s, kind, groups, in_ap, out_ap, reads=(), writes=(), pwrites=()):
        op = s.emit("pool", lambda e: e.collective_compute(kind, ALU.bypass, replica_groups=groups,
                                                            ins=[in_ap], outs=[out_ap]), reads, writes, pwrites)
        s.cccnt += 1
        op.is_dma = True; op.sem = s.ccsem; op.val = s.cccnt; op.inc = 1
        return op

    def barrier(s):
        deps = []
        latest = {}
        for eng, ops in s.ops.items():
            lastc = None
            for op in ops:
                if op.is_dma:
                    latest[op.sem.num] = op
                else:
                    lastc = op
            if lastc is not None:
                deps.append(lastc)
        deps += list(latest.values())
        for eng in s.ops:
            s.emit(eng, lambda e: e.nop(), extra=deps)

    def finalize(s):
        nc = s.nc
        for eng, ops in s.ops.items():
            for op in ops:
                for d in op.deps:
                    if not d.is_dma:
                        d.marked = True
        for eng, ops in s.ops.items():
            cnt = {}
            for op in ops:
                if (not op.is_dma) and op.marked:
                    cnt[op.epoch] = cnt.get(op.epoch, 0) + 1
                    s.esem(eng, op.epoch)
                op.cum = cnt.get(op.epoch, 0)

        def run(engname, e):
            waited = {}
            for op in s.ops[engname]:
                for d in op.deps:
                    if d.is_dma:
                        key = ("s", d.sem.num); val = d.val; sem = d.sem
                    else:
                        if d.eng == "pe" and engname == "pe":
                            continue
                        key = ("e", d.eng, d.epoch); val = d.cum; sem = s.engsem[(d.eng, d.epoch)]
                    if waited.get(key, 0) >= val:
                        continue
                    waited[key] = val
                    e.wait_ge(sem, val)
                ins = op.fn(e)
                if op.is_dma:
                    if op.inc == 16:
                        ins.then_inc(op.sem, 16)
                    else:
                        ins.then_inc(op.sem)
                elif op.marked:
                    ins.then_inc(s.engsem[(op.eng, op.epoch)], 1)

        with nc.Block() as block:
            @block.sync
            def _(e):
                run("sp", e)

            @block.gpsimd
            def _(e):
                run("pool", e)

            @block.scalar
            def _(e):
                run("act", e)

            @block.vector
            def _(e):
                run("dve", e)

            @block.tensor
            def _(e):
                run("pe", e)


def _fm(v):
    v = np.asarray(v, np.float32)
    return np.ascontiguousarray(v.reshape(-1, 128).T)


def rope_tables(cfg, tok0, n, rot_dim):
    axis_dim = rot_dim // 2
    inv = np.power(np.float32(ROPE_BASE), -np.arange(0, axis_dim, 2, dtype=np.float32) / np.float32(axis_dim)).astype(np.float32)
    t = np.arange(tok0, tok0 + n)
    rows = (t // cfg.GRID_W).astype(np.float32); cols = (t % cfg.GRID_W).astype(np.float32)
    ar = rows[:, None] * inv; ac = cols[:, None] * inv
    ang = np.concatenate([ar, ar, ac, ac], axis=-1).astype(np.float32)
    return np.cos(ang).T.astype(np.float32), np.sin(ang).T.astype(np.float32)


def rot_matrix_T(rot_dim):
    half = rot_dim // 2; q = half // 2
    R = np.zeros((rot_dim, rot_dim), np.float32)
    for base in (0, half):
        for i in range(q):
            R[base + i, base + i + q] = -1.0
            R[base + q + i, base + i] = 1.0
    return np.ascontiguousarray(R.T)


def build(cfg, debug=False):
    nc = bass.Bass("TRN2", target_bir_lowering=False)
    P = Prog(nc)
    D, TL, C, FF, DC, FC, NTOK, NR = cfg.D, cfg.TL, cfg.C, cfg.FF, cfg.DC, cfg.FC, cfg.NTOK, cfg.NR
    L = cfg.depth
    LA = (L + 1) // 2; LB = max(L // 2, 1)
    MH, QR, KVR, GH, GKV = cfg.MH, cfg.QR, cfg.KVR, cfg.GH, cfg.GKV
    QC = QR // 128; KC2 = KVR // 128
    T = NR * TL + C
    NCH = T // 128
    W1 = cfg.W1; W4 = cfg.W4

    def din(name, shape, dt=F32):
        return nc.dram_tensor(name, list(shape), dt, kind="ExternalInput").ap()

    def dscr(name, shape, dt):
        return nc.dram_tensor(name, list(shape), dt).ap()

    xT_in = din("xT", [D, TL]); ctxT_in = din("ctxT", [D, C]); cvec = din("cvec", [128, DC * 2])
    cosg = din("cosg", [128, TL]); sing = din("sing", [128, TL]); cosm = din("cosm", [128, TL]); sinm = din("sinm", [128, TL])
    RgT = din("RgT", [128, 128]); RmT = din("RmT", [128, 128]); bones = din("bones", [128, 128]); sel = din("sel", [128, 2 * NR])
    NV = 2 * DC + 6 * DC + 4 * FC + QC + KC2 + 6
    vecs = din("vecs", [L, 128, NV])
    w_mod = din("w_mod", [L, D, 6 * D])
    mla_w_dq = din("mla_w_dq", [LA, D, QR]); mla_w_uq = din("mla_w_uq", [LA, QR, MH * 192])
    mla_w_dkv = din("mla_w_dkv", [LA, D, KVR + 64]); mla_w_ukv = din("mla_w_ukv", [LA, KVR, MH * 256])
    mla_w_o = din("mla_w_o", [LA, MH * 128, D])
    gqa_w_q = din("gqa_w_q", [LB, D, GH * 128]); gqa_w_kv = din("gqa_w_kv", [LB, D, 2 * GKV * 128])
    gqa_w_o = din("gqa_w_o", [LB, GH * 128, D])
    ffn_w_up = din("ffn_w_up", [L, D, 2 * FF]); ffn_w_down = din("ffn_w_down", [L, FF, D])
    yT = nc.dram_tensor("yT", [D, TL], F32, kind="ExternalOutput").ap()

    xT_s = dscr("xT_s", [D, TL], F32); ctxT_s = dscr("ctxT_s", [D, C], F32)
    qT_d = dscr("qT_d", [max(MH, GH) * 128, NTOK], BF16); qpeT_d = dscr("qpeT_d", [MH // 2 * 128, NTOK], BF16)
    kvs = {}
    for nm, nkv in (("m", MH), ("g", GKV)):
        HB = 2 if nkv % 2 == 0 else 1
        while HB > 1 and HB * 128 * TL * 2 > (1 << 20):
            HB //= 2
        nb = nkv // HB
        kvs[nm] = dict(HB=HB, NB=nb,
                       kT_src=[dscr("kT_src%s%d" % (nm, i), [HB * 128, TL], BF16) for i in range(nb)],
                       kT_all=[dscr("kT_all%s%d" % (nm, i), [NR * HB * 128, TL], BF16) for i in range(nb)],
                       v_src=[dscr("v_src%s%d" % (nm, i), [TL, HB * 128], BF16) for i in range(nb)],
                       v_all=[dscr("v_all%s%d" % (nm, i), [NR * TL, HB * 128], BF16) for i in range(nb)],
                       kT_ctx=dscr("kT_ctx" + nm, [nkv * 128, C], BF16), v_ctx=dscr("v_ctx" + nm, [C, nkv * 128], BF16))
    kpe_src = dscr("kpe_src", [128, TL], BF16); kpe_all = dscr("kpe_all", [NR * 128, TL], BF16); kpe_ctx = dscr("kpe_ctx", [128, C], BF16)
    h2T_d = dscr("h2T_d", [D, TL + 2], BF16); h2cT_d = dscr("h2cT_d", [D, C + 2], BF16)
    hsrc = dscr("hsrc", [128, DC * 2], BF16); hall = dscr("hall", [NR * 128, DC * 2], BF16)
    NUT = (FC + 1) // 2; NDT = (DC + 1) // 2
    wupT = dscr("wupT", [NUT * 2, 128, DC * 256], BF16); wdnT = dscr("wdnT", [NDT, 128, FC * 256], BF16)
    Dwc = Buf()
    Dx = Buf(); Dctx = Buf(); Dq = Buf(); Dqpe = Buf(); Dksrc = Buf(); Dkall = Buf(); Dvsrc = Buf(); Dvall = Buf()
    Dkpesrc = Buf(); Dkpeall = Buf(); Dkctx = Buf(); Dvctx = Buf(); Dkpectx = Buf(); Dh2 = Buf(); Dh2c = Buf(); Dhsrc = Buf(); Dhall = Buf()
    Dy = Buf()

    ARENA_BYTES = 207 * 1024
    arena = nc.alloc_sbuf_tensor("arena", [128, ARENA_BYTES // 4], F32)
    apos = [0]
    sempool = {}
    semidx = [0]
    persist_sems = [0]

    def phase_reset(base):
        apos[0] = base; semidx[0] = persist_sems[0]

    def carve(shape, dt, sem=True):
        esz = 4 if dt == F32 else 2
        nb = int(np.prod(shape)) * esz
        nb_al = (nb + 63) // 64 * 64
        off = apos[0]; apos[0] += nb_al
        assert apos[0] <= ARENA_BYTES, ("SBUF arena overflow", apos[0])
        a = arena[:, off // 4:(off + nb) // 4]
        if dt != F32:
            a = a.bitcast(dt)
        if len(shape) == 2:
            a = a.rearrange("p (a b) -> p a b", a=shape[0])
        elif len(shape) == 3:
            a = a.rearrange("p (a b c) -> p a b c", a=shape[0], b=shape[1])
        sm = None
        if sem:
            k = semidx[0]; semidx[0] += 1
            if k not in sempool:
                sempool[k] = [nc.alloc_semaphore("bs%d" % k), 0]
            sm = sempool[k]
        b = Buf(a, sm[0] if sm else None)
        if sm:
            b.cnt = sm[1]; b.semrec = sm
        return b

    def mm(out, lhsT, rhs, start, stop, reads, writes=(), pwrites=()):
        return P.emit("pe", lambda e: e.matmul(out, lhsT, rhs, start=start, stop=stop), reads=reads, writes=writes, pwrites=pwrites)

    def act(out, in_, func, reads, writes=(), pwrites=(), bias=None, scale=None, eng="act"):
        kw = {}
        if bias is not None:
            kw["bias"] = bias
        if scale is not None:
            kw["scale"] = scale
        return P.emit(eng, lambda e: e.activation(out, in_, func, **kw), reads=reads, writes=writes, pwrites=pwrites)

    def stt(out, in0, scalar, in1, op0, op1, reads, writes=(), pwrites=(), eng="dve"):
        return P.emit(eng, lambda e: e.scalar_tensor_tensor(out, in0, scalar, in1, op0, op1), reads=reads, writes=writes, pwrites=pwrites)

    def tt(out, in0, in1, op, reads, writes=(), pwrites=(), eng="dve"):
        return P.emit(eng, lambda e: e.tensor_tensor(out, in0, in1, op), reads=reads, writes=writes, pwrites=pwrites)

    def tsm(out, in0, s1, reads, writes=(), pwrites=(), eng="dve"):
        return P.emit(eng, lambda e: e.tensor_scalar(out, in0, s1, None, ALU.mult), reads=reads, writes=writes, pwrites=pwrites)

    def recip(out, in_, reads, writes=(), pwrites=()):
        return P.emit("dve", lambda e: e.reciprocal(out, in_), reads=reads, writes=writes, pwrites=pwrites)

    def cpy(out, in_, reads, writes=(), pwrites=(), eng="dve"):
        return P.emit(eng, lambda e: e.tensor_copy(out, in_), reads=reads, writes=writes, pwrites=pwrites)

    def mset(out, val, writes=(), pwrites=(), reads=(), eng="dve"):
        return P.emit(eng, lambda e: e.memset(out, val), reads=reads, writes=writes, pwrites=pwrites)

    def dma(eng, out, in_, sb, reads=(), writes=(), pwrites=()):
        sb.cnt = sb.semrec[1]
        op = P.dma(eng, out, in_, sb, reads=reads, writes=writes, pwrites=pwrites)
        sb.semrec[1] = sb.cnt
        return op

    ones_f = carve([128], F32, False); ones_b = carve([128], BF16, False); bones_f = carve([128], F32)
    Rg_f = carve([128], F32); Rm_f = carve([128], F32); sel_f = carve([2 * NR], F32)
    sv_f = carve([DC * 2], F32); sv_b = carve([DC, 2], BF16, False)
    vec = carve([NV], F32)
    mod = carve([6 * DC, 2], F32, False)
    A1 = carve([DC, 2], F32, False); A2 = carve([DC, 2], F32, False)
    gsc = carve([8], F32, False)
    bnd = carve([DC, 2], BF16); halo = carve([DC, 2], BF16, False); hs = carve([NR, DC, 2], BF16)
    halo_f = carve([DC, 2], F32, False)
    eps_c = carve([1], F32, False)
    bones_b = carve([128], BF16, False); Rg_b = carve([128], BF16, False); Rm_b = carve([128], BF16, False)
    cvl = [carve([1], F32) for _ in range(3)]
    psum = [Buf(nc.alloc_psum_tensor("ps%d" % i, [128, 512], F32)[:]) for i in range(8)]
    PERSIST = apos[0]
    persist_sems[0] = semidx[0]

    o = 0
    V_NMIX = o; o += DC
    V_NFFN = o; o += DC
    V_BMOD = o; o += 6 * DC
    V_CW = o; o += 3 * FC
    V_CB = o; o += FC
    V_GDQ = o; o += QC
    V_GDKV = o; o += KC2
    V_GQN = o; o += 1
    V_GQP = o; o += 1
    V_GKP = o; o += 1
    V_GKN = o; o += 1
    V_GQ = o; o += 1
    V_GK = o; o += 1
    assert o == NV

    def vcol(i):
        return vec.ap[:, i:i + 1]

    def modcol(m, c, jj):
        return mod.ap[:, m * DC + c, jj:jj + 1]

    NWS = cfg.NWS

    mset(ones_f.ap, 1.0, writes=[ones_f]); mset(ones_b.ap, 1.0, writes=[ones_b]); mset(eps_c.ap, EPS, writes=[eps_c])
    dma("sp", bones_f.ap, bones[:, :], bones_f, writes=[bones_f])
    dma("sp", Rg_f.ap, RgT[:, :], Rg_f, writes=[Rg_f])
    dma("sp", Rm_f.ap, RmT[:, :], Rm_f, writes=[Rm_f])
    dma("sp", sel_f.ap, sel[:, :], sel_f, writes=[sel_f])
    dma("sp", sv_f.ap, cvec[:, :], sv_f, writes=[sv_f])
    act(sv_b.ap.rearrange("p a b -> p (a b)"), sv_f.ap, AF.Silu, reads=[sv_f], writes=[sv_b])
    cpy(bones_b.ap, bones_f.ap, reads=[bones_f], writes=[bones_b])
    cpy(Rg_b.ap, Rg_f.ap, reads=[Rg_f], writes=[Rg_b])
    cpy(Rm_b.ap, Rm_f.ap, reads=[Rm_f], writes=[Rm_b])

    lat_tiles = [("lat", t0, min(W1, TL - t0)) for t0 in range(0, TL, W1)]
    ctx_tiles = [("ctx", t0, min(W1, C - t0)) for t0 in range(0, C, W1)]
    groups = [list(range(g * NR, (g + 1) * NR)) for g in range(8 // NR)]

    def xsrc(kind, l):
        if kind == "lat":
            return (xT_in if l == 0 else xT_s), Dx
        return (ctxT_in if l == 0 else ctxT_s), Dctx

    def fm3(dram2d, c0, n):
        return dram2d.rearrange("(c p) t -> p c t", p=128)[:, :, c0:c0 + n]

    def rstd_from_ss(ssps, n, nd, rs_sq, rs):
        act(rs_sq.ap[:, :n], ssps.ap[:, :n], AF.Sqrt, reads=[ssps, eps_c], writes=[rs_sq], bias=eps_c.ap[:, 0:1], scale=1.0 / nd)
        recip(rs.ap[:, :n], rs_sq.ap[:, :n], reads=[rs_sq], writes=[rs])

    for l in range(L):
        last = (l == L - 1)
        is_mla = (l % 2 == 0)
        j = l // 2
        P.new_epoch()
        NH = MH if is_mla else GH
        NKV = MH if is_mla else GKV
        GRP = NH // NKV
        KV = kvs["m" if is_mla else "g"]
        kT_src, kT_all, v_src, v_all, kT_ctx, v_ctx = KV["kT_src"], KV["kT_all"], KV["v_src"], KV["v_all"], KV["kT_ctx"], KV["v_ctx"]
        HB, NB = KV["HB"], KV["NB"]

        def ksrc_rows(hd):
            return kT_src[hd // HB][(hd % HB) * 128:(hd % HB + 1) * 128, :]

        def vsrc_cols(tok0, ntok, h0, nh):
            out = []
            hd = h0
            while hd < h0 + nh:
                k = min(HB - hd % HB, h0 + nh - hd)
                out.append((v_src[hd // HB][tok0:tok0 + ntok, (hd % HB) * 128:(hd % HB + k) * 128], (hd - h0) * 128, k * 128))
                hd += k
            return out

        phase_reset(PERSIST)
        dma("sp", vec.ap, vecs[l], vec, writes=[vec])
        wsm = [carve([DC, 512], BF16) for _ in range(3)]
        modps = psum[0]
        modps.new_gen()
        nfg = (6 * DC + 3) // 4
        wm3 = w_mod[l].rearrange("(kc p) f -> p kc f", p=128)
        for fg in range(nfg):
            nf = min(4, 6 * DC - fg * 4)
            wb = wsm[fg % 3]
            dma("pool", wb.ap[:, :, :nf * 128], wm3[:, :, fg * 512:fg * 512 + nf * 128], wb, writes=[wb])
            for fc in range(nf):
                col = (fg * 4 + fc) * 2
                for kc in range(DC):
                    mm(modps.ap[:, col:col + 2], wb.ap[:, kc, fc * 128:(fc + 1) * 128], sv_b.ap[:, kc, :], kc == 0, kc == DC - 1,
                       reads=[wb, sv_b], pwrites=[modps])
        mp3 = modps.ap[:, 0:12 * DC].rearrange("p (a b) -> p a b", b=2)
        for jj in range(2):
            tt(mod.ap[:, :, jj], mp3[:, :, jj], vec.ap[:, V_BMOD:V_BMOD + 6 * DC], ALU.add, reads=[modps, vec],
               writes=[mod] if jj == 0 else (), pwrites=[mod] if jj else ())
        for (Ab, voff, m) in ((A1, V_NMIX, 1), (A2, V_NFFN, 4)):
            for jj in range(2):
                stt(Ab.ap[:, :, jj], mod.ap[:, m * DC:(m + 1) * DC, jj], 1.0, vec.ap[:, voff:voff + DC], ALU.add, ALU.mult,
                    reads=[mod, vec], writes=[Ab] if jj == 0 else (), pwrites=[Ab] if jj else ())
        if is_mla:
            sc = 1.0 / math.sqrt(192.0)
            glist = [(0, V_GQN, sc), (1, V_GQP, sc), (2, V_GKP, 1.0), (3, V_GKN, 1.0)]
        else:
            sc = 1.0 / math.sqrt(128.0)
            glist = [(0, V_GQ, sc), (1, V_GK, 1.0)]
        for gi, (gidx, vo, mul) in enumerate(glist):
            tsm(gsc.ap[:, gidx:gidx + 1], vec.ap[:, vo:vo + 1], mul, reads=[vec], writes=[gsc] if gi == 0 else (), pwrites=[gsc] if gi else ())
        P.barrier()

        phase_reset(PERSIST)
        xt = [carve([DC, W1], F32) for _ in range(1)]
        hT = [carve([DC, W1], BF16, False) for _ in range(2)]
        sq = [carve([W1], BF16, False) for _ in range(2)]
        rs_sq = carve([W1], F32, False); rs = carve([W1], F32, False)
        tmpf = [carve([W1], F32, False) for _ in range(2)]
        cs = [carve([W1], F32) for _ in range(2)]; sn = [carve([W1], F32) for _ in range(2)]
        NSET = 3
        nsets = [dict(sq=carve([W1], BF16, False), rs_sq=carve([W1], F32, False), rs=carve([W1], F32, False),
                      qn=carve([W1], BF16, False), t1=carve([W1], F32, False), t2=carve([W1], F32, False)) for _ in range(NSET)]
        stg = [carve([W1], BF16) for _ in range(3)]
        vstg = [carve([512], BF16) for _ in range(2)]
        ws = [carve([max(DC, QC, KC2), 512], BF16) for _ in range(NWS)]
        if is_mla:
            cq_f = carve([max(QC, KC2), W1], F32, False); cqn = carve([QC, W1], BF16, False); ckvn = carve([KC2, W1], BF16, False)
        cnt = dict(ws=0, stg=0, vstg=0, pi=0)

        def wload(ring, src3, kc, m, split=1):
            b = ring[cnt["ws"] % len(ring)]; cnt["ws"] += 1
            view = b.ap.rearrange("p a b -> p (a b)")[:, :kc * m].rearrange("p (a b) -> p a b", a=kc)
            step = (kc + split - 1) // split
            for i, k0 in enumerate(range(0, kc, step)):
                k1 = min(kc, k0 + step)
                dma("pool", view[:, k0:k1, :], src3[:, k0:k1, :], b, writes=[b] if i == 0 else (), pwrites=[b] if i else ())
            return b, view

        def wload4(ring, src4, kc, nh):
            b = ring[cnt["ws"] % len(ring)]; cnt["ws"] += 1
            view = b.ap.rearrange("p a b -> p (a b)")[:, :kc * nh * 128].rearrange("p (a h e) -> p a h e", a=kc, h=nh)
            for hq in range(nh):
                dma("pool", view[:, :, hq, :], src4[:, :, hq, :], b, writes=[b] if hq == 0 else (), pwrites=[b] if hq else ())
            return b, view

        def wload2(ring, srcA, srcB, kc):
            b = ring[cnt["ws"] % len(ring)]; cnt["ws"] += 1
            view = b.ap.rearrange("p a b -> p (a b)")[:, :kc * 128].rearrange("p (a b) -> p a b", a=kc)
            dma("pool", view[:, :, 0:64], srcA, b, writes=[b])
            dma("pool", view[:, :, 64:128], srcB, b, pwrites=[b])
            return b, view

        def modulate(xb, n, Ab, shm, jj, hb, sq, rs_sq, rs, tmpf):
            ssps = psum[1]
            for c in range(DC):
                s_ = sq[c % 2]
                act(s_.ap[:, :n], xb.ap[:, c, :n], AF.Square, reads=[xb], writes=[s_])
                mm(ssps.ap[:, :n], ones_b.ap, s_.ap[:, :n], c == 0, c == DC - 1, reads=[s_, ones_b],
                   writes=[ssps] if c == 0 else (), pwrites=() if c == 0 else [ssps])
            rstd_from_ss(ssps, n, float(D), rs_sq, rs)
            hb.new_gen()
            for c in range(DC):
                tf = tmpf[c % 2]
                stt(tf.ap[:, :n], xb.ap[:, c, :n], Ab.ap[:, c, jj:jj + 1], rs.ap[:, :n], ALU.mult, ALU.mult, reads=[xb, Ab, rs], writes=[tf])
                act(hb.ap[:, c, :n], tf.ap[:, :n], AF.Identity, reads=[tf, mod], pwrites=[hb], bias=modcol(shm, c, jj), scale=1.0)

        def run_jobs(jobs, n, cb, sb_):
            N = len(jobs)
            pbuf = [psum[3], psum[4], psum[5]]
            st_of = {}

            def A(i):
                J = jobs[i]
                wb, wv_ = J["w"]()
                ps = pbuf[i % 3]
                proj(ps, wv_, wb, J["src"], J["kcs"], n)
                S_ = nsets[i % NSET]
                act(S_["sq"].ap[:, :n], ps.ap[:, :n], AF.Square, reads=[ps], writes=[S_["sq"]])

            def B(i):
                J = jobs[i]
                ps = pbuf[i % 3]
                S_ = nsets[i % NSET]
                ssps = psum[1 + i % 2]
                mm(ssps.ap[:, :n], J["ones"].ap, S_["sq"].ap[:, :n], True, True, reads=[S_["sq"], J["ones"]], writes=[ssps])
                rstd_from_ss(ssps, n, float(J["nd"]), S_["rs_sq"], S_["rs"])
                if not J["rope"]:
                    st = stg[cnt["stg"] % 3]; cnt["stg"] += 1
                    stt(st.ap[:, :n], ps.ap[:, :n], J["gcol"], S_["rs"].ap[:, :n], ALU.mult, ALU.mult, reads=[ps, S_["rs"], gsc], writes=[st])
                    dma("sp", J["dst"], st.ap[:, :n], st, reads=[st], pwrites=[J["Ddst"]])
                else:
                    stt(S_["qn"].ap[:, :n], ps.ap[:, :n], J["gcol"], S_["rs"].ap[:, :n], ALU.mult, ALU.mult, reads=[ps, S_["rs"], gsc], writes=[S_["qn"]])

            def C(i):
                J = jobs[i]
                if not J["rope"]:
                    return
                S_ = nsets[i % NSET]
                qn = S_["qn"]; t1 = S_["t1"]; t2 = S_["t2"]
                rps = psum[7] if i % 2 else psum[0]
                mm(rps.ap[:, :n], J["R"].ap, qn.ap[:, :n], True, True, reads=[J["R"], qn], writes=[rps])
                tt(t1.ap[:, :n], qn.ap[:, :n], cb.ap[:, :n], ALU.mult, reads=[qn, cb], writes=[t1])
                tt(t2.ap[:, :n], rps.ap[:, :n], sb_.ap[:, :n], ALU.mult, reads=[rps, sb_], writes=[t2])
                st = stg[cnt["stg"] % 3]; cnt["stg"] += 1
                tt(st.ap[:, :n], t1.ap[:, :n], t2.ap[:, :n], ALU.add, reads=[t1, t2], writes=[st])
                dma("sp", J["dst"], st.ap[:, :n], st, reads=[st], pwrites=[J["Ddst"]])

            for i in range(N + 2):
                if i < N:
                    A(i)
                if 0 <= i - 1 < N:
                    B(i - 1)
                if 0 <= i - 2 < N:
                    C(i - 2)

        def job(wfn, src, kcs, nd, onesb, gcol, rope, Rb, dst_ap, Ddst):
            return dict(w=wfn, src=src, kcs=kcs, nd=nd, ones=onesb, gcol=gcol, rope=rope, R=Rb, dst=dst_ap, Ddst=Ddst)

        def shared_w(loader):
            box = []

            def get():
                if not box:
                    box.append(loader())
                return box[0]
            return get

        def proj(ps, wview, wb, src, kcs, n):
            for kc in range(kcs):
                mm(ps.ap[:, :n], wview[:, kc, :], src.ap[:, kc, :n], kc == 0, kc == kcs - 1, reads=[wb, src],
                   writes=[ps] if kc == 0 else (), pwrites=() if kc == 0 else [ps])

        def nextps():
            p_ = psum[3 + cnt["pi"] % 2]; cnt["pi"] += 1
            return p_

        def vproj(src, kcs, n, wbs, dst_fn, Ddst):
            ncol = len(wbs) * 128
            for s0 in range(0, n, 128):
                vps = psum[6]
                vps.new_gen()
                for qi, (wb, wv_) in enumerate(wbs):
                    for kc in range(kcs):
                        mm(vps.ap[:, qi * 128:(qi + 1) * 128], src.ap[:, kc, s0:s0 + 128], wv_[:, kc, :], kc == 0, kc == kcs - 1,
                           reads=[src, wb], pwrites=[vps])
                vs = vstg[cnt["vstg"] % 2]; cnt["vstg"] += 1
                act(vs.ap[:, :ncol], vps.ap[:, :ncol], AF.Copy, reads=[vps], writes=[vs])
                for (dap, coff, ncl) in dst_fn(s0):
                    dma("sp", dap, vs.ap[:, coff:coff + ncl], vs, reads=[vs], pwrites=[Ddst])

        for D_ in (Dq, Dqpe, Dksrc, Dvsrc, Dkpesrc, Dkctx, Dvctx, Dkpectx):
            D_.new_gen()
        tiles = lat_tiles + ctx_tiles
        for ti, (kind, t0, n) in enumerate(tiles):
            jj = 0 if kind == "lat" else 1
            tokq = t0 if kind == "lat" else TL + t0
            xb = xt[0]; hb = hT[ti % 2]
            src_d, Dsrc = xsrc(kind, l)
            dma("sp", xb.ap[:, :, :n], fm3(src_d, t0, n), xb, reads=[Dsrc], writes=[xb])
            modulate(xb, n, A1, 0, jj, hb, sq, rs_sq, rs, tmpf)
            rope = (kind == "lat")
            need_q = not (kind == "ctx" and last)
            cb = sb_ = None
            if rope:
                cb = cs[ti % 2]; sb_ = sn[ti % 2]
                dma("sp", cb.ap[:, :n], (cosm if is_mla else cosg)[:, t0:t0 + n], cb, writes=[cb])
                dma("sp", sb_.ap[:, :n], (sinm if is_mla else sing)[:, t0:t0 + n], sb_, writes=[sb_])
            kdst = (lambda r0, r1: ksrc_rows(r0 // 128)[:, t0:t0 + n]) if kind == "lat" else (lambda r0, r1: kT_ctx[r0:r1, t0:t0 + n])
            Dk = Dksrc if kind == "lat" else Dkctx
            Dv = Dvsrc if kind == "lat" else Dvctx
            if not is_mla:
                wq = gqa_w_q[j].rearrange("(kc p) m -> p kc m", p=128)
                wkv = gqa_w_kv[j].rearrange("(kc p) m -> p kc m", p=128)
                jobs = []
                if need_q:
                    for h0 in range(0, GH, 4):
                        nh = min(4, GH - h0)
                        sw = shared_w(lambda h0=h0, nh=nh: wload(ws, wq[:, :, h0 * 128:(h0 + nh) * 128], DC, nh * 128))
                        for hq in range(nh):
                            h = h0 + hq
                            jobs.append(job((lambda sw=sw, hq=hq: (sw()[0], sw()[1][:, :, hq * 128:(hq + 1) * 128])), hb, DC, 128, ones_b,
                                            gsc.ap[:, 0:1], rope, Rg_b, qT_d[h * 128:(h + 1) * 128, tokq:tokq + n], Dq))
                for g0 in range(0, GKV, 4):
                    ng = min(4, GKV - g0)
                    sw = shared_w(lambda g0=g0, ng=ng: wload(ws, wkv[:, :, g0 * 128:(g0 + ng) * 128], DC, ng * 128))
                    for gq in range(ng):
                        g = g0 + gq
                        jobs.append(job((lambda sw=sw, gq=gq: (sw()[0], sw()[1][:, :, gq * 128:(gq + 1) * 128])), hb, DC, 128, ones_b,
                                        gsc.ap[:, 1:2], rope, Rg_b, kdst(g * 128, (g + 1) * 128), Dk))
                run_jobs(jobs, n, cb, sb_)
                for c0 in range(0, GKV, 4):
                    ng = min(4, GKV - c0)
                    wb, wv_ = wload(ws, wkv[:, :, (GKV + c0) * 128:(GKV + c0 + ng) * 128], DC, ng * 128)
                    wbs = [(wb, wv_[:, :, q4 * 128:(q4 + 1) * 128]) for q4 in range(ng)]
                    if kind == "lat":
                        vproj(hb, DC, n, wbs, (lambda s0, c0=c0, ng=ng: vsrc_cols(t0 + s0, 128, c0, ng)), Dv)
                    else:
                        vproj(hb, DC, n, wbs, (lambda s0, c0=c0, ng=ng: [(v_ctx[t0 + s0:t0 + s0 + 128, c0 * 128:(c0 + ng) * 128], 0, ng * 128)]), Dv)
            else:
                wdq = mla_w_dq[j].rearrange("(kc p) m -> p kc m", p=128)
                wuq = mla_w_uq[j].rearrange("(kc p) m -> p kc m", p=128)
                wdkv = mla_w_dkv[j].rearrange("(kc p) m -> p kc m", p=128)
                wukv = mla_w_ukv[j].rearrange("(kc p) m -> p kc m", p=128)

                def compress(wsrc, nchunk, gv_off, outn):
                    ssps = psum[1]
                    outn.new_gen(); cq_f.new_gen()
                    for oc in range(nchunk):
                        if oc % 4 == 0:
                            nq = min(4, nchunk - oc)
                            wb, wvw = wload(ws, wsrc[:, :, oc * 128:(oc + nq) * 128], DC, nq * 128)
                        wv_ = wvw[:, :, (oc % 4) * 128:(oc % 4 + 1) * 128]
                        ps = nextps()
                        proj(ps, wv_, wb, hb, DC, n)
                        act(cq_f.ap[:, oc, :n], ps.ap[:, :n], AF.Copy, reads=[ps], pwrites=[cq_f])
                        s_ = sq[oc % 2]
                        act(s_.ap[:, :n], ps.ap[:, :n], AF.Square, reads=[ps], writes=[s_])
                        mm(ssps.ap[:, :n], ones_b.ap, s_.ap[:, :n], oc == 0, oc == nchunk - 1, reads=[s_, ones_b],
                           writes=[ssps] if oc == 0 else (), pwrites=() if oc == 0 else [ssps])
                    rstd_from_ss(ssps, n, float(nchunk * 128), rs_sq, rs)
                    for oc in range(nchunk):
                        stt(outn.ap[:, oc, :n], cq_f.ap[:, oc, :n], vcol(gv_off + oc), rs.ap[:, :n], ALU.mult, ALU.mult,
                            reads=[cq_f, rs, vec], pwrites=[outn])

                if need_q:
                    compress(wdq, QC, V_GDQ, cqn)
                    wuq4 = wuq.rearrange("p k (h e) -> p k h e", e=192)
                    jobs = []
                    for h0 in range(0, MH, 4):
                        nq = min(4, MH - h0)
                        sw = shared_w(lambda h0=h0, nq=nq: wload4(ws, wuq4[:, :, h0:h0 + nq, 0:128], QC, nq))
                        for hq in range(nq):
                            h = h0 + hq
                            jobs.append(job((lambda sw=sw, hq=hq: (sw()[0], sw()[1][:, :, hq, :])), cqn, QC, 128, ones_b,
                                            gsc.ap[:, 0:1], False, None, qT_d[h * 128:(h + 1) * 128, tokq:tokq + n], Dq))
                    for hp in range(MH // 2):
                        h0 = 2 * hp
                        jobs.append(job((lambda h0=h0: wload2(ws, wuq[:, :, h0 * 192 + 128:h0 * 192 + 192], wuq[:, :, (h0 + 1) * 192 + 128:(h0 + 1) * 192 + 192], QC)),
                                        cqn, QC, 64, bones_b, gsc.ap[:, 1:2], rope, Rm_b, qpeT_d[hp * 128:(hp + 1) * 128, tokq:tokq + n], Dqpe))
                    run_jobs(jobs, n, cb, sb_)
                compress(wdkv, KC2, V_GDKV, ckvn)
                jobs = []
                jobs.append(job((lambda: wload2(ws, wdkv[:, :, KVR:KVR + 64], wdkv[:, :, KVR:KVR + 64], DC)), hb, DC, 64, bones_b, gsc.ap[:, 2:3],
                                rope, Rm_b, (kpe_src[:, t0:t0 + n] if kind == "lat" else kpe_ctx[:, t0:t0 + n]), (Dkpesrc if kind == "lat" else Dkpectx)))
                wukv4 = wukv.rearrange("p k (h e) -> p k h e", e=256)
                for h0 in range(0, MH, 4):
                    nq = min(4, MH - h0)
                    sw = shared_w(lambda h0=h0, nq=nq: wload4(ws, wukv4[:, :, h0:h0 + nq, 0:128], KC2, nq))
                    for hq in range(nq):
                        h = h0 + hq
                        jobs.append(job((lambda sw=sw, hq=hq: (sw()[0], sw()[1][:, :, hq, :])), ckvn, KC2, 128, ones_b,
                                        gsc.ap[:, 3:4], False, None, kdst(h * 128, (h + 1) * 128), Dk))
                run_jobs(jobs, n, cb, sb_)
                for h0 in range(0, MH, 4):
                    nh = min(4, MH - h0)
                    wb, wvw = wload4(ws, wukv4[:, :, h0:h0 + nh, 128:256], KC2, nh)
                    wbs = [(wb, wvw[:, :, q4, :]) for q4 in range(nh)]
                    if kind == "lat":
                        vproj(ckvn, KC2, n, wbs, (lambda s0, h0=h0, nh=nh: vsrc_cols(t0 + s0, 128, h0, nh)), Dv)
                    else:
                        vproj(ckvn, KC2, n, wbs, (lambda s0, h0=h0, nh=nh: [(v_ctx[t0 + s0:t0 + s0 + 128, h0 * 128:(h0 + nh) * 128], 0, nh * 128)]), Dv)
        P.barrier()

        Dkall.new_gen(); Dvall.new_gen()
        for bi in range(NB):
            P.cc("AllGather", groups, kT_src[bi][:, :], kT_all[bi][:, :], reads=[Dksrc], pwrites=[Dkall])
            P.cc("AllGather", groups, v_src[bi][:, :], v_all[bi][:, :], reads=[Dvsrc], pwrites=[Dvall])
        if is_mla:
            P.cc("AllGather", groups, kpe_src[:, :], kpe_all[:, :], reads=[Dkpesrc], writes=[Dkpeall])
        P.barrier()

        phase_reset(PERSIST)
        oT = carve([NH, NTOK], BF16, False)
        P3BASE = apos[0]
        KA = [carve([T], BF16) for _ in range(2)]
        VV = [carve([NCH, 128], BF16) for _ in range(2)]
        KBm = [carve([T], BF16) for _ in range(2)] if is_mla else None
        QA = [carve([W1], BF16) for _ in range(2)]
        QB = [carve([W1], BF16) for _ in range(2)] if is_mla else None
        pT = [carve([W1], BF16, False) for _ in range(4)]
        rec = carve([W1], F32, False)
        accD = [carve([W1], F32, False) for _ in range(2)]
        if is_mla:
            kpa = kpe_all.rearrange("(r d) t -> d r t", d=128)
            for hf in range(2):
                lo, hi = hf * 64, hf * 64 + 64
                zl, zh = (64, 128) if hf == 0 else (0, 64)
                mset(KBm[hf].ap[zl:zh, :], 0.0, writes=[KBm[hf]])
                dma("sp", KBm[hf].ap[lo:hi, 0:NR * TL].rearrange("p (r t) -> p r t", r=NR), kpa[lo:hi], KBm[hf], reads=[Dkpeall], pwrites=[KBm[hf]])
                dma("sp", KBm[hf].ap[lo:hi, NR * TL:T], kpe_ctx[lo:hi, :], KBm[hf], reads=[Dkpectx], pwrites=[KBm[hf]])
        wup = ffn_w_up[l].rearrange("(kc p) m -> p kc m", p=128)
        wdn = ffn_w_down[l].rearrange("(kc p) m -> p kc m", p=128)
        Dwc.new_gen()
        ncv = 0
        for g in range(NUT):
            nq = min(2, FC - 2 * g)
            for part in range(2):
                ln = cvl[ncv % len(cvl)]; ncv += 1
                dma("pool", wupT[2 * g + part][:, :DC * nq * 128].rearrange("p (kc m) -> p kc m", kc=DC),
                    wup[:, :, part * FF + g * 256:part * FF + g * 256 + nq * 128], ln, writes=[ln], pwrites=[Dwc])
        for g in range(NDT):
            nq = min(2, DC - 2 * g)
            ln = cvl[ncv % len(cvl)]; ncv += 1
            dma("pool", wdnT[g][:, :FC * nq * 128].rearrange("p (kc m) -> p kc m", kc=FC),
                wdn[:, :, g * 256:g * 256 + nq * 128], ln, writes=[ln], pwrites=[Dwc])
        qtiles = lat_tiles + ([] if last else ctx_tiles)
        oT.new_gen()
        it = 0
        kall4 = [a.rearrange("(r g d) t -> d g r t", r=NR, g=HB) for a in kT_all]
        vall4 = [a.rearrange("(ch p) (g d) -> p ch g d", p=128, d=128) for a in v_all]
        vctx4 = v_ctx.rearrange("(ch p) (g d) -> p ch g d", p=128, d=128)
        for g in range(NKV):
            ka = KA[g % 2]; vv = VV[g % 2]
            dma("sp", ka.ap[:, 0:NR * TL].rearrange("p (r t) -> p r t", r=NR), kall4[g // HB][:, g % HB], ka, reads=[Dkall], writes=[ka])
            dma("sp", ka.ap[:, NR * TL:T], kT_ctx[g * 128:(g + 1) * 128, :], ka, reads=[Dkctx], pwrites=[ka])
            dma("sp", vv.ap[:, 0:NR * TL // 128, :], vall4[g // HB][:, :, g % HB, :], vv, reads=[Dvall], writes=[vv])
            dma("sp", vv.ap[:, NR * TL // 128:NCH, :], vctx4[:, :, g, :], vv, reads=[Dvctx], pwrites=[vv])
            for hh in range(GRP):
                h = g * GRP + hh
                hp = (h % 2) * 64
                for (kind, t0, n) in qtiles:
                    tokq = t0 if kind == "lat" else TL + t0
                    qa = QA[it % 2]; qb = QB[it % 2] if is_mla else None
                    ops_ = psum[3 + it % 2]; sums = psum[5 + it % 2]
                    it += 1
                    dma("sp", qa.ap[:, :n], qT_d[h * 128:(h + 1) * 128, tokq:tokq + n], qa, reads=[Dq], writes=[qa])
                    if is_mla:
                        dma("sp", qb.ap[:, :n], qpeT_d[(h // 2) * 128:(h // 2 + 1) * 128, tokq:tokq + n], qb, reads=[Dqpe], writes=[qb])
                    chunks = list(range(NCH)) if kind == "lat" else list(range(NR * TL // 128, NCH))
                    nchk = len(chunks)

                    aD = accD[it % 2]
                    kbm = KBm[h % 2] if is_mla else None
                    pe_chunks = [ci for ci in range(nchk) if (not is_mla) and ci % 4 == 3]
                    dve_chunks = [ci for ci in range(nchk) if ci not in pe_chunks]

                    def emit_S(ci):
                        ch = chunks[ci]
                        sp_ = psum[ci % 3]
                        mm(sp_.ap[:, :n], ka.ap[:, ch * 128:(ch + 1) * 128], qa.ap[:, :n], True, not is_mla, reads=[ka, qa], writes=[sp_])
                        if is_mla:
                            mm(sp_.ap[:, :n], kbm.ap[:, ch * 128:(ch + 1) * 128], qb.ap[:, :n], False, True, reads=[kbm, qb], pwrites=[sp_])
                        pt = pT[ci % 4]
                        act(pt.ap[:, :n], sp_.ap[:, :n], AF.Exp, reads=[sp_], writes=[pt])

                    def emit_PV(ci):
                        ch = chunks[ci]
                        pt = pT[ci % 4]
                        mm(ops_.ap[:, :n], vv.ap[:, ch, :], pt.ap[:, :n], ci == 0, ci == nchk - 1, reads=[vv, pt],
                           writes=[ops_] if ci == 0 else (), pwrites=() if ci == 0 else [ops_])
                        if ci in pe_chunks:
                            mm(sums.ap[:, :n], ones_b.ap, pt.ap[:, :n], ci == pe_chunks[0], False, reads=[ones_b, pt],
                               writes=[sums] if ci == pe_chunks[0] else (), pwrites=() if ci == pe_chunks[0] else [sums])
                        elif ci == dve_chunks[0]:
                            cpy(aD.ap[:, :n], pt.ap[:, :n], reads=[pt], writes=[aD])
                        else:
                            tt(aD.ap[:, :n], aD.ap[:, :n], pt.ap[:, :n], ALU.add, reads=[pt, aD], pwrites=[aD])

                    emit_S(0)
                    if nchk > 1:
                        emit_S(1)
                    for ci in range(nchk):
                        if ci + 2 < nchk:
                            emit_S(ci + 2)
                        emit_PV(ci)
                    mm(sums.ap[:, :n], ones_f.ap, aD.ap[:, :n], not pe_chunks, True, reads=[ones_f, aD],
                       writes=[sums] if not pe_chunks else (), pwrites=[sums] if pe_chunks else ())
                    recip(rec.ap[:, :n], sums.ap[:, :n], reads=[sums], writes=[rec])
                    tt(oT.ap[:, h, tokq:tokq + n], ops_.ap[:, :n], rec.ap[:, :n], ALU.mult, reads=[ops_, rec], pwrites=[oT])
        P.barrier()

        phase_reset(P3BASE)
        xt3 = [carve([DC, W1], F32) for _ in range(2)]
        h3 = [carve([DC, W1], BF16) for _ in range(1)]
        sq3 = [carve([W1], BF16, False) for _ in range(2)]
        rs_sq3 = carve([W1], F32, False); rs3 = carve([W1], F32, False)
        tmpf3 = [carve([W1], F32, False) for _ in range(2)]
        ws3 = [carve([NH, 256], BF16) for _ in range(NWS)]
        w_o = (mla_w_o if is_mla else gqa_w_o)[j].rearrange("(kc p) m -> p kc m", p=128)
        tiles3 = lat_tiles + ([] if last else ctx_tiles)
        Dh2.new_gen(); Dh2c.new_gen(); bnd.new_gen()
        for ti, (kind, t0, n) in enumerate(tiles3):
            jj = 0 if kind == "lat" else 1
            tokq = t0 if kind == "lat" else TL + t0
            xb = xt3[ti % 2]; hb = h3[0]
            src_d, Dsrc = xsrc(kind, l)
            dst_d = xT_s if kind == "lat" else ctxT_s
            dma("sp", xb.ap[:, :, :n], fm3(src_d, t0, n), xb, reads=[Dsrc], writes=[xb])
            for dc in range(DC):
                if dc % 2 == 0:
                    nq = min(2, DC - dc)
                    wb, wvw = wload(ws3, w_o[:, :, dc * 128:(dc + nq) * 128], NH, nq * 128)
                wv_ = wvw[:, :, (dc % 2) * 128:(dc % 2 + 1) * 128]
                ps = nextps()
                for kc in range(NH):
                    mm(ps.ap[:, :n], wv_[:, kc, :], oT.ap[:, kc, tokq:tokq + n], kc == 0, kc == NH - 1, reads=[wb, oT],
                       writes=[ps] if kc == 0 else (), pwrites=() if kc == 0 else [ps])
                stt(xb.ap[:, dc, :n], ps.ap[:, :n], modcol(2, dc, jj), xb.ap[:, dc, :n], ALU.mult, ALU.add, reads=[ps, mod, xb], pwrites=[xb])
            dma("sp", fm3(dst_d, t0, n), xb.ap[:, :, :n], xb, reads=[xb], pwrites=[Dsrc])
            modulate(xb, n, A2, 3, jj, hb, sq3, rs_sq3, rs3, tmpf3)
            if kind == "lat":
                dma("sp", fm3(h2T_d, 1 + t0, n), hb.ap[:, :, :n], hb, reads=[hb], pwrites=[Dh2])
                if t0 == 0:
                    cpy(bnd.ap[:, :, 0], hb.ap[:, :, 0], reads=[hb], pwrites=[bnd])
                if t0 + n == TL:
                    cpy(bnd.ap[:, :, 1], hb.ap[:, :, n - 1], reads=[hb], pwrites=[bnd])
            else:
                dma("sp", fm3(h2cT_d, 1 + t0, n), hb.ap[:, :, :n], hb, reads=[hb], pwrites=[Dh2c])
        dma("sp", hsrc[:, :], bnd.ap.rearrange("p a b -> p (a b)"), bnd, reads=[bnd], writes=[Dhsrc])
        P.barrier()
        P.cc("AllGather", groups, hsrc[:, :], hall[:, :], reads=[Dhsrc], writes=[Dhall])
        dma("sp", hs.ap, hall.rearrange("(r p) (c t) -> p r c t", p=128, t=2), hs, reads=[Dhall], writes=[hs])
        for side in range(2):
            for r in range(NR):
                srcc = hs.ap[:, r, :, 1 - side]
                scol = sel_f.ap[:, side * NR + r:side * NR + r + 1]
                if r == 0:
                    tsm(halo_f.ap[:, :, side], srcc, scol, reads=[hs, sel_f], writes=[halo_f] if side == 0 else (), pwrites=() if side == 0 else [halo_f])
                else:
                    stt(halo_f.ap[:, :, side], srcc, scol, halo_f.ap[:, :, side], ALU.mult, ALU.add, reads=[hs, sel_f, halo_f], pwrites=[halo_f])
        cpy(halo.ap, halo_f.ap, reads=[halo_f], writes=[halo])
        P.barrier()

        phase_reset(PERSIST)
        WIN = W4 + 2
        h2w = [carve([DC, WIN], BF16) for _ in range(2)]
        aT = carve([FC, W4], BF16, False)
        xw = [carve([DC, W4], F32) for _ in range(1)]
        c1 = [carve([W4], F32, False) for _ in range(2)]; c2 = [carve([W4], F32, False) for _ in range(2)]
        c3 = [carve([W4], F32, False) for _ in range(2)]; sg = [carve([W4], F32, False) for _ in range(2)]
        wsu = [carve([DC, 256], BF16) for _ in range(cfg.NWU)]
        wsd = [carve([FC, 256], BF16) for _ in range(cfg.NWD)]
        def wload_bf(ring, src2, kc, m):
            b = ring[cnt["ws"] % len(ring)]; cnt["ws"] += 1
            v2 = b.ap.rearrange("p a b -> p (a b)")[:, :kc * m]
            dma("pool", v2, src2, b, reads=[Dwc], writes=[b])
            return b, v2.rearrange("p (a b) -> p a b", a=kc)

        wins = [("lat", s, min(W4, TL - s)) for s in range(0, TL, W4)]
        if not last:
            wins += [("ctx", s, min(W4, C - s)) for s in range(0, C, W4)]
        pi = 0
        for wi, (kind, s0, nout) in enumerate(wins):
            jj = 0 if kind == "lat" else 1
            nin = nout + 2
            hw = h2w[wi % 2]; xb = xw[0]
            tot = TL if kind == "lat" else C
            if kind == "lat":
                dma("sp", hw.ap[:, :, :nin], fm3(h2T_d, s0, nin), hw, reads=[Dh2], writes=[hw])
            else:
                dma("sp", hw.ap[:, :, :nin], fm3(h2cT_d, s0, nin), hw, reads=[Dh2c], writes=[hw])
            if s0 == 0:
                if kind == "lat":
                    cpy(hw.ap[:, :, 0], halo.ap[:, :, 0], reads=[halo, hw], pwrites=[hw])
                else:
                    mset(hw.ap[:, :, 0], 0.0, reads=[hw], pwrites=[hw])
            if s0 + nout == tot:
                if kind == "lat":
                    cpy(hw.ap[:, :, nin - 1], halo.ap[:, :, 1], reads=[halo, hw], pwrites=[hw])
                else:
                    mset(hw.ap[:, :, nin - 1], 0.0, reads=[hw], pwrites=[hw])
            src_d = xT_s if kind == "lat" else ctxT_s
            Dsrc = Dx if kind == "lat" else Dctx
            dma("sp", xb.ap[:, :, :nout], fm3(src_d, s0, nout), xb, reads=[Dsrc], writes=[xb])
            aT.new_gen()
            for fc in range(FC):
                if fc % 2 == 0:
                    nq = min(2, FC - fc)
                    wgb, wgw = wload_bf(wsu, wupT[fc + 0][:, :DC * nq * 128], DC, nq * 128)
                    wvb, wvw = wload_bf(wsu, wupT[fc + 1][:, :DC * nq * 128], DC, nq * 128)
                wgv = wgw[:, :, (fc % 2) * 128:(fc % 2 + 1) * 128]
                wvv = wvw[:, :, (fc % 2) * 128:(fc % 2 + 1) * 128]
                gps = psum[1 + pi % 2]; vps = psum[3 + pi % 2]; pi += 1
                for kc in range(DC):
                    mm(gps.ap[:, :nin], wgv[:, kc, :], hw.ap[:, kc, :nin], kc == 0, kc == DC - 1, reads=[wgb, hw],
                       writes=[gps] if kc == 0 else (), pwrites=() if kc == 0 else [gps])
                for kc in range(DC):
                    mm(vps.ap[:, :nin], wvv[:, kc, :], hw.ap[:, kc, :nin], kc == 0, kc == DC - 1, reads=[wvb, hw],
                       writes=[vps] if kc == 0 else (), pwrites=() if kc == 0 else [vps])
                a1 = c1[fc % 2]; a2 = c2[fc % 2]; a3 = c3[fc % 2]; sgb = sg[fc % 2]
                act(a1.ap[:, :nout], gps.ap[:, 1:1 + nout], AF.Identity, reads=[gps, vec], writes=[a1], bias=vcol(V_CB + fc), scale=vcol(V_CW + FC + fc))
                stt(a2.ap[:, :nout], gps.ap[:, 0:nout], vcol(V_CW + fc), a1.ap[:, :nout], ALU.mult, ALU.add, reads=[gps, vec, a1], writes=[a2])
                stt(a3.ap[:, :nout], gps.ap[:, 2:2 + nout], vcol(V_CW + 2 * FC + fc), a2.ap[:, :nout], ALU.mult, ALU.add, reads=[gps, vec, a2], writes=[a3])
                act(sgb.ap[:, :nout], a3.ap[:, :nout], AF.Silu, reads=[a3], writes=[sgb])
                tt(aT.ap[:, fc, :nout], sgb.ap[:, :nout], vps.ap[:, 1:1 + nout], ALU.mult, reads=[sgb, vps], pwrites=[aT])
            for dc in range(DC):
                if dc % 2 == 0:
                    nq = min(2, DC - dc)
                    wb, wdw = wload_bf(wsd, wdnT[dc // 2][:, :FC * nq * 128], FC, nq * 128)
                wv_ = wdw[:, :, (dc % 2) * 128:(dc % 2 + 1) * 128]
                ps = psum[5 + pi % 2]; pi += 1
                for kc in range(FC):
                    mm(ps.ap[:, :nout], wv_[:, kc, :], aT.ap[:, kc, :nout], kc == 0, kc == FC - 1, reads=[wb, aT],
                       writes=[ps] if kc == 0 else (), pwrites=() if kc == 0 else [ps])
                stt(xb.ap[:, dc, :nout], ps.ap[:, :nout], modcol(5, dc, jj), xb.ap[:, dc, :nout], ALU.mult, ALU.add, reads=[ps, mod, xb], pwrites=[xb])
            if kind == "lat" and last:
                dma("sp", fm3(yT, s0, nout), xb.ap[:, :, :nout], xb, reads=[xb], pwrites=[Dy])
            else:
                dst_d = xT_s if kind == "lat" else ctxT_s
                dma("sp", fm3(dst_d, s0, nout), xb.ap[:, :, :nout], xb, reads=[xb], pwrites=[Dsrc])
        P.barrier()

    P.emit("sp", lambda e: e.nop(), reads=[Dy])
    P.finalize()
    global LASTP
    LASTP = P
    return nc


def pack_vecs(cfg, inp, l):
    DC, FC = cfg.DC, cfg.FC
    j = l // 2
    cols = [_fm(inp["norm_mix"][l]), _fm(inp["norm_ffn"][l]), _fm(inp["b_mod"][l])]
    cw = np.asarray(inp["ffn_conv_w"][l], np.float32)
    cols += [_fm(cw[0]), _fm(cw[1]), _fm(cw[2]), _fm(inp["ffn_conv_b"][l])]
    QC = cfg.QR // 128; KC2 = cfg.KVR // 128
    z1 = np.zeros((128, 1), np.float32)
    if l % 2 == 0:
        dup = lambda v: np.concatenate([np.asarray(v, np.float32)] * 2)[:, None]
        cols += [_fm(inp["mla_g_dq"][j]), _fm(inp["mla_g_dkv"][j]), np.asarray(inp["mla_g_q_nope"][j], np.float32)[:, None],
                 dup(inp["mla_g_q_pe"][j]), dup(inp["mla_g_k_pe"][j]), np.asarray(inp["mla_g_k_nope"][j], np.float32)[:, None], z1, z1]
    else:
        cols += [np.zeros((128, QC), np.float32), np.zeros((128, KC2), np.float32), z1, z1, z1, z1,
                 np.asarray(inp["gqa_g_q"][j], np.float32)[:, None], np.asarray(inp["gqa_g_k"][j], np.float32)[:, None]]
    return np.concatenate(cols, axis=1).astype(np.float32)


def prep_inputs(cfg, inp):
    NR, TL, D, C = cfg.NR, cfg.TL, cfg.D, cfg.C
    L = cfg.depth
    f = lambda a: np.ascontiguousarray(np.asarray(a, np.float32))
    vecs = np.stack([pack_vecs(cfg, inp, l) for l in range(L)])
    RgT = rot_matrix_T(128)
    Rm = rot_matrix_T(64)
    RmT = np.zeros((128, 128), np.float32); RmT[:64, :64] = Rm; RmT[64:, 64:] = Rm
    bones = np.zeros((128, 128), np.float32); bones[:64, :64] = 1; bones[64:, 64:] = 1
    shared = dict(RgT=RgT, RmT=RmT, bones=bones, vecs=vecs, w_mod=f(inp["w_mod"]),
                  mla_w_dq=f(inp["mla_w_dq"]), mla_w_uq=f(inp["mla_w_uq"]), mla_w_dkv=f(inp["mla_w_dkv"]),
                  mla_w_ukv=f(inp["mla_w_ukv"]), mla_w_o=f(inp["mla_w_o"]),
                  gqa_w_q=f(inp["gqa_w_q"]), gqa_w_kv=f(inp["gqa_w_kv"]), gqa_w_o=f(inp["gqa_w_o"]),
                  ffn_w_up=f(inp["ffn_w_up"]), ffn_w_down=f(inp["ffn_w_down"]))
    x = np.asarray(inp["x"], np.float32); ctx = np.asarray(inp["ctx"], np.float32)
    c = np.asarray(inp["c"], np.float32); c_ctx = np.asarray(inp["c_ctx"], np.float32)
    maps = []
    for core in range(8):
        b = core // NR; r = core % NR
        t0 = r * TL
        m = dict(shared)
        m["xT"] = np.ascontiguousarray(x[b, t0:t0 + TL].T)
        m["ctxT"] = np.ascontiguousarray(ctx[b].T)
        cv = np.stack([_fm(c[b]), _fm(c_ctx)], axis=-1)
        m["cvec"] = np.ascontiguousarray(cv.reshape(128, -1))
        cg, sg = rope_tables(cfg, t0, TL, 128)
        cm, sm = rope_tables(cfg, t0, TL, 64)
        m["cosg"] = cg; m["sing"] = sg
        m["cosm"] = np.concatenate([cm, cm], 0); m["sinm"] = np.concatenate([sm, sm], 0)
        s = np.zeros((128, 2 * NR), np.float32)
        if r > 0:
            s[:, r - 1] = 1.0
        if r < NR - 1:
            s[:, NR + r + 1] = 1.0
        m["sel"] = s
        maps.append(m)
    return maps


_NC_CACHE = {}


def run(cfg, inp, debug=False):
    key = (cfg.D, cfg.S, cfg.C, cfg.FF, cfg.depth)
    if key not in _NC_CACHE:
        _NC_CACHE[key] = build(cfg, debug)
    nc = _NC_CACHE[key]
    maps = prep_inputs(cfg, inp)
    res = run_bass_kernel_spmd(nc, maps, core_ids=list(range(8)))
    out = np.zeros((cfg.B, cfg.S, cfg.D), np.float32)
    for core in range(8):
        b = core // cfg.NR; r = core % cfg.NR
        out[b, r * cfg.TL:(r + 1) * cfg.TL] = res.results[core]["yT"].T
    return out


def kernel(**inputs):
    return run(Cfg(), inputs)
```

```python
import math
import numpy as np
import concourse.bass as bass
import concourse.mybir as mybir
from concourse.bass_utils import run_bass_kernel_spmd

F32 = mybir.dt.float32
BF16 = mybir.dt.bfloat16
AF = mybir.ActivationFunctionType
ALU = mybir.AluOpType
EPS = 1e-6
ROPE_BASE = 10000.0


class Cfg:
    def __init__(s, **kw):
        s.D = 2048; s.S = 8192; s.B = 2; s.C = 256; s.FF = 5632; s.depth = 4; s.GRID_W = 64
        s.MH = 16; s.QR = 512; s.KVR = 512; s.GH = 16; s.GKV = 4
        s.W1 = 512; s.W4 = 410; s.NR = 4
        s.NWS = 3; s.NWU = 4; s.NWD = 2; s.WDSPLIT = 1
        for k, v in kw.items():
            setattr(s, k, v)
        s.TL = s.S // s.NR
        s.DC = s.D // 128; s.FC = s.FF // 128
        s.NTOK = s.TL + s.C


class Op:
    __slots__ = ("eng", "fn", "deps", "marked", "cum", "is_dma", "sem", "val", "inc", "idx", "epoch")


class Buf:
    def __init__(s, ap=None, sem=None):
        s.ap = ap; s.w = []; s.r = []; s.gen = []; s.sem = sem; s.cnt = 0

    def new_gen(s):
        s.gen = s.w + s.r; s.w = []; s.r = []


class Prog:
    def __init__(s, nc):
        s.nc = nc
        s.ops = {k: [] for k in ("pe", "act", "dve", "pool", "sp")}
        s.engsem = {}
        s.epoch = 0; s.nidx = 0
        s.ccsem = nc.alloc_semaphore("ccsem"); s.cccnt = 0

    def new_epoch(s):
        s.epoch += 1

    def esem(s, eng, epoch):
        k = (eng, epoch)
        if k not in s.engsem:
            s.engsem[k] = s.nc.alloc_semaphore("es_%s_%d" % (eng, epoch))
        return s.engsem[k]

    def emit(s, eng, fn, reads=(), writes=(), pwrites=(), extra=()):
        deps = list(extra)
        for b in reads:
            deps += b.w
        for b in writes:
            b.new_gen(); deps += b.gen
        for b in pwrites:
            deps += b.gen
        best = {}
        for d in deps:
            if d.is_dma:
                k = ("s", d.sem.num); v = d.val
            else:
                k = ("e", d.eng); v = d.idx
            o = best.get(k)
            if o is None or v > o[0]:
                best[k] = (v, d)
        deps = [o[1] for o in best.values()]
        op = Op(); op.eng = eng; op.fn = fn; op.deps = deps; op.marked = False; op.cum = 0
        op.is_dma = False; op.sem = None; op.val = 0; op.inc = 0
        op.idx = s.nidx; s.nidx += 1; op.epoch = s.epoch
        for b in reads:
            b.r.append(op)
        for b in writes:
            b.w.append(op)
        for b in pwrites:
            b.w.append(op)
        s.ops[eng].append(op)
        return op

    def dma(s, eng, out, in_, sb, reads=(), writes=(), pwrites=(), extra=()):
        op = s.emit(eng, lambda e: e.dma_start(out=out, in_=in_), reads, writes, pwrites, extra)
        sb.cnt += 16
        op.is_dma = True; op.sem = sb.sem; op.val = sb.cnt; op.inc = 16
        return op

    def cc(s, kind, groups, in_ap, out_ap, reads=(), writes=(), pwrites=()):
        op = s.emit("pool", lambda e: e.collective_compute(kind, ALU.bypass, replica_groups=groups,
                                                            ins=[in_ap], outs=[out_ap]), reads, writes, pwrites)
        s.cccnt += 1
        op.is_dma = True; op.sem = s.ccsem; op.val = s.cccnt; op.inc = 1
        return op

    def barrier(s):
        deps = []
        latest = {}
        for eng, ops in s.ops.items():
            lastc = None
            for op in ops:
                if op.is_dma:
                    latest[op.sem.num] = op
                else:
                    lastc = op
            if lastc is not None:
                deps.append(lastc)
        deps += list(latest.values())
        for eng in s.ops:
            s.emit(eng, lambda e: e.nop(), extra=deps)

    def finalize(s):
        nc = s.nc
        for eng, ops in s.ops.items():
            for op in ops:
                for d in op.deps:
                    if not d.is_dma:
                        d.marked = True
        for eng, ops in s.ops.items():
            cnt = {}
            for op in ops:
                if (not op.is_dma) and op.marked:
                    cnt[op.epoch] = cnt.get(op.epoch, 0) + 1
                    s.esem(eng, op.epoch)
                op.cum = cnt.get(op.epoch, 0)

        def run(engname, e):
            waited = {}
            for op in s.ops[engname]:
                for d in op.deps:
                    if d.is_dma:
                        key = ("s", d.sem.num); val = d.val; sem = d.sem
                    else:
                        if d.eng == "pe" and engname == "pe":
                            continue
                        key = ("e", d.eng, d.epoch); val = d.cum; sem = s.engsem[(d.eng, d.epoch)]
                    if waited.get(key, 0) >= val:
                        continue
                    waited[key] = val
                    e.wait_ge(sem, val)
                ins = op.fn(e)
                if op.is_dma:
                    if op.inc == 16:
                        ins.then_inc(op.sem, 16)
                    else:
                        ins.then_inc(op.sem)
                elif op.marked:
                    ins.then_inc(s.engsem[(op.eng, op.epoch)], 1)

        with nc.Block() as block:
            @block.sync
            def _(e):
                run("sp", e)

            @block.gpsimd
            def _(e):
                run("pool", e)

            @block.scalar
            def _(e):
                run("act", e)

            @block.vector
            def _(e):
                run("dve", e)

            @block.tensor
            def _(e):
                run("pe", e)


def _fm(v):
    v = np.asarray(v, np.float32)
    return np.ascontiguousarray(v.reshape(-1, 128).T)


def rope_tables(cfg, tok0, n, rot_dim):
    axis_dim = rot_dim // 2
    inv = np.power(np.float32(ROPE_BASE), -np.arange(0, axis_dim, 2, dtype=np.float32) / np.float32(axis_dim)).astype(np.float32)
    t = np.arange(tok0, tok0 + n)
    rows = (t // cfg.GRID_W).astype(np.float32); cols = (t % cfg.GRID_W).astype(np.float32)
    ar = rows[:, None] * inv; ac = cols[:, None] * inv
    ang = np.concatenate([ar, ar, ac, ac], axis=-1).astype(np.float32)
    return np.cos(ang).T.astype(np.float32), np.sin(ang).T.astype(np.float32)


def rot_matrix_T(rot_dim):
    half = rot_dim // 2; q = half // 2
    R = np.zeros((rot_dim, rot_dim), np.float32)
    for base in (0, half):
        for i in range(q):
            R[base + i, base + i + q] = -1.0
            R[base + q + i, base + i] = 1.0
    return np.ascontiguousarray(R.T)


def build(cfg, debug=False):
    nc = bass.Bass("TRN2", target_bir_lowering=False)
    P = Prog(nc)
    D, TL, C, FF, DC, FC, NTOK, NR = cfg.D, cfg.TL, cfg.C, cfg.FF, cfg.DC, cfg.FC, cfg.NTOK, cfg.NR
    L = cfg.depth
    LA = (L + 1) // 2; LB = max(L // 2, 1)
    MH, QR, KVR, GH, GKV = cfg.MH, cfg.QR, cfg.KVR, cfg.GH, cfg.GKV
    QC = QR // 128; KC2 = KVR // 128
    T = NR * TL + C
    NCH = T // 128
    W1 = cfg.W1; W4 = cfg.W4

    def din(name, shape, dt=F32):
        return nc.dram_tensor(name, list(shape), dt, kind="ExternalInput").ap()

    def dscr(name, shape, dt):
        return nc.dram_tensor(name, list(shape), dt).ap()

    xT_in = din("xT", [D, TL]); ctxT_in = din("ctxT", [D, C]); cvec = din("cvec", [128, DC * 2])
    cosg = din("cosg", [128, TL]); sing = din("sing", [128, TL]); cosm = din("cosm", [128, TL]); sinm = din("sinm", [128, TL])
    RgT = din("RgT", [128, 128]); RmT = din("RmT", [128, 128]); bones = din("bones", [128, 128]); sel = din("sel", [128, 2 * NR])
    NV = 2 * DC + 6 * DC + 4 * FC + QC + KC2 + 6
    vecs = din("vecs", [L, 128, NV])
    w_mod = din("w_mod", [L, D, 6 * D])
    mla_w_dq = din("mla_w_dq", [LA, D, QR]); mla_w_uq = din("mla_w_uq", [LA, QR, MH * 192])
    mla_w_dkv = din("mla_w_dkv", [LA, D, KVR + 64]); mla_w_ukv = din("mla_w_ukv", [LA, KVR, MH * 256])
    mla_w_o = din("mla_w_o", [LA, MH * 128, D])
    gqa_w_q = din("gqa_w_q", [LB, D, GH * 128]); gqa_w_kv = din("gqa_w_kv", [LB, D, 2 * GKV * 128])
    gqa_w_o = din("gqa_w_o", [LB, GH * 128, D])
    ffn_w_up = din("ffn_w_up", [L, D, 2 * FF]); ffn_w_down = din("ffn_w_down", [L, FF, D])
    yT = nc.dram_tensor("yT", [D, TL], F32, kind="ExternalOutput").ap()

    xT_s = dscr("xT_s", [D, TL], F32); ctxT_s = dscr("ctxT_s", [D, C], F32)
    qT_d = dscr("qT_d", [max(MH, GH) * 128, NTOK], BF16); qpeT_d = dscr("qpeT_d", [MH // 2 * 128, NTOK], BF16)
    kvs = {}
    for nm, nkv in (("m", MH), ("g", GKV)):
        HB = 2 if nkv % 2 == 0 else 1
        while HB > 1 and HB * 128 * TL * 2 > (1 << 20):
            HB //= 2
        nb = nkv // HB
        kvs[nm] = dict(HB=HB, NB=nb,
                       kT_src=[dscr("kT_src%s%d" % (nm, i), [HB * 128, TL], BF16) for i in range(nb)],
                       kT_all=[dscr("kT_all%s%d" % (nm, i), [NR * HB * 128, TL], BF16) for i in range(nb)],
                       v_src=[dscr("v_src%s%d" % (nm, i), [TL, HB * 128], BF16) for i in range(nb)],
                       v_all=[dscr("v_all%s%d" % (nm, i), [NR * TL, HB * 128], BF16) for i in range(nb)],
                       kT_ctx=dscr("kT_ctx" + nm, [nkv * 128, C], BF16), v_ctx=dscr("v_ctx" + nm, [C, nkv * 128], BF16))
    kpe_src = dscr("kpe_src", [128, TL], BF16); kpe_all = dscr("kpe_all", [NR * 128, TL], BF16); kpe_ctx = dscr("kpe_ctx", [128, C], BF16)
    h2T_d = dscr("h2T_d", [D, TL + 2], BF16); h2cT_d = dscr("h2cT_d", [D, C + 2], BF16)
    hsrc = dscr("hsrc", [128, DC * 2], BF16); hall = dscr("hall", [NR * 128, DC * 2], BF16)
    NUT = (FC + 1) // 2; NDT = (DC + 1) // 2
    wupT = dscr("wupT", [NUT * 2, 128, DC * 256], BF16); wdnT = dscr("wdnT", [NDT, 128, FC * 256], BF16)
    Dwc = Buf()
    Dx = Buf(); Dctx = Buf(); Dq = Buf(); Dqpe = Buf(); Dksrc = Buf(); Dkall = Buf(); Dvsrc = Buf(); Dvall = Buf()
    Dkpesrc = Buf(); Dkpeall = Buf(); Dkctx = Buf(); Dvctx = Buf(); Dkpectx = Buf(); Dh2 = Buf(); Dh2c = Buf(); Dhsrc = Buf(); Dhall = Buf()
    Dy = Buf()

    ARENA_BYTES = 207 * 1024
    arena = nc.alloc_sbuf_tensor("arena", [128, ARENA_BYTES // 4], F32)
    apos = [0]
    sempool = {}
    semidx = [0]
    persist_sems = [0]

    def phase_reset(base):
        apos[0] = base; semidx[0] = persist_sems[0]

    def carve(shape, dt, sem=True):
        esz = 4 if dt == F32 else 2
        nb = int(np.prod(shape)) * esz
        nb_al = (nb + 63) // 64 * 64
        off = apos[0]; apos[0] += nb_al
        assert apos[0] <= ARENA_BYTES, ("SBUF arena overflow", apos[0])
        a = arena[:, off // 4:(off + nb) // 4]
        if dt != F32:
            a = a.bitcast(dt)
        if len(shape) == 2:
            a = a.rearrange("p (a b) -> p a b", a=shape[0])
        elif len(shape) == 3:
            a = a.rearrange("p (a b c) -> p a b c", a=shape[0], b=shape[1])
        sm = None
        if sem:
            k = semidx[0]; semidx[0] += 1
            if k not in sempool:
                sempool[k] = [nc.alloc_semaphore("bs%d" % k), 0]
            sm = sempool[k]
        b = Buf(a, sm[0] if sm else None)
        if sm:
            b.cnt = sm[1]; b.semrec = sm
        return b

    def mm(out, lhsT, rhs, start, stop, reads, writes=(), pwrites=()):
        return P.emit("pe", lambda e: e.matmul(out, lhsT, rhs, start=start, stop=stop), reads=reads, writes=writes, pwrites=pwrites)

    def act(out, in_, func, reads, writes=(), pwrites=(), bias=None, scale=None, eng="act"):
        kw = {}
        if bias is not None:
            kw["bias"] = bias
        if scale is not None:
            kw["scale"] = scale
        return P.emit(eng, lambda e: e.activation(out, in_, func, **kw), reads=reads, writes=writes, pwrites=pwrites)

    def stt(out, in0, scalar, in1, op0, op1, reads, writes=(), pwrites=(), eng="dve"):
        return P.emit(eng, lambda e: e.scalar_tensor_tensor(out, in0, scalar, in1, op0, op1), reads=reads, writes=writes, pwrites=pwrites)

    def tt(out, in0, in1, op, reads, writes=(), pwrites=(), eng="dve"):
        return P.emit(eng, lambda e: e.tensor_tensor(out, in0, in1, op), reads=reads, writes=writes, pwrites=pwrites)

    def tsm(out, in0, s1, reads, writes=(), pwrites=(), eng="dve"):
        return P.emit(eng, lambda e: e.tensor_scalar(out, in0, s1, None, ALU.mult), reads=reads, writes=writes, pwrites=pwrites)

    def recip(out, in_, reads, writes=(), pwrites=()):
        return P.emit("dve", lambda e: e.reciprocal(out, in_), reads=reads, writes=writes, pwrites=pwrites)

    def cpy(out, in_, reads, writes=(), pwrites=(), eng="dve"):
        return P.emit(eng, lambda e: e.tensor_copy(out, in_), reads=reads, writes=writes, pwrites=pwrites)

    def mset(out, val, writes=(), pwrites=(), reads=(), eng="dve"):
        return P.emit(eng, lambda e: e.memset(out, val), reads=reads, writes=writes, pwrites=pwrites)

    def dma(eng, out, in_, sb, reads=(), writes=(), pwrites=()):
        sb.cnt = sb.semrec[1]
        op = P.dma(eng, out, in_, sb, reads=reads, writes=writes, pwrites=pwrites)
        sb.semrec[1] = sb.cnt
        return op

    ones_f = carve([128], F32, False); ones_b = carve([128], BF16, False); bones_f = carve([128], F32)
    Rg_f = carve([128], F32); Rm_f = carve([128], F32); sel_f = carve([2 * NR], F32)
    sv_f = carve([DC * 2], F32); sv_b = carve([DC, 2], BF16, False)
    vec = carve([NV], F32)
    mod = carve([6 * DC, 2], F32, False)
    A1 = carve([DC, 2], F32, False); A2 = carve([DC, 2], F32, False)
    gsc = carve([8], F32, False)
    bnd = carve([DC, 2], BF16); halo = carve([DC, 2], BF16, False); hs = carve([NR, DC, 2], BF16)
    halo_f = carve([DC, 2], F32, False)
    eps_c = carve([1], F32, False)
    bones_b = carve([128], BF16, False); Rg_b = carve([128], BF16, False); Rm_b = carve([128], BF16, False)
    cvl = [carve([1], F32) for _ in range(3)]
    psum = [Buf(nc.alloc_psum_tensor("ps%d" % i, [128, 512], F32)[:]) for i in range(8)]
    PERSIST = apos[0]
    persist_sems[0] = semidx[0]

    o = 0
    V_NMIX = o; o += DC
    V_NFFN = o; o += DC
    V_BMOD = o; o += 6 * DC
    V_CW = o; o += 3 * FC
    V_CB = o; o += FC
    V_GDQ = o; o += QC
    V_GDKV = o; o += KC2
    V_GQN = o; o += 1
    V_GQP = o; o += 1
    V_GKP = o; o += 1
    V_GKN = o; o += 1
    V_GQ = o; o += 1
    V_GK = o; o += 1
    assert o == NV

    def vcol(i):
        return vec.ap[:, i:i + 1]

    def modcol(m, c, jj):
        return mod.ap[:, m * DC + c, jj:jj + 1]

    NWS = cfg.NWS

    mset(ones_f.ap, 1.0, writes=[ones_f]); mset(ones_b.ap, 1.0, writes=[ones_b]); mset(eps_c.ap, EPS, writes=[eps_c])
    dma("sp", bones_f.ap, bones[:, :], bones_f, writes=[bones_f])
    dma("sp", Rg_f.ap, RgT[:, :], Rg_f, writes=[Rg_f])
    dma("sp", Rm_f.ap, RmT[:, :], Rm_f, writes=[Rm_f])
    dma("sp", sel_f.ap, sel[:, :], sel_f, writes=[sel_f])
    dma("sp", sv_f.ap, cvec[:, :], sv_f, writes=[sv_f])
    act(sv_b.ap.rearrange("p a b -> p (a b)"), sv_f.ap, AF.Silu, reads=[sv_f], writes=[sv_b])
    cpy(bones_b.ap, bones_f.ap, reads=[bones_f], writes=[bones_b])
    cpy(Rg_b.ap, Rg_f.ap, reads=[Rg_f], writes=[Rg_b])
    cpy(Rm_b.ap, Rm_f.ap, reads=[Rm_f], writes=[Rm_b])

    lat_tiles = [("lat", t0, min(W1, TL - t0)) for t0 in range(0, TL, W1)]
    ctx_tiles = [("ctx", t0, min(W1, C - t0)) for t0 in range(0, C, W1)]
    groups = [list(range(g * NR, (g + 1) * NR)) for g in range(8 // NR)]

    def xsrc(kind, l):
        if kind == "lat":
            return (xT_in if l == 0 else xT_s), Dx
        return (ctxT_in if l == 0 else ctxT_s), Dctx

    def fm3(dram2d, c0, n):
        return dram2d.rearrange("(c p) t -> p c t", p=128)[:, :, c0:c0 + n]

    def rstd_from_ss(ssps, n, nd, rs_sq, rs):
        act(rs_sq.ap[:, :n], ssps.ap[:, :n], AF.Sqrt, reads=[ssps, eps_c], writes=[rs_sq], bias=eps_c.ap[:, 0:1], scale=1.0 / nd)
        recip(rs.ap[:, :n], rs_sq.ap[:, :n], reads=[rs_sq], writes=[rs])

    for l in range(L):
        last = (l == L - 1)
        is_mla = (l % 2 == 0)
        j = l // 2
        P.new_epoch()
        NH = MH if is_mla else GH
        NKV = MH if is_mla else GKV
        GRP = NH // NKV
        KV = kvs["m" if is_mla else "g"]
        kT_src, kT_all, v_src, v_all, kT_ctx, v_ctx = KV["kT_src"], KV["kT_all"], KV["v_src"], KV["v_all"], KV["kT_ctx"], KV["v_ctx"]
        HB, NB = KV["HB"], KV["NB"]

        def ksrc_rows(hd):
            return kT_src[hd // HB][(hd % HB) * 128:(hd % HB + 1) * 128, :]

        def vsrc_cols(tok0, ntok, h0, nh):
            out = []
            hd = h0
            while hd < h0 + nh:
                k = min(HB - hd % HB, h0 + nh - hd)
                out.append((v_src[hd // HB][tok0:tok0 + ntok, (hd % HB) * 128:(hd % HB + k) * 128], (hd - h0) * 128, k * 128))
                hd += k
            return out

        phase_reset(PERSIST)
        dma("sp", vec.ap, vecs[l], vec, writes=[vec])
        wsm = [carve([DC, 512], BF16) for _ in range(3)]
        modps = psum[0]
        modps.new_gen()
        nfg = (6 * DC + 3) // 4
        wm3 = w_mod[l].rearrange("(kc p) f -> p kc f", p=128)
        for fg in range(nfg):
            nf = min(4, 6 * DC - fg * 4)
            wb = wsm[fg % 3]
            dma("pool", wb.ap[:, :, :nf * 128], wm3[:, :, fg * 512:fg * 512 + nf * 128], wb, writes=[wb])
            for fc in range(nf):
                col = (fg * 4 + fc) * 2
                for kc in range(DC):
                    mm(modps.ap[:, col:col + 2], wb.ap[:, kc, fc * 128:(fc + 1) * 128], sv_b.ap[:, kc, :], kc == 0, kc == DC - 1,
                       reads=[wb, sv_b], pwrites=[modps])
        mp3 = modps.ap[:, 0:12 * DC].rearrange("p (a b) -> p a b", b=2)
        for jj in range(2):
            tt(mod.ap[:, :, jj], mp3[:, :, jj], vec.ap[:, V_BMOD:V_BMOD + 6 * DC], ALU.add, reads=[modps, vec],
               writes=[mod] if jj == 0 else (), pwrites=[mod] if jj else ())
        for (Ab, voff, m) in ((A1, V_NMIX, 1), (A2, V_NFFN, 4)):
            for jj in range(2):
                stt(Ab.ap[:, :, jj], mod.ap[:, m * DC:(m + 1) * DC, jj], 1.0, vec.ap[:, voff:voff + DC], ALU.add, ALU.mult,
                    reads=[mod, vec], writes=[Ab] if jj == 0 else (), pwrites=[Ab] if jj else ())
        if is_mla:
            sc = 1.0 / math.sqrt(192.0)
            glist = [(0, V_GQN, sc), (1, V_GQP, sc), (2, V_GKP, 1.0), (3, V_GKN, 1.0)]
        else:
            sc = 1.0 / math.sqrt(128.0)
            glist = [(0, V_GQ, sc), (1, V_GK, 1.0)]
        for gi, (gidx, vo, mul) in enumerate(glist):
            tsm(gsc.ap[:, gidx:gidx + 1], vec.ap[:, vo:vo + 1], mul, reads=[vec], writes=[gsc] if gi == 0 else (), pwrites=[gsc] if gi else ())
        P.barrier()

        phase_reset(PERSIST)
        xt = [carve([DC, W1], F32) for _ in range(1)]
        hT = [carve([DC, W1], BF16, False) for _ in range(2)]
        sq = [carve([W1], BF16, False) for _ in range(2)]
        rs_sq = carve([W1], F32, False); rs = carve([W1], F32, False)
        tmpf = [carve([W1], F32, False) for _ in range(2)]
        cs = [carve([W1], F32) for _ in range(2)]; sn = [carve([W1], F32) for _ in range(2)]
        NSET = 3
        nsets = [dict(sq=carve([W1], BF16, False), rs_sq=carve([W1], F32, False), rs=carve([W1], F32, False),
                      qn=carve([W1], BF16, False), t1=carve([W1], F32, False), t2=carve([W1], F32, False)) for _ in range(NSET)]
        stg = [carve([W1], BF16) for _ in range(3)]
        vstg = [carve([512], BF16) for _ in range(2)]
        ws = [carve([max(DC, QC, KC2), 512], BF16) for _ in range(NWS)]
        if is_mla:
            cq_f = carve([max(QC, KC2), W1], F32, False); cqn = carve([QC, W1], BF16, False); ckvn = carve([KC2, W1], BF16, False)
        cnt = dict(ws=0, stg=0, vstg=0, pi=0)

        def wload(ring, src3, kc, m, split=1):
            b = ring[cnt["ws"] % len(ring)]; cnt["ws"] += 1
            view = b.ap.rearrange("p a b -> p (a b)")[:, :kc * m].rearrange("p (a b) -> p a b", a=kc)
            step = (kc + split - 1) // split
            for i, k0 in enumerate(range(0, kc, step)):
                k1 = min(kc, k0 + step)
                dma("pool", view[:, k0:k1, :], src3[:, k0:k1, :], b, writes=[b] if i == 0 else (), pwrites=[b] if i else ())
            return b, view

        def wload4(ring, src4, kc, nh):
            b = ring[cnt["ws"] % len(ring)]; cnt["ws"] += 1
            view = b.ap.rearrange("p a b -> p (a b)")[:, :kc * nh * 128].rearrange("p (a h e) -> p a h e", a=kc, h=nh)
            for hq in range(nh):
                dma("pool", view[:, :, hq, :], src4[:, :, hq, :], b, writes=[b] if hq == 0 else (), pwrites=[b] if hq else ())
            return b, view

        def wload2(ring, srcA, srcB, kc):
            b = ring[cnt["ws"] % len(ring)]; cnt["ws"] += 1
            view = b.ap.rearrange("p a b -> p (a b)")[:, :kc * 128].rearrange("p (a b) -> p a b", a=kc)
            dma("pool", view[:, :, 0:64], srcA, b, writes=[b])
            dma("pool", view[:, :, 64:128], srcB, b, pwrites=[b])
            return b, view

        def modulate(xb, n, Ab, shm, jj, hb, sq, rs_sq, rs, tmpf):
            ssps = psum[1]
            for c in range(DC):
                s_ = sq[c % 2]
                act(s_.ap[:, :n], xb.ap[:, c, :n], AF.Square, reads=[xb], writes=[s_])
                mm(ssps.ap[:, :n], ones_b.ap, s_.ap[:, :n], c == 0, c == DC - 1, reads=[s_, ones_b],
                   writes=[ssps] if c == 0 else (), pwrites=() if c == 0 else [ssps])
            rstd_from_ss(ssps, n, float(D), rs_sq, rs)
            hb.new_gen()
            for c in range(DC):
                tf = tmpf[c % 2]
                stt(tf.ap[:, :n], xb.ap[:, c, :n], Ab.ap[:, c, jj:jj + 1], rs.ap[:, :n], ALU.mult, ALU.mult, reads=[xb, Ab, rs], writes=[tf])
                act(hb.ap[:, c, :n], tf.ap[:, :n], AF.Identity, reads=[tf, mod], pwrites=[hb], bias=modcol(shm, c, jj), scale=1.0)

        def run_jobs(jobs, n, cb, sb_):
            N = len(jobs)
            pbuf = [psum[3], psum[4], psum[5]]
            st_of = {}

            def A(i):
                J = jobs[i]
                wb, wv_ = J["w"]()
                ps = pbuf[i % 3]
                proj(ps, wv_, wb, J["src"], J["kcs"], n)
                S_ = nsets[i % NSET]
                act(S_["sq"].ap[:, :n], ps.ap[:, :n], AF.Square, reads=[ps], writes=[S_["sq"]])

            def B(i):
                J = jobs[i]
                ps = pbuf[i % 3]
                S_ = nsets[i % NSET]
                ssps = psum[1 + i % 2]
                mm(ssps.ap[:, :n], J["ones"].ap, S_["sq"].ap[:, :n], True, True, reads=[S_["sq"], J["ones"]], writes=[ssps])
                rstd_from_ss(ssps, n, float(J["nd"]), S_["rs_sq"], S_["rs"])
                if not J["rope"]:
                    st = stg[cnt["stg"] % 3]; cnt["stg"] += 1
                    stt(st.ap[:, :n], ps.ap[:, :n], J["gcol"], S_["rs"].ap[:, :n], ALU.mult, ALU.mult, reads=[ps, S_["rs"], gsc], writes=[st])
                    dma("sp", J["dst"], st.ap[:, :n], st, reads=[st], pwrites=[J["Ddst"]])
                else:
                    stt(S_["qn"].ap[:, :n], ps.ap[:, :n], J["gcol"], S_["rs"].ap[:, :n], ALU.mult, ALU.mult, reads=[ps, S_["rs"], gsc], writes=[S_["qn"]])

            def C(i):
                J = jobs[i]
                if not J["rope"]:
                    return
                S_ = nsets[i % NSET]
                qn = S_["qn"]; t1 = S_["t1"]; t2 = S_["t2"]
                rps = psum[7] if i % 2 else psum[0]
                mm(rps.ap[:, :n], J["R"].ap, qn.ap[:, :n], True, True, reads=[J["R"], qn], writes=[rps])
                tt(t1.ap[:, :n], qn.ap[:, :n], cb.ap[:, :n], ALU.mult, reads=[qn, cb], writes=[t1])
                tt(t2.ap[:, :n], rps.ap[:, :n], sb_.ap[:, :n], ALU.mult, reads=[rps, sb_], writes=[t2])
                st = stg[cnt["stg"] % 3]; cnt["stg"] += 1
                tt(st.ap[:, :n], t1.ap[:, :n], t2.ap[:, :n], ALU.add, reads=[t1, t2], writes=[st])
                dma("sp", J["dst"], st.ap[:, :n], st, reads=[st], pwrites=[J["Ddst"]])

            for i in range(N + 2):
                if i < N:
                    A(i)
                if 0 <= i - 1 < N:
                    B(i - 1)
                if 0 <= i - 2 < N:
                    C(i - 2)

        def job(wfn, src, kcs, nd, onesb, gcol, rope, Rb, dst_ap, Ddst):
            return dict(w=wfn, src=src, kcs=kcs, nd=nd, ones=onesb, gcol=gcol, rope=rope, R=Rb, dst=dst_ap, Ddst=Ddst)

        def shared_w(loader):
            box = []

            def get():
                if not box:
                    box.append(loader())
                return box[0]
            return get

        def proj(ps, wview, wb, src, kcs, n):
            for kc in range(kcs):
                mm(ps.ap[:, :n], wview[:, kc, :], src.ap[:, kc, :n], kc == 0, kc == kcs - 1, reads=[wb, src],
                   writes=[ps] if kc == 0 else (), pwrites=() if kc == 0 else [ps])

        def nextps():
            p_ = psum[3 + cnt["pi"] % 2]; cnt["pi"] += 1
            return p_

        def vproj(src, kcs, n, wbs, dst_fn, Ddst):
            ncol = len(wbs) * 128
            for s0 in range(0, n, 128):
                vps = psum[6]
                vps.new_gen()
                for qi, (wb, wv_) in enumerate(wbs):
                    for kc in range(kcs):
                        mm(vps.ap[:, qi * 128:(qi + 1) * 128], src.ap[:, kc, s0:s0 + 128], wv_[:, kc, :], kc == 0, kc == kcs - 1,
                           reads=[src, wb], pwrites=[vps])
                vs = vstg[cnt["vstg"] % 2]; cnt["vstg"] += 1
                act(vs.ap[:, :ncol], vps.ap[:, :ncol], AF.Copy, reads=[vps], writes=[vs])
                for (dap, coff, ncl) in dst_fn(s0):
                    dma("sp", dap, vs.ap[:, coff:coff + ncl], vs, reads=[vs], pwrites=[Ddst])

        for D_ in (Dq, Dqpe, Dksrc, Dvsrc, Dkpesrc, Dkctx, Dvctx, Dkpectx):
            D_.new_gen()
        tiles = lat_tiles + ctx_tiles
        def p1_tile(ti, kind, t0, n, do_q, do_kv):
            jj = 0 if kind == "lat" else 1
            tokq = t0 if kind == "lat" else TL + t0
            xb = xt[0]; hb = hT[ti % 2]
            src_d, Dsrc = xsrc(kind, l)
            dma("sp", xb.ap[:, :, :n], fm3(src_d, t0, n), xb, reads=[Dsrc], writes=[xb])
            modulate(xb, n, A1, 0, jj, hb, sq, rs_sq, rs, tmpf)
            rope = (kind == "lat")
            need_q = not (kind == "ctx" and last)
            cb = sb_ = None
            if rope:
                cb = cs[ti % 2]; sb_ = sn[ti % 2]
                dma("sp", cb.ap[:, :n], (cosm if is_mla else cosg)[:, t0:t0 + n], cb, writes=[cb])
                dma("sp", sb_.ap[:, :n], (sinm if is_mla else sing)[:, t0:t0 + n], sb_, writes=[sb_])
            kdst = (lambda r0, r1: ksrc_rows(r0 // 128)[:, t0:t0 + n]) if kind == "lat" else (lambda r0, r1: kT_ctx[r0:r1, t0:t0 + n])
            Dk = Dksrc if kind == "lat" else Dkctx
            Dv = Dvsrc if kind == "lat" else Dvctx
            if not is_mla:
                wq = gqa_w_q[j].rearrange("(kc p) m -> p kc m", p=128)
                wkv = gqa_w_kv[j].rearrange("(kc p) m -> p kc m", p=128)
                jobs = []
                if need_q:
                    for h0 in range(0, GH, 4):
                        nh = min(4, GH - h0)
                        sw = shared_w(lambda h0=h0, nh=nh: wload(ws, wq[:, :, h0 * 128:(h0 + nh) * 128], DC, nh * 128))
                        for hq in range(nh):
                            h = h0 + hq
                            jobs.append(job((lambda sw=sw, hq=hq: (sw()[0], sw()[1][:, :, hq * 128:(hq + 1) * 128])), hb, DC, 128, ones_b,
                                            gsc.ap[:, 0:1], rope, Rg_b, qT_d[h * 128:(h + 1) * 128, tokq:tokq + n], Dq))
                for g0 in range(0, GKV, 4):
                    ng = min(4, GKV - g0)
                    sw = shared_w(lambda g0=g0, ng=ng: wload(ws, wkv[:, :, g0 * 128:(g0 + ng) * 128], DC, ng * 128))
                    for gq in range(ng):
                        g = g0 + gq
                        jobs.append(job((lambda sw=sw, gq=gq: (sw()[0], sw()[1][:, :, gq * 128:(gq + 1) * 128])), hb, DC, 128, ones_b,
                                        gsc.ap[:, 1:2], rope, Rg_b, kdst(g * 128, (g + 1) * 128), Dk))
                run_jobs(jobs, n, cb, sb_)
                for c0 in range(0, GKV, 4):
                    ng = min(4, GKV - c0)
                    wb, wv_ = wload(ws, wkv[:, :, (GKV + c0) * 128:(GKV + c0 + ng) * 128], DC, ng * 128)
                    wbs = [(wb, wv_[:, :, q4 * 128:(q4 + 1) * 128]) for q4 in range(ng)]
                    if kind == "lat":
                        vproj(hb, DC, n, wbs, (lambda s0, c0=c0, ng=ng: vsrc_cols(t0 + s0, 128, c0, ng)), Dv)
                    else:
                        vproj(hb, DC, n, wbs, (lambda s0, c0=c0, ng=ng: [(v_ctx[t0 + s0:t0 + s0 + 128, c0 * 128:(c0 + ng) * 128], 0, ng * 128)]), Dv)
            else:
                wdq = mla_w_dq[j].rearrange("(kc p) m -> p kc m", p=128)
                wuq = mla_w_uq[j].rearrange("(kc p) m -> p kc m", p=128)
                wdkv = mla_w_dkv[j].rearrange("(kc p) m -> p kc m", p=128)
                wukv = mla_w_ukv[j].rearrange("(kc p) m -> p kc m", p=128)

                def compress(wsrc, nchunk, gv_off, outn):
                    ssps = psum[1]
                    outn.new_gen(); cq_f.new_gen()
                    for oc in range(nchunk):
                        if oc % 4 == 0:
                            nq = min(4, nchunk - oc)
                            wb, wvw = wload(ws, wsrc[:, :, oc * 128:(oc + nq) * 128], DC, nq * 128)
                        wv_ = wvw[:, :, (oc % 4) * 128:(oc % 4 + 1) * 128]
                        ps = nextps()
                        proj(ps, wv_, wb, hb, DC, n)
                        act(cq_f.ap[:, oc, :n], ps.ap[:, :n], AF.Copy, reads=[ps], pwrites=[cq_f])
                        s_ = sq[oc % 2]
                        act(s_.ap[:, :n], ps.ap[:, :n], AF.Square, reads=[ps], writes=[s_])
                        mm(ssps.ap[:, :n], ones_b.ap, s_.ap[:, :n], oc == 0, oc == nchunk - 1, reads=[s_, ones_b],
                           writes=[ssps] if oc == 0 else (), pwrites=() if oc == 0 else [ssps])
                    rstd_from_ss(ssps, n, float(nchunk * 128), rs_sq, rs)
                    for oc in range(nchunk):
                        stt(outn.ap[:, oc, :n], cq_f.ap[:, oc, :n], vcol(gv_off + oc), rs.ap[:, :n], ALU.mult, ALU.mult,
                            reads=[cq_f, rs, vec], pwrites=[outn])

                if need_q and do_q:
                    compress(wdq, QC, V_GDQ, cqn)
                    wuq4 = wuq.rearrange("p k (h e) -> p k h e", e=192)
                    jobs = []
                    for h0 in range(0, MH, 4):
                        nq = min(4, MH - h0)
                        sw = shared_w(lambda h0=h0, nq=nq: wload4(ws, wuq4[:, :, h0:h0 + nq, 0:128], QC, nq))
                        for hq in range(nq):
                            h = h0 + hq
                            jobs.append(job((lambda sw=sw, hq=hq: (sw()[0], sw()[1][:, :, hq, :])), cqn, QC, 128, ones_b,
                                            gsc.ap[:, 0:1], False, None, qT_d[h * 128:(h + 1) * 128, tokq:tokq + n], Dq))
                    for hp in range(MH // 2):
                        h0 = 2 * hp
                        jobs.append(job((lambda h0=h0: wload2(ws, wuq[:, :, h0 * 192 + 128:h0 * 192 + 192], wuq[:, :, (h0 + 1) * 192 + 128:(h0 + 1) * 192 + 192], QC)),
                                        cqn, QC, 64, bones_b, gsc.ap[:, 1:2], rope, Rm_b, qpeT_d[hp * 128:(hp + 1) * 128, tokq:tokq + n], Dqpe))
                    run_jobs(jobs, n, cb, sb_)
                if not do_kv:
                    return
                compress(wdkv, KC2, V_GDKV, ckvn)
                jobs = []
                jobs.append(job((lambda: wload2(ws, wdkv[:, :, KVR:KVR + 64], wdkv[:, :, KVR:KVR + 64], DC)), hb, DC, 64, bones_b, gsc.ap[:, 2:3],
                                rope, Rm_b, (kpe_src[:, t0:t0 + n] if kind == "lat" else kpe_ctx[:, t0:t0 + n]), (Dkpesrc if kind == "lat" else Dkpectx)))
                wukv4 = wukv.rearrange("p k (h e) -> p k h e", e=256)
                for h0 in range(0, MH, 4):
                    nq = min(4, MH - h0)
                    sw = shared_w(lambda h0=h0, nq=nq: wload4(ws, wukv4[:, :, h0:h0 + nq, 0:128], KC2, nq))
                    for hq in range(nq):
                        h = h0 + hq
                        jobs.append(job((lambda sw=sw, hq=hq: (sw()[0], sw()[1][:, :, hq, :])), ckvn, KC2, 128, ones_b,
                                        gsc.ap[:, 3:4], False, None, kdst(h * 128, (h + 1) * 128), Dk))
                run_jobs(jobs, n, cb, sb_)
                for h0 in range(0, MH, 4):
                    nh = min(4, MH - h0)
                    wb, wvw = wload4(ws, wukv4[:, :, h0:h0 + nh, 128:256], KC2, nh)
                    wbs = [(wb, wvw[:, :, q4, :]) for q4 in range(nh)]
                    if kind == "lat":
                        vproj(ckvn, KC2, n, wbs, (lambda s0, h0=h0, nh=nh: vsrc_cols(t0 + s0, 128, h0, nh)), Dv)
                    else:
                        vproj(ckvn, KC2, n, wbs, (lambda s0, h0=h0, nh=nh: [(v_ctx[t0 + s0:t0 + s0 + 128, h0 * 128:(h0 + nh) * 128], 0, nh * 128)]), Dv)
        def emit_ag():
            Dkall.new_gen(); Dvall.new_gen()
            for bi in range(NB):
                P.cc("AllGather", groups, kT_src[bi][:, :], kT_all[bi][:, :], reads=[Dksrc], pwrites=[Dkall])
                P.cc("AllGather", groups, v_src[bi][:, :], v_all[bi][:, :], reads=[Dvsrc], pwrites=[Dvall])
            if is_mla:
                P.cc("AllGather", groups, kpe_src[:, :], kpe_all[:, :], reads=[Dkpesrc], writes=[Dkpeall])

        if is_mla:
            for ti, (kind, t0, n) in enumerate(tiles):
                p1_tile(ti, kind, t0, n, False, True)
            emit_ag()
            for ti, (kind, t0, n) in enumerate(tiles):
                if not (kind == "ctx" and last):
                    p1_tile(ti + len(tiles), kind, t0, n, True, False)
        else:
            for ti, (kind, t0, n) in enumerate(tiles):
                p1_tile(ti, kind, t0, n, True, True)
            emit_ag()
        P.barrier()

        phase_reset(PERSIST)
        oT = carve([NH, NTOK], BF16, False)
        P3BASE = apos[0]
        KA = [carve([T], BF16) for _ in range(2)]
        VV = [carve([NCH, 128], BF16) for _ in range(2)]
        KBm = [carve([T], BF16) for _ in range(2)] if is_mla else None
        QA = [carve([W1], BF16) for _ in range(2)]
        QB = [carve([W1], BF16) for _ in range(2)] if is_mla else None
        pT = [carve([W1], BF16, False) for _ in range(4)]
        rec = carve([W1], F32, False)
        accD = [carve([W1], F32, False) for _ in range(2)]
        if is_mla:
            kpa = kpe_all.rearrange("(r d) t -> d r t", d=128)
            for hf in range(2):
                lo, hi = hf * 64, hf * 64 + 64
                zl, zh = (64, 128) if hf == 0 else (0, 64)
                mset(KBm[hf].ap[zl:zh, :], 0.0, writes=[KBm[hf]])
                dma("sp", KBm[hf].ap[lo:hi, 0:NR * TL].rearrange("p (r t) -> p r t", r=NR), kpa[lo:hi], KBm[hf], reads=[Dkpeall], pwrites=[KBm[hf]])
                dma("sp", KBm[hf].ap[lo:hi, NR * TL:T], kpe_ctx[lo:hi, :], KBm[hf], reads=[Dkpectx], pwrites=[KBm[hf]])
        wup = ffn_w_up[l].rearrange("(kc p) m -> p kc m", p=128)
        wdn = ffn_w_down[l].rearrange("(kc p) m -> p kc m", p=128)
        Dwc.new_gen()
        ncv = 0
        for g in range(NUT):
            nq = min(2, FC - 2 * g)
            for part in range(2):
                ln = cvl[ncv % len(cvl)]; ncv += 1
                dma("pool", wupT[2 * g + part][:, :DC * nq * 128].rearrange("p (kc m) -> p kc m", kc=DC),
                    wup[:, :, part * FF + g * 256:part * FF + g * 256 + nq * 128], ln, writes=[ln], pwrites=[Dwc])
        for g in range(NDT):
            nq = min(2, DC - 2 * g)
            ln = cvl[ncv % len(cvl)]; ncv += 1
            dma("pool", wdnT[g][:, :FC * nq * 128].rearrange("p (kc m) -> p kc m", kc=FC),
                wdn[:, :, g * 256:g * 256 + nq * 128], ln, writes=[ln], pwrites=[Dwc])
        qtiles = lat_tiles + ([] if last else ctx_tiles)
        oT.new_gen()
        it = 0
        kall4 = [a.rearrange("(r g d) t -> d g r t", r=NR, g=HB) for a in kT_all]
        vall4 = [a.rearrange("(ch p) (g d) -> p ch g d", p=128, d=128) for a in v_all]
        vctx4 = v_ctx.rearrange("(ch p) (g d) -> p ch g d", p=128, d=128)
        for g in range(NKV):
            ka = KA[g % 2]; vv = VV[g % 2]
            dma("sp", ka.ap[:, 0:NR * TL].rearrange("p (r t) -> p r t", r=NR), kall4[g // HB][:, g % HB], ka, reads=[Dkall], writes=[ka])
            dma("sp", ka.ap[:, NR * TL:T], kT_ctx[g * 128:(g + 1) * 128, :], ka, reads=[Dkctx], pwrites=[ka])
            dma("sp", vv.ap[:, 0:NR * TL // 128, :], vall4[g // HB][:, :, g % HB, :], vv, reads=[Dvall], writes=[vv])
            dma("sp", vv.ap[:, NR * TL // 128:NCH, :], vctx4[:, :, g, :], vv, reads=[Dvctx], pwrites=[vv])
            for hh in range(GRP):
                h = g * GRP + hh
                hp = (h % 2) * 64
                for (kind, t0, n) in qtiles:
                    tokq = t0 if kind == "lat" else TL + t0
                    qa = QA[it % 2]; qb = QB[it % 2] if is_mla else None
                    ops_ = psum[3 + it % 2]; sums = psum[5 + it % 2]
                    it += 1
                    dma("sp", qa.ap[:, :n], qT_d[h * 128:(h + 1) * 128, tokq:tokq + n], qa, reads=[Dq], writes=[qa])
                    if is_mla:
                        dma("sp", qb.ap[:, :n], qpeT_d[(h // 2) * 128:(h // 2 + 1) * 128, tokq:tokq + n], qb, reads=[Dqpe], writes=[qb])
                    chunks = list(range(NCH)) if kind == "lat" else list(range(NR * TL // 128, NCH))
                    nchk = len(chunks)

                    aD = accD[it % 2]
                    kbm = KBm[h % 2] if is_mla else None
                    pe_chunks = [ci for ci in range(nchk) if ci % 4 == 3]
                    dve_chunks = [ci for ci in range(nchk) if ci not in pe_chunks]

                    def emit_S(ci):
                        ch = chunks[ci]
                        sp_ = psum[ci % 3]
                        mm(sp_.ap[:, :n], ka.ap[:, ch * 128:(ch + 1) * 128], qa.ap[:, :n], True, not is_mla, reads=[ka, qa], writes=[sp_])
                        if is_mla:
                            mm(sp_.ap[:, :n], kbm.ap[:, ch * 128:(ch + 1) * 128], qb.ap[:, :n], False, True, reads=[kbm, qb], pwrites=[sp_])
                        pt = pT[ci % 4]
                        act(pt.ap[:, :n], sp_.ap[:, :n], AF.Exp, reads=[sp_], writes=[pt])

                    def emit_PV(ci):
                        ch = chunks[ci]
                        pt = pT[ci % 4]
                        mm(ops_.ap[:, :n], vv.ap[:, ch, :], pt.ap[:, :n], ci == 0, ci == nchk - 1, reads=[vv, pt],
                           writes=[ops_] if ci == 0 else (), pwrites=() if ci == 0 else [ops_])
                        if ci in pe_chunks:
                            mm(sums.ap[:, :n], ones_b.ap, pt.ap[:, :n], ci == pe_chunks[0], False, reads=[ones_b, pt],
                               writes=[sums] if ci == pe_chunks[0] else (), pwrites=() if ci == pe_chunks[0] else [sums])
                        elif ci == dve_chunks[0]:
                            cpy(aD.ap[:, :n], pt.ap[:, :n], reads=[pt], writes=[aD])
                        else:
                            tt(aD.ap[:, :n], aD.ap[:, :n], pt.ap[:, :n], ALU.add, reads=[pt, aD], pwrites=[aD])

                    emit_S(0)
                    if nchk > 1:
                        emit_S(1)
                    for ci in range(nchk):
                        if ci + 2 < nchk:
                            emit_S(ci + 2)
                        emit_PV(ci)
                    mm(sums.ap[:, :n], ones_f.ap, aD.ap[:, :n], not pe_chunks, True, reads=[ones_f, aD],
                       writes=[sums] if not pe_chunks else (), pwrites=[sums] if pe_chunks else ())
                    recip(rec.ap[:, :n], sums.ap[:, :n], reads=[sums], writes=[rec])
                    tt(oT.ap[:, h, tokq:tokq + n], ops_.ap[:, :n], rec.ap[:, :n], ALU.mult, reads=[ops_, rec], pwrites=[oT])
        P.barrier()

        phase_reset(P3BASE)
        xt3 = [carve([DC, W1], F32) for _ in range(2)]
        h3 = [carve([DC, W1], BF16) for _ in range(1)]
        sq3 = [carve([W1], BF16, False) for _ in range(2)]
        rs_sq3 = carve([W1], F32, False); rs3 = carve([W1], F32, False)
        tmpf3 = [carve([W1], F32, False) for _ in range(2)]
        ws3 = [carve([NH, 256], BF16) for _ in range(NWS)]
        w_o = (mla_w_o if is_mla else gqa_w_o)[j].rearrange("(kc p) m -> p kc m", p=128)
        tiles3 = lat_tiles + ([] if last else ctx_tiles)
        Dh2.new_gen(); Dh2c.new_gen(); bnd.new_gen()
        for ti, (kind, t0, n) in enumerate(tiles3):
            jj = 0 if kind == "lat" else 1
            tokq = t0 if kind == "lat" else TL + t0
            xb = xt3[ti % 2]; hb = h3[0]
            src_d, Dsrc = xsrc(kind, l)
            dst_d = xT_s if kind == "lat" else ctxT_s
            dma("sp", xb.ap[:, :, :n], fm3(src_d, t0, n), xb, reads=[Dsrc], writes=[xb])
            for dc in range(DC):
                if dc % 2 == 0:
                    nq = min(2, DC - dc)
                    wb, wvw = wload(ws3, w_o[:, :, dc * 128:(dc + nq) * 128], NH, nq * 128)
                wv_ = wvw[:, :, (dc % 2) * 128:(dc % 2 + 1) * 128]
                ps = nextps()
                for kc in range(NH):
                    mm(ps.ap[:, :n], wv_[:, kc, :], oT.ap[:, kc, tokq:tokq + n], kc == 0, kc == NH - 1, reads=[wb, oT],
                       writes=[ps] if kc == 0 else (), pwrites=() if kc == 0 else [ps])
                stt(xb.ap[:, dc, :n], ps.ap[:, :n], modcol(2, dc, jj), xb.ap[:, dc, :n], ALU.mult, ALU.add, reads=[ps, mod, xb], pwrites=[xb])
            dma("sp", fm3(dst_d, t0, n), xb.ap[:, :, :n], xb, reads=[xb], pwrites=[Dsrc])
            modulate(xb, n, A2, 3, jj, hb, sq3, rs_sq3, rs3, tmpf3)
            if kind == "lat":
                dma("sp", fm3(h2T_d, 1 + t0, n), hb.ap[:, :, :n], hb, reads=[hb], pwrites=[Dh2])
                if t0 == 0:
                    cpy(bnd.ap[:, :, 0], hb.ap[:, :, 0], reads=[hb], pwrites=[bnd])
                if t0 + n == TL:
                    cpy(bnd.ap[:, :, 1], hb.ap[:, :, n - 1], reads=[hb], pwrites=[bnd])
            else:
                dma("sp", fm3(h2cT_d, 1 + t0, n), hb.ap[:, :, :n], hb, reads=[hb], pwrites=[Dh2c])
        dma("sp", hsrc[:, :], bnd.ap.rearrange("p a b -> p (a b)"), bnd, reads=[bnd], writes=[Dhsrc])
        P.barrier()
        P.cc("AllGather", groups, hsrc[:, :], hall[:, :], reads=[Dhsrc], writes=[Dhall])
        dma("sp", hs.ap, hall.rearrange("(r p) (c t) -> p r c t", p=128, t=2), hs, reads=[Dhall], writes=[hs])
        for side in range(2):
            for r in range(NR):
                srcc = hs.ap[:, r, :, 1 - side]
                scol = sel_f.ap[:, side * NR + r:side * NR + r + 1]
                if r == 0:
                    tsm(halo_f.ap[:, :, side], srcc, scol, reads=[hs, sel_f], writes=[halo_f] if side == 0 else (), pwrites=() if side == 0 else [halo_f])
                else:
                    stt(halo_f.ap[:, :, side], srcc, scol, halo_f.ap[:, :, side], ALU.mult, ALU.add, reads=[hs, sel_f, halo_f], pwrites=[halo_f])
        cpy(halo.ap, halo_f.ap, reads=[halo_f], writes=[halo])
        P.barrier()

        phase_reset(PERSIST)
        WIN = W4 + 2
        h2w = [carve([DC, WIN], BF16) for _ in range(2)]
        aT = carve([FC, W4], BF16, False)
        xw = [carve([DC, W4], F32) for _ in range(1)]
        c1 = [carve([W4], F32, False) for _ in range(2)]; c2 = [carve([W4], F32, False) for _ in range(2)]
        c3 = [carve([W4], F32, False) for _ in range(2)]; sg = [carve([W4], F32, False) for _ in range(2)]
        wsu = [carve([DC, 256], BF16) for _ in range(cfg.NWU)]
        wsd = [carve([FC, 256], BF16) for _ in range(cfg.NWD)]
        def wload_bf(ring, src2, kc, m):
            b = ring[cnt["ws"] % len(ring)]; cnt["ws"] += 1
            v2 = b.ap.rearrange("p a b -> p (a b)")[:, :kc * m]
            dma("pool", v2, src2, b, reads=[Dwc], writes=[b])
            return b, v2.rearrange("p (a b) -> p a b", a=kc)

        wins = [("lat", s, min(W4, TL - s)) for s in range(0, TL, W4)]
        if not last:
            wins += [("ctx", s, min(W4, C - s)) for s in range(0, C, W4)]
        pi = 0
        for wi, (kind, s0, nout) in enumerate(wins):
            jj = 0 if kind == "lat" else 1
            nin = nout + 2
            hw = h2w[wi % 2]; xb = xw[0]
            tot = TL if kind == "lat" else C
            if kind == "lat":
                dma("sp", hw.ap[:, :, :nin], fm3(h2T_d, s0, nin), hw, reads=[Dh2], writes=[hw])
            else:
                dma("sp", hw.ap[:, :, :nin], fm3(h2cT_d, s0, nin), hw, reads=[Dh2c], writes=[hw])
            if s0 == 0:
                if kind == "lat":
                    cpy(hw.ap[:, :, 0], halo.ap[:, :, 0], reads=[halo, hw], pwrites=[hw])
                else:
                    mset(hw.ap[:, :, 0], 0.0, reads=[hw], pwrites=[hw])
            if s0 + nout == tot:
                if kind == "lat":
                    cpy(hw.ap[:, :, nin - 1], halo.ap[:, :, 1], reads=[halo, hw], pwrites=[hw])
                else:
                    mset(hw.ap[:, :, nin - 1], 0.0, reads=[hw], pwrites=[hw])
            src_d = xT_s if kind == "lat" else ctxT_s
            Dsrc = Dx if kind == "lat" else Dctx
            dma("sp", xb.ap[:, :, :nout], fm3(src_d, s0, nout), xb, reads=[Dsrc], writes=[xb])
            aT.new_gen()
            for fc in range(FC):
                if fc % 2 == 0:
                    nq = min(2, FC - fc)
                    wgb, wgw = wload_bf(wsu, wupT[fc + 0][:, :DC * nq * 128], DC, nq * 128)
                    wvb, wvw = wload_bf(wsu, wupT[fc + 1][:, :DC * nq * 128], DC, nq * 128)
                wgv = wgw[:, :, (fc % 2) * 128:(fc % 2 + 1) * 128]
                wvv = wvw[:, :, (fc % 2) * 128:(fc % 2 + 1) * 128]
                gps = psum[1 + pi % 2]; vps = psum[3 + pi % 2]; pi += 1
                for kc in range(DC):
                    mm(gps.ap[:, :nin], wgv[:, kc, :], hw.ap[:, kc, :nin], kc == 0, kc == DC - 1, reads=[wgb, hw],
                       writes=[gps] if kc == 0 else (), pwrites=() if kc == 0 else [gps])
                for kc in range(DC):
                    mm(vps.ap[:, :nin], wvv[:, kc, :], hw.ap[:, kc, :nin], kc == 0, kc == DC - 1, reads=[wvb, hw],
                       writes=[vps] if kc == 0 else (), pwrites=() if kc == 0 else [vps])
                a1 = c1[fc % 2]; a2 = c2[fc % 2]; a3 = c3[fc % 2]; sgb = sg[fc % 2]
                act(a1.ap[:, :nout], gps.ap[:, 1:1 + nout], AF.Identity, reads=[gps, vec], writes=[a1], bias=vcol(V_CB + fc), scale=vcol(V_CW + FC + fc))
                stt(a2.ap[:, :nout], gps.ap[:, 0:nout], vcol(V_CW + fc), a1.ap[:, :nout], ALU.mult, ALU.add, reads=[gps, vec, a1], writes=[a2])
                stt(a3.ap[:, :nout], gps.ap[:, 2:2 + nout], vcol(V_CW + 2 * FC + fc), a2.ap[:, :nout], ALU.mult, ALU.add, reads=[gps, vec, a2], writes=[a3])
                act(sgb.ap[:, :nout], a3.ap[:, :nout], AF.Silu, reads=[a3], writes=[sgb])
                tt(aT.ap[:, fc, :nout], sgb.ap[:, :nout], vps.ap[:, 1:1 + nout], ALU.mult, reads=[sgb, vps], pwrites=[aT])
            for dc in range(DC):
                if dc % 2 == 0:
                    nq = min(2, DC - dc)
                    wb, wdw = wload_bf(wsd, wdnT[dc // 2][:, :FC * nq * 128], FC, nq * 128)
                wv_ = wdw[:, :, (dc % 2) * 128:(dc % 2 + 1) * 128]
                ps = psum[5 + pi % 2]; pi += 1
                for kc in range(FC):
                    mm(ps.ap[:, :nout], wv_[:, kc, :], aT.ap[:, kc, :nout], kc == 0, kc == FC - 1, reads=[wb, aT],
                       writes=[ps] if kc == 0 else (), pwrites=() if kc == 0 else [ps])
                stt(xb.ap[:, dc, :nout], ps.ap[:, :nout], modcol(5, dc, jj), xb.ap[:, dc, :nout], ALU.mult, ALU.add, reads=[ps, mod, xb], pwrites=[xb])
            if kind == "lat" and last:
                dma("sp", fm3(yT, s0, nout), xb.ap[:, :, :nout], xb, reads=[xb], pwrites=[Dy])
            else:
                dst_d = xT_s if kind == "lat" else ctxT_s
                dma("sp", fm3(dst_d, s0, nout), xb.ap[:, :, :nout], xb, reads=[xb], pwrites=[Dsrc])
        P.barrier()

    P.emit("sp", lambda e: e.nop(), reads=[Dy])
    P.finalize()
    global LASTP
    LASTP = P
    return nc


def pack_vecs(cfg, inp, l):
    DC, FC = cfg.DC, cfg.FC
    j = l // 2
    cols = [_fm(inp["norm_mix"][l]), _fm(inp["norm_ffn"][l]), _fm(inp["b_mod"][l])]
    cw = np.asarray(inp["ffn_conv_w"][l], np.float32)
    cols += [_fm(cw[0]), _fm(cw[1]), _fm(cw[2]), _fm(inp["ffn_conv_b"][l])]
    QC = cfg.QR // 128; KC2 = cfg.KVR // 128
    z1 = np.zeros((128, 1), np.float32)
    if l % 2 == 0:
        dup = lambda v: np.concatenate([np.asarray(v, np.float32)] * 2)[:, None]
        cols += [_fm(inp["mla_g_dq"][j]), _fm(inp["mla_g_dkv"][j]), np.asarray(inp["mla_g_q_nope"][j], np.float32)[:, None],
                 dup(inp["mla_g_q_pe"][j]), dup(inp["mla_g_k_pe"][j]), np.asarray(inp["mla_g_k_nope"][j], np.float32)[:, None], z1, z1]
    else:
        cols += [np.zeros((128, QC), np.float32), np.zeros((128, KC2), np.float32), z1, z1, z1, z1,
                 np.asarray(inp["gqa_g_q"][j], np.float32)[:, None], np.asarray(inp["gqa_g_k"][j], np.float32)[:, None]]
    return np.concatenate(cols, axis=1).astype(np.float32)


def prep_inputs(cfg, inp):
    NR, TL, D, C = cfg.NR, cfg.TL, cfg.D, cfg.C
    L = cfg.depth
    f = lambda a: np.ascontiguousarray(np.asarray(a, np.float32))
    vecs = np.stack([pack_vecs(cfg, inp, l) for l in range(L)])
    RgT = rot_matrix_T(128)
    Rm = rot_matrix_T(64)
    RmT = np.zeros((128, 128), np.float32); RmT[:64, :64] = Rm; RmT[64:, 64:] = Rm
    bones = np.zeros((128, 128), np.float32); bones[:64, :64] = 1; bones[64:, 64:] = 1
    shared = dict(RgT=RgT, RmT=RmT, bones=bones, vecs=vecs, w_mod=f(inp["w_mod"]),
                  mla_w_dq=f(inp["mla_w_dq"]), mla_w_uq=f(inp["mla_w_uq"]), mla_w_dkv=f(inp["mla_w_dkv"]),
                  mla_w_ukv=f(inp["mla_w_ukv"]), mla_w_o=f(inp["mla_w_o"]),
                  gqa_w_q=f(inp["gqa_w_q"]), gqa_w_kv=f(inp["gqa_w_kv"]), gqa_w_o=f(inp["gqa_w_o"]),
                  ffn_w_up=f(inp["ffn_w_up"]), ffn_w_down=f(inp["ffn_w_down"]))
    x = np.asarray(inp["x"], np.float32); ctx = np.asarray(inp["ctx"], np.float32)
    c = np.asarray(inp["c"], np.float32); c_ctx = np.asarray(inp["c_ctx"], np.float32)
    maps = []
    for core in range(8):
        b = core // NR; r = core % NR
        t0 = r * TL
        m = dict(shared)
        m["xT"] = np.ascontiguousarray(x[b, t0:t0 + TL].T)
        m["ctxT"] = np.ascontiguousarray(ctx[b].T)
        cv = np.stack([_fm(c[b]), _fm(c_ctx)], axis=-1)
        m["cvec"] = np.ascontiguousarray(cv.reshape(128, -1))
        cg, sg = rope_tables(cfg, t0, TL, 128)
        cm, sm = rope_tables(cfg, t0, TL, 64)
        m["cosg"] = cg; m["sing"] = sg
        m["cosm"] = np.concatenate([cm, cm], 0); m["sinm"] = np.concatenate([sm, sm], 0)
        s = np.zeros((128, 2 * NR), np.float32)
        if r > 0:
            s[:, r - 1] = 1.0
        if r < NR - 1:
            s[:, NR + r + 1] = 1.0
        m["sel"] = s
        maps.append(m)
    return maps


_NC_CACHE = {}


def run(cfg, inp, debug=False):
    key = (cfg.D, cfg.S, cfg.C, cfg.FF, cfg.depth)
    if key not in _NC_CACHE:
        _NC_CACHE[key] = build(cfg, debug)
    nc = _NC_CACHE[key]
    maps = prep_inputs(cfg, inp)
    res = run_bass_kernel_spmd(nc, maps, core_ids=list(range(8)))
    out = np.zeros((cfg.B, cfg.S, cfg.D), np.float32)
    for core in range(8):
        b = core // cfg.NR; r = core % cfg.NR
        out[b, r * cfg.TL:(r + 1) * cfg.TL] = res.results[core]["yT"].T
    return out


def kernel(**inputs):
    return run(Cfg(), inputs)
```

```python
import math
import numpy as np
import concourse.bass as bass
import concourse.mybir as mybir
from concourse.bass_utils import run_bass_kernel_spmd

F32 = mybir.dt.float32
BF16 = mybir.dt.bfloat16
AF = mybir.ActivationFunctionType
ALU = mybir.AluOpType
EPS = 1e-6
ROPE_BASE = 10000.0


class Cfg:
    def __init__(s, **kw):
        s.D = 2048; s.S = 8192; s.B = 2; s.C = 256; s.FF = 5632; s.depth = 4; s.GRID_W = 64
        s.MH = 16; s.QR = 512; s.KVR = 512; s.GH = 16; s.GKV = 4
        s.W1 = 512; s.W4 = 410; s.NR = 4
        s.NWS = 3; s.NWU = 4; s.NWD = 2; s.WDSPLIT = 1
        for k, v in kw.items():
            setattr(s, k, v)
        s.TL = s.S // s.NR
        s.DC = s.D // 128; s.FC = s.FF // 128
        s.NTOK = s.TL + s.C


class Op:
    __slots__ = ("eng", "fn", "deps", "marked", "cum", "is_dma", "sem", "val", "inc", "idx", "epoch")


class Buf:
    def __init__(s, ap=None, sem=None):
        s.ap = ap; s.w = []; s.r = []; s.gen = []; s.sem = sem; s.cnt = 0

    def new_gen(s):
        s.gen = s.w + s.r; s.w = []; s.r = []


class Prog:
    def __init__(s, nc):
        s.nc = nc
        s.ops = {k: [] for k in ("pe", "act", "dve", "pool", "sp")}
        s.engsem = {}
        s.epoch = 0; s.nidx = 0
        s.ccsem = nc.alloc_semaphore("ccsem"); s.cccnt = 0

    def new_epoch(s):
        s.epoch += 1

    def esem(s, eng, epoch):
        k = (eng, epoch)
        if k not in s.engsem:
            s.engsem[k] = s.nc.alloc_semaphore("es_%s_%d" % (eng, epoch))
        return s.engsem[k]

    def emit(s, eng, fn, reads=(), writes=(), pwrites=(), extra=()):
        deps = list(extra)
        for b in reads:
            deps += b.w
        for b in writes:
            b.new_gen(); deps += b.gen
        for b in pwrites:
            deps += b.gen
        best = {}
        for d in deps:
            if d.is_dma:
                k = ("s", d.sem.num); v = d.val
            else:
                k = ("e", d.eng); v = d.idx
            o = best.get(k)
            if o is None or v > o[0]:
                best[k] = (v, d)
        deps = [o[1] for o in best.values()]
        op = Op(); op.eng = eng; op.fn = fn; op.deps = deps; op.marked = False; op.cum = 0
        op.is_dma = False; op.sem = None; op.val = 0; op.inc = 0
        op.idx = s.nidx; s.nidx += 1; op.epoch = s.epoch
        for b in reads:
            b.r.append(op)
        for b in writes:
            b.w.append(op)
        for b in pwrites:
            b.w.append(op)
        s.ops[eng].append(op)
        return op

    def dma(s, eng, out, in_, sb, reads=(), writes=(), pwrites=(), extra=()):
        op = s.emit(eng, lambda e: e.dma_start(out=out, in_=in_), reads, writes, pwrites, extra)
        sb.cnt += 16
        op.is_dma = True; op.sem = sb.sem; op.val = sb.cnt; op.inc = 16
        return op

    def cc(s, kind, groups, in_ap, out_ap, reads=(), writes=(), pwrites=()):
        op = s.emit("pool", lambda e: e.collective_compute(kind, ALU.bypass, replica_groups=groups,
                                                            ins=[in_ap], outs=[out_ap]), reads, writes, pwrites)
        s.cccnt += 1
        op.is_dma = True; op.sem = s.ccsem; op.val = s.cccnt; op.inc = 1
        return op

    def barrier(s):
        deps = []
        latest = {}
        for eng, ops in s.ops.items():
            lastc = None
            for op in ops:
                if op.is_dma:
                    latest[op.sem.num] = op
                else:
                    lastc = op
            if lastc is not None:
                deps.append(lastc)
        deps += list(latest.values())
        for eng in s.ops:
            s.emit(eng, lambda e: e.nop(), extra=deps)

    def finalize(s):
        nc = s.nc
        for eng, ops in s.ops.items():
            for op in ops:
                for d in op.deps:
                    if not d.is_dma:
                        d.marked = True
        for eng, ops in s.ops.items():
            cnt = {}
            for op in ops:
                if (not op.is_dma) and op.marked:
                    cnt[op.epoch] = cnt.get(op.epoch, 0) + 1
                    s.esem(eng, op.epoch)
                op.cum = cnt.get(op.epoch, 0)

        def run(engname, e):
            waited = {}
            for op in s.ops[engname]:
                for d in op.deps:
                    if d.is_dma:
                        key = ("s", d.sem.num); val = d.val; sem = d.sem
                    else:
                        if d.eng == "pe" and engname == "pe":
                            continue
                        key = ("e", d.eng, d.epoch); val = d.cum; sem = s.engsem[(d.eng, d.epoch)]
                    if waited.get(key, 0) >= val:
                        continue
                    waited[key] = val
                    e.wait_ge(sem, val)
                ins = op.fn(e)
                if op.is_dma:
                    if op.inc == 16:
                        ins.then_inc(op.sem, 16)
                    else:
                        ins.then_inc(op.sem)
                elif op.marked:
                    ins.then_inc(s.engsem[(op.eng, op.epoch)], 1)

        with nc.Block() as block:
            @block.sync
            def _(e):
                run("sp", e)

            @block.gpsimd
            def _(e):
                run("pool", e)

            @block.scalar
            def _(e):
                run("act", e)

            @block.vector
            def _(e):
                run("dve", e)

            @block.tensor
            def _(e):
                run("pe", e)


def _fm(v):
    v = np.asarray(v, np.float32)
    return np.ascontiguousarray(v.reshape(-1, 128).T)


def rope_tables(cfg, tok0, n, rot_dim):
    axis_dim = rot_dim // 2
    inv = np.power(np.float32(ROPE_BASE), -np.arange(0, axis_dim, 2, dtype=np.float32) / np.float32(axis_dim)).astype(np.float32)
    t = np.arange(tok0, tok0 + n)
    rows = (t // cfg.GRID_W).astype(np.float32); cols = (t % cfg.GRID_W).astype(np.float32)
    ar = rows[:, None] * inv; ac = cols[:, None] * inv
    ang = np.concatenate([ar, ar, ac, ac], axis=-1).astype(np.float32)
    return np.cos(ang).T.astype(np.float32), np.sin(ang).T.astype(np.float32)


def rot_matrix_T(rot_dim):
    half = rot_dim // 2; q = half // 2
    R = np.zeros((rot_dim, rot_dim), np.float32)
    for base in (0, half):
        for i in range(q):
            R[base + i, base + i + q] = -1.0
            R[base + q + i, base + i] = 1.0
    return np.ascontiguousarray(R.T)


def build(cfg, debug=False):
    nc = bass.Bass("TRN2", target_bir_lowering=False)
    P = Prog(nc)
    D, TL, C, FF, DC, FC, NTOK, NR = cfg.D, cfg.TL, cfg.C, cfg.FF, cfg.DC, cfg.FC, cfg.NTOK, cfg.NR
    L = cfg.depth
    LA = (L + 1) // 2; LB = max(L // 2, 1)
    MH, QR, KVR, GH, GKV = cfg.MH, cfg.QR, cfg.KVR, cfg.GH, cfg.GKV
    QC = QR // 128; KC2 = KVR // 128
    T = NR * TL + C
    NCH = T // 128
    W1 = cfg.W1; W4 = cfg.W4

    def din(name, shape, dt=F32):
        return nc.dram_tensor(name, list(shape), dt, kind="ExternalInput").ap()

    def dscr(name, shape, dt):
        return nc.dram_tensor(name, list(shape), dt).ap()

    xT_in = din("xT", [D, TL]); ctxT_in = din("ctxT", [D, C]); cvec = din("cvec", [128, DC * 2])
    cosg = din("cosg", [128, TL]); sing = din("sing", [128, TL]); cosm = din("cosm", [128, TL]); sinm = din("sinm", [128, TL])
    RgT = din("RgT", [128, 128]); RmT = din("RmT", [128, 128]); bones = din("bones", [128, 128]); sel = din("sel", [128, 2 * NR])
    NV = 2 * DC + 6 * DC + 4 * FC + QC + KC2 + 6
    vecs = din("vecs", [L, 128, NV])
    w_mod = din("w_mod", [L, D, 6 * D])
    mla_w_dq = din("mla_w_dq", [LA, D, QR]); mla_w_uq = din("mla_w_uq", [LA, QR, MH * 192])
    mla_w_dkv = din("mla_w_dkv", [LA, D, KVR + 64]); mla_w_ukv = din("mla_w_ukv", [LA, KVR, MH * 256])
    mla_w_o = din("mla_w_o", [LA, MH * 128, D])
    gqa_w_q = din("gqa_w_q", [LB, D, GH * 128]); gqa_w_kv = din("gqa_w_kv", [LB, D, 2 * GKV * 128])
    gqa_w_o = din("gqa_w_o", [LB, GH * 128, D])
    ffn_w_up = din("ffn_w_up", [L, D, 2 * FF]); ffn_w_down = din("ffn_w_down", [L, FF, D])
    yT = nc.dram_tensor("yT", [D, TL], F32, kind="ExternalOutput").ap()

    xT_s = dscr("xT_s", [D, TL], F32); ctxT_s = dscr("ctxT_s", [D, C], F32)
    qT_d = dscr("qT_d", [max(MH, GH) * 128, NTOK], BF16); qpeT_d = dscr("qpeT_d", [MH // 2 * 128, NTOK], BF16)
    kvs = {}
    for nm, nkv in (("m", MH), ("g", GKV)):
        HB = 2 if nkv % 2 == 0 else 1
        while HB > 1 and HB * 128 * TL * 2 > (1 << 20):
            HB //= 2
        nb = nkv // HB
        kvs[nm] = dict(HB=HB, NB=nb,
                       kT_src=[dscr("kT_src%s%d" % (nm, i), [HB * 128, TL], BF16) for i in range(nb)],
                       kT_all=[dscr("kT_all%s%d" % (nm, i), [NR * HB * 128, TL], BF16) for i in range(nb)],
                       v_src=[dscr("v_src%s%d" % (nm, i), [TL, HB * 128], BF16) for i in range(nb)],
                       v_all=[dscr("v_all%s%d" % (nm, i), [NR * TL, HB * 128], BF16) for i in range(nb)],
                       kT_ctx=dscr("kT_ctx" + nm, [nkv * 128, C], BF16), v_ctx=dscr("v_ctx" + nm, [C, nkv * 128], BF16))
    kpe_src = dscr("kpe_src", [128, TL], BF16); kpe_all = dscr("kpe_all", [NR * 128, TL], BF16); kpe_ctx = dscr("kpe_ctx", [128, C], BF16)
    h2T_d = dscr("h2T_d", [D, TL + 2], BF16); h2cT_d = dscr("h2cT_d", [D, C + 2], BF16)
    hsrc = dscr("hsrc", [128, DC * 2], BF16); hall = dscr("hall", [NR * 128, DC * 2], BF16)
    NUT = (FC + 1) // 2; NDT = (DC + 1) // 2
    wupT = dscr("wupT", [NUT * 2, 128, DC * 256], BF16); wdnT = dscr("wdnT", [NDT, 128, FC * 256], BF16)
    Dwc = Buf()
    Dx = Buf(); Dctx = Buf(); Dq = Buf(); Dqpe = Buf(); Dksrc = Buf(); Dkall = Buf(); Dvsrc = Buf(); Dvall = Buf()
    Dkpesrc = Buf(); Dkpeall = Buf(); Dkctx = Buf(); Dvctx = Buf(); Dkpectx = Buf(); Dh2 = Buf(); Dh2c = Buf(); Dhsrc = Buf(); Dhall = Buf()
    Dy = Buf()

    ARENA_BYTES = 207 * 1024
    arena = nc.alloc_sbuf_tensor("arena", [128, ARENA_BYTES // 4], F32)
    apos = [0]
    sempool = {}
    semidx = [0]
    persist_sems = [0]

    def phase_reset(base):
        apos[0] = base; semidx[0] = persist_sems[0]

    def carve(shape, dt, sem=True):
        esz = 4 if dt == F32 else 2
        nb = int(np.prod(shape)) * esz
        nb_al = (nb + 63) // 64 * 64
        off = apos[0]; apos[0] += nb_al
        assert apos[0] <= ARENA_BYTES, ("SBUF arena overflow", apos[0])
        a = arena[:, off // 4:(off + nb) // 4]
        if dt != F32:
            a = a.bitcast(dt)
        if len(shape) == 2:
            a = a.rearrange("p (a b) -> p a b", a=shape[0])
        elif len(shape) == 3:
            a = a.rearrange("p (a b c) -> p a b c", a=shape[0], b=shape[1])
        sm = None
        if sem:
            k = semidx[0]; semidx[0] += 1
            if k not in sempool:
                sempool[k] = [nc.alloc_semaphore("bs%d" % k), 0]
            sm = sempool[k]
        b = Buf(a, sm[0] if sm else None)
        if sm:
            b.cnt = sm[1]; b.semrec = sm
        return b

    def mm(out, lhsT, rhs, start, stop, reads, writes=(), pwrites=()):
        return P.emit("pe", lambda e: e.matmul(out, lhsT, rhs, start=start, stop=stop), reads=reads, writes=writes, pwrites=pwrites)

    def act(out, in_, func, reads, writes=(), pwrites=(), bias=None, scale=None, eng="act"):
        kw = {}
        if bias is not None:
            kw["bias"] = bias
        if scale is not None:
            kw["scale"] = scale
        return P.emit(eng, lambda e: e.activation(out, in_, func, **kw), reads=reads, writes=writes, pwrites=pwrites)

    def stt(out, in0, scalar, in1, op0, op1, reads, writes=(), pwrites=(), eng="dve"):
        return P.emit(eng, lambda e: e.scalar_tensor_tensor(out, in0, scalar, in1, op0, op1), reads=reads, writes=writes, pwrites=pwrites)

    def tt(out, in0, in1, op, reads, writes=(), pwrites=(), eng="dve"):
        return P.emit(eng, lambda e: e.tensor_tensor(out, in0, in1, op), reads=reads, writes=writes, pwrites=pwrites)

    def tsm(out, in0, s1, reads, writes=(), pwrites=(), eng="dve"):
        return P.emit(eng, lambda e: e.tensor_scalar(out, in0, s1, None, ALU.mult), reads=reads, writes=writes, pwrites=pwrites)

    def recip(out, in_, reads, writes=(), pwrites=()):
        return P.emit("dve", lambda e: e.reciprocal(out, in_), reads=reads, writes=writes, pwrites=pwrites)

    def cpy(out, in_, reads, writes=(), pwrites=(), eng="dve"):
        return P.emit(eng, lambda e: e.tensor_copy(out, in_), reads=reads, writes=writes, pwrites=pwrites)

    def mset(out, val, writes=(), pwrites=(), reads=(), eng="dve"):
        return P.emit(eng, lambda e: e.memset(out, val), reads=reads, writes=writes, pwrites=pwrites)

    def dma(eng, out, in_, sb, reads=(), writes=(), pwrites=()):
        sb.cnt = sb.semrec[1]
        op = P.dma(eng, out, in_, sb, reads=reads, writes=writes, pwrites=pwrites)
        sb.semrec[1] = sb.cnt
        return op

    ones_f = carve([128], F32, False); ones_b = carve([128], BF16, False); bones_f = carve([128], F32)
    Rg_f = carve([128], F32); Rm_f = carve([128], F32); sel_f = carve([2 * NR], F32)
    sv_f = carve([DC * 2], F32); sv_b = carve([DC, 2], BF16, False)
    vec = carve([NV], F32)
    mod = carve([6 * DC, 2], F32, False)
    A1 = carve([DC, 2], F32, False); A2 = carve([DC, 2], F32, False)
    gsc = carve([8], F32, False)
    bnd = carve([DC, 2], BF16); halo = carve([DC, 2], BF16, False); hs = carve([NR, DC, 2], BF16)
    halo_f = carve([DC, 2], F32, False)
    eps_c = carve([1], F32, False)
    bones_b = carve([128], BF16, False); Rg_b = carve([128], BF16, False); Rm_b = carve([128], BF16, False)
    cvl = [carve([1], F32) for _ in range(3)]
    psum = [Buf(nc.alloc_psum_tensor("ps%d" % i, [128, 512], F32)[:]) for i in range(8)]
    PERSIST = apos[0]
    persist_sems[0] = semidx[0]

    o = 0
    V_NMIX = o; o += DC
    V_NFFN = o; o += DC
    V_BMOD = o; o += 6 * DC
    V_CW = o; o += 3 * FC
    V_CB = o; o += FC
    V_GDQ = o; o += QC
    V_GDKV = o; o += KC2
    V_GQN = o; o += 1
    V_GQP = o; o += 1
    V_GKP = o; o += 1
    V_GKN = o; o += 1
    V_GQ = o; o += 1
    V_GK = o; o += 1
    assert o == NV

    def vcol(i):
        return vec.ap[:, i:i + 1]

    def modcol(m, c, jj):
        return mod.ap[:, m * DC + c, jj:jj + 1]

    NWS = cfg.NWS

    mset(ones_f.ap, 1.0, writes=[ones_f]); mset(ones_b.ap, 1.0, writes=[ones_b]); mset(eps_c.ap, EPS, writes=[eps_c])
    dma("sp", bones_f.ap, bones[:, :], bones_f, writes=[bones_f])
    dma("sp", Rg_f.ap, RgT[:, :], Rg_f, writes=[Rg_f])
    dma("sp", Rm_f.ap, RmT[:, :], Rm_f, writes=[Rm_f])
    dma("sp", sel_f.ap, sel[:, :], sel_f, writes=[sel_f])
    dma("sp", sv_f.ap, cvec[:, :], sv_f, writes=[sv_f])
    act(sv_b.ap.rearrange("p a b -> p (a b)"), sv_f.ap, AF.Silu, reads=[sv_f], writes=[sv_b])
    cpy(bones_b.ap, bones_f.ap, reads=[bones_f], writes=[bones_b])
    cpy(Rg_b.ap, Rg_f.ap, reads=[Rg_f], writes=[Rg_b])
    cpy(Rm_b.ap, Rm_f.ap, reads=[Rm_f], writes=[Rm_b])

    lat_tiles = [("lat", t0, min(W1, TL - t0)) for t0 in range(0, TL, W1)]
    ctx_tiles = [("ctx", t0, min(W1, C - t0)) for t0 in range(0, C, W1)]
    groups = [list(range(g * NR, (g + 1) * NR)) for g in range(8 // NR)]

    def xsrc(kind, l):
        if kind == "lat":
            return (xT_in if l == 0 else xT_s), Dx
        return (ctxT_in if l == 0 else ctxT_s), Dctx

    def fm3(dram2d, c0, n):
        return dram2d.rearrange("(c p) t -> p c t", p=128)[:, :, c0:c0 + n]

    def rstd_from_ss(ssps, n, nd, rs_sq, rs):
        act(rs_sq.ap[:, :n], ssps.ap[:, :n], AF.Sqrt, reads=[ssps, eps_c], writes=[rs_sq], bias=eps_c.ap[:, 0:1], scale=1.0 / nd)
        recip(rs.ap[:, :n], rs_sq.ap[:, :n], reads=[rs_sq], writes=[rs])

    for l in range(L):
        last = (l == L - 1)
        is_mla = (l % 2 == 0)
        j = l // 2
        P.new_epoch()
        NH = MH if is_mla else GH
        NKV = MH if is_mla else GKV
        GRP = NH // NKV
        KV = kvs["m" if is_mla else "g"]
        kT_src, kT_all, v_src, v_all, kT_ctx, v_ctx = KV["kT_src"], KV["kT_all"], KV["v_src"], KV["v_all"], KV["kT_ctx"], KV["v_ctx"]
        HB, NB = KV["HB"], KV["NB"]

        def ksrc_rows(hd):
            return kT_src[hd // HB][(hd % HB) * 128:(hd % HB + 1) * 128, :]

        def vsrc_cols(tok0, ntok, h0, nh):
            out = []
            hd = h0
            while hd < h0 + nh:
                k = min(HB - hd % HB, h0 + nh - hd)
                out.append((v_src[hd // HB][tok0:tok0 + ntok, (hd % HB) * 128:(hd % HB + k) * 128], (hd - h0) * 128, k * 128))
                hd += k
            return out

        phase_reset(PERSIST)
        dma("sp", vec.ap, vecs[l], vec, writes=[vec])
        wsm = [carve([DC, 512], BF16) for _ in range(3)]
        modps = psum[0]
        modps.new_gen()
        nfg = (6 * DC + 3) // 4
        wm3 = w_mod[l].rearrange("(kc p) f -> p kc f", p=128)
        for fg in range(nfg):
            nf = min(4, 6 * DC - fg * 4)
            wb = wsm[fg % 3]
            dma("pool", wb.ap[:, :, :nf * 128], wm3[:, :, fg * 512:fg * 512 + nf * 128], wb, writes=[wb])
            for fc in range(nf):
                col = (fg * 4 + fc) * 2
                for kc in range(DC):
                    mm(modps.ap[:, col:col + 2], wb.ap[:, kc, fc * 128:(fc + 1) * 128], sv_b.ap[:, kc, :], kc == 0, kc == DC - 1,
                       reads=[wb, sv_b], pwrites=[modps])
        mp3 = modps.ap[:, 0:12 * DC].rearrange("p (a b) -> p a b", b=2)
        for jj in range(2):
            tt(mod.ap[:, :, jj], mp3[:, :, jj], vec.ap[:, V_BMOD:V_BMOD + 6 * DC], ALU.add, reads=[modps, vec],
               writes=[mod] if jj == 0 else (), pwrites=[mod] if jj else ())
        for (Ab, voff, m) in ((A1, V_NMIX, 1), (A2, V_NFFN, 4)):
            for jj in range(2):
                stt(Ab.ap[:, :, jj], mod.ap[:, m * DC:(m + 1) * DC, jj], 1.0, vec.ap[:, voff:voff + DC], ALU.add, ALU.mult,
                    reads=[mod, vec], writes=[Ab] if jj == 0 else (), pwrites=[Ab] if jj else ())
        if is_mla:
            sc = 1.0 / math.sqrt(192.0)
            glist = [(0, V_GQN, sc), (1, V_GQP, sc), (2, V_GKP, 1.0), (3, V_GKN, 1.0)]
        else:
            sc = 1.0 / math.sqrt(128.0)
            glist = [(0, V_GQ, sc), (1, V_GK, 1.0)]
        for gi, (gidx, vo, mul) in enumerate(glist):
            tsm(gsc.ap[:, gidx:gidx + 1], vec.ap[:, vo:vo + 1], mul, reads=[vec], writes=[gsc] if gi == 0 else (), pwrites=[gsc] if gi else ())
        P.barrier()

        phase_reset(PERSIST)
        xt = [carve([DC, W1], F32) for _ in range(1)]
        hT = [carve([DC, W1], BF16, False) for _ in range(2)]
        sq = [carve([W1], BF16, False) for _ in range(2)]
        rs_sq = carve([W1], F32, False); rs = carve([W1], F32, False)
        tmpf = [carve([W1], F32, False) for _ in range(2)]
        cs = [carve([W1], F32) for _ in range(2)]; sn = [carve([W1], F32) for _ in range(2)]
        NSET = 3
        nsets = [dict(sq=carve([W1], BF16, False), rs_sq=carve([W1], F32, False), rs=carve([W1], F32, False),
                      qn=carve([W1], BF16, False), t1=carve([W1], F32, False), t2=carve([W1], F32, False)) for _ in range(NSET)]
        stg = [carve([W1], BF16) for _ in range(3)]
        vstg = [carve([512], BF16) for _ in range(2)]
        ws = [carve([max(DC, QC, KC2), 512], BF16) for _ in range(NWS)]
        if is_mla:
            cq_f = carve([max(QC, KC2), W1], F32, False); cqn = carve([QC, W1], BF16, False); ckvn = carve([KC2, W1], BF16, False)
        cnt = dict(ws=0, stg=0, vstg=0, pi=0)

        def wload(ring, src3, kc, m, split=1):
            b = ring[cnt["ws"] % len(ring)]; cnt["ws"] += 1
            view = b.ap.rearrange("p a b -> p (a b)")[:, :kc * m].rearrange("p (a b) -> p a b", a=kc)
            step = (kc + split - 1) // split
            for i, k0 in enumerate(range(0, kc, step)):
                k1 = min(kc, k0 + step)
                dma("pool", view[:, k0:k1, :], src3[:, k0:k1, :], b, writes=[b] if i == 0 else (), pwrites=[b] if i else ())
            return b, view

        def wload4(ring, src4, kc, nh):
            b = ring[cnt["ws"] % len(ring)]; cnt["ws"] += 1
            view = b.ap.rearrange("p a b -> p (a b)")[:, :kc * nh * 128].rearrange("p (a h e) -> p a h e", a=kc, h=nh)
            for hq in range(nh):
                dma("pool", view[:, :, hq, :], src4[:, :, hq, :], b, writes=[b] if hq == 0 else (), pwrites=[b] if hq else ())
            return b, view

        def wload2(ring, srcA, srcB, kc):
            b = ring[cnt["ws"] % len(ring)]; cnt["ws"] += 1
            view = b.ap.rearrange("p a b -> p (a b)")[:, :kc * 128].rearrange("p (a b) -> p a b", a=kc)
            dma("pool", view[:, :, 0:64], srcA, b, writes=[b])
            dma("pool", view[:, :, 64:128], srcB, b, pwrites=[b])
            return b, view

        def modulate(xb, n, Ab, shm, jj, hb, sq, rs_sq, rs, tmpf):
            ssps = psum[1]
            for c in range(DC):
                s_ = sq[c % 2]
                act(s_.ap[:, :n], xb.ap[:, c, :n], AF.Square, reads=[xb], writes=[s_])
                mm(ssps.ap[:, :n], ones_b.ap, s_.ap[:, :n], c == 0, c == DC - 1, reads=[s_, ones_b],
                   writes=[ssps] if c == 0 else (), pwrites=() if c == 0 else [ssps])
            rstd_from_ss(ssps, n, float(D), rs_sq, rs)
            hb.new_gen()
            for c in range(DC):
                tf = tmpf[c % 2]
                stt(tf.ap[:, :n], xb.ap[:, c, :n], Ab.ap[:, c, jj:jj + 1], rs.ap[:, :n], ALU.mult, ALU.mult, reads=[xb, Ab, rs], writes=[tf])
                act(hb.ap[:, c, :n], tf.ap[:, :n], AF.Identity, reads=[tf, mod], pwrites=[hb], bias=modcol(shm, c, jj), scale=1.0)

        def run_jobs(jobs, n, cb, sb_):
            N = len(jobs)
            pbuf = [psum[3], psum[4], psum[5]]
            st_of = {}

            def A(i):
                J = jobs[i]
                wb, wv_ = J["w"]()
                ps = pbuf[i % 3]
                proj(ps, wv_, wb, J["src"], J["kcs"], n)
                S_ = nsets[i % NSET]
                act(S_["sq"].ap[:, :n], ps.ap[:, :n], AF.Square, reads=[ps], writes=[S_["sq"]])

            def B(i):
                J = jobs[i]
                ps = pbuf[i % 3]
                S_ = nsets[i % NSET]
                ssps = psum[1 + i % 2]
                mm(ssps.ap[:, :n], J["ones"].ap, S_["sq"].ap[:, :n], True, True, reads=[S_["sq"], J["ones"]], writes=[ssps])
                rstd_from_ss(ssps, n, float(J["nd"]), S_["rs_sq"], S_["rs"])
                if not J["rope"]:
                    st = stg[cnt["stg"] % 3]; cnt["stg"] += 1
                    stt(st.ap[:, :n], ps.ap[:, :n], J["gcol"], S_["rs"].ap[:, :n], ALU.mult, ALU.mult, reads=[ps, S_["rs"], gsc], writes=[st])
                    dma("sp", J["dst"], st.ap[:, :n], st, reads=[st], pwrites=[J["Ddst"]])
                else:
                    stt(S_["qn"].ap[:, :n], ps.ap[:, :n], J["gcol"], S_["rs"].ap[:, :n], ALU.mult, ALU.mult, reads=[ps, S_["rs"], gsc], writes=[S_["qn"]])

            def C(i):
                J = jobs[i]
                if not J["rope"]:
                    return
                S_ = nsets[i % NSET]
                qn = S_["qn"]; t1 = S_["t1"]; t2 = S_["t2"]
                rps = psum[7] if i % 2 else psum[0]
                mm(rps.ap[:, :n], J["R"].ap, qn.ap[:, :n], True, True, reads=[J["R"], qn], writes=[rps])
                tt(t1.ap[:, :n], qn.ap[:, :n], cb.ap[:, :n], ALU.mult, reads=[qn, cb], writes=[t1])
                tt(t2.ap[:, :n], rps.ap[:, :n], sb_.ap[:, :n], ALU.mult, reads=[rps, sb_], writes=[t2])
                st = stg[cnt["stg"] % 3]; cnt["stg"] += 1
                tt(st.ap[:, :n], t1.ap[:, :n], t2.ap[:, :n], ALU.add, reads=[t1, t2], writes=[st])
                dma("sp", J["dst"], st.ap[:, :n], st, reads=[st], pwrites=[J["Ddst"]])

            for i in range(N + 2):
                if i < N:
                    A(i)
                if 0 <= i - 1 < N:
                    B(i - 1)
                if 0 <= i - 2 < N:
                    C(i - 2)

        def job(wfn, src, kcs, nd, onesb, gcol, rope, Rb, dst_ap, Ddst):
            return dict(w=wfn, src=src, kcs=kcs, nd=nd, ones=onesb, gcol=gcol, rope=rope, R=Rb, dst=dst_ap, Ddst=Ddst)

        def shared_w(loader):
            box = []

            def get():
                if not box:
                    box.append(loader())
                return box[0]
            return get

        def proj(ps, wview, wb, src, kcs, n):
            for kc in range(kcs):
                mm(ps.ap[:, :n], wview[:, kc, :], src.ap[:, kc, :n], kc == 0, kc == kcs - 1, reads=[wb, src],
                   writes=[ps] if kc == 0 else (), pwrites=() if kc == 0 else [ps])

        def nextps():
            p_ = psum[3 + cnt["pi"] % 2]; cnt["pi"] += 1
            return p_

        def vproj(src, kcs, n, wbs, dst_fn, Ddst):
            ncol = len(wbs) * 128
            for s0 in range(0, n, 128):
                vps = psum[6]
                vps.new_gen()
                for qi, (wb, wv_) in enumerate(wbs):
                    for kc in range(kcs):
                        mm(vps.ap[:, qi * 128:(qi + 1) * 128], src.ap[:, kc, s0:s0 + 128], wv_[:, kc, :], kc == 0, kc == kcs - 1,
                           reads=[src, wb], pwrites=[vps])
                vs = vstg[cnt["vstg"] % 2]; cnt["vstg"] += 1
                act(vs.ap[:, :ncol], vps.ap[:, :ncol], AF.Copy, reads=[vps], writes=[vs])
                for (dap, coff, ncl) in dst_fn(s0):
                    dma("sp", dap, vs.ap[:, coff:coff + ncl], vs, reads=[vs], pwrites=[Ddst])

        for D_ in (Dq, Dqpe, Dksrc, Dvsrc, Dkpesrc, Dkctx, Dvctx, Dkpectx):
            D_.new_gen()
        tiles = lat_tiles + ctx_tiles
        def p1_tile(ti, kind, t0, n, do_q, do_kv):
            jj = 0 if kind == "lat" else 1
            tokq = t0 if kind == "lat" else TL + t0
            xb = xt[0]; hb = hT[ti % 2]
            src_d, Dsrc = xsrc(kind, l)
            dma("sp", xb.ap[:, :, :n], fm3(src_d, t0, n), xb, reads=[Dsrc], writes=[xb])
            modulate(xb, n, A1, 0, jj, hb, sq, rs_sq, rs, tmpf)
            rope = (kind == "lat")
            need_q = not (kind == "ctx" and last)
            cb = sb_ = None
            if rope:
                cb = cs[ti % 2]; sb_ = sn[ti % 2]
                dma("sp", cb.ap[:, :n], (cosm if is_mla else cosg)[:, t0:t0 + n], cb, writes=[cb])
                dma("sp", sb_.ap[:, :n], (sinm if is_mla else sing)[:, t0:t0 + n], sb_, writes=[sb_])
            kdst = (lambda r0, r1: ksrc_rows(r0 // 128)[:, t0:t0 + n]) if kind == "lat" else (lambda r0, r1: kT_ctx[r0:r1, t0:t0 + n])
            Dk = Dksrc if kind == "lat" else Dkctx
            Dv = Dvsrc if kind == "lat" else Dvctx
            if not is_mla:
                wq = gqa_w_q[j].rearrange("(kc p) m -> p kc m", p=128)
                wkv = gqa_w_kv[j].rearrange("(kc p) m -> p kc m", p=128)
                jobs = []
                if need_q:
                    for h0 in range(0, GH, 4):
                        nh = min(4, GH - h0)
                        sw = shared_w(lambda h0=h0, nh=nh: wload(ws, wq[:, :, h0 * 128:(h0 + nh) * 128], DC, nh * 128))
                        for hq in range(nh):
                            h = h0 + hq
                            jobs.append(job((lambda sw=sw, hq=hq: (sw()[0], sw()[1][:, :, hq * 128:(hq + 1) * 128])), hb, DC, 128, ones_b,
                                            gsc.ap[:, 0:1], rope, Rg_b, qT_d[h * 128:(h + 1) * 128, tokq:tokq + n], Dq))
                for g0 in range(0, GKV, 4):
                    ng = min(4, GKV - g0)
                    sw = shared_w(lambda g0=g0, ng=ng: wload(ws, wkv[:, :, g0 * 128:(g0 + ng) * 128], DC, ng * 128))
                    for gq in range(ng):
                        g = g0 + gq
                        jobs.append(job((lambda sw=sw, gq=gq: (sw()[0], sw()[1][:, :, gq * 128:(gq + 1) * 128])), hb, DC, 128, ones_b,
                                        gsc.ap[:, 1:2], rope, Rg_b, kdst(g * 128, (g + 1) * 128), Dk))
                run_jobs(jobs, n, cb, sb_)
                for c0 in range(0, GKV, 4):
                    ng = min(4, GKV - c0)
                    wb, wv_ = wload(ws, wkv[:, :, (GKV + c0) * 128:(GKV + c0 + ng) * 128], DC, ng * 128)
                    wbs = [(wb, wv_[:, :, q4 * 128:(q4 + 1) * 128]) for q4 in range(ng)]
                    if kind == "lat":
                        vproj(hb, DC, n, wbs, (lambda s0, c0=c0, ng=ng: vsrc_cols(t0 + s0, 128, c0, ng)), Dv)
                    else:
                        vproj(hb, DC, n, wbs, (lambda s0, c0=c0, ng=ng: [(v_ctx[t0 + s0:t0 + s0 + 128, c0 * 128:(c0 + ng) * 128], 0, ng * 128)]), Dv)
            else:
                wdq = mla_w_dq[j].rearrange("(kc p) m -> p kc m", p=128)
                wuq = mla_w_uq[j].rearrange("(kc p) m -> p kc m", p=128)
                wdkv = mla_w_dkv[j].rearrange("(kc p) m -> p kc m", p=128)
                wukv = mla_w_ukv[j].rearrange("(kc p) m -> p kc m", p=128)

                def compress(wsrc, nchunk, gv_off, outn):
                    ssps = psum[1]
                    outn.new_gen(); cq_f.new_gen()
                    for oc in range(nchunk):
                        if oc % 4 == 0:
                            nq = min(4, nchunk - oc)
                            wb, wvw = wload(ws, wsrc[:, :, oc * 128:(oc + nq) * 128], DC, nq * 128)
                        wv_ = wvw[:, :, (oc % 4) * 128:(oc % 4 + 1) * 128]
                        ps = nextps()
                        proj(ps, wv_, wb, hb, DC, n)
                        act(cq_f.ap[:, oc, :n], ps.ap[:, :n], AF.Copy, reads=[ps], pwrites=[cq_f])
                        s_ = sq[oc % 2]
                        act(s_.ap[:, :n], ps.ap[:, :n], AF.Square, reads=[ps], writes=[s_])
                        mm(ssps.ap[:, :n], ones_b.ap, s_.ap[:, :n], oc == 0, oc == nchunk - 1, reads=[s_, ones_b],
                           writes=[ssps] if oc == 0 else (), pwrites=() if oc == 0 else [ssps])
                    rstd_from_ss(ssps, n, float(nchunk * 128), rs_sq, rs)
                    for oc in range(nchunk):
                        stt(outn.ap[:, oc, :n], cq_f.ap[:, oc, :n], vcol(gv_off + oc), rs.ap[:, :n], ALU.mult, ALU.mult,
                            reads=[cq_f, rs, vec], pwrites=[outn])

                if need_q and do_q:
                    compress(wdq, QC, V_GDQ, cqn)
                    wuq4 = wuq.rearrange("p k (h e) -> p k h e", e=192)
                    jobs = []
                    for h0 in range(0, MH, 4):
                        nq = min(4, MH - h0)
                        sw = shared_w(lambda h0=h0, nq=nq: wload4(ws, wuq4[:, :, h0:h0 + nq, 0:128], QC, nq))
                        for hq in range(nq):
                            h = h0 + hq
                            jobs.append(job((lambda sw=sw, hq=hq: (sw()[0], sw()[1][:, :, hq, :])), cqn, QC, 128, ones_b,
                                            gsc.ap[:, 0:1], False, None, qT_d[h * 128:(h + 1) * 128, tokq:tokq + n], Dq))
                    for hp in range(MH // 2):
                        h0 = 2 * hp
                        jobs.append(job((lambda h0=h0: wload2(ws, wuq[:, :, h0 * 192 + 128:h0 * 192 + 192], wuq[:, :, (h0 + 1) * 192 + 128:(h0 + 1) * 192 + 192], QC)),
                                        cqn, QC, 64, bones_b, gsc.ap[:, 1:2], rope, Rm_b, qpeT_d[hp * 128:(hp + 1) * 128, tokq:tokq + n], Dqpe))
                    run_jobs(jobs, n, cb, sb_)
                if not do_kv:
                    return
                compress(wdkv, KC2, V_GDKV, ckvn)
                jobs = []
                jobs.append(job((lambda: wload2(ws, wdkv[:, :, KVR:KVR + 64], wdkv[:, :, KVR:KVR + 64], DC)), hb, DC, 64, bones_b, gsc.ap[:, 2:3],
                                rope, Rm_b, (kpe_src[:, t0:t0 + n] if kind == "lat" else kpe_ctx[:, t0:t0 + n]), (Dkpesrc if kind == "lat" else Dkpectx)))
                wukv4 = wukv.rearrange("p k (h e) -> p k h e", e=256)
                for h0 in range(0, MH, 4):
                    nq = min(4, MH - h0)
                    sw = shared_w(lambda h0=h0, nq=nq: wload4(ws, wukv4[:, :, h0:h0 + nq, 0:128], KC2, nq))
                    for hq in range(nq):
                        h = h0 + hq
                        jobs.append(job((lambda sw=sw, hq=hq: (sw()[0], sw()[1][:, :, hq, :])), ckvn, KC2, 128, ones_b,
                                        gsc.ap[:, 3:4], False, None, kdst(h * 128, (h + 1) * 128), Dk))
                run_jobs(jobs, n, cb, sb_)
                for h0 in range(0, MH, 4):
                    nh = min(4, MH - h0)
                    wb, wvw = wload4(ws, wukv4[:, :, h0:h0 + nh, 128:256], KC2, nh)
                    wbs = [(wb, wvw[:, :, q4, :]) for q4 in range(nh)]
                    if kind == "lat":
                        vproj(ckvn, KC2, n, wbs, (lambda s0, h0=h0, nh=nh: vsrc_cols(t0 + s0, 128, h0, nh)), Dv)
                    else:
                        vproj(ckvn, KC2, n, wbs, (lambda s0, h0=h0, nh=nh: [(v_ctx[t0 + s0:t0 + s0 + 128, h0 * 128:(h0 + nh) * 128], 0, nh * 128)]), Dv)
        def emit_ag():
            Dkall.new_gen(); Dvall.new_gen()
            for bi in range(NB):
                P.cc("AllGather", groups, kT_src[bi][:, :], kT_all[bi][:, :], reads=[Dksrc], pwrites=[Dkall])
                P.cc("AllGather", groups, v_src[bi][:, :], v_all[bi][:, :], reads=[Dvsrc], pwrites=[Dvall])
            if is_mla:
                P.cc("AllGather", groups, kpe_src[:, :], kpe_all[:, :], reads=[Dkpesrc], writes=[Dkpeall])

        if is_mla:
            for ti, (kind, t0, n) in enumerate(tiles):
                p1_tile(ti, kind, t0, n, False, True)
            emit_ag()
            for ti, (kind, t0, n) in enumerate(tiles):
                if not (kind == "ctx" and last):
                    p1_tile(ti + len(tiles), kind, t0, n, True, False)
        else:
            for ti, (kind, t0, n) in enumerate(tiles):
                p1_tile(ti, kind, t0, n, True, True)
            emit_ag()
        P.barrier()

        phase_reset(PERSIST)
        oT = carve([NH, NTOK], BF16, False)
        P3BASE = apos[0]
        KA = [carve([T], BF16) for _ in range(2)]
        VV = [carve([NCH, 128], BF16) for _ in range(2)]
        KBm = [carve([T], BF16) for _ in range(2)] if is_mla else None
        QA = [carve([W1], BF16) for _ in range(2)]
        QB = [carve([W1], BF16) for _ in range(2)] if is_mla else None
        pT = [carve([W1], BF16, False) for _ in range(4)]
        rec = carve([W1], F32, False)
        accD = [carve([W1], F32, False) for _ in range(2)]
        if is_mla:
            kpa = kpe_all.rearrange("(r d) t -> d r t", d=128)
            for hf in range(2):
                lo, hi = hf * 64, hf * 64 + 64
                zl, zh = (64, 128) if hf == 0 else (0, 64)
                mset(KBm[hf].ap[zl:zh, :], 0.0, writes=[KBm[hf]])
                dma("sp", KBm[hf].ap[lo:hi, 0:NR * TL].rearrange("p (r t) -> p r t", r=NR), kpa[lo:hi], KBm[hf], reads=[Dkpeall], pwrites=[KBm[hf]])
                dma("sp", KBm[hf].ap[lo:hi, NR * TL:T], kpe_ctx[lo:hi, :], KBm[hf], reads=[Dkpectx], pwrites=[KBm[hf]])
        wup = ffn_w_up[l].rearrange("(kc p) m -> p kc m", p=128)
        wdn = ffn_w_down[l].rearrange("(kc p) m -> p kc m", p=128)
        Dwc.new_gen()
        ncv = 0
        for g in range(NUT):
            nq = min(2, FC - 2 * g)
            for part in range(2):
                ln = cvl[ncv % len(cvl)]; ncv += 1
                dma("pool", wupT[2 * g + part][:, :DC * nq * 128].rearrange("p (kc m) -> p kc m", kc=DC),
                    wup[:, :, part * FF + g * 256:part * FF + g * 256 + nq * 128], ln, writes=[ln], pwrites=[Dwc])
        for g in range(NDT):
            nq = min(2, DC - 2 * g)
            ln = cvl[ncv % len(cvl)]; ncv += 1
            dma("pool", wdnT[g][:, :FC * nq * 128].rearrange("p (kc m) -> p kc m", kc=FC),
                wdn[:, :, g * 256:g * 256 + nq * 128], ln, writes=[ln], pwrites=[Dwc])
        qtiles = lat_tiles + ([] if last else ctx_tiles)
        oT.new_gen()
        it = 0
        kall4 = [a.rearrange("(r g d) t -> d g r t", r=NR, g=HB) for a in kT_all]
        vall4 = [a.rearrange("(ch p) (g d) -> p ch g d", p=128, d=128) for a in v_all]
        vctx4 = v_ctx.rearrange("(ch p) (g d) -> p ch g d", p=128, d=128)
        for g in range(NKV):
            ka = KA[g % 2]; vv = VV[g % 2]
            dma("sp", ka.ap[:, 0:NR * TL].rearrange("p (r t) -> p r t", r=NR), kall4[g // HB][:, g % HB], ka, reads=[Dkall], writes=[ka])
            dma("sp", ka.ap[:, NR * TL:T], kT_ctx[g * 128:(g + 1) * 128, :], ka, reads=[Dkctx], pwrites=[ka])
            dma("sp", vv.ap[:, 0:NR * TL // 128, :], vall4[g // HB][:, :, g % HB, :], vv, reads=[Dvall], writes=[vv])
            dma("sp", vv.ap[:, NR * TL // 128:NCH, :], vctx4[:, :, g, :], vv, reads=[Dvctx], pwrites=[vv])
            for hh in range(GRP):
                h = g * GRP + hh
                hp = (h % 2) * 64
                for (kind, t0, n) in qtiles:
                    tokq = t0 if kind == "lat" else TL + t0
                    qa = QA[it % 2]; qb = QB[it % 2] if is_mla else None
                    ops_ = psum[3 + it % 2]; sums = psum[5 + it % 2]
                    it += 1
                    dma("sp", qa.ap[:, :n], qT_d[h * 128:(h + 1) * 128, tokq:tokq + n], qa, reads=[Dq], writes=[qa])
                    if is_mla:
                        dma("sp", qb.ap[:, :n], qpeT_d[(h // 2) * 128:(h // 2 + 1) * 128, tokq:tokq + n], qb, reads=[Dqpe], writes=[qb])
                    chunks = list(range(NCH)) if kind == "lat" else list(range(NR * TL // 128, NCH))
                    nchk = len(chunks)

                    aD = accD[it % 2]
                    kbm = KBm[h % 2] if is_mla else None
                    pe_chunks = [ci for ci in range(nchk) if ci % 4 == 3]
                    dve_chunks = [ci for ci in range(nchk) if ci not in pe_chunks]

                    def emit_S(ci):
                        ch = chunks[ci]
                        sp_ = psum[ci % 3]
                        mm(sp_.ap[:, :n], ka.ap[:, ch * 128:(ch + 1) * 128], qa.ap[:, :n], True, not is_mla, reads=[ka, qa], writes=[sp_])
                        if is_mla:
                            mm(sp_.ap[:, :n], kbm.ap[:, ch * 128:(ch + 1) * 128], qb.ap[:, :n], False, True, reads=[kbm, qb], pwrites=[sp_])
                        pt = pT[ci % 4]
                        act(pt.ap[:, :n], sp_.ap[:, :n], AF.Exp, reads=[sp_], writes=[pt])

                    def emit_PV(ci):
                        ch = chunks[ci]
                        pt = pT[ci % 4]
                        mm(ops_.ap[:, :n], vv.ap[:, ch, :], pt.ap[:, :n], ci == 0, ci == nchk - 1, reads=[vv, pt],
                           writes=[ops_] if ci == 0 else (), pwrites=() if ci == 0 else [ops_])
                        if ci in pe_chunks:
                            mm(sums.ap[:, :n], ones_b.ap, pt.ap[:, :n], ci == pe_chunks[0], False, reads=[ones_b, pt],
                               writes=[sums] if ci == pe_chunks[0] else (), pwrites=() if ci == pe_chunks[0] else [sums])
                        elif ci == dve_chunks[0]:
                            cpy(aD.ap[:, :n], pt.ap[:, :n], reads=[pt], writes=[aD])
                        else:
                            tt(aD.ap[:, :n], aD.ap[:, :n], pt.ap[:, :n], ALU.add, reads=[pt, aD], pwrites=[aD])

                    emit_S(0)
                    if nchk > 1:
                        emit_S(1)
                    for ci in range(nchk):
                        if ci + 2 < nchk:
                            emit_S(ci + 2)
                        emit_PV(ci)
                    mm(sums.ap[:, :n], ones_f.ap, aD.ap[:, :n], not pe_chunks, True, reads=[ones_f, aD],
                       writes=[sums] if not pe_chunks else (), pwrites=[sums] if pe_chunks else ())
                    recip(rec.ap[:, :n], sums.ap[:, :n], reads=[sums], writes=[rec])
                    tt(oT.ap[:, h, tokq:tokq + n], ops_.ap[:, :n], rec.ap[:, :n], ALU.mult, reads=[ops_, rec], pwrites=[oT])
        P.barrier()

        phase_reset(P3BASE)
        xt3 = [carve([DC, W1], F32) for _ in range(2)]
        h3 = [carve([DC, W1], BF16) for _ in range(1)]
        sq3 = [carve([W1], BF16, False) for _ in range(2)]
        rs_sq3 = carve([W1], F32, False); rs3 = carve([W1], F32, False)
        tmpf3 = [carve([W1], F32, False) for _ in range(2)]
        ws3 = [carve([NH, 256], BF16) for _ in range(NWS)]
        w_o = (mla_w_o if is_mla else gqa_w_o)[j].rearrange("(kc p) m -> p kc m", p=128)
        tiles3 = lat_tiles + ([] if last else ctx_tiles)
        Dh2.new_gen(); Dh2c.new_gen(); bnd.new_gen()
        for ti, (kind, t0, n) in enumerate(tiles3):
            jj = 0 if kind == "lat" else 1
            tokq = t0 if kind == "lat" else TL + t0
            xb = xt3[ti % 2]; hb = h3[0]
            src_d, Dsrc = xsrc(kind, l)
            dst_d = xT_s if kind == "lat" else ctxT_s
            dma("sp", xb.ap[:, :, :n], fm3(src_d, t0, n), xb, reads=[Dsrc], writes=[xb])
            for dc in range(DC):
                if dc % 2 == 0:
                    nq = min(2, DC - dc)
                    wb, wvw = wload(ws3, w_o[:, :, dc * 128:(dc + nq) * 128], NH, nq * 128)
                wv_ = wvw[:, :, (dc % 2) * 128:(dc % 2 + 1) * 128]
                ps = nextps()
                for kc in range(NH):
                    mm(ps.ap[:, :n], wv_[:, kc, :], oT.ap[:, kc, tokq:tokq + n], kc == 0, kc == NH - 1, reads=[wb, oT],
                       writes=[ps] if kc == 0 else (), pwrites=() if kc == 0 else [ps])
                stt(xb.ap[:, dc, :n], ps.ap[:, :n], modcol(2, dc, jj), xb.ap[:, dc, :n], ALU.mult, ALU.add, reads=[ps, mod, xb], pwrites=[xb])
            dma("sp", fm3(dst_d, t0, n), xb.ap[:, :, :n], xb, reads=[xb], pwrites=[Dsrc])
            modulate(xb, n, A2, 3, jj, hb, sq3, rs_sq3, rs3, tmpf3)
            if kind == "lat":
                dma("sp", fm3(h2T_d, 1 + t0, n), hb.ap[:, :, :n], hb, reads=[hb], pwrites=[Dh2])
                if t0 == 0:
                    cpy(bnd.ap[:, :, 0], hb.ap[:, :, 0], reads=[hb], pwrites=[bnd])
                if t0 + n == TL:
                    cpy(bnd.ap[:, :, 1], hb.ap[:, :, n - 1], reads=[hb], pwrites=[bnd])
            else:
                dma("sp", fm3(h2cT_d, 1 + t0, n), hb.ap[:, :, :n], hb, reads=[hb], pwrites=[Dh2c])
        dma("sp", hsrc[:, :], bnd.ap.rearrange("p a b -> p (a b)"), bnd, reads=[bnd], writes=[Dhsrc])
        P.barrier()
        P.cc("AllGather", groups, hsrc[:, :], hall[:, :], reads=[Dhsrc], writes=[Dhall])
        dma("sp", hs.ap, hall.rearrange("(r p) (c t) -> p r c t", p=128, t=2), hs, reads=[Dhall], writes=[hs])
        for side in range(2):
            for r in range(NR):
                srcc = hs.ap[:, r, :, 1 - side]
                scol = sel_f.ap[:, side * NR + r:side * NR + r + 1]
                if r == 0:
                    tsm(halo_f.ap[:, :, side], srcc, scol, reads=[hs, sel_f], writes=[halo_f] if side == 0 else (), pwrites=() if side == 0 else [halo_f])
                else:
                    stt(halo_f.ap[:, :, side], srcc, scol, halo_f.ap[:, :, side], ALU.mult, ALU.add, reads=[hs, sel_f, halo_f], pwrites=[halo_f])
        cpy(halo.ap, halo_f.ap, reads=[halo_f], writes=[halo])
        P.barrier()

        phase_reset(PERSIST)
        WIN = W4 + 2
        h2w = [carve([DC, WIN], BF16) for _ in range(2)]
        aT = carve([FC, W4], BF16, False)
        xw = [carve([DC, W4], F32) for _ in range(1)]
        c1 = [carve([W4], F32, False) for _ in range(2)]; c2 = [carve([W4], F32, False) for _ in range(2)]
        c3 = [carve([W4], F32, False) for _ in range(2)]; sg = [carve([W4], F32, False) for _ in range(2)]
        wsu = [carve([DC, 256], BF16) for _ in range(cfg.NWU)]
        wsd = [carve([FC, 256], BF16) for _ in range(cfg.NWD)]
        def wload_bf(ring, src2, kc, m):
            b = ring[cnt["ws"] % len(ring)]; cnt["ws"] += 1
            v2 = b.ap.rearrange("p a b -> p (a b)")[:, :kc * m]
            dma("pool", v2, src2, b, reads=[Dwc], writes=[b])
            return b, v2.rearrange("p (a b) -> p a b", a=kc)

        wins = [("lat", s, min(W4, TL - s)) for s in range(0, TL, W4)]
        if not last:
            wins += [("ctx", s, min(W4, C - s)) for s in range(0, C, W4)]
        pi = 0
        def load_hw(wi):
            kind, s0, nout = wins[wi]
            nin = nout + 2
            hw = h2w[wi % 2]
            tot = TL if kind == "lat" else C
            if kind == "lat":
                dma("sp", hw.ap[:, :, :nin], fm3(h2T_d, s0, nin), hw, reads=[Dh2], writes=[hw])
            else:
                dma("sp", hw.ap[:, :, :nin], fm3(h2cT_d, s0, nin), hw, reads=[Dh2c], writes=[hw])
            if s0 == 0:
                if kind == "lat":
                    cpy(hw.ap[:, :, 0], halo.ap[:, :, 0], reads=[halo, hw], pwrites=[hw])
                else:
                    mset(hw.ap[:, :, 0], 0.0, reads=[hw], pwrites=[hw])
            if s0 + nout == tot:
                if kind == "lat":
                    cpy(hw.ap[:, :, nin - 1], halo.ap[:, :, 1], reads=[halo, hw], pwrites=[hw])
                else:
                    mset(hw.ap[:, :, nin - 1], 0.0, reads=[hw], pwrites=[hw])

        load_hw(0)
        for wi, (kind, s0, nout) in enumerate(wins):
            jj = 0 if kind == "lat" else 1
            nin = nout + 2
            hw = h2w[wi % 2]; xb = xw[0]
            src_d = xT_s if kind == "lat" else ctxT_s
            Dsrc = Dx if kind == "lat" else Dctx
            dma("sp", xb.ap[:, :, :nout], fm3(src_d, s0, nout), xb, reads=[Dsrc], writes=[xb])
            pre_wd = {}
            aT.new_gen()
            for fc in range(FC):
                if fc % 2 == 0:
                    nq = min(2, FC - fc)
                    wgb, wgw = wload_bf(wsu, wupT[fc + 0][:, :DC * nq * 128], DC, nq * 128)
                    wvb, wvw = wload_bf(wsu, wupT[fc + 1][:, :DC * nq * 128], DC, nq * 128)
                wgv = wgw[:, :, (fc % 2) * 128:(fc % 2 + 1) * 128]
                wvv = wvw[:, :, (fc % 2) * 128:(fc % 2 + 1) * 128]
                gps = psum[1 + pi % 2]; vps = psum[3 + pi % 2]; pi += 1
                for kc in range(DC):
                    mm(gps.ap[:, :nin], wgv[:, kc, :], hw.ap[:, kc, :nin], kc == 0, kc == DC - 1, reads=[wgb, hw],
                       writes=[gps] if kc == 0 else (), pwrites=() if kc == 0 else [gps])
                for kc in range(DC):
                    mm(vps.ap[:, :nin], wvv[:, kc, :], hw.ap[:, kc, :nin], kc == 0, kc == DC - 1, reads=[wvb, hw],
                       writes=[vps] if kc == 0 else (), pwrites=() if kc == 0 else [vps])
                a1 = c1[fc % 2]; a2 = c2[fc % 2]; a3 = c3[fc % 2]; sgb = sg[fc % 2]
                act(a1.ap[:, :nout], gps.ap[:, 1:1 + nout], AF.Identity, reads=[gps, vec], writes=[a1], bias=vcol(V_CB + fc), scale=vcol(V_CW + FC + fc))
                stt(a2.ap[:, :nout], gps.ap[:, 0:nout], vcol(V_CW + fc), a1.ap[:, :nout], ALU.mult, ALU.add, reads=[gps, vec, a1], writes=[a2])
                stt(a3.ap[:, :nout], gps.ap[:, 2:2 + nout], vcol(V_CW + 2 * FC + fc), a2.ap[:, :nout], ALU.mult, ALU.add, reads=[gps, vec, a2], writes=[a3])
                act(sgb.ap[:, :nout], a3.ap[:, :nout], AF.Silu, reads=[a3], writes=[sgb])
                tt(aT.ap[:, fc, :nout], sgb.ap[:, :nout], vps.ap[:, 1:1 + nout], ALU.mult, reads=[sgb, vps], pwrites=[aT])
                if fc == FC // 2:
                    for g_ in range(min(len(wsd), NDT)):
                        nq_ = min(2, DC - 2 * g_)
                        pre_wd[g_] = wload_bf(wsd, wdnT[g_][:, :FC * nq_ * 128], FC, nq_ * 128)
            if wi + 1 < len(wins):
                load_hw(wi + 1)
            for dc in range(DC):
                if dc % 2 == 0:
                    nq = min(2, DC - dc)
                    if dc // 2 in pre_wd:
                        wb, wdw = pre_wd[dc // 2]
                    else:
                        wb, wdw = wload_bf(wsd, wdnT[dc // 2][:, :FC * nq * 128], FC, nq * 128)
                wv_ = wdw[:, :, (dc % 2) * 128:(dc % 2 + 1) * 128]
                ps = psum[5 + pi % 2]; pi += 1
                for kc in range(FC):
                    mm(ps.ap[:, :nout], wv_[:, kc, :], aT.ap[:, kc, :nout], kc == 0, kc == FC - 1, reads=[wb, aT],
                       writes=[ps] if kc == 0 else (), pwrites=() if kc == 0 else [ps])
                stt(xb.ap[:, dc, :nout], ps.ap[:, :nout], modcol(5, dc, jj), xb.ap[:, dc, :nout], ALU.mult, ALU.add, reads=[ps, mod, xb], pwrites=[xb])
            if kind == "lat" and last:
                dma("sp", fm3(yT, s0, nout), xb.ap[:, :, :nout], xb, reads=[xb], pwrites=[Dy])
            else:
                dst_d = xT_s if kind == "lat" else ctxT_s
                dma("sp", fm3(dst_d, s0, nout), xb.ap[:, :, :nout], xb, reads=[xb], pwrites=[Dsrc])
        P.barrier()

    P.emit("sp", lambda e: e.nop(), reads=[Dy])
    P.finalize()
    global LASTP
    LASTP = P
    return nc


def pack_vecs(cfg, inp, l):
    DC, FC = cfg.DC, cfg.FC
    j = l // 2
    cols = [_fm(inp["norm_mix"][l]), _fm(inp["norm_ffn"][l]), _fm(inp["b_mod"][l])]
    cw = np.asarray(inp["ffn_conv_w"][l], np.float32)
    cols += [_fm(cw[0]), _fm(cw[1]), _fm(cw[2]), _fm(inp["ffn_conv_b"][l])]
    QC = cfg.QR // 128; KC2 = cfg.KVR // 128
    z1 = np.zeros((128, 1), np.float32)
    if l % 2 == 0:
        dup = lambda v: np.concatenate([np.asarray(v, np.float32)] * 2)[:, None]
        cols += [_fm(inp["mla_g_dq"][j]), _fm(inp["mla_g_dkv"][j]), np.asarray(inp["mla_g_q_nope"][j], np.float32)[:, None],
                 dup(inp["mla_g_q_pe"][j]), dup(inp["mla_g_k_pe"][j]), np.asarray(inp["mla_g_k_nope"][j], np.float32)[:, None], z1, z1]
    else:
        cols += [np.zeros((128, QC), np.float32), np.zeros((128, KC2), np.float32), z1, z1, z1, z1,
                 np.asarray(inp["gqa_g_q"][j], np.float32)[:, None], np.asarray(inp["gqa_g_k"][j], np.float32)[:, None]]
    return np.concatenate(cols, axis=1).astype(np.float32)


def prep_inputs(cfg, inp):
    NR, TL, D, C = cfg.NR, cfg.TL, cfg.D, cfg.C
    L = cfg.depth
    f = lambda a: np.ascontiguousarray(np.asarray(a, np.float32))
    vecs = np.stack([pack_vecs(cfg, inp, l) for l in range(L)])
    RgT = rot_matrix_T(128)
    Rm = rot_matrix_T(64)
    RmT = np.zeros((128, 128), np.float32); RmT[:64, :64] = Rm; RmT[64:, 64:] = Rm
    bones = np.zeros((128, 128), np.float32); bones[:64, :64] = 1; bones[64:, 64:] = 1
    shared = dict(RgT=RgT, RmT=RmT, bones=bones, vecs=vecs, w_mod=f(inp["w_mod"]),
                  mla_w_dq=f(inp["mla_w_dq"]), mla_w_uq=f(inp["mla_w_uq"]), mla_w_dkv=f(inp["mla_w_dkv"]),
                  mla_w_ukv=f(inp["mla_w_ukv"]), mla_w_o=f(inp["mla_w_o"]),
                  gqa_w_q=f(inp["gqa_w_q"]), gqa_w_kv=f(inp["gqa_w_kv"]), gqa_w_o=f(inp["gqa_w_o"]),
                  ffn_w_up=f(inp["ffn_w_up"]), ffn_w_down=f(inp["ffn_w_down"]))
    x = np.asarray(inp["x"], np.float32); ctx = np.asarray(inp["ctx"], np.float32)
    c = np.asarray(inp["c"], np.float32); c_ctx = np.asarray(inp["c_ctx"], np.float32)
    maps = []
    for core in range(8):
        b = core // NR; r = core % NR
        t0 = r * TL
        m = dict(shared)
        m["xT"] = np.ascontiguousarray(x[b, t0:t0 + TL].T)
        m["ctxT"] = np.ascontiguousarray(ctx[b].T)
        cv = np.stack([_fm(c[b]), _fm(c_ctx)], axis=-1)
        m["cvec"] = np.ascontiguousarray(cv.reshape(128, -1))
        cg, sg = rope_tables(cfg, t0, TL, 128)
        cm, sm = rope_tables(cfg, t0, TL, 64)
        m["cosg"] = cg; m["sing"] = sg
        m["cosm"] = np.concatenate([cm, cm], 0); m["sinm"] = np.concatenate([sm, sm], 0)
        s = np.zeros((128, 2 * NR), np.float32)
        if r > 0:
            s[:, r - 1] = 1.0
        if r < NR - 1:
            s[:, NR + r + 1] = 1.0
        m["sel"] = s
        maps.append(m)
    return maps


_NC_CACHE = {}


def run(cfg, inp, debug=False):
    key = (cfg.D, cfg.S, cfg.C, cfg.FF, cfg.depth)
    if key not in _NC_CACHE:
        _NC_CACHE[key] = build(cfg, debug)
    nc = _NC_CACHE[key]
    maps = prep_inputs(cfg, inp)
    res = run_bass_kernel_spmd(nc, maps, core_ids=list(range(8)))
    out = np.zeros((cfg.B, cfg.S, cfg.D), np.float32)
    for core in range(8):
        b = core // cfg.NR; r = core % cfg.NR
        out[b, r * cfg.TL:(r + 1) * cfg.TL] = res.results[core]["yT"].T
    return out


def kernel(**inputs):
    return run(Cfg(), inputs)
```

```python
import math
import numpy as np
import concourse.bass as bass
import concourse.mybir as mybir
from concourse.bass_utils import run_bass_kernel_spmd

F32 = mybir.dt.float32
BF16 = mybir.dt.bfloat16
AF = mybir.ActivationFunctionType
ALU = mybir.AluOpType
EPS = 1e-6
ROPE_BASE = 10000.0


class Cfg:
    def __init__(s, **kw):
        s.D = 2048; s.S = 8192; s.B = 2; s.C = 256; s.FF = 5632; s.depth = 4; s.GRID_W = 64
        s.MH = 16; s.QR = 512; s.KVR = 512; s.GH = 16; s.GKV = 4
        s.W1 = 512; s.W4 = 410; s.NR = 4
        s.NWS = 3; s.NWU = 4; s.NWD = 2; s.WDSPLIT = 1
        for k, v in kw.items():
            setattr(s, k, v)
        s.TL = s.S // s.NR
        s.DC = s.D // 128; s.FC = s.FF // 128
        s.NTOK = s.TL + s.C


class Op:
    __slots__ = ("eng", "fn", "deps", "marked", "cum", "is_dma", "sem", "val", "inc", "idx", "epoch")


class Buf:
    def __init__(s, ap=None, sem=None):
        s.ap = ap; s.w = []; s.r = []; s.gen = []; s.sem = sem; s.cnt = 0

    def new_gen(s):
        s.gen = s.w + s.r; s.w = []; s.r = []


class Prog:
    def __init__(s, nc):
        s.nc = nc
        s.ops = {k: [] for k in ("pe", "act", "dve", "pool", "sp")}
        s.engsem = {}
        s.epoch = 0; s.nidx = 0
        s.ccsem = nc.alloc_semaphore("ccsem"); s.cccnt = 0

    def new_epoch(s):
        s.epoch += 1

    def esem(s, eng, epoch):
        k = (eng, epoch)
        if k not in s.engsem:
            s.engsem[k] = s.nc.alloc_semaphore("es_%s_%d" % (eng, epoch))
        return s.engsem[k]

    def emit(s, eng, fn, reads=(), writes=(), pwrites=(), extra=()):
        deps = list(extra)
        for b in reads:
            deps += b.w
        for b in writes:
            b.new_gen(); deps += b.gen
        for b in pwrites:
            deps += b.gen
        best = {}
        for d in deps:
            if d.is_dma:
                k = ("s", d.sem.num); v = d.val
            else:
                k = ("e", d.eng); v = d.idx
            o = best.get(k)
            if o is None or v > o[0]:
                best[k] = (v, d)
        deps = [o[1] for o in best.values()]
        op = Op(); op.eng = eng; op.fn = fn; op.deps = deps; op.marked = False; op.cum = 0
        op.is_dma = False; op.sem = None; op.val = 0; op.inc = 0
        op.idx = s.nidx; s.nidx += 1; op.epoch = s.epoch
        for b in reads:
            b.r.append(op)
        for b in writes:
            b.w.append(op)
        for b in pwrites:
            b.w.append(op)
        s.ops[eng].append(op)
        return op

    def dma(s, eng, out, in_, sb, reads=(), writes=(), pwrites=(), extra=()):
        op = s.emit(eng, lambda e: e.dma_start(out=out, in_=in_), reads, writes, pwrites, extra)
        sb.cnt += 16
        op.is_dma = True; op.sem = sb.sem; op.val = sb.cnt; op.inc = 16
        return op

    def cc(s, kind, groups, in_ap, out_ap, reads=(), writes=(), pwrites=()):
        op = s.emit("pool", lambda e: e.collective_compute(kind, ALU.bypass, replica_groups=groups,
                                                            ins=[in_ap], outs=[out_ap]), reads, writes, pwrites)
        s.cccnt += 1
        op.is_dma = True; op.sem = s.ccsem; op.val = s.cccnt; op.inc = 1
        return op

    def barrier(s):
        deps = []
        latest = {}
        for eng, ops in s.ops.items():
            lastc = None
            for op in ops:
                if op.is_dma:
                    latest[op.sem.num] = op
                else:
                    lastc = op
            if lastc is not None:
                deps.append(lastc)
        deps += list(latest.values())
        for eng in s.ops:
            s.emit(eng, lambda e: e.nop(), extra=deps)

    def finalize(s):
        nc = s.nc
        for eng, ops in s.ops.items():
            for op in ops:
                for d in op.deps:
                    if not d.is_dma:
                        d.marked = True
        for eng, ops in s.ops.items():
            cnt = {}
            for op in ops:
                if (not op.is_dma) and op.marked:
                    cnt[op.epoch] = cnt.get(op.epoch, 0) + 1
                    s.esem(eng, op.epoch)
                op.cum = cnt.get(op.epoch, 0)

        def run(engname, e):
            waited = {}
            for op in s.ops[engname]:
                for d in op.deps:
                    if d.is_dma:
                        key = ("s", d.sem.num); val = d.val; sem = d.sem
                    else:
                        if d.eng == "pe" and engname == "pe":
                            continue
                        key = ("e", d.eng, d.epoch); val = d.cum; sem = s.engsem[(d.eng, d.epoch)]
                    if waited.get(key, 0) >= val:
                        continue
                    waited[key] = val
                    e.wait_ge(sem, val)
                ins = op.fn(e)
                if op.is_dma:
                    if op.inc == 16:
                        ins.then_inc(op.sem, 16)
                    else:
                        ins.then_inc(op.sem)
                elif op.marked:
                    ins.then_inc(s.engsem[(op.eng, op.epoch)], 1)

        with nc.Block() as block:
            @block.sync
            def _(e):
                run("sp", e)

            @block.gpsimd
            def _(e):
                run("pool", e)

            @block.scalar
            def _(e):
                run("act", e)

            @block.vector
            def _(e):
                run("dve", e)

            @block.tensor
            def _(e):
                run("pe", e)


def _fm(v):
    v = np.asarray(v, np.float32)
    return np.ascontiguousarray(v.reshape(-1, 128).T)


def rope_tables(cfg, tok0, n, rot_dim):
    axis_dim = rot_dim // 2
    inv = np.power(np.float32(ROPE_BASE), -np.arange(0, axis_dim, 2, dtype=np.float32) / np.float32(axis_dim)).astype(np.float32)
    t = np.arange(tok0, tok0 + n)
    rows = (t // cfg.GRID_W).astype(np.float32); cols = (t % cfg.GRID_W).astype(np.float32)
    ar = rows[:, None] * inv; ac = cols[:, None] * inv
    ang = np.concatenate([ar, ar, ac, ac], axis=-1).astype(np.float32)
    return np.cos(ang).T.astype(np.float32), np.sin(ang).T.astype(np.float32)


def rot_matrix_T(rot_dim):
    half = rot_dim // 2; q = half // 2
    R = np.zeros((rot_dim, rot_dim), np.float32)
    for base in (0, half):
        for i in range(q):
            R[base + i, base + i + q] = -1.0
            R[base + q + i, base + i] = 1.0
    return np.ascontiguousarray(R.T)


def build(cfg, debug=False):
    nc = bass.Bass("TRN2", target_bir_lowering=False)
    P = Prog(nc)
    D, TL, C, FF, DC, FC, NTOK, NR = cfg.D, cfg.TL, cfg.C, cfg.FF, cfg.DC, cfg.FC, cfg.NTOK, cfg.NR
    L = cfg.depth
    LA = (L + 1) // 2; LB = max(L // 2, 1)
    MH, QR, KVR, GH, GKV = cfg.MH, cfg.QR, cfg.KVR, cfg.GH, cfg.GKV
    QC = QR // 128; KC2 = KVR // 128
    T = NR * TL + C
    NCH = T // 128
    W1 = cfg.W1; W4 = cfg.W4

    def din(name, shape, dt=F32):
        return nc.dram_tensor(name, list(shape), dt, kind="ExternalInput").ap()

    def dscr(name, shape, dt):
        return nc.dram_tensor(name, list(shape), dt).ap()

    xT_in = din("xT", [D, TL]); ctxT_in = din("ctxT", [D, C]); cvec = din("cvec", [128, DC * 2])
    cosg = din("cosg", [128, TL]); sing = din("sing", [128, TL]); cosm = din("cosm", [128, TL]); sinm = din("sinm", [128, TL])
    RgT = din("RgT", [128, 128]); RmT = din("RmT", [128, 128]); bones = din("bones", [128, 128]); sel = din("sel", [128, 2 * NR])
    NV = 2 * DC + 6 * DC + 4 * FC + QC + KC2 + 6
    vecs = din("vecs", [L, 128, NV])
    w_mod = din("w_mod", [L, D, 6 * D])
    mla_w_dq = din("mla_w_dq", [LA, D, QR]); mla_w_uq = din("mla_w_uq", [LA, QR, MH * 192])
    mla_w_dkv = din("mla_w_dkv", [LA, D, KVR + 64]); mla_w_ukv = din("mla_w_ukv", [LA, KVR, MH * 256])
    mla_w_o = din("mla_w_o", [LA, MH * 128, D])
    gqa_w_q = din("gqa_w_q", [LB, D, GH * 128]); gqa_w_kv = din("gqa_w_kv", [LB, D, 2 * GKV * 128])
    gqa_w_o = din("gqa_w_o", [LB, GH * 128, D])
    ffn_w_up = din("ffn_w_up", [L, D, 2 * FF]); ffn_w_down = din("ffn_w_down", [L, FF, D])
    yT = nc.dram_tensor("yT", [D, TL], F32, kind="ExternalOutput").ap()

    xT_s = dscr("xT_s", [D, TL], F32); ctxT_s = dscr("ctxT_s", [D, C], F32)
    qT_d = dscr("qT_d", [max(MH, GH) * 128, NTOK], BF16); qpeT_d = dscr("qpeT_d", [MH // 2 * 128, NTOK], BF16)
    kvs = {}
    for nm, nkv in (("m", MH), ("g", GKV)):
        HB = 2 if nkv % 2 == 0 else 1
        while HB > 1 and HB * 128 * TL * 2 > (1 << 20):
            HB //= 2
        nb = nkv // HB
        kvs[nm] = dict(HB=HB, NB=nb,
                       kT_src=[dscr("kT_src%s%d" % (nm, i), [HB * 128, TL], BF16) for i in range(nb)],
                       kT_all=[dscr("kT_all%s%d" % (nm, i), [NR * HB * 128, TL], BF16) for i in range(nb)],
                       v_src=[dscr("v_src%s%d" % (nm, i), [TL, HB * 128], BF16) for i in range(nb)],
                       v_all=[dscr("v_all%s%d" % (nm, i), [NR * TL, HB * 128], BF16) for i in range(nb)],
                       kT_ctx=dscr("kT_ctx" + nm, [nkv * 128, C], BF16), v_ctx=dscr("v_ctx" + nm, [C, nkv * 128], BF16))
    kpe_src = dscr("kpe_src", [128, TL], BF16); kpe_all = dscr("kpe_all", [NR * 128, TL], BF16); kpe_ctx = dscr("kpe_ctx", [128, C], BF16)
    h2T_d = dscr("h2T_d", [D, TL + 2], BF16); h2cT_d = dscr("h2cT_d", [D, C + 2], BF16)
    hsrc = dscr("hsrc", [128, DC * 2], BF16); hall = dscr("hall", [NR * 128, DC * 2], BF16)
    NUT = (FC + 1) // 2; NDT = (DC + 1) // 2
    wupT = dscr("wupT", [NUT * 2, 128, DC * 256], BF16); wdnT = dscr("wdnT", [NDT, 128, FC * 256], BF16)
    Dwc = Buf()
    Dx = Buf(); Dctx = Buf(); Dq = Buf(); Dqpe = Buf(); Dksrc = Buf(); Dkall = Buf(); Dvsrc = Buf(); Dvall = Buf()
    Dkpesrc = Buf(); Dkpeall = Buf(); Dkctx = Buf(); Dvctx = Buf(); Dkpectx = Buf(); Dh2 = Buf(); Dh2c = Buf(); Dhsrc = Buf(); Dhall = Buf()
    Dy = Buf()

    ARENA_BYTES = 207 * 1024
    arena = nc.alloc_sbuf_tensor("arena", [128, ARENA_BYTES // 4], F32)
    apos = [0]
    sempool = {}
    semidx = [0]
    persist_sems = [0]

    def phase_reset(base):
        apos[0] = base; semidx[0] = persist_sems[0]

    def carve(shape, dt, sem=True):
        esz = 4 if dt == F32 else 2
        nb = int(np.prod(shape)) * esz
        nb_al = (nb + 63) // 64 * 64
        off = apos[0]; apos[0] += nb_al
        assert apos[0] <= ARENA_BYTES, ("SBUF arena overflow", apos[0])
        a = arena[:, off // 4:(off + nb) // 4]
        if dt != F32:
            a = a.bitcast(dt)
        if len(shape) == 2:
            a = a.rearrange("p (a b) -> p a b", a=shape[0])
        elif len(shape) == 3:
            a = a.rearrange("p (a b c) -> p a b c", a=shape[0], b=shape[1])
        sm = None
        if sem:
            k = semidx[0]; semidx[0] += 1
            if k not in sempool:
                sempool[k] = [nc.alloc_semaphore("bs%d" % k), 0]
            sm = sempool[k]
        b = Buf(a, sm[0] if sm else None)
        if sm:
            b.cnt = sm[1]; b.semrec = sm
        return b

    def mm(out, lhsT, rhs, start, stop, reads, writes=(), pwrites=()):
        return P.emit("pe", lambda e: e.matmul(out, lhsT, rhs, start=start, stop=stop), reads=reads, writes=writes, pwrites=pwrites)

    def act(out, in_, func, reads, writes=(), pwrites=(), bias=None, scale=None, eng="act"):
        kw = {}
        if bias is not None:
            kw["bias"] = bias
        if scale is not None:
            kw["scale"] = scale
        return P.emit(eng, lambda e: e.activation(out, in_, func, **kw), reads=reads, writes=writes, pwrites=pwrites)

    def stt(out, in0, scalar, in1, op0, op1, reads, writes=(), pwrites=(), eng="dve"):
        return P.emit(eng, lambda e: e.scalar_tensor_tensor(out, in0, scalar, in1, op0, op1), reads=reads, writes=writes, pwrites=pwrites)

    def tt(out, in0, in1, op, reads, writes=(), pwrites=(), eng="dve"):
        return P.emit(eng, lambda e: e.tensor_tensor(out, in0, in1, op), reads=reads, writes=writes, pwrites=pwrites)

    def tsm(out, in0, s1, reads, writes=(), pwrites=(), eng="dve"):
        return P.emit(eng, lambda e: e.tensor_scalar(out, in0, s1, None, ALU.mult), reads=reads, writes=writes, pwrites=pwrites)

    def recip(out, in_, reads, writes=(), pwrites=()):
        return P.emit("dve", lambda e: e.reciprocal(out, in_), reads=reads, writes=writes, pwrites=pwrites)

    def cpy(out, in_, reads, writes=(), pwrites=(), eng="dve"):
        return P.emit(eng, lambda e: e.tensor_copy(out, in_), reads=reads, writes=writes, pwrites=pwrites)

    def mset(out, val, writes=(), pwrites=(), reads=(), eng="dve"):
        return P.emit(eng, lambda e: e.memset(out, val), reads=reads, writes=writes, pwrites=pwrites)

    def dma(eng, out, in_, sb, reads=(), writes=(), pwrites=()):
        sb.cnt = sb.semrec[1]
        op = P.dma(eng, out, in_, sb, reads=reads, writes=writes, pwrites=pwrites)
        sb.semrec[1] = sb.cnt
        return op

    ones_f = carve([128], F32, False); ones_b = carve([128], BF16, False); bones_f = carve([128], F32)
    Rg_f = carve([128], F32); Rm_f = carve([128], F32); sel_f = carve([2 * NR], F32)
    sv_f = carve([DC * 2], F32); sv_b = carve([DC, 2], BF16, False)
    vec = carve([NV], F32)
    mod = carve([6 * DC, 2], F32, False)
    A1 = carve([DC, 2], F32, False); A2 = carve([DC, 2], F32, False)
    gsc = carve([8], F32, False)
    bnd = carve([DC, 2], BF16); halo = carve([DC, 2], BF16, False); hs = carve([NR, DC, 2], BF16)
    halo_f = carve([DC, 2], F32, False)
    eps_c = carve([1], F32, False)
    modraw = carve([6 * DC, 2], F32, False)
    have_modraw = [False]
    bones_b = carve([128], BF16, False); Rg_b = carve([128], BF16, False); Rm_b = carve([128], BF16, False)
    cvl = [carve([1], F32) for _ in range(3)]
    psum = [Buf(nc.alloc_psum_tensor("ps%d" % i, [128, 512], F32)[:]) for i in range(8)]
    PERSIST = apos[0]
    persist_sems[0] = semidx[0]

    o = 0
    V_NMIX = o; o += DC
    V_NFFN = o; o += DC
    V_BMOD = o; o += 6 * DC
    V_CW = o; o += 3 * FC
    V_CB = o; o += FC
    V_GDQ = o; o += QC
    V_GDKV = o; o += KC2
    V_GQN = o; o += 1
    V_GQP = o; o += 1
    V_GKP = o; o += 1
    V_GKN = o; o += 1
    V_GQ = o; o += 1
    V_GK = o; o += 1
    assert o == NV

    def vcol(i):
        return vec.ap[:, i:i + 1]

    def modcol(m, c, jj):
        return mod.ap[:, m * DC + c, jj:jj + 1]

    NWS = cfg.NWS

    mset(ones_f.ap, 1.0, writes=[ones_f]); mset(ones_b.ap, 1.0, writes=[ones_b]); mset(eps_c.ap, EPS, writes=[eps_c])
    dma("sp", bones_f.ap, bones[:, :], bones_f, writes=[bones_f])
    dma("sp", Rg_f.ap, RgT[:, :], Rg_f, writes=[Rg_f])
    dma("sp", Rm_f.ap, RmT[:, :], Rm_f, writes=[Rm_f])
    dma("sp", sel_f.ap, sel[:, :], sel_f, writes=[sel_f])
    dma("sp", sv_f.ap, cvec[:, :], sv_f, writes=[sv_f])
    act(sv_b.ap.rearrange("p a b -> p (a b)"), sv_f.ap, AF.Silu, reads=[sv_f], writes=[sv_b])
    cpy(bones_b.ap, bones_f.ap, reads=[bones_f], writes=[bones_b])
    cpy(Rg_b.ap, Rg_f.ap, reads=[Rg_f], writes=[Rg_b])
    cpy(Rm_b.ap, Rm_f.ap, reads=[Rm_f], writes=[Rm_b])

    lat_tiles = [("lat", t0, min(W1, TL - t0)) for t0 in range(0, TL, W1)]
    ctx_tiles = [("ctx", t0, min(W1, C - t0)) for t0 in range(0, C, W1)]
    groups = [list(range(g * NR, (g + 1) * NR)) for g in range(8 // NR)]

    def xsrc(kind, l):
        if kind == "lat":
            return (xT_in if l == 0 else xT_s), Dx
        return (ctxT_in if l == 0 else ctxT_s), Dctx

    def fm3(dram2d, c0, n):
        return dram2d.rearrange("(c p) t -> p c t", p=128)[:, :, c0:c0 + n]

    def rstd_from_ss(ssps, n, nd, rs_sq, rs):
        act(rs_sq.ap[:, :n], ssps.ap[:, :n], AF.Sqrt, reads=[ssps, eps_c], writes=[rs_sq], bias=eps_c.ap[:, 0:1], scale=1.0 / nd)
        recip(rs.ap[:, :n], rs_sq.ap[:, :n], reads=[rs_sq], writes=[rs])

    for l in range(L):
        last = (l == L - 1)
        is_mla = (l % 2 == 0)
        j = l // 2
        P.new_epoch()
        NH = MH if is_mla else GH
        NKV = MH if is_mla else GKV
        GRP = NH // NKV
        KV = kvs["m" if is_mla else "g"]
        kT_src, kT_all, v_src, v_all, kT_ctx, v_ctx = KV["kT_src"], KV["kT_all"], KV["v_src"], KV["v_all"], KV["kT_ctx"], KV["v_ctx"]
        HB, NB = KV["HB"], KV["NB"]

        def ksrc_rows(hd):
            return kT_src[hd // HB][(hd % HB) * 128:(hd % HB + 1) * 128, :]

        def vsrc_cols(tok0, ntok, h0, nh):
            out = []
            hd = h0
            while hd < h0 + nh:
                k = min(HB - hd % HB, h0 + nh - hd)
                out.append((v_src[hd // HB][tok0:tok0 + ntok, (hd % HB) * 128:(hd % HB + k) * 128], (hd - h0) * 128, k * 128))
                hd += k
            return out

        phase_reset(PERSIST)
        dma("sp", vec.ap, vecs[l], vec, writes=[vec])
        if not have_modraw[0]:
            wsm = [carve([DC, 512], BF16) for _ in range(3)]
            modps = psum[0]
            modps.new_gen()
            nfg = (6 * DC + 3) // 4
            wm3 = w_mod[l].rearrange("(kc p) f -> p kc f", p=128)
            for fg in range(nfg):
                nf = min(4, 6 * DC - fg * 4)
                wb = wsm[fg % 3]
                dma("pool", wb.ap[:, :, :nf * 128], wm3[:, :, fg * 512:fg * 512 + nf * 128], wb, writes=[wb])
                for fc in range(nf):
                    col = (fg * 4 + fc) * 2
                    for kc in range(DC):
                        mm(modps.ap[:, col:col + 2], wb.ap[:, kc, fc * 128:(fc + 1) * 128], sv_b.ap[:, kc, :], kc == 0, kc == DC - 1,
                           reads=[wb, sv_b], pwrites=[modps])
            mp3 = modps.ap[:, 0:12 * DC].rearrange("p (a b) -> p a b", b=2)
            for jj in range(2):
                tt(mod.ap[:, :, jj], mp3[:, :, jj], vec.ap[:, V_BMOD:V_BMOD + 6 * DC], ALU.add, reads=[modps, vec],
                   writes=[mod] if jj == 0 else (), pwrites=[mod] if jj else ())
        else:
            for jj in range(2):
                tt(mod.ap[:, :, jj], modraw.ap[:, :, jj], vec.ap[:, V_BMOD:V_BMOD + 6 * DC], ALU.add, reads=[modraw, vec],
                   writes=[mod] if jj == 0 else (), pwrites=[mod] if jj else ())
        for (Ab, voff, m) in ((A1, V_NMIX, 1), (A2, V_NFFN, 4)):
            for jj in range(2):
                stt(Ab.ap[:, :, jj], mod.ap[:, m * DC:(m + 1) * DC, jj], 1.0, vec.ap[:, voff:voff + DC], ALU.add, ALU.mult,
                    reads=[mod, vec], writes=[Ab] if jj == 0 else (), pwrites=[Ab] if jj else ())
        if is_mla:
            sc = 1.0 / math.sqrt(192.0)
            glist = [(0, V_GQN, sc), (1, V_GQP, sc), (2, V_GKP, 1.0), (3, V_GKN, 1.0)]
        else:
            sc = 1.0 / math.sqrt(128.0)
            glist = [(0, V_GQ, sc), (1, V_GK, 1.0)]
        for gi, (gidx, vo, mul) in enumerate(glist):
            tsm(gsc.ap[:, gidx:gidx + 1], vec.ap[:, vo:vo + 1], mul, reads=[vec], writes=[gsc] if gi == 0 else (), pwrites=[gsc] if gi else ())
        P.barrier()

        phase_reset(PERSIST)
        xt = [carve([DC, W1], F32) for _ in range(1)]
        hT = [carve([DC, W1], BF16, False) for _ in range(2)]
        sq = [carve([W1], BF16, False) for _ in range(2)]
        rs_sq = carve([W1], F32, False); rs = carve([W1], F32, False)
        tmpf = [carve([W1], F32, False) for _ in range(2)]
        cs = [carve([W1], F32) for _ in range(2)]; sn = [carve([W1], F32) for _ in range(2)]
        NSET = 3
        nsets = [dict(sq=carve([W1], BF16, False), rs_sq=carve([W1], F32, False), rs=carve([W1], F32, False),
                      qn=carve([W1], BF16, False), t1=carve([W1], F32, False), t2=carve([W1], F32, False)) for _ in range(NSET)]
        stg = [carve([W1], BF16) for _ in range(3)]
        vstg = [carve([512], BF16) for _ in range(2)]
        ws = [carve([max(DC, QC, KC2), 512], BF16) for _ in range(NWS)]
        if is_mla:
            cq_f = carve([max(QC, KC2), W1], F32, False); cqn = carve([QC, W1], BF16, False); ckvn = carve([KC2, W1], BF16, False)
        cnt = dict(ws=0, stg=0, vstg=0, pi=0)

        def wload(ring, src3, kc, m, split=1):
            b = ring[cnt["ws"] % len(ring)]; cnt["ws"] += 1
            view = b.ap.rearrange("p a b -> p (a b)")[:, :kc * m].rearrange("p (a b) -> p a b", a=kc)
            step = (kc + split - 1) // split
            for i, k0 in enumerate(range(0, kc, step)):
                k1 = min(kc, k0 + step)
                dma("pool", view[:, k0:k1, :], src3[:, k0:k1, :], b, writes=[b] if i == 0 else (), pwrites=[b] if i else ())
            return b, view

        def wload4(ring, src4, kc, nh):
            b = ring[cnt["ws"] % len(ring)]; cnt["ws"] += 1
            view = b.ap.rearrange("p a b -> p (a b)")[:, :kc * nh * 128].rearrange("p (a h e) -> p a h e", a=kc, h=nh)
            for hq in range(nh):
                dma("pool", view[:, :, hq, :], src4[:, :, hq, :], b, writes=[b] if hq == 0 else (), pwrites=[b] if hq else ())
            return b, view

        def wload2(ring, srcA, srcB, kc):
            b = ring[cnt["ws"] % len(ring)]; cnt["ws"] += 1
            view = b.ap.rearrange("p a b -> p (a b)")[:, :kc * 128].rearrange("p (a b) -> p a b", a=kc)
            dma("pool", view[:, :, 0:64], srcA, b, writes=[b])
            dma("pool", view[:, :, 64:128], srcB, b, pwrites=[b])
            return b, view

        def modulate(xb, n, Ab, shm, jj, hb, sq, rs_sq, rs, tmpf):
            ssps = psum[1]
            for c in range(DC):
                s_ = sq[c % 2]
                act(s_.ap[:, :n], xb.ap[:, c, :n], AF.Square, reads=[xb], writes=[s_])
                mm(ssps.ap[:, :n], ones_b.ap, s_.ap[:, :n], c == 0, c == DC - 1, reads=[s_, ones_b],
                   writes=[ssps] if c == 0 else (), pwrites=() if c == 0 else [ssps])
            rstd_from_ss(ssps, n, float(D), rs_sq, rs)
            hb.new_gen()
            for c in range(DC):
                tf = tmpf[c % 2]
                stt(tf.ap[:, :n], xb.ap[:, c, :n], Ab.ap[:, c, jj:jj + 1], rs.ap[:, :n], ALU.mult, ALU.mult, reads=[xb, Ab, rs], writes=[tf])
                act(hb.ap[:, c, :n], tf.ap[:, :n], AF.Identity, reads=[tf, mod], pwrites=[hb], bias=modcol(shm, c, jj), scale=1.0)

        def run_jobs(jobs, n, cb, sb_):
            N = len(jobs)
            pbuf = [psum[3], psum[4], psum[5]]
            st_of = {}

            def A(i):
                J = jobs[i]
                wb, wv_ = J["w"]()
                ps = pbuf[i % 3]
                proj(ps, wv_, wb, J["src"], J["kcs"], n)
                S_ = nsets[i % NSET]
                act(S_["sq"].ap[:, :n], ps.ap[:, :n], AF.Square, reads=[ps], writes=[S_["sq"]])

            def B(i):
                J = jobs[i]
                ps = pbuf[i % 3]
                S_ = nsets[i % NSET]
                ssps = psum[1 + i % 2]
                mm(ssps.ap[:, :n], J["ones"].ap, S_["sq"].ap[:, :n], True, True, reads=[S_["sq"], J["ones"]], writes=[ssps])
                rstd_from_ss(ssps, n, float(J["nd"]), S_["rs_sq"], S_["rs"])
                if not J["rope"]:
                    st = stg[cnt["stg"] % 3]; cnt["stg"] += 1
                    stt(st.ap[:, :n], ps.ap[:, :n], J["gcol"], S_["rs"].ap[:, :n], ALU.mult, ALU.mult, reads=[ps, S_["rs"], gsc], writes=[st])
                    dma("sp", J["dst"], st.ap[:, :n], st, reads=[st], pwrites=[J["Ddst"]])
                else:
                    stt(S_["qn"].ap[:, :n], ps.ap[:, :n], J["gcol"], S_["rs"].ap[:, :n], ALU.mult, ALU.mult, reads=[ps, S_["rs"], gsc], writes=[S_["qn"]])

            def C(i):
                J = jobs[i]
                if not J["rope"]:
                    return
                S_ = nsets[i % NSET]
                qn = S_["qn"]; t1 = S_["t1"]; t2 = S_["t2"]
                rps = psum[7] if i % 2 else psum[0]
                mm(rps.ap[:, :n], J["R"].ap, qn.ap[:, :n], True, True, reads=[J["R"], qn], writes=[rps])
                tt(t1.ap[:, :n], qn.ap[:, :n], cb.ap[:, :n], ALU.mult, reads=[qn, cb], writes=[t1])
                tt(t2.ap[:, :n], rps.ap[:, :n], sb_.ap[:, :n], ALU.mult, reads=[rps, sb_], writes=[t2])
                st = stg[cnt["stg"] % 3]; cnt["stg"] += 1
                tt(st.ap[:, :n], t1.ap[:, :n], t2.ap[:, :n], ALU.add, reads=[t1, t2], writes=[st])
                dma("sp", J["dst"], st.ap[:, :n], st, reads=[st], pwrites=[J["Ddst"]])

            for i in range(N + 2):
                if i < N:
                    A(i)
                if 0 <= i - 1 < N:
                    B(i - 1)
                if 0 <= i - 2 < N:
                    C(i - 2)

        def job(wfn, src, kcs, nd, onesb, gcol, rope, Rb, dst_ap, Ddst):
            return dict(w=wfn, src=src, kcs=kcs, nd=nd, ones=onesb, gcol=gcol, rope=rope, R=Rb, dst=dst_ap, Ddst=Ddst)

        def shared_w(loader):
            box = []

            def get():
                if not box:
                    box.append(loader())
                return box[0]
            return get

        def proj(ps, wview, wb, src, kcs, n):
            for kc in range(kcs):
                mm(ps.ap[:, :n], wview[:, kc, :], src.ap[:, kc, :n], kc == 0, kc == kcs - 1, reads=[wb, src],
                   writes=[ps] if kc == 0 else (), pwrites=() if kc == 0 else [ps])

        def nextps():
            p_ = psum[3 + cnt["pi"] % 2]; cnt["pi"] += 1
            return p_

        def vproj(src, kcs, n, wbs, dst_fn, Ddst):
            ncol = len(wbs) * 128
            for s0 in range(0, n, 128):
                vps = psum[6]
                vps.new_gen()
                for qi, (wb, wv_) in enumerate(wbs):
                    for kc in range(kcs):
                        mm(vps.ap[:, qi * 128:(qi + 1) * 128], src.ap[:, kc, s0:s0 + 128], wv_[:, kc, :], kc == 0, kc == kcs - 1,
                           reads=[src, wb], pwrites=[vps])
                vs = vstg[cnt["vstg"] % 2]; cnt["vstg"] += 1
                act(vs.ap[:, :ncol], vps.ap[:, :ncol], AF.Copy, reads=[vps], writes=[vs])
                for (dap, coff, ncl) in dst_fn(s0):
                    dma("sp", dap, vs.ap[:, coff:coff + ncl], vs, reads=[vs], pwrites=[Ddst])

        for D_ in (Dq, Dqpe, Dksrc, Dvsrc, Dkpesrc, Dkctx, Dvctx, Dkpectx):
            D_.new_gen()
        tiles = lat_tiles + ctx_tiles
        def p1_tile(ti, kind, t0, n, do_q, do_kv):
            jj = 0 if kind == "lat" else 1
            tokq = t0 if kind == "lat" else TL + t0
            xb = xt[0]; hb = hT[ti % 2]
            src_d, Dsrc = xsrc(kind, l)
            dma("sp", xb.ap[:, :, :n], fm3(src_d, t0, n), xb, reads=[Dsrc], writes=[xb])
            modulate(xb, n, A1, 0, jj, hb, sq, rs_sq, rs, tmpf)
            rope = (kind == "lat")
            need_q = not (kind == "ctx" and last)
            cb = sb_ = None
            if rope:
                cb = cs[ti % 2]; sb_ = sn[ti % 2]
                dma("sp", cb.ap[:, :n], (cosm if is_mla else cosg)[:, t0:t0 + n], cb, writes=[cb])
                dma("sp", sb_.ap[:, :n], (sinm if is_mla else sing)[:, t0:t0 + n], sb_, writes=[sb_])
            kdst = (lambda r0, r1: ksrc_rows(r0 // 128)[:, t0:t0 + n]) if kind == "lat" else (lambda r0, r1: kT_ctx[r0:r1, t0:t0 + n])
            Dk = Dksrc if kind == "lat" else Dkctx
            Dv = Dvsrc if kind == "lat" else Dvctx
            if not is_mla:
                wq = gqa_w_q[j].rearrange("(kc p) m -> p kc m", p=128)
                wkv = gqa_w_kv[j].rearrange("(kc p) m -> p kc m", p=128)
                jobs = []
                if need_q:
                    for h0 in range(0, GH, 4):
                        nh = min(4, GH - h0)
                        sw = shared_w(lambda h0=h0, nh=nh: wload(ws, wq[:, :, h0 * 128:(h0 + nh) * 128], DC, nh * 128))
                        for hq in range(nh):
                            h = h0 + hq
                            jobs.append(job((lambda sw=sw, hq=hq: (sw()[0], sw()[1][:, :, hq * 128:(hq + 1) * 128])), hb, DC, 128, ones_b,
                                            gsc.ap[:, 0:1], rope, Rg_b, qT_d[h * 128:(h + 1) * 128, tokq:tokq + n], Dq))
                for g0 in range(0, GKV, 4):
                    ng = min(4, GKV - g0)
                    sw = shared_w(lambda g0=g0, ng=ng: wload(ws, wkv[:, :, g0 * 128:(g0 + ng) * 128], DC, ng * 128))
                    for gq in range(ng):
                        g = g0 + gq
                        jobs.append(job((lambda sw=sw, gq=gq: (sw()[0], sw()[1][:, :, gq * 128:(gq + 1) * 128])), hb, DC, 128, ones_b,
                                        gsc.ap[:, 1:2], rope, Rg_b, kdst(g * 128, (g + 1) * 128), Dk))
                run_jobs(jobs, n, cb, sb_)
                for c0 in range(0, GKV, 4):
                    ng = min(4, GKV - c0)
                    wb, wv_ = wload(ws, wkv[:, :, (GKV + c0) * 128:(GKV + c0 + ng) * 128], DC, ng * 128)
                    wbs = [(wb, wv_[:, :, q4 * 128:(q4 + 1) * 128]) for q4 in range(ng)]
                    if kind == "lat":
                        vproj(hb, DC, n, wbs, (lambda s0, c0=c0, ng=ng: vsrc_cols(t0 + s0, 128, c0, ng)), Dv)
                    else:
                        vproj(hb, DC, n, wbs, (lambda s0, c0=c0, ng=ng: [(v_ctx[t0 + s0:t0 + s0 + 128, c0 * 128:(c0 + ng) * 128], 0, ng * 128)]), Dv)
            else:
                wdq = mla_w_dq[j].rearrange("(kc p) m -> p kc m", p=128)
                wuq = mla_w_uq[j].rearrange("(kc p) m -> p kc m", p=128)
                wdkv = mla_w_dkv[j].rearrange("(kc p) m -> p kc m", p=128)
                wukv = mla_w_ukv[j].rearrange("(kc p) m -> p kc m", p=128)

                def compress(wsrc, nchunk, gv_off, outn):
                    ssps = psum[1]
                    outn.new_gen(); cq_f.new_gen()
                    for oc in range(nchunk):
                        if oc % 4 == 0:
                            nq = min(4, nchunk - oc)
                            wb, wvw = wload(ws, wsrc[:, :, oc * 128:(oc + nq) * 128], DC, nq * 128)
                        wv_ = wvw[:, :, (oc % 4) * 128:(oc % 4 + 1) * 128]
                        ps = nextps()
                        proj(ps, wv_, wb, hb, DC, n)
                        act(cq_f.ap[:, oc, :n], ps.ap[:, :n], AF.Copy, reads=[ps], pwrites=[cq_f])
                        s_ = sq[oc % 2]
                        act(s_.ap[:, :n], ps.ap[:, :n], AF.Square, reads=[ps], writes=[s_])
                        mm(ssps.ap[:, :n], ones_b.ap, s_.ap[:, :n], oc == 0, oc == nchunk - 1, reads=[s_, ones_b],
                           writes=[ssps] if oc == 0 else (), pwrites=() if oc == 0 else [ssps])
                    rstd_from_ss(ssps, n, float(nchunk * 128), rs_sq, rs)
                    for oc in range(nchunk):
                        stt(outn.ap[:, oc, :n], cq_f.ap[:, oc, :n], vcol(gv_off + oc), rs.ap[:, :n], ALU.mult, ALU.mult,
                            reads=[cq_f, rs, vec], pwrites=[outn])

                if need_q and do_q:
                    compress(wdq, QC, V_GDQ, cqn)
                    wuq4 = wuq.rearrange("p k (h e) -> p k h e", e=192)
                    jobs = []
                    for h0 in range(0, MH, 4):
                        nq = min(4, MH - h0)
                        sw = shared_w(lambda h0=h0, nq=nq: wload4(ws, wuq4[:, :, h0:h0 + nq, 0:128], QC, nq))
                        for hq in range(nq):
                            h = h0 + hq
                            jobs.append(job((lambda sw=sw, hq=hq: (sw()[0], sw()[1][:, :, hq, :])), cqn, QC, 128, ones_b,
                                            gsc.ap[:, 0:1], False, None, qT_d[h * 128:(h + 1) * 128, tokq:tokq + n], Dq))
                    for hp in range(MH // 2):
                        h0 = 2 * hp
                        jobs.append(job((lambda h0=h0: wload2(ws, wuq[:, :, h0 * 192 + 128:h0 * 192 + 192], wuq[:, :, (h0 + 1) * 192 + 128:(h0 + 1) * 192 + 192], QC)),
                                        cqn, QC, 64, bones_b, gsc.ap[:, 1:2], rope, Rm_b, qpeT_d[hp * 128:(hp + 1) * 128, tokq:tokq + n], Dqpe))
                    run_jobs(jobs, n, cb, sb_)
                if not do_kv:
                    return
                compress(wdkv, KC2, V_GDKV, ckvn)
                jobs = []
                jobs.append(job((lambda: wload2(ws, wdkv[:, :, KVR:KVR + 64], wdkv[:, :, KVR:KVR + 64], DC)), hb, DC, 64, bones_b, gsc.ap[:, 2:3],
                                rope, Rm_b, (kpe_src[:, t0:t0 + n] if kind == "lat" else kpe_ctx[:, t0:t0 + n]), (Dkpesrc if kind == "lat" else Dkpectx)))
                wukv4 = wukv.rearrange("p k (h e) -> p k h e", e=256)
                for h0 in range(0, MH, 4):
                    nq = min(4, MH - h0)
                    sw = shared_w(lambda h0=h0, nq=nq: wload4(ws, wukv4[:, :, h0:h0 + nq, 0:128], KC2, nq))
                    for hq in range(nq):
                        h = h0 + hq
                        jobs.append(job((lambda sw=sw, hq=hq: (sw()[0], sw()[1][:, :, hq, :])), ckvn, KC2, 128, ones_b,
                                        gsc.ap[:, 3:4], False, None, kdst(h * 128, (h + 1) * 128), Dk))
                run_jobs(jobs, n, cb, sb_)
                for h0 in range(0, MH, 4):
                    nh = min(4, MH - h0)
                    wb, wvw = wload4(ws, wukv4[:, :, h0:h0 + nh, 128:256], KC2, nh)
                    wbs = [(wb, wvw[:, :, q4, :]) for q4 in range(nh)]
                    if kind == "lat":
                        vproj(ckvn, KC2, n, wbs, (lambda s0, h0=h0, nh=nh: vsrc_cols(t0 + s0, 128, h0, nh)), Dv)
                    else:
                        vproj(ckvn, KC2, n, wbs, (lambda s0, h0=h0, nh=nh: [(v_ctx[t0 + s0:t0 + s0 + 128, h0 * 128:(h0 + nh) * 128], 0, nh * 128)]), Dv)
        def emit_ag():
            Dkall.new_gen(); Dvall.new_gen()
            for bi in range(NB):
                P.cc("AllGather", groups, kT_src[bi][:, :], kT_all[bi][:, :], reads=[Dksrc], pwrites=[Dkall])
                P.cc("AllGather", groups, v_src[bi][:, :], v_all[bi][:, :], reads=[Dvsrc], pwrites=[Dvall])
            if is_mla:
                P.cc("AllGather", groups, kpe_src[:, :], kpe_all[:, :], reads=[Dkpesrc], writes=[Dkpeall])

        if is_mla:
            for ti, (kind, t0, n) in enumerate(tiles):
                p1_tile(ti, kind, t0, n, False, True)
            emit_ag()
            for ti, (kind, t0, n) in enumerate(tiles):
                if not (kind == "ctx" and last):
                    p1_tile(ti + len(tiles), kind, t0, n, True, False)
        else:
            for ti, (kind, t0, n) in enumerate(tiles):
                p1_tile(ti, kind, t0, n, True, True)
            emit_ag()
        P.barrier()

        phase_reset(PERSIST)
        oT = carve([NH, NTOK], BF16, False)
        P3BASE = apos[0]
        KA = [carve([T], BF16) for _ in range(2)]
        VV = [carve([NCH, 128], BF16) for _ in range(2)]
        KBm = [carve([T], BF16) for _ in range(2)] if is_mla else None
        QA = [carve([W1], BF16) for _ in range(2)]
        QB = [carve([W1], BF16) for _ in range(2)] if is_mla else None
        pT = [carve([W1], BF16, False) for _ in range(4)]
        rec = carve([W1], F32, False)
        accD = [carve([W1], F32, False) for _ in range(2)]
        if is_mla:
            kpa = kpe_all.rearrange("(r d) t -> d r t", d=128)
            for hf in range(2):
                lo, hi = hf * 64, hf * 64 + 64
                zl, zh = (64, 128) if hf == 0 else (0, 64)
                mset(KBm[hf].ap[zl:zh, :], 0.0, writes=[KBm[hf]])
                dma("sp", KBm[hf].ap[lo:hi, 0:NR * TL].rearrange("p (r t) -> p r t", r=NR), kpa[lo:hi], KBm[hf], reads=[Dkpeall], pwrites=[KBm[hf]])
                dma("sp", KBm[hf].ap[lo:hi, NR * TL:T], kpe_ctx[lo:hi, :], KBm[hf], reads=[Dkpectx], pwrites=[KBm[hf]])
        wup = ffn_w_up[l].rearrange("(kc p) m -> p kc m", p=128)
        wdn = ffn_w_down[l].rearrange("(kc p) m -> p kc m", p=128)
        Dwc.new_gen()
        ncv = 0
        for g in range(NUT):
            nq = min(2, FC - 2 * g)
            for part in range(2):
                ln = cvl[ncv % len(cvl)]; ncv += 1
                dma("pool", wupT[2 * g + part][:, :DC * nq * 128].rearrange("p (kc m) -> p kc m", kc=DC),
                    wup[:, :, part * FF + g * 256:part * FF + g * 256 + nq * 128], ln, writes=[ln], pwrites=[Dwc])
        for g in range(NDT):
            nq = min(2, DC - 2 * g)
            ln = cvl[ncv % len(cvl)]; ncv += 1
            dma("pool", wdnT[g][:, :FC * nq * 128].rearrange("p (kc m) -> p kc m", kc=FC),
                wdn[:, :, g * 256:g * 256 + nq * 128], ln, writes=[ln], pwrites=[Dwc])
        qtiles = lat_tiles + ([] if last else ctx_tiles)
        oT.new_gen()
        it = 0
        kall4 = [a.rearrange("(r g d) t -> d g r t", r=NR, g=HB) for a in kT_all]
        vall4 = [a.rearrange("(ch p) (g d) -> p ch g d", p=128, d=128) for a in v_all]
        vctx4 = v_ctx.rearrange("(ch p) (g d) -> p ch g d", p=128, d=128)
        for g in range(NKV):
            ka = KA[g % 2]; vv = VV[g % 2]
            dma("sp", ka.ap[:, 0:NR * TL].rearrange("p (r t) -> p r t", r=NR), kall4[g // HB][:, g % HB], ka, reads=[Dkall], writes=[ka])
            dma("sp", ka.ap[:, NR * TL:T], kT_ctx[g * 128:(g + 1) * 128, :], ka, reads=[Dkctx], pwrites=[ka])
            dma("sp", vv.ap[:, 0:NR * TL // 128, :], vall4[g // HB][:, :, g % HB, :], vv, reads=[Dvall], writes=[vv])
            dma("sp", vv.ap[:, NR * TL // 128:NCH, :], vctx4[:, :, g, :], vv, reads=[Dvctx], pwrites=[vv])
            for hh in range(GRP):
                h = g * GRP + hh
                hp = (h % 2) * 64
                for (kind, t0, n) in qtiles:
                    tokq = t0 if kind == "lat" else TL + t0
                    qa = QA[it % 2]; qb = QB[it % 2] if is_mla else None
                    ops_ = psum[3 + it % 2]; sums = psum[5 + it % 2]
                    it += 1
                    dma("sp", qa.ap[:, :n], qT_d[h * 128:(h + 1) * 128, tokq:tokq + n], qa, reads=[Dq], writes=[qa])
                    if is_mla:
                        dma("sp", qb.ap[:, :n], qpeT_d[(h // 2) * 128:(h // 2 + 1) * 128, tokq:tokq + n], qb, reads=[Dqpe], writes=[qb])
                    chunks = list(range(NCH)) if kind == "lat" else list(range(NR * TL // 128, NCH))
                    nchk = len(chunks)

                    aD = accD[it % 2]
                    kbm = KBm[h % 2] if is_mla else None
                    pe_chunks = [ci for ci in range(nchk) if ci % 4 == 3]
                    dve_chunks = [ci for ci in range(nchk) if ci not in pe_chunks]

                    def emit_S(ci):
                        ch = chunks[ci]
                        sp_ = psum[ci % 3]
                        mm(sp_.ap[:, :n], ka.ap[:, ch * 128:(ch + 1) * 128], qa.ap[:, :n], True, not is_mla, reads=[ka, qa], writes=[sp_])
                        if is_mla:
                            mm(sp_.ap[:, :n], kbm.ap[:, ch * 128:(ch + 1) * 128], qb.ap[:, :n], False, True, reads=[kbm, qb], pwrites=[sp_])
                        pt = pT[ci % 4]
                        act(pt.ap[:, :n], sp_.ap[:, :n], AF.Exp, reads=[sp_], writes=[pt])

                    def emit_PV(ci):
                        ch = chunks[ci]
                        pt = pT[ci % 4]
                        mm(ops_.ap[:, :n], vv.ap[:, ch, :], pt.ap[:, :n], ci == 0, ci == nchk - 1, reads=[vv, pt],
                           writes=[ops_] if ci == 0 else (), pwrites=() if ci == 0 else [ops_])
                        if ci in pe_chunks:
                            mm(sums.ap[:, :n], ones_b.ap, pt.ap[:, :n], ci == pe_chunks[0], False, reads=[ones_b, pt],
                               writes=[sums] if ci == pe_chunks[0] else (), pwrites=() if ci == pe_chunks[0] else [sums])
                        elif ci == dve_chunks[0]:
                            cpy(aD.ap[:, :n], pt.ap[:, :n], reads=[pt], writes=[aD])
                        else:
                            tt(aD.ap[:, :n], aD.ap[:, :n], pt.ap[:, :n], ALU.add, reads=[pt, aD], pwrites=[aD])

                    emit_S(0)
                    if nchk > 1:
                        emit_S(1)
                    for ci in range(nchk):
                        if ci + 2 < nchk:
                            emit_S(ci + 2)
                        emit_PV(ci)
                    mm(sums.ap[:, :n], ones_f.ap, aD.ap[:, :n], not pe_chunks, True, reads=[ones_f, aD],
                       writes=[sums] if not pe_chunks else (), pwrites=[sums] if pe_chunks else ())
                    recip(rec.ap[:, :n], sums.ap[:, :n], reads=[sums], writes=[rec])
                    tt(oT.ap[:, h, tokq:tokq + n], ops_.ap[:, :n], rec.ap[:, :n], ALU.mult, reads=[ops_, rec], pwrites=[oT])
        P.barrier()

        phase_reset(P3BASE)
        xt3 = [carve([DC, W1], F32) for _ in range(2)]
        h3 = [carve([DC, W1], BF16) for _ in range(1)]
        sq3 = [carve([W1], BF16, False) for _ in range(2)]
        rs_sq3 = carve([W1], F32, False); rs3 = carve([W1], F32, False)
        tmpf3 = [carve([W1], F32, False) for _ in range(2)]
        ws3 = [carve([NH, 256], BF16) for _ in range(NWS)]
        w_o = (mla_w_o if is_mla else gqa_w_o)[j].rearrange("(kc p) m -> p kc m", p=128)
        tiles3 = lat_tiles + ([] if last else ctx_tiles)
        Dh2.new_gen(); Dh2c.new_gen(); bnd.new_gen()
        for ti, (kind, t0, n) in enumerate(tiles3):
            jj = 0 if kind == "lat" else 1
            tokq = t0 if kind == "lat" else TL + t0
            xb = xt3[ti % 2]; hb = h3[0]
            src_d, Dsrc = xsrc(kind, l)
            dst_d = xT_s if kind == "lat" else ctxT_s
            dma("sp", xb.ap[:, :, :n], fm3(src_d, t0, n), xb, reads=[Dsrc], writes=[xb])
            for dc in range(DC):
                if dc % 2 == 0:
                    nq = min(2, DC - dc)
                    wb, wvw = wload(ws3, w_o[:, :, dc * 128:(dc + nq) * 128], NH, nq * 128)
                wv_ = wvw[:, :, (dc % 2) * 128:(dc % 2 + 1) * 128]
                ps = nextps()
                for kc in range(NH):
                    mm(ps.ap[:, :n], wv_[:, kc, :], oT.ap[:, kc, tokq:tokq + n], kc == 0, kc == NH - 1, reads=[wb, oT],
                       writes=[ps] if kc == 0 else (), pwrites=() if kc == 0 else [ps])
                stt(xb.ap[:, dc, :n], ps.ap[:, :n], modcol(2, dc, jj), xb.ap[:, dc, :n], ALU.mult, ALU.add, reads=[ps, mod, xb], pwrites=[xb])
            dma("sp", fm3(dst_d, t0, n), xb.ap[:, :, :n], xb, reads=[xb], pwrites=[Dsrc])
            modulate(xb, n, A2, 3, jj, hb, sq3, rs_sq3, rs3, tmpf3)
            if kind == "lat":
                dma("sp", fm3(h2T_d, 1 + t0, n), hb.ap[:, :, :n], hb, reads=[hb], pwrites=[Dh2])
                if t0 == 0:
                    cpy(bnd.ap[:, :, 0], hb.ap[:, :, 0], reads=[hb], pwrites=[bnd])
                if t0 + n == TL:
                    cpy(bnd.ap[:, :, 1], hb.ap[:, :, n - 1], reads=[hb], pwrites=[bnd])
            else:
                dma("sp", fm3(h2cT_d, 1 + t0, n), hb.ap[:, :, :n], hb, reads=[hb], pwrites=[Dh2c])
        dma("sp", hsrc[:, :], bnd.ap.rearrange("p a b -> p (a b)"), bnd, reads=[bnd], writes=[Dhsrc])
        P.barrier()
        P.cc("AllGather", groups, hsrc[:, :], hall[:, :], reads=[Dhsrc], writes=[Dhall])
        dma("sp", hs.ap, hall.rearrange("(r p) (c t) -> p r c t", p=128, t=2), hs, reads=[Dhall], writes=[hs])
        for side in range(2):
            for r in range(NR):
                srcc = hs.ap[:, r, :, 1 - side]
                scol = sel_f.ap[:, side * NR + r:side * NR + r + 1]
                if r == 0:
                    tsm(halo_f.ap[:, :, side], srcc, scol, reads=[hs, sel_f], writes=[halo_f] if side == 0 else (), pwrites=() if side == 0 else [halo_f])
                else:
                    stt(halo_f.ap[:, :, side], srcc, scol, halo_f.ap[:, :, side], ALU.mult, ALU.add, reads=[hs, sel_f, halo_f], pwrites=[halo_f])
        cpy(halo.ap, halo_f.ap, reads=[halo_f], writes=[halo])
        P.barrier()

        phase_reset(PERSIST)
        WIN = W4 + 2
        h2w = [carve([DC, WIN], BF16) for _ in range(2)]
        aT = carve([FC, W4], BF16, False)
        xw = [carve([DC, W4], F32) for _ in range(1)]
        c1 = [carve([W4], F32, False) for _ in range(2)]; c2 = [carve([W4], F32, False) for _ in range(2)]
        c3 = [carve([W4], F32, False) for _ in range(2)]; sg = [carve([W4], F32, False) for _ in range(2)]
        wsu = [carve([DC, 256], BF16) for _ in range(cfg.NWU)]
        wsd = [carve([FC, 256], BF16) for _ in range(cfg.NWD)]
        nmod_total = 6 * DC if not last else 0
        nmod_done = [0]
        if not last:
            wsm4 = [carve([DC, 128], BF16) for _ in range(3)]
            modps4 = psum[0]
            modps4.new_gen()
            wm3n = w_mod[l + 1].rearrange("(kc p) f -> p kc f", p=128)

        def emit_mod_group():
            fcm = nmod_done[0]
            if fcm >= nmod_total:
                return
            nmod_done[0] += 1
            wbm = wsm4[fcm % 3]
            dma("pool", wbm.ap, wm3n[:, :, fcm * 128:(fcm + 1) * 128], wbm, writes=[wbm])
            for kc in range(DC):
                mm(modps4.ap[:, fcm * 2:fcm * 2 + 2], wbm.ap[:, kc, :], sv_b.ap[:, kc, :], kc == 0, kc == DC - 1,
                   reads=[wbm, sv_b], pwrites=[modps4])
        def wload_bf(ring, src2, kc, m):
            b = ring[cnt["ws"] % len(ring)]; cnt["ws"] += 1
            v2 = b.ap.rearrange("p a b -> p (a b)")[:, :kc * m]
            dma("pool", v2, src2, b, reads=[Dwc], writes=[b])
            return b, v2.rearrange("p (a b) -> p a b", a=kc)

        wins = [("lat", s, min(W4, TL - s)) for s in range(0, TL, W4)]
        if not last:
            wins += [("ctx", s, min(W4, C - s)) for s in range(0, C, W4)]
        pi = 0
        def load_hw(wi):
            kind, s0, nout = wins[wi]
            nin = nout + 2
            hw = h2w[wi % 2]
            tot = TL if kind == "lat" else C
            if kind == "lat":
                dma("sp", hw.ap[:, :, :nin], fm3(h2T_d, s0, nin), hw, reads=[Dh2], writes=[hw])
            else:
                dma("sp", hw.ap[:, :, :nin], fm3(h2cT_d, s0, nin), hw, reads=[Dh2c], writes=[hw])
            if s0 == 0:
                if kind == "lat":
                    cpy(hw.ap[:, :, 0], halo.ap[:, :, 0], reads=[halo, hw], pwrites=[hw])
                else:
                    mset(hw.ap[:, :, 0], 0.0, reads=[hw], pwrites=[hw])
            if s0 + nout == tot:
                if kind == "lat":
                    cpy(hw.ap[:, :, nin - 1], halo.ap[:, :, 1], reads=[halo, hw], pwrites=[hw])
                else:
                    mset(hw.ap[:, :, nin - 1], 0.0, reads=[hw], pwrites=[hw])

        load_hw(0)
        for wi, (kind, s0, nout) in enumerate(wins):
            jj = 0 if kind == "lat" else 1
            nin = nout + 2
            hw = h2w[wi % 2]; xb = xw[0]
            src_d = xT_s if kind == "lat" else ctxT_s
            Dsrc = Dx if kind == "lat" else Dctx
            dma("sp", xb.ap[:, :, :nout], fm3(src_d, s0, nout), xb, reads=[Dsrc], writes=[xb])
            pre_wd = {}
            aT.new_gen()
            for fc in range(FC):
                if fc % 2 == 0:
                    nq = min(2, FC - fc)
                    wgb, wgw = wload_bf(wsu, wupT[fc + 0][:, :DC * nq * 128], DC, nq * 128)
                    wvb, wvw = wload_bf(wsu, wupT[fc + 1][:, :DC * nq * 128], DC, nq * 128)
                wgv = wgw[:, :, (fc % 2) * 128:(fc % 2 + 1) * 128]
                wvv = wvw[:, :, (fc % 2) * 128:(fc % 2 + 1) * 128]
                gps = psum[1 + pi % 2]; vps = psum[3 + pi % 2]; pi += 1
                for kc in range(DC):
                    mm(gps.ap[:, :nin], wgv[:, kc, :], hw.ap[:, kc, :nin], kc == 0, kc == DC - 1, reads=[wgb, hw],
                       writes=[gps] if kc == 0 else (), pwrites=() if kc == 0 else [gps])
                for kc in range(DC):
                    mm(vps.ap[:, :nin], wvv[:, kc, :], hw.ap[:, kc, :nin], kc == 0, kc == DC - 1, reads=[wvb, hw],
                       writes=[vps] if kc == 0 else (), pwrites=() if kc == 0 else [vps])
                a1 = c1[fc % 2]; a2 = c2[fc % 2]; a3 = c3[fc % 2]; sgb = sg[fc % 2]
                act(a1.ap[:, :nout], gps.ap[:, 1:1 + nout], AF.Identity, reads=[gps, vec], writes=[a1], bias=vcol(V_CB + fc), scale=vcol(V_CW + FC + fc))
                stt(a2.ap[:, :nout], gps.ap[:, 0:nout], vcol(V_CW + fc), a1.ap[:, :nout], ALU.mult, ALU.add, reads=[gps, vec, a1], writes=[a2])
                stt(a3.ap[:, :nout], gps.ap[:, 2:2 + nout], vcol(V_CW + 2 * FC + fc), a2.ap[:, :nout], ALU.mult, ALU.add, reads=[gps, vec, a2], writes=[a3])
                act(sgb.ap[:, :nout], a3.ap[:, :nout], AF.Silu, reads=[a3], writes=[sgb])
                tt(aT.ap[:, fc, :nout], sgb.ap[:, :nout], vps.ap[:, 1:1 + nout], ALU.mult, reads=[sgb, vps], pwrites=[aT])
                if fc % 2 == 1:
                    emit_mod_group()
                if fc == FC // 2:
                    for g_ in range(min(len(wsd), NDT)):
                        nq_ = min(2, DC - 2 * g_)
                        pre_wd[g_] = wload_bf(wsd, wdnT[g_][:, :FC * nq_ * 128], FC, nq_ * 128)
            if wi + 1 < len(wins):
                load_hw(wi + 1)
            for dc in range(DC):
                if dc % 2 == 0:
                    nq = min(2, DC - dc)
                    if dc // 2 in pre_wd:
                        wb, wdw = pre_wd[dc // 2]
                    else:
                        wb, wdw = wload_bf(wsd, wdnT[dc // 2][:, :FC * nq * 128], FC, nq * 128)
                wv_ = wdw[:, :, (dc % 2) * 128:(dc % 2 + 1) * 128]
                ps = psum[5 + pi % 2]; pi += 1
                for kc in range(FC):
                    mm(ps.ap[:, :nout], wv_[:, kc, :], aT.ap[:, kc, :nout], kc == 0, kc == FC - 1, reads=[wb, aT],
                       writes=[ps] if kc == 0 else (), pwrites=() if kc == 0 else [ps])
                stt(xb.ap[:, dc, :nout], ps.ap[:, :nout], modcol(5, dc, jj), xb.ap[:, dc, :nout], ALU.mult, ALU.add, reads=[ps, mod, xb], pwrites=[xb])
            if kind == "lat" and last:
                dma("sp", fm3(yT, s0, nout), xb.ap[:, :, :nout], xb, reads=[xb], pwrites=[Dy])
            else:
                dst_d = xT_s if kind == "lat" else ctxT_s
                dma("sp", fm3(dst_d, s0, nout), xb.ap[:, :, :nout], xb, reads=[xb], pwrites=[Dsrc])
        if not last:
            while nmod_done[0] < nmod_total:
                emit_mod_group()
            cpy(modraw.ap.rearrange("p a b -> p (a b)"), modps4.ap[:, 0:12 * DC], reads=[modps4], writes=[modraw])
            have_modraw[0] = True
        P.barrier()

    P.emit("sp", lambda e: e.nop(), reads=[Dy])
    P.finalize()
    global LASTP
    LASTP = P
    return nc


def pack_vecs(cfg, inp, l):
    DC, FC = cfg.DC, cfg.FC
    j = l // 2
    cols = [_fm(inp["norm_mix"][l]), _fm(inp["norm_ffn"][l]), _fm(inp["b_mod"][l])]
    cw = np.asarray(inp["ffn_conv_w"][l], np.float32)
    cols += [_fm(cw[0]), _fm(cw[1]), _fm(cw[2]), _fm(inp["ffn_conv_b"][l])]
    QC = cfg.QR // 128; KC2 = cfg.KVR // 128
    z1 = np.zeros((128, 1), np.float32)
    if l % 2 == 0:
        dup = lambda v: np.concatenate([np.asarray(v, np.float32)] * 2)[:, None]
        cols += [_fm(inp["mla_g_dq"][j]), _fm(inp["mla_g_dkv"][j]), np.asarray(inp["mla_g_q_nope"][j], np.float32)[:, None],
                 dup(inp["mla_g_q_pe"][j]), dup(inp["mla_g_k_pe"][j]), np.asarray(inp["mla_g_k_nope"][j], np.float32)[:, None], z1, z1]
    else:
        cols += [np.zeros((128, QC), np.float32), np.zeros((128, KC2), np.float32), z1, z1, z1, z1,
                 np.asarray(inp["gqa_g_q"][j], np.float32)[:, None], np.asarray(inp["gqa_g_k"][j], np.float32)[:, None]]
    return np.concatenate(cols, axis=1).astype(np.float32)


def prep_inputs(cfg, inp):
    NR, TL, D, C = cfg.NR, cfg.TL, cfg.D, cfg.C
    L = cfg.depth
    f = lambda a: np.ascontiguousarray(np.asarray(a, np.float32))
    vecs = np.stack([pack_vecs(cfg, inp, l) for l in range(L)])
    RgT = rot_matrix_T(128)
    Rm = rot_matrix_T(64)
    RmT = np.zeros((128, 128), np.float32); RmT[:64, :64] = Rm; RmT[64:, 64:] = Rm
    bones = np.zeros((128, 128), np.float32); bones[:64, :64] = 1; bones[64:, 64:] = 1
    shared = dict(RgT=RgT, RmT=RmT, bones=bones, vecs=vecs, w_mod=f(inp["w_mod"]),
                  mla_w_dq=f(inp["mla_w_dq"]), mla_w_uq=f(inp["mla_w_uq"]), mla_w_dkv=f(inp["mla_w_dkv"]),
                  mla_w_ukv=f(inp["mla_w_ukv"]), mla_w_o=f(inp["mla_w_o"]),
                  gqa_w_q=f(inp["gqa_w_q"]), gqa_w_kv=f(inp["gqa_w_kv"]), gqa_w_o=f(inp["gqa_w_o"]),
                  ffn_w_up=f(inp["ffn_w_up"]), ffn_w_down=f(inp["ffn_w_down"]))
    x = np.asarray(inp["x"], np.float32); ctx = np.asarray(inp["ctx"], np.float32)
    c = np.asarray(inp["c"], np.float32); c_ctx = np.asarray(inp["c_ctx"], np.float32)
    maps = []
    for core in range(8):
        b = core // NR; r = core % NR
        t0 = r * TL
        m = dict(shared)
        m["xT"] = np.ascontiguousarray(x[b, t0:t0 + TL].T)
        m["ctxT"] = np.ascontiguousarray(ctx[b].T)
        cv = np.stack([_fm(c[b]), _fm(c_ctx)], axis=-1)
        m["cvec"] = np.ascontiguousarray(cv.reshape(128, -1))
        cg, sg = rope_tables(cfg, t0, TL, 128)
        cm, sm = rope_tables(cfg, t0, TL, 64)
        m["cosg"] = cg; m["sing"] = sg
        m["cosm"] = np.concatenate([cm, cm], 0); m["sinm"] = np.concatenate([sm, sm], 0)
        s = np.zeros((128, 2 * NR), np.float32)
        if r > 0:
            s[:, r - 1] = 1.0
        if r < NR - 1:
            s[:, NR + r + 1] = 1.0
        m["sel"] = s
        maps.append(m)
    return maps


_NC_CACHE = {}


def run(cfg, inp, debug=False):
    key = (cfg.D, cfg.S, cfg.C, cfg.FF, cfg.depth)
    if key not in _NC_CACHE:
        _NC_CACHE[key] = build(cfg, debug)
    nc = _NC_CACHE[key]
    maps = prep_inputs(cfg, inp)
    res = run_bass_kernel_spmd(nc, maps, core_ids=list(range(8)))
    out = np.zeros((cfg.B, cfg.S, cfg.D), np.float32)
    for core in range(8):
        b = core // cfg.NR; r = core % cfg.NR
        out[b, r * cfg.TL:(r + 1) * cfg.TL] = res.results[core]["yT"].T
    return out


def kernel(**inputs):
    return run(Cfg(), inputs)
```

```python
import math
import numpy as np
import concourse.bass as bass
import concourse.mybir as mybir
from concourse.bass_utils import run_bass_kernel_spmd

F32 = mybir.dt.float32
BF16 = mybir.dt.bfloat16
AF = mybir.ActivationFunctionType
ALU = mybir.AluOpType
EPS = 1e-6
ROPE_BASE = 10000.0


class Cfg:
    def __init__(s, **kw):
        s.D = 2048; s.S = 8192; s.B = 2; s.C = 256; s.FF = 5632; s.depth = 4; s.GRID_W = 64
        s.MH = 16; s.QR = 512; s.KVR = 512; s.GH = 16; s.GKV = 4
        s.W1 = 512; s.W4 = 410; s.NR = 4
        s.NWS = 3; s.NWU = 4; s.NWD = 2; s.WDSPLIT = 1
        for k, v in kw.items():
            setattr(s, k, v)
        s.TL = s.S // s.NR
        s.DC = s.D // 128; s.FC = s.FF // 128
        s.NTOK = s.TL + s.C


class Op:
    __slots__ = ("eng", "fn", "deps", "marked", "cum", "is_dma", "sem", "val", "inc", "idx", "epoch")


class Buf:
    def __init__(s, ap=None, sem=None):
        s.ap = ap; s.w = []; s.r = []; s.gen = []; s.sem = sem; s.cnt = 0

    def new_gen(s):
        s.gen = s.w + s.r; s.w = []; s.r = []


class Prog:
    def __init__(s, nc):
        s.nc = nc
        s.ops = {k: [] for k in ("pe", "act", "dve", "pool", "sp")}
        s.engsem = {}
        s.epoch = 0; s.nidx = 0
        s.ccsem = nc.alloc_semaphore("ccsem"); s.cccnt = 0

    def new_epoch(s):
        s.epoch += 1

    def esem(s, eng, epoch):
        k = (eng, epoch)
        if k not in s.engsem:
            s.engsem[k] = s.nc.alloc_semaphore("es_%s_%d" % (eng, epoch))
        return s.engsem[k]

    def emit(s, eng, fn, reads=(), writes=(), pwrites=(), extra=()):
        deps = list(extra)
        for b in reads:
            deps += b.w
        for b in writes:
            b.new_gen(); deps += b.gen
        for b in pwrites:
            deps += b.gen
        best = {}
        for d in deps:
            if d.is_dma:
                k = ("s", d.sem.num); v = d.val
            else:
                k = ("e", d.eng); v = d.idx
            o = best.get(k)
            if o is None or v > o[0]:
                best[k] = (v, d)
        deps = [o[1] for o in best.values()]
        op = Op(); op.eng = eng; op.fn = fn; op.deps = deps; op.marked = False; op.cum = 0
        op.is_dma = False; op.sem = None; op.val = 0; op.inc = 0
        op.idx = s.nidx; s.nidx += 1; op.epoch = s.epoch
        for b in reads:
            b.r.append(op)
        for b in writes:
            b.w.append(op)
        for b in pwrites:
            b.w.append(op)
        s.ops[eng].append(op)
        return op

    def dma(s, eng, out, in_, sb, reads=(), writes=(), pwrites=(), extra=()):
        op = s.emit(eng, lambda e: e.dma_start(out=out, in_=in_), reads, writes, pwrites, extra)
        sb.cnt += 16
        op.is_dma = True; op.sem = sb.sem; op.val = sb.cnt; op.inc = 16
        return op

    def cc(s, kind, groups, in_ap, out_ap, reads=(), writes=(), pwrites=()):
        op = s.emit("pool", lambda e: e.collective_compute(kind, ALU.bypass, replica_groups=groups,
                                                            ins=[in_ap], outs=[out_ap]), reads, writes, pwrites)
        s.cccnt += 1
        op.is_dma = True; op.sem = s.ccsem; op.val = s.cccnt; op.inc = 1
        return op

    def barrier(s):
        deps = []
        latest = {}
        for eng, ops in s.ops.items():
            lastc = None
            for op in ops:
                if op.is_dma:
                    latest[op.sem.num] = op
                else:
                    lastc = op
            if lastc is not None:
                deps.append(lastc)
        deps += list(latest.values())
        for eng in s.ops:
            s.emit(eng, lambda e: e.nop(), extra=deps)

    def finalize(s):
        nc = s.nc
        for eng, ops in s.ops.items():
            for op in ops:
                for d in op.deps:
                    if not d.is_dma:
                        d.marked = True
        for eng, ops in s.ops.items():
            cnt = {}
            for op in ops:
                if (not op.is_dma) and op.marked:
                    cnt[op.epoch] = cnt.get(op.epoch, 0) + 1
                    s.esem(eng, op.epoch)
                op.cum = cnt.get(op.epoch, 0)

        def run(engname, e):
            waited = {}
            for op in s.ops[engname]:
                for d in op.deps:
                    if d.is_dma:
                        key = ("s", d.sem.num); val = d.val; sem = d.sem
                    else:
                        if d.eng == "pe" and engname == "pe":
                            continue
                        key = ("e", d.eng, d.epoch); val = d.cum; sem = s.engsem[(d.eng, d.epoch)]
                    if waited.get(key, 0) >= val:
                        continue
                    waited[key] = val
                    e.wait_ge(sem, val)
                ins = op.fn(e)
                if op.is_dma:
                    if op.inc == 16:
                        ins.then_inc(op.sem, 16)
                    else:
                        ins.then_inc(op.sem)
                elif op.marked:
                    ins.then_inc(s.engsem[(op.eng, op.epoch)], 1)

        with nc.Block() as block:
            @block.sync
            def _(e):
                run("sp", e)

            @block.gpsimd
            def _(e):
                run("pool", e)

            @block.scalar
            def _(e):
                run("act", e)

            @block.vector
            def _(e):
                run("dve", e)

            @block.tensor
            def _(e):
                run("pe", e)


def _fm(v):
    v = np.asarray(v, np.float32)
    return np.ascontiguousarray(v.reshape(-1, 128).T)


def rope_tables(cfg, tok0, n, rot_dim):
    axis_dim = rot_dim // 2
    inv = np.power(np.float32(ROPE_BASE), -np.arange(0, axis_dim, 2, dtype=np.float32) / np.float32(axis_dim)).astype(np.float32)
    t = np.arange(tok0, tok0 + n)
    rows = (t // cfg.GRID_W).astype(np.float32); cols = (t % cfg.GRID_W).astype(np.float32)
    ar = rows[:, None] * inv; ac = cols[:, None] * inv
    ang = np.concatenate([ar, ar, ac, ac], axis=-1).astype(np.float32)
    return np.cos(ang).T.astype(np.float32), np.sin(ang).T.astype(np.float32)


def rot_matrix_T(rot_dim):
    half = rot_dim // 2; q = half // 2
    R = np.zeros((rot_dim, rot_dim), np.float32)
    for base in (0, half):
        for i in range(q):
            R[base + i, base + i + q] = -1.0
            R[base + q + i, base + i] = 1.0
    return np.ascontiguousarray(R.T)


def build(cfg, debug=False):
    nc = bass.Bass("TRN2", target_bir_lowering=False)
    P = Prog(nc)
    D, TL, C, FF, DC, FC, NTOK, NR = cfg.D, cfg.TL, cfg.C, cfg.FF, cfg.DC, cfg.FC, cfg.NTOK, cfg.NR
    L = cfg.depth
    LA = (L + 1) // 2; LB = max(L // 2, 1)
    MH, QR, KVR, GH, GKV = cfg.MH, cfg.QR, cfg.KVR, cfg.GH, cfg.GKV
    QC = QR // 128; KC2 = KVR // 128
    T = NR * TL + C
    NCH = T // 128
    W1 = cfg.W1; W4 = cfg.W4

    def din(name, shape, dt=F32):
        return nc.dram_tensor(name, list(shape), dt, kind="ExternalInput").ap()

    def dscr(name, shape, dt):
        return nc.dram_tensor(name, list(shape), dt).ap()

    xT_in = din("xT", [D, TL]); ctxT_in = din("ctxT", [D, C]); cvec = din("cvec", [128, DC * 2])
    cosg = din("cosg", [128, TL]); sing = din("sing", [128, TL]); cosm = din("cosm", [128, TL]); sinm = din("sinm", [128, TL])
    RgT = din("RgT", [128, 128]); RmT = din("RmT", [128, 128]); bones = din("bones", [128, 128]); sel = din("sel", [128, 2 * NR])
    NV = 2 * DC + 6 * DC + 4 * FC + QC + KC2 + 6
    vecs = din("vecs", [L, 128, NV])
    w_mod = din("w_mod", [L, D, 6 * D])
    mla_w_dq = din("mla_w_dq", [LA, D, QR]); mla_w_uq = din("mla_w_uq", [LA, QR, MH * 192])
    mla_w_dkv = din("mla_w_dkv", [LA, D, KVR + 64]); mla_w_ukv = din("mla_w_ukv", [LA, KVR, MH * 256])
    mla_w_o = din("mla_w_o", [LA, MH * 128, D])
    gqa_w_q = din("gqa_w_q", [LB, D, GH * 128]); gqa_w_kv = din("gqa_w_kv", [LB, D, 2 * GKV * 128])
    gqa_w_o = din("gqa_w_o", [LB, GH * 128, D])
    ffn_w_up = din("ffn_w_up", [L, D, 2 * FF]); ffn_w_down = din("ffn_w_down", [L, FF, D])
    yT = nc.dram_tensor("yT", [D, TL], F32, kind="ExternalOutput").ap()

    xT_s = dscr("xT_s", [D, TL], F32); ctxT_s = dscr("ctxT_s", [D, C], F32)
    qT_d = dscr("qT_d", [max(MH, GH) * 128, NTOK], BF16); qpeT_d = dscr("qpeT_d", [MH // 2 * 128, NTOK], BF16)
    kvs = {}
    for nm, nkv in (("m", MH), ("g", GKV)):
        HB = 2 if nkv % 2 == 0 else 1
        while HB > 1 and HB * 128 * TL * 2 > (1 << 20):
            HB //= 2
        nb = nkv // HB
        kvs[nm] = dict(HB=HB, NB=nb,
                       kT_src=[dscr("kT_src%s%d" % (nm, i), [HB * 128, TL], BF16) for i in range(nb)],
                       kT_all=[dscr("kT_all%s%d" % (nm, i), [NR * HB * 128, TL], BF16) for i in range(nb)],
                       v_src=[dscr("v_src%s%d" % (nm, i), [TL, HB * 128], BF16) for i in range(nb)],
                       v_all=[dscr("v_all%s%d" % (nm, i), [NR * TL, HB * 128], BF16) for i in range(nb)],
                       kT_ctx=dscr("kT_ctx" + nm, [nkv * 128, C], BF16), v_ctx=dscr("v_ctx" + nm, [C, nkv * 128], BF16))
    kpe_src = dscr("kpe_src", [128, TL], BF16); kpe_all = dscr("kpe_all", [NR * 128, TL], BF16); kpe_ctx = dscr("kpe_ctx", [128, C], BF16)
    h2T_d = dscr("h2T_d", [D, TL + 2], BF16); h2cT_d = dscr("h2cT_d", [D, C + 2], BF16)
    hsrc = dscr("hsrc", [128, DC * 2], BF16); hall = dscr("hall", [NR * 128, DC * 2], BF16)
    NUT = (FC + 1) // 2; NDT = (DC + 1) // 2
    wupT = dscr("wupT", [NUT * 2, 128, DC * 256], BF16); wdnT = dscr("wdnT", [NDT, 128, FC * 256], BF16)
    Dwc = Buf()
    Dx = Buf(); Dctx = Buf(); Dq = Buf(); Dqpe = Buf(); Dksrc = Buf(); Dkall = Buf(); Dvsrc = Buf(); Dvall = Buf()
    Dkpesrc = Buf(); Dkpeall = Buf(); Dkctx = Buf(); Dvctx = Buf(); Dkpectx = Buf(); Dh2 = Buf(); Dh2c = Buf(); Dhsrc = Buf(); Dhall = Buf()
    Dy = Buf()

    ARENA_BYTES = 207 * 1024
    arena = nc.alloc_sbuf_tensor("arena", [128, ARENA_BYTES // 4], F32)
    apos = [0]
    sempool = {}
    semidx = [0]
    persist_sems = [0]

    def phase_reset(base):
        apos[0] = base; semidx[0] = persist_sems[0]

    def carve(shape, dt, sem=True):
        esz = 4 if dt == F32 else 2
        nb = int(np.prod(shape)) * esz
        nb_al = (nb + 63) // 64 * 64
        off = apos[0]; apos[0] += nb_al
        assert apos[0] <= ARENA_BYTES, ("SBUF arena overflow", apos[0])
        a = arena[:, off // 4:(off + nb) // 4]
        if dt != F32:
            a = a.bitcast(dt)
        if len(shape) == 2:
            a = a.rearrange("p (a b) -> p a b", a=shape[0])
        elif len(shape) == 3:
            a = a.rearrange("p (a b c) -> p a b c", a=shape[0], b=shape[1])
        sm = None
        if sem:
            k = semidx[0]; semidx[0] += 1
            if k not in sempool:
                sempool[k] = [nc.alloc_semaphore("bs%d" % k), 0]
            sm = sempool[k]
        b = Buf(a, sm[0] if sm else None)
        if sm:
            b.cnt = sm[1]; b.semrec = sm
        return b

    def mm(out, lhsT, rhs, start, stop, reads, writes=(), pwrites=()):
        return P.emit("pe", lambda e: e.matmul(out, lhsT, rhs, start=start, stop=stop), reads=reads, writes=writes, pwrites=pwrites)

    def act(out, in_, func, reads, writes=(), pwrites=(), bias=None, scale=None, eng="act"):
        kw = {}
        if bias is not None:
            kw["bias"] = bias
        if scale is not None:
            kw["scale"] = scale
        return P.emit(eng, lambda e: e.activation(out, in_, func, **kw), reads=reads, writes=writes, pwrites=pwrites)

    def stt(out, in0, scalar, in1, op0, op1, reads, writes=(), pwrites=(), eng="dve"):
        return P.emit(eng, lambda e: e.scalar_tensor_tensor(out, in0, scalar, in1, op0, op1), reads=reads, writes=writes, pwrites=pwrites)

    def tt(out, in0, in1, op, reads, writes=(), pwrites=(), eng="dve"):
        return P.emit(eng, lambda e: e.tensor_tensor(out, in0, in1, op), reads=reads, writes=writes, pwrites=pwrites)

    def tsm(out, in0, s1, reads, writes=(), pwrites=(), eng="dve"):
        return P.emit(eng, lambda e: e.tensor_scalar(out, in0, s1, None, ALU.mult), reads=reads, writes=writes, pwrites=pwrites)

    def recip(out, in_, reads, writes=(), pwrites=()):
        return P.emit("dve", lambda e: e.reciprocal(out, in_), reads=reads, writes=writes, pwrites=pwrites)

    def cpy(out, in_, reads, writes=(), pwrites=(), eng="dve"):
        return P.emit(eng, lambda e: e.tensor_copy(out, in_), reads=reads, writes=writes, pwrites=pwrites)

    def mset(out, val, writes=(), pwrites=(), reads=(), eng="dve"):
        return P.emit(eng, lambda e: e.memset(out, val), reads=reads, writes=writes, pwrites=pwrites)

    def dma(eng, out, in_, sb, reads=(), writes=(), pwrites=()):
        sb.cnt = sb.semrec[1]
        op = P.dma(eng, out, in_, sb, reads=reads, writes=writes, pwrites=pwrites)
        sb.semrec[1] = sb.cnt
        return op

    ones_f = carve([128], F32, False); ones_b = carve([128], BF16, False); bones_f = carve([128], F32)
    Rg_f = carve([128], F32); Rm_f = carve([128], F32); sel_f = carve([2 * NR], F32)
    sv_f = carve([DC * 2], F32); sv_b = carve([DC, 2], BF16, False)
    vec = carve([NV], F32)
    mod = carve([6 * DC, 2], F32, False)
    A1 = carve([DC, 2], F32, False); A2 = carve([DC, 2], F32, False)
    gsc = carve([8], F32, False)
    bnd = carve([DC, 2], BF16); halo = carve([DC, 2], BF16, False); hs = carve([NR, DC, 2], BF16)
    halo_f = carve([DC, 2], F32, False)
    eps_c = carve([1], F32, False)
    modraw = carve([6 * DC, 2], F32, False)
    have_modraw = [False]
    bones_b = carve([128], BF16, False); Rg_b = carve([128], BF16, False); Rm_b = carve([128], BF16, False)
    cvl = [carve([1], F32) for _ in range(3)]
    psum = [Buf(nc.alloc_psum_tensor("ps%d" % i, [128, 512], F32)[:]) for i in range(8)]
    PERSIST = apos[0]
    persist_sems[0] = semidx[0]

    o = 0
    V_NMIX = o; o += DC
    V_NFFN = o; o += DC
    V_BMOD = o; o += 6 * DC
    V_CW = o; o += 3 * FC
    V_CB = o; o += FC
    V_GDQ = o; o += QC
    V_GDKV = o; o += KC2
    V_GQN = o; o += 1
    V_GQP = o; o += 1
    V_GKP = o; o += 1
    V_GKN = o; o += 1
    V_GQ = o; o += 1
    V_GK = o; o += 1
    assert o == NV

    def vcol(i):
        return vec.ap[:, i:i + 1]

    def modcol(m, c, jj):
        return mod.ap[:, m * DC + c, jj:jj + 1]

    NWS = cfg.NWS

    mset(ones_f.ap, 1.0, writes=[ones_f]); mset(ones_b.ap, 1.0, writes=[ones_b]); mset(eps_c.ap, EPS, writes=[eps_c])
    dma("sp", bones_f.ap, bones[:, :], bones_f, writes=[bones_f])
    dma("sp", Rg_f.ap, RgT[:, :], Rg_f, writes=[Rg_f])
    dma("sp", Rm_f.ap, RmT[:, :], Rm_f, writes=[Rm_f])
    dma("sp", sel_f.ap, sel[:, :], sel_f, writes=[sel_f])
    dma("sp", sv_f.ap, cvec[:, :], sv_f, writes=[sv_f])
    act(sv_b.ap.rearrange("p a b -> p (a b)"), sv_f.ap, AF.Silu, reads=[sv_f], writes=[sv_b])
    cpy(bones_b.ap, bones_f.ap, reads=[bones_f], writes=[bones_b])
    cpy(Rg_b.ap, Rg_f.ap, reads=[Rg_f], writes=[Rg_b])
    cpy(Rm_b.ap, Rm_f.ap, reads=[Rm_f], writes=[Rm_b])

    lat_tiles = [("lat", t0, min(W1, TL - t0)) for t0 in range(0, TL, W1)]
    ctx_tiles = [("ctx", t0, min(W1, C - t0)) for t0 in range(0, C, W1)]
    groups = [list(range(g * NR, (g + 1) * NR)) for g in range(8 // NR)]

    def xsrc(kind, l):
        if kind == "lat":
            return (xT_in if l == 0 else xT_s), Dx
        return (ctxT_in if l == 0 else ctxT_s), Dctx

    def fm3(dram2d, c0, n):
        return dram2d.rearrange("(c p) t -> p c t", p=128)[:, :, c0:c0 + n]

    def rstd_from_ss(ssps, n, nd, rs_sq, rs):
        act(rs_sq.ap[:, :n], ssps.ap[:, :n], AF.Sqrt, reads=[ssps, eps_c], writes=[rs_sq], bias=eps_c.ap[:, 0:1], scale=1.0 / nd)
        recip(rs.ap[:, :n], rs_sq.ap[:, :n], reads=[rs_sq], writes=[rs])

    for l in range(L):
        last = (l == L - 1)
        is_mla = (l % 2 == 0)
        j = l // 2
        P.new_epoch()
        NH = MH if is_mla else GH
        NKV = MH if is_mla else GKV
        GRP = NH // NKV
        KV = kvs["m" if is_mla else "g"]
        kT_src, kT_all, v_src, v_all, kT_ctx, v_ctx = KV["kT_src"], KV["kT_all"], KV["v_src"], KV["v_all"], KV["kT_ctx"], KV["v_ctx"]
        HB, NB = KV["HB"], KV["NB"]

        def ksrc_rows(hd):
            return kT_src[hd // HB][(hd % HB) * 128:(hd % HB + 1) * 128, :]

        def vsrc_cols(tok0, ntok, h0, nh):
            out = []
            hd = h0
            while hd < h0 + nh:
                k = min(HB - hd % HB, h0 + nh - hd)
                out.append((v_src[hd // HB][tok0:tok0 + ntok, (hd % HB) * 128:(hd % HB + k) * 128], (hd - h0) * 128, k * 128))
                hd += k
            return out

        phase_reset(PERSIST)
        dma("sp", vec.ap, vecs[l], vec, writes=[vec])
        if not have_modraw[0]:
            wsm = [carve([DC, 512], BF16) for _ in range(3)]
            modps = psum[0]
            modps.new_gen()
            nfg = (6 * DC + 3) // 4
            wm3 = w_mod[l].rearrange("(kc p) f -> p kc f", p=128)
            for fg in range(nfg):
                nf = min(4, 6 * DC - fg * 4)
                wb = wsm[fg % 3]
                dma("pool", wb.ap[:, :, :nf * 128], wm3[:, :, fg * 512:fg * 512 + nf * 128], wb, writes=[wb])
                for fc in range(nf):
                    col = (fg * 4 + fc) * 2
                    for kc in range(DC):
                        mm(modps.ap[:, col:col + 2], wb.ap[:, kc, fc * 128:(fc + 1) * 128], sv_b.ap[:, kc, :], kc == 0, kc == DC - 1,
                           reads=[wb, sv_b], pwrites=[modps])
            mp3 = modps.ap[:, 0:12 * DC].rearrange("p (a b) -> p a b", b=2)
            for jj in range(2):
                tt(mod.ap[:, :, jj], mp3[:, :, jj], vec.ap[:, V_BMOD:V_BMOD + 6 * DC], ALU.add, reads=[modps, vec],
                   writes=[mod] if jj == 0 else (), pwrites=[mod] if jj else ())
        else:
            for jj in range(2):
                tt(mod.ap[:, :, jj], modraw.ap[:, :, jj], vec.ap[:, V_BMOD:V_BMOD + 6 * DC], ALU.add, reads=[modraw, vec],
                   writes=[mod] if jj == 0 else (), pwrites=[mod] if jj else ())
        for (Ab, voff, m) in ((A1, V_NMIX, 1), (A2, V_NFFN, 4)):
            for jj in range(2):
                stt(Ab.ap[:, :, jj], mod.ap[:, m * DC:(m + 1) * DC, jj], 1.0, vec.ap[:, voff:voff + DC], ALU.add, ALU.mult,
                    reads=[mod, vec], writes=[Ab] if jj == 0 else (), pwrites=[Ab] if jj else ())
        if is_mla:
            sc = 1.0 / math.sqrt(192.0)
            glist = [(0, V_GQN, sc), (1, V_GQP, sc), (2, V_GKP, 1.0), (3, V_GKN, 1.0)]
        else:
            sc = 1.0 / math.sqrt(128.0)
            glist = [(0, V_GQ, sc), (1, V_GK, 1.0)]
        for gi, (gidx, vo, mul) in enumerate(glist):
            tsm(gsc.ap[:, gidx:gidx + 1], vec.ap[:, vo:vo + 1], mul, reads=[vec], writes=[gsc] if gi == 0 else (), pwrites=[gsc] if gi else ())
        P.barrier()

        phase_reset(PERSIST)
        xt = [carve([DC, W1], F32) for _ in range(1)]
        hT = [carve([DC, W1], BF16, False) for _ in range(2)]
        sq = [carve([W1], BF16, False) for _ in range(2)]
        rs_sq = carve([W1], F32, False); rs = carve([W1], F32, False)
        tmpf = [carve([W1], F32, False) for _ in range(2)]
        cs = [carve([W1], F32) for _ in range(2)]; sn = [carve([W1], F32) for _ in range(2)]
        NSET = 3
        nsets = [dict(sq=carve([W1], BF16, False), rs_sq=carve([W1], F32, False), rs=carve([W1], F32, False),
                      qn=carve([W1], BF16, False), t1=carve([W1], F32, False), t2=carve([W1], F32, False)) for _ in range(NSET)]
        stg = [carve([W1], BF16) for _ in range(3)]
        vstg = [carve([512], BF16) for _ in range(2)]
        ws = [carve([max(DC, QC, KC2), 512], BF16) for _ in range(NWS)]
        if is_mla:
            cq_f = carve([max(QC, KC2), W1], F32, False); cqn = carve([QC, W1], BF16, False); ckvn = carve([KC2, W1], BF16, False)
        cnt = dict(ws=0, stg=0, vstg=0, pi=0)

        def wload(ring, src3, kc, m, split=1):
            b = ring[cnt["ws"] % len(ring)]; cnt["ws"] += 1
            view = b.ap.rearrange("p a b -> p (a b)")[:, :kc * m].rearrange("p (a b) -> p a b", a=kc)
            step = (kc + split - 1) // split
            for i, k0 in enumerate(range(0, kc, step)):
                k1 = min(kc, k0 + step)
                dma("pool", view[:, k0:k1, :], src3[:, k0:k1, :], b, writes=[b] if i == 0 else (), pwrites=[b] if i else ())
            return b, view

        def wload4(ring, src4, kc, nh):
            b = ring[cnt["ws"] % len(ring)]; cnt["ws"] += 1
            view = b.ap.rearrange("p a b -> p (a b)")[:, :kc * nh * 128].rearrange("p (a h e) -> p a h e", a=kc, h=nh)
            for hq in range(nh):
                dma("pool", view[:, :, hq, :], src4[:, :, hq, :], b, writes=[b] if hq == 0 else (), pwrites=[b] if hq else ())
            return b, view

        def wload2(ring, srcA, srcB, kc):
            b = ring[cnt["ws"] % len(ring)]; cnt["ws"] += 1
            view = b.ap.rearrange("p a b -> p (a b)")[:, :kc * 128].rearrange("p (a b) -> p a b", a=kc)
            dma("pool", view[:, :, 0:64], srcA, b, writes=[b])
            dma("pool", view[:, :, 64:128], srcB, b, pwrites=[b])
            return b, view

        def modulate(xb, n, Ab, shm, jj, hb, sq, rs_sq, rs, tmpf):
            ssps = psum[1]
            for c in range(DC):
                s_ = sq[c % 2]
                act(s_.ap[:, :n], xb.ap[:, c, :n], AF.Square, reads=[xb], writes=[s_])
                mm(ssps.ap[:, :n], ones_b.ap, s_.ap[:, :n], c == 0, c == DC - 1, reads=[s_, ones_b],
                   writes=[ssps] if c == 0 else (), pwrites=() if c == 0 else [ssps])
            rstd_from_ss(ssps, n, float(D), rs_sq, rs)
            hb.new_gen()
            for c in range(DC):
                tf = tmpf[c % 2]
                stt(tf.ap[:, :n], xb.ap[:, c, :n], Ab.ap[:, c, jj:jj + 1], rs.ap[:, :n], ALU.mult, ALU.mult, reads=[xb, Ab, rs], writes=[tf])
                act(hb.ap[:, c, :n], tf.ap[:, :n], AF.Identity, reads=[tf, mod], pwrites=[hb], bias=modcol(shm, c, jj), scale=1.0)

        def run_jobs(jobs, n, cb, sb_):
            N = len(jobs)
            pbuf = [psum[3], psum[4], psum[5]]
            st_of = {}

            def A(i):
                J = jobs[i]
                wb, wv_ = J["w"]()
                ps = pbuf[i % 3]
                proj(ps, wv_, wb, J["src"], J["kcs"], n)
                S_ = nsets[i % NSET]
                act(S_["sq"].ap[:, :n], ps.ap[:, :n], AF.Square, reads=[ps], writes=[S_["sq"]])

            def B(i):
                J = jobs[i]
                ps = pbuf[i % 3]
                S_ = nsets[i % NSET]
                ssps = psum[1 + i % 2]
                mm(ssps.ap[:, :n], J["ones"].ap, S_["sq"].ap[:, :n], True, True, reads=[S_["sq"], J["ones"]], writes=[ssps])
                rstd_from_ss(ssps, n, float(J["nd"]), S_["rs_sq"], S_["rs"])
                if not J["rope"]:
                    st = stg[cnt["stg"] % 3]; cnt["stg"] += 1
                    stt(st.ap[:, :n], ps.ap[:, :n], J["gcol"], S_["rs"].ap[:, :n], ALU.mult, ALU.mult, reads=[ps, S_["rs"], gsc], writes=[st])
                    dma("sp", J["dst"], st.ap[:, :n], st, reads=[st], pwrites=[J["Ddst"]])
                else:
                    stt(S_["qn"].ap[:, :n], ps.ap[:, :n], J["gcol"], S_["rs"].ap[:, :n], ALU.mult, ALU.mult, reads=[ps, S_["rs"], gsc], writes=[S_["qn"]])

            def C(i):
                J = jobs[i]
                if not J["rope"]:
                    return
                S_ = nsets[i % NSET]
                qn = S_["qn"]; t1 = S_["t1"]; t2 = S_["t2"]
                rps = psum[7] if i % 2 else psum[0]
                mm(rps.ap[:, :n], J["R"].ap, qn.ap[:, :n], True, True, reads=[J["R"], qn], writes=[rps])
                tt(t1.ap[:, :n], qn.ap[:, :n], cb.ap[:, :n], ALU.mult, reads=[qn, cb], writes=[t1])
                tt(t2.ap[:, :n], rps.ap[:, :n], sb_.ap[:, :n], ALU.mult, reads=[rps, sb_], writes=[t2])
                st = stg[cnt["stg"] % 3]; cnt["stg"] += 1
                tt(st.ap[:, :n], t1.ap[:, :n], t2.ap[:, :n], ALU.add, reads=[t1, t2], writes=[st])
                dma("sp", J["dst"], st.ap[:, :n], st, reads=[st], pwrites=[J["Ddst"]])

            for i in range(N + 2):
                if i < N:
                    A(i)
                if 0 <= i - 1 < N:
                    B(i - 1)
                if 0 <= i - 2 < N:
                    C(i - 2)

        def job(wfn, src, kcs, nd, onesb, gcol, rope, Rb, dst_ap, Ddst):
            return dict(w=wfn, src=src, kcs=kcs, nd=nd, ones=onesb, gcol=gcol, rope=rope, R=Rb, dst=dst_ap, Ddst=Ddst)

        def shared_w(loader):
            box = []

            def get():
                if not box:
                    box.append(loader())
                return box[0]
            return get

        def proj(ps, wview, wb, src, kcs, n):
            for kc in range(kcs):
                mm(ps.ap[:, :n], wview[:, kc, :], src.ap[:, kc, :n], kc == 0, kc == kcs - 1, reads=[wb, src],
                   writes=[ps] if kc == 0 else (), pwrites=() if kc == 0 else [ps])

        def nextps():
            p_ = psum[3 + cnt["pi"] % 2]; cnt["pi"] += 1
            return p_

        def vproj(src, kcs, n, wbs, dst_fn, Ddst, wide=None):
            ncol = len(wbs) * 128
            for s0 in range(0, n, 128):
                vps = psum[6]
                vps.new_gen()
                if wide is not None:
                    wb, wvv = wide
                    for kc in range(kcs):
                        mm(vps.ap[:, :ncol], src.ap[:, kc, s0:s0 + 128], wvv[:, kc, :], kc == 0, kc == kcs - 1,
                           reads=[src, wb], pwrites=[vps])
                else:
                    for qi, (wb, wv_) in enumerate(wbs):
                        for kc in range(kcs):
                            mm(vps.ap[:, qi * 128:(qi + 1) * 128], src.ap[:, kc, s0:s0 + 128], wv_[:, kc, :], kc == 0, kc == kcs - 1,
                               reads=[src, wb], pwrites=[vps])
                vs = vstg[cnt["vstg"] % 2]; cnt["vstg"] += 1
                act(vs.ap[:, :ncol], vps.ap[:, :ncol], AF.Copy, reads=[vps], writes=[vs])
                for (dap, coff, ncl) in dst_fn(s0):
                    dma("sp", dap, vs.ap[:, coff:coff + ncl], vs, reads=[vs], pwrites=[Ddst])

        for D_ in (Dq, Dqpe, Dksrc, Dvsrc, Dkpesrc, Dkctx, Dvctx, Dkpectx):
            D_.new_gen()
        tiles = lat_tiles + ctx_tiles
        def p1_tile(ti, kind, t0, n, do_q, do_kv):
            jj = 0 if kind == "lat" else 1
            tokq = t0 if kind == "lat" else TL + t0
            xb = xt[0]; hb = hT[ti % 2]
            src_d, Dsrc = xsrc(kind, l)
            dma("sp", xb.ap[:, :, :n], fm3(src_d, t0, n), xb, reads=[Dsrc], writes=[xb])
            modulate(xb, n, A1, 0, jj, hb, sq, rs_sq, rs, tmpf)
            rope = (kind == "lat")
            need_q = not (kind == "ctx" and last)
            cb = sb_ = None
            if rope:
                cb = cs[ti % 2]; sb_ = sn[ti % 2]
                dma("sp", cb.ap[:, :n], (cosm if is_mla else cosg)[:, t0:t0 + n], cb, writes=[cb])
                dma("sp", sb_.ap[:, :n], (sinm if is_mla else sing)[:, t0:t0 + n], sb_, writes=[sb_])
            kdst = (lambda r0, r1: ksrc_rows(r0 // 128)[:, t0:t0 + n]) if kind == "lat" else (lambda r0, r1: kT_ctx[r0:r1, t0:t0 + n])
            Dk = Dksrc if kind == "lat" else Dkctx
            Dv = Dvsrc if kind == "lat" else Dvctx
            if not is_mla:
                wq = gqa_w_q[j].rearrange("(kc p) m -> p kc m", p=128)
                wkv = gqa_w_kv[j].rearrange("(kc p) m -> p kc m", p=128)
                jobs = []
                if need_q:
                    for h0 in range(0, GH, 4):
                        nh = min(4, GH - h0)
                        sw = shared_w(lambda h0=h0, nh=nh: wload(ws, wq[:, :, h0 * 128:(h0 + nh) * 128], DC, nh * 128))
                        for hq in range(nh):
                            h = h0 + hq
                            jobs.append(job((lambda sw=sw, hq=hq: (sw()[0], sw()[1][:, :, hq * 128:(hq + 1) * 128])), hb, DC, 128, ones_b,
                                            gsc.ap[:, 0:1], rope, Rg_b, qT_d[h * 128:(h + 1) * 128, tokq:tokq + n], Dq))
                for g0 in range(0, GKV, 4):
                    ng = min(4, GKV - g0)
                    sw = shared_w(lambda g0=g0, ng=ng: wload(ws, wkv[:, :, g0 * 128:(g0 + ng) * 128], DC, ng * 128))
                    for gq in range(ng):
                        g = g0 + gq
                        jobs.append(job((lambda sw=sw, gq=gq: (sw()[0], sw()[1][:, :, gq * 128:(gq + 1) * 128])), hb, DC, 128, ones_b,
                                        gsc.ap[:, 1:2], rope, Rg_b, kdst(g * 128, (g + 1) * 128), Dk))
                run_jobs(jobs, n, cb, sb_)
                for c0 in range(0, GKV, 4):
                    ng = min(4, GKV - c0)
                    wb, wv_ = wload(ws, wkv[:, :, (GKV + c0) * 128:(GKV + c0 + ng) * 128], DC, ng * 128)
                    wbs = [(wb, wv_[:, :, q4 * 128:(q4 + 1) * 128]) for q4 in range(ng)]
                    if kind == "lat":
                        vproj(hb, DC, n, wbs, (lambda s0, c0=c0, ng=ng: vsrc_cols(t0 + s0, 128, c0, ng)), Dv, wide=(wb, wv_))
                    else:
                        vproj(hb, DC, n, wbs, (lambda s0, c0=c0, ng=ng: [(v_ctx[t0 + s0:t0 + s0 + 128, c0 * 128:(c0 + ng) * 128], 0, ng * 128)]), Dv, wide=(wb, wv_))
            else:
                wdq = mla_w_dq[j].rearrange("(kc p) m -> p kc m", p=128)
                wuq = mla_w_uq[j].rearrange("(kc p) m -> p kc m", p=128)
                wdkv = mla_w_dkv[j].rearrange("(kc p) m -> p kc m", p=128)
                wukv = mla_w_ukv[j].rearrange("(kc p) m -> p kc m", p=128)

                def compress(wsrc, nchunk, gv_off, outn):
                    ssps = psum[1]
                    outn.new_gen(); cq_f.new_gen()
                    for oc in range(nchunk):
                        if oc % 4 == 0:
                            nq = min(4, nchunk - oc)
                            wb, wvw = wload(ws, wsrc[:, :, oc * 128:(oc + nq) * 128], DC, nq * 128)
                        wv_ = wvw[:, :, (oc % 4) * 128:(oc % 4 + 1) * 128]
                        ps = nextps()
                        proj(ps, wv_, wb, hb, DC, n)
                        act(cq_f.ap[:, oc, :n], ps.ap[:, :n], AF.Copy, reads=[ps], pwrites=[cq_f])
                        s_ = sq[oc % 2]
                        act(s_.ap[:, :n], ps.ap[:, :n], AF.Square, reads=[ps], writes=[s_])
                        mm(ssps.ap[:, :n], ones_b.ap, s_.ap[:, :n], oc == 0, oc == nchunk - 1, reads=[s_, ones_b],
                           writes=[ssps] if oc == 0 else (), pwrites=() if oc == 0 else [ssps])
                    rstd_from_ss(ssps, n, float(nchunk * 128), rs_sq, rs)
                    for oc in range(nchunk):
                        stt(outn.ap[:, oc, :n], cq_f.ap[:, oc, :n], vcol(gv_off + oc), rs.ap[:, :n], ALU.mult, ALU.mult,
                            reads=[cq_f, rs, vec], pwrites=[outn])

                if need_q and do_q:
                    compress(wdq, QC, V_GDQ, cqn)
                    wuq4 = wuq.rearrange("p k (h e) -> p k h e", e=192)
                    jobs = []
                    for h0 in range(0, MH, 4):
                        nq = min(4, MH - h0)
                        sw = shared_w(lambda h0=h0, nq=nq: wload4(ws, wuq4[:, :, h0:h0 + nq, 0:128], QC, nq))
                        for hq in range(nq):
                            h = h0 + hq
                            jobs.append(job((lambda sw=sw, hq=hq: (sw()[0], sw()[1][:, :, hq, :])), cqn, QC, 128, ones_b,
                                            gsc.ap[:, 0:1], False, None, qT_d[h * 128:(h + 1) * 128, tokq:tokq + n], Dq))
                    for hp in range(MH // 2):
                        h0 = 2 * hp
                        jobs.append(job((lambda h0=h0: wload2(ws, wuq[:, :, h0 * 192 + 128:h0 * 192 + 192], wuq[:, :, (h0 + 1) * 192 + 128:(h0 + 1) * 192 + 192], QC)),
                                        cqn, QC, 64, bones_b, gsc.ap[:, 1:2], rope, Rm_b, qpeT_d[hp * 128:(hp + 1) * 128, tokq:tokq + n], Dqpe))
                    run_jobs(jobs, n, cb, sb_)
                if not do_kv:
                    return
                compress(wdkv, KC2, V_GDKV, ckvn)
                jobs = []
                jobs.append(job((lambda: wload2(ws, wdkv[:, :, KVR:KVR + 64], wdkv[:, :, KVR:KVR + 64], DC)), hb, DC, 64, bones_b, gsc.ap[:, 2:3],
                                rope, Rm_b, (kpe_src[:, t0:t0 + n] if kind == "lat" else kpe_ctx[:, t0:t0 + n]), (Dkpesrc if kind == "lat" else Dkpectx)))
                wukv4 = wukv.rearrange("p k (h e) -> p k h e", e=256)
                for h0 in range(0, MH, 4):
                    nq = min(4, MH - h0)
                    sw = shared_w(lambda h0=h0, nq=nq: wload4(ws, wukv4[:, :, h0:h0 + nq, 0:128], KC2, nq))
                    for hq in range(nq):
                        h = h0 + hq
                        jobs.append(job((lambda sw=sw, hq=hq: (sw()[0], sw()[1][:, :, hq, :])), ckvn, KC2, 128, ones_b,
                                        gsc.ap[:, 3:4], False, None, kdst(h * 128, (h + 1) * 128), Dk))
                run_jobs(jobs, n, cb, sb_)
                for h0 in range(0, MH, 4):
                    nh = min(4, MH - h0)
                    wb, wvw = wload4(ws, wukv4[:, :, h0:h0 + nh, 128:256], KC2, nh)
                    wbs = [(wb, wvw[:, :, q4, :]) for q4 in range(nh)]
                    wflat = wvw.rearrange("p k h e -> p k (h e)")
                    if kind == "lat":
                        vproj(ckvn, KC2, n, wbs, (lambda s0, h0=h0, nh=nh: vsrc_cols(t0 + s0, 128, h0, nh)), Dv, wide=(wb, wflat))
                    else:
                        vproj(ckvn, KC2, n, wbs, (lambda s0, h0=h0, nh=nh: [(v_ctx[t0 + s0:t0 + s0 + 128, h0 * 128:(h0 + nh) * 128], 0, nh * 128)]), Dv, wide=(wb, wflat))
        def emit_ag():
            Dkall.new_gen(); Dvall.new_gen()
            for bi in range(NB):
                P.cc("AllGather", groups, kT_src[bi][:, :], kT_all[bi][:, :], reads=[Dksrc], pwrites=[Dkall])
                P.cc("AllGather", groups, v_src[bi][:, :], v_all[bi][:, :], reads=[Dvsrc], pwrites=[Dvall])
            if is_mla:
                P.cc("AllGather", groups, kpe_src[:, :], kpe_all[:, :], reads=[Dkpesrc], writes=[Dkpeall])

        if is_mla:
            for ti, (kind, t0, n) in enumerate(tiles):
                p1_tile(ti, kind, t0, n, False, True)
            emit_ag()
            for ti, (kind, t0, n) in enumerate(tiles):
                if not (kind == "ctx" and last):
                    p1_tile(ti + len(tiles), kind, t0, n, True, False)
        else:
            for ti, (kind, t0, n) in enumerate(tiles):
                p1_tile(ti, kind, t0, n, True, True)
            emit_ag()
        P.barrier()

        phase_reset(PERSIST)
        oT = carve([NH, NTOK], BF16, False)
        P3BASE = apos[0]
        KA = [carve([T], BF16) for _ in range(2)]
        VV = [carve([NCH, 128], BF16) for _ in range(2)]
        KBm = [carve([T], BF16) for _ in range(2)] if is_mla else None
        QA = [carve([W1], BF16) for _ in range(2)]
        QB = [carve([W1], BF16) for _ in range(2)] if is_mla else None
        pT = [carve([W1], BF16, False) for _ in range(4)]
        rec = carve([W1], F32, False)
        accD = [carve([W1], F32, False) for _ in range(2)]
        if is_mla:
            kpa = kpe_all.rearrange("(r d) t -> d r t", d=128)
            for hf in range(2):
                lo, hi = hf * 64, hf * 64 + 64
                zl, zh = (64, 128) if hf == 0 else (0, 64)
                mset(KBm[hf].ap[zl:zh, :], 0.0, writes=[KBm[hf]])
                dma("sp", KBm[hf].ap[lo:hi, 0:NR * TL].rearrange("p (r t) -> p r t", r=NR), kpa[lo:hi], KBm[hf], reads=[Dkpeall], pwrites=[KBm[hf]])
                dma("sp", KBm[hf].ap[lo:hi, NR * TL:T], kpe_ctx[lo:hi, :], KBm[hf], reads=[Dkpectx], pwrites=[KBm[hf]])
        wup = ffn_w_up[l].rearrange("(kc p) m -> p kc m", p=128)
        wdn = ffn_w_down[l].rearrange("(kc p) m -> p kc m", p=128)
        Dwc.new_gen()
        ncv = 0
        for g in range(NUT):
            nq = min(2, FC - 2 * g)
            for part in range(2):
                ln = cvl[ncv % len(cvl)]; ncv += 1
                dma("pool", wupT[2 * g + part][:, :DC * nq * 128].rearrange("p (kc m) -> p kc m", kc=DC),
                    wup[:, :, part * FF + g * 256:part * FF + g * 256 + nq * 128], ln, writes=[ln], pwrites=[Dwc])
        for g in range(NDT):
            nq = min(2, DC - 2 * g)
            ln = cvl[ncv % len(cvl)]; ncv += 1
            dma("pool", wdnT[g][:, :FC * nq * 128].rearrange("p (kc m) -> p kc m", kc=FC),
                wdn[:, :, g * 256:g * 256 + nq * 128], ln, writes=[ln], pwrites=[Dwc])
        qtiles = lat_tiles + ([] if last else ctx_tiles)
        oT.new_gen()
        it = 0
        kall4 = [a.rearrange("(r g d) t -> d g r t", r=NR, g=HB) for a in kT_all]
        vall4 = [a.rearrange("(ch p) (g d) -> p ch g d", p=128, d=128) for a in v_all]
        vctx4 = v_ctx.rearrange("(ch p) (g d) -> p ch g d", p=128, d=128)
        for g in range(NKV):
            ka = KA[g % 2]; vv = VV[g % 2]
            dma("sp", ka.ap[:, 0:NR * TL].rearrange("p (r t) -> p r t", r=NR), kall4[g // HB][:, g % HB], ka, reads=[Dkall], writes=[ka])
            dma("sp", ka.ap[:, NR * TL:T], kT_ctx[g * 128:(g + 1) * 128, :], ka, reads=[Dkctx], pwrites=[ka])
            dma("sp", vv.ap[:, 0:NR * TL // 128, :], vall4[g // HB][:, :, g % HB, :], vv, reads=[Dvall], writes=[vv])
            dma("sp", vv.ap[:, NR * TL // 128:NCH, :], vctx4[:, :, g, :], vv, reads=[Dvctx], pwrites=[vv])
            for hh in range(GRP):
                h = g * GRP + hh
                hp = (h % 2) * 64
                for (kind, t0, n) in qtiles:
                    tokq = t0 if kind == "lat" else TL + t0
                    qa = QA[it % 2]; qb = QB[it % 2] if is_mla else None
                    ops_ = psum[3 + it % 2]; sums = psum[5 + it % 2]
                    it += 1
                    dma("sp", qa.ap[:, :n], qT_d[h * 128:(h + 1) * 128, tokq:tokq + n], qa, reads=[Dq], writes=[qa])
                    if is_mla:
                        dma("sp", qb.ap[:, :n], qpeT_d[(h // 2) * 128:(h // 2 + 1) * 128, tokq:tokq + n], qb, reads=[Dqpe], writes=[qb])
                    chunks = list(range(NCH)) if kind == "lat" else list(range(NR * TL // 128, NCH))
                    nchk = len(chunks)

                    aD = accD[it % 2]
                    kbm = KBm[h % 2] if is_mla else None
                    pe_chunks = [ci for ci in range(nchk) if (ci % 6 == 5 if is_mla else ci % 4 == 3)]
                    dve_chunks = [ci for ci in range(nchk) if ci not in pe_chunks]

                    def emit_S(ci):
                        ch = chunks[ci]
                        sp_ = psum[ci % 3]
                        mm(sp_.ap[:, :n], ka.ap[:, ch * 128:(ch + 1) * 128], qa.ap[:, :n], True, not is_mla, reads=[ka, qa], writes=[sp_])
                        if is_mla:
                            mm(sp_.ap[:, :n], kbm.ap[:, ch * 128:(ch + 1) * 128], qb.ap[:, :n], False, True, reads=[kbm, qb], pwrites=[sp_])
                        pt = pT[ci % 4]
                        act(pt.ap[:, :n], sp_.ap[:, :n], AF.Exp, reads=[sp_], writes=[pt])

                    def emit_PV(ci):
                        ch = chunks[ci]
                        pt = pT[ci % 4]
                        mm(ops_.ap[:, :n], vv.ap[:, ch, :], pt.ap[:, :n], ci == 0, ci == nchk - 1, reads=[vv, pt],
                           writes=[ops_] if ci == 0 else (), pwrites=() if ci == 0 else [ops_])
                        if ci in pe_chunks:
                            mm(sums.ap[:, :n], ones_b.ap, pt.ap[:, :n], ci == pe_chunks[0], False, reads=[ones_b, pt],
                               writes=[sums] if ci == pe_chunks[0] else (), pwrites=() if ci == pe_chunks[0] else [sums])
                        elif ci == dve_chunks[0]:
                            cpy(aD.ap[:, :n], pt.ap[:, :n], reads=[pt], writes=[aD])
                        else:
                            tt(aD.ap[:, :n], aD.ap[:, :n], pt.ap[:, :n], ALU.add, reads=[pt, aD], pwrites=[aD])

                    emit_S(0)
                    if nchk > 1:
                        emit_S(1)
                    for ci in range(nchk):
                        if ci + 2 < nchk:
                            emit_S(ci + 2)
                        emit_PV(ci)
                    mm(sums.ap[:, :n], ones_f.ap, aD.ap[:, :n], not pe_chunks, True, reads=[ones_f, aD],
                       writes=[sums] if not pe_chunks else (), pwrites=[sums] if pe_chunks else ())
                    recip(rec.ap[:, :n], sums.ap[:, :n], reads=[sums], writes=[rec])
                    tt(oT.ap[:, h, tokq:tokq + n], ops_.ap[:, :n], rec.ap[:, :n], ALU.mult, reads=[ops_, rec], pwrites=[oT])
        P.barrier()

        phase_reset(P3BASE)
        xt3 = [carve([DC, W1], F32) for _ in range(2)]
        h3 = [carve([DC, W1], BF16) for _ in range(1)]
        sq3 = [carve([W1], BF16, False) for _ in range(2)]
        rs_sq3 = carve([W1], F32, False); rs3 = carve([W1], F32, False)
        tmpf3 = [carve([W1], F32, False) for _ in range(2)]
        ws3 = [carve([NH, 256], BF16) for _ in range(NWS)]
        w_o = (mla_w_o if is_mla else gqa_w_o)[j].rearrange("(kc p) m -> p kc m", p=128)
        tiles3 = lat_tiles + ([] if last else ctx_tiles)
        Dh2.new_gen(); Dh2c.new_gen(); bnd.new_gen()
        for ti, (kind, t0, n) in enumerate(tiles3):
            jj = 0 if kind == "lat" else 1
            tokq = t0 if kind == "lat" else TL + t0
            xb = xt3[ti % 2]; hb = h3[0]
            src_d, Dsrc = xsrc(kind, l)
            dst_d = xT_s if kind == "lat" else ctxT_s
            dma("sp", xb.ap[:, :, :n], fm3(src_d, t0, n), xb, reads=[Dsrc], writes=[xb])
            for dc in range(DC):
                if dc % 2 == 0:
                    nq = min(2, DC - dc)
                    wb, wvw = wload(ws3, w_o[:, :, dc * 128:(dc + nq) * 128], NH, nq * 128)
                wv_ = wvw[:, :, (dc % 2) * 128:(dc % 2 + 1) * 128]
                ps = nextps()
                for kc in range(NH):
                    mm(ps.ap[:, :n], wv_[:, kc, :], oT.ap[:, kc, tokq:tokq + n], kc == 0, kc == NH - 1, reads=[wb, oT],
                       writes=[ps] if kc == 0 else (), pwrites=() if kc == 0 else [ps])
                stt(xb.ap[:, dc, :n], ps.ap[:, :n], modcol(2, dc, jj), xb.ap[:, dc, :n], ALU.mult, ALU.add, reads=[ps, mod, xb], pwrites=[xb])
            dma("sp", fm3(dst_d, t0, n), xb.ap[:, :, :n], xb, reads=[xb], pwrites=[Dsrc])
            modulate(xb, n, A2, 3, jj, hb, sq3, rs_sq3, rs3, tmpf3)
            if kind == "lat":
                dma("sp", fm3(h2T_d, 1 + t0, n), hb.ap[:, :, :n], hb, reads=[hb], pwrites=[Dh2])
                if t0 == 0:
                    cpy(bnd.ap[:, :, 0], hb.ap[:, :, 0], reads=[hb], pwrites=[bnd])
                if t0 + n == TL:
                    cpy(bnd.ap[:, :, 1], hb.ap[:, :, n - 1], reads=[hb], pwrites=[bnd])
            else:
                dma("sp", fm3(h2cT_d, 1 + t0, n), hb.ap[:, :, :n], hb, reads=[hb], pwrites=[Dh2c])
        dma("sp", hsrc[:, :], bnd.ap.rearrange("p a b -> p (a b)"), bnd, reads=[bnd], writes=[Dhsrc])
        P.barrier()
        P.cc("AllGather", groups, hsrc[:, :], hall[:, :], reads=[Dhsrc], writes=[Dhall])
        dma("sp", hs.ap, hall.rearrange("(r p) (c t) -> p r c t", p=128, t=2), hs, reads=[Dhall], writes=[hs])
        for side in range(2):
            for r in range(NR):
                srcc = hs.ap[:, r, :, 1 - side]
                scol = sel_f.ap[:, side * NR + r:side * NR + r + 1]
                if r == 0:
                    tsm(halo_f.ap[:, :, side], srcc, scol, reads=[hs, sel_f], writes=[halo_f] if side == 0 else (), pwrites=() if side == 0 else [halo_f])
                else:
                    stt(halo_f.ap[:, :, side], srcc, scol, halo_f.ap[:, :, side], ALU.mult, ALU.add, reads=[hs, sel_f, halo_f], pwrites=[halo_f])
        cpy(halo.ap, halo_f.ap, reads=[halo_f], writes=[halo])
        P.barrier()

        phase_reset(PERSIST)
        WIN = W4 + 2
        h2w = [carve([DC, WIN], BF16) for _ in range(2)]
        aT = carve([FC, W4], BF16, False)
        xw = [carve([DC, W4], F32) for _ in range(1)]
        c1 = [carve([W4], F32, False) for _ in range(2)]; c2 = [carve([W4], F32, False) for _ in range(2)]
        c3 = [carve([W4], F32, False) for _ in range(2)]; sg = [carve([W4], F32, False) for _ in range(2)]
        wsu = [carve([DC, 256], BF16) for _ in range(cfg.NWU)]
        wsd = [carve([FC, 256], BF16) for _ in range(cfg.NWD)]
        nmod_total = 6 * DC if not last else 0
        nmod_done = [0]
        if not last:
            wsm4 = [carve([DC, 128], BF16) for _ in range(3)]
            modps4 = psum[0]
            modps4.new_gen()
            wm3n = w_mod[l + 1].rearrange("(kc p) f -> p kc f", p=128)

        def emit_mod_group():
            fcm = nmod_done[0]
            if fcm >= nmod_total:
                return
            nmod_done[0] += 1
            wbm = wsm4[fcm % 3]
            dma("pool", wbm.ap, wm3n[:, :, fcm * 128:(fcm + 1) * 128], wbm, writes=[wbm])
            for kc in range(DC):
                mm(modps4.ap[:, fcm * 2:fcm * 2 + 2], wbm.ap[:, kc, :], sv_b.ap[:, kc, :], kc == 0, kc == DC - 1,
                   reads=[wbm, sv_b], pwrites=[modps4])
        def wload_bf(ring, src2, kc, m):
            b = ring[cnt["ws"] % len(ring)]; cnt["ws"] += 1
            v2 = b.ap.rearrange("p a b -> p (a b)")[:, :kc * m]
            dma("pool", v2, src2, b, reads=[Dwc], writes=[b])
            return b, v2.rearrange("p (a b) -> p a b", a=kc)

        wins = [("lat", s, min(W4, TL - s)) for s in range(0, TL, W4)]
        if not last:
            wins += [("ctx", s, min(W4, C - s)) for s in range(0, C, W4)]
        pi = 0
        def load_hw(wi):
            kind, s0, nout = wins[wi]
            nin = nout + 2
            hw = h2w[wi % 2]
            tot = TL if kind == "lat" else C
            if kind == "lat":
                dma("sp", hw.ap[:, :, :nin], fm3(h2T_d, s0, nin), hw, reads=[Dh2], writes=[hw])
            else:
                dma("sp", hw.ap[:, :, :nin], fm3(h2cT_d, s0, nin), hw, reads=[Dh2c], writes=[hw])
            if s0 == 0:
                if kind == "lat":
                    cpy(hw.ap[:, :, 0], halo.ap[:, :, 0], reads=[halo, hw], pwrites=[hw])
                else:
                    mset(hw.ap[:, :, 0], 0.0, reads=[hw], pwrites=[hw])
            if s0 + nout == tot:
                if kind == "lat":
                    cpy(hw.ap[:, :, nin - 1], halo.ap[:, :, 1], reads=[halo, hw], pwrites=[hw])
                else:
                    mset(hw.ap[:, :, nin - 1], 0.0, reads=[hw], pwrites=[hw])

        load_hw(0)
        for wi, (kind, s0, nout) in enumerate(wins):
            jj = 0 if kind == "lat" else 1
            nin = nout + 2
            hw = h2w[wi % 2]; xb = xw[0]
            src_d = xT_s if kind == "lat" else ctxT_s
            Dsrc = Dx if kind == "lat" else Dctx
            dma("sp", xb.ap[:, :, :nout], fm3(src_d, s0, nout), xb, reads=[Dsrc], writes=[xb])
            pre_wd = {}
            aT.new_gen()
            for fc in range(FC):
                if fc % 2 == 0:
                    nq = min(2, FC - fc)
                    wgb, wgw = wload_bf(wsu, wupT[fc + 0][:, :DC * nq * 128], DC, nq * 128)
                    wvb, wvw = wload_bf(wsu, wupT[fc + 1][:, :DC * nq * 128], DC, nq * 128)
                wgv = wgw[:, :, (fc % 2) * 128:(fc % 2 + 1) * 128]
                wvv = wvw[:, :, (fc % 2) * 128:(fc % 2 + 1) * 128]
                gps = psum[1 + pi % 2]; vps = psum[3 + pi % 2]; pi += 1
                for kc in range(DC):
                    mm(gps.ap[:, :nin], wgv[:, kc, :], hw.ap[:, kc, :nin], kc == 0, kc == DC - 1, reads=[wgb, hw],
                       writes=[gps] if kc == 0 else (), pwrites=() if kc == 0 else [gps])
                for kc in range(DC):
                    mm(vps.ap[:, :nin], wvv[:, kc, :], hw.ap[:, kc, :nin], kc == 0, kc == DC - 1, reads=[wvb, hw],
                       writes=[vps] if kc == 0 else (), pwrites=() if kc == 0 else [vps])
                a1 = c1[fc % 2]; a2 = c2[fc % 2]; a3 = c3[fc % 2]; sgb = sg[fc % 2]
                act(a1.ap[:, :nout], gps.ap[:, 1:1 + nout], AF.Identity, reads=[gps, vec], writes=[a1], bias=vcol(V_CB + fc), scale=vcol(V_CW + FC + fc))
                stt(a2.ap[:, :nout], gps.ap[:, 0:nout], vcol(V_CW + fc), a1.ap[:, :nout], ALU.mult, ALU.add, reads=[gps, vec, a1], writes=[a2])
                stt(a3.ap[:, :nout], gps.ap[:, 2:2 + nout], vcol(V_CW + 2 * FC + fc), a2.ap[:, :nout], ALU.mult, ALU.add, reads=[gps, vec, a2], writes=[a3])
                act(sgb.ap[:, :nout], a3.ap[:, :nout], AF.Silu, reads=[a3], writes=[sgb])
                tt(aT.ap[:, fc, :nout], sgb.ap[:, :nout], vps.ap[:, 1:1 + nout], ALU.mult, reads=[sgb, vps], pwrites=[aT])
                if fc % 2 == 1:
                    emit_mod_group()
                if fc == FC // 2:
                    for g_ in range(min(len(wsd), NDT)):
                        nq_ = min(2, DC - 2 * g_)
                        pre_wd[g_] = wload_bf(wsd, wdnT[g_][:, :FC * nq_ * 128], FC, nq_ * 128)
            if wi + 1 < len(wins):
                load_hw(wi + 1)
            for dc in range(DC):
                if dc % 2 == 0:
                    nq = min(2, DC - dc)
                    if dc // 2 in pre_wd:
                        wb, wdw = pre_wd[dc // 2]
                    else:
                        wb, wdw = wload_bf(wsd, wdnT[dc // 2][:, :FC * nq * 128], FC, nq * 128)
                wv_ = wdw[:, :, (dc % 2) * 128:(dc % 2 + 1) * 128]
                ps = psum[5 + pi % 2]; pi += 1
                for kc in range(FC):
                    mm(ps.ap[:, :nout], wv_[:, kc, :], aT.ap[:, kc, :nout], kc == 0, kc == FC - 1, reads=[wb, aT],
                       writes=[ps] if kc == 0 else (), pwrites=() if kc == 0 else [ps])
                stt(xb.ap[:, dc, :nout], ps.ap[:, :nout], modcol(5, dc, jj), xb.ap[:, dc, :nout], ALU.mult, ALU.add, reads=[ps, mod, xb], pwrites=[xb])
            if kind == "lat" and last:
                dma("sp", fm3(yT, s0, nout), xb.ap[:, :, :nout], xb, reads=[xb], pwrites=[Dy])
            else:
                dst_d = xT_s if kind == "lat" else ctxT_s
                dma("sp", fm3(dst_d, s0, nout), xb.ap[:, :, :nout], xb, reads=[xb], pwrites=[Dsrc])
        if not last:
            while nmod_done[0] < nmod_total:
                emit_mod_group()
            cpy(modraw.ap.rearrange("p a b -> p (a b)"), modps4.ap[:, 0:12 * DC], reads=[modps4], writes=[modraw])
            have_modraw[0] = True
        P.barrier()

    P.emit("sp", lambda e: e.nop(), reads=[Dy])
    P.finalize()
    global LASTP
    LASTP = P
    return nc


def pack_vecs(cfg, inp, l):
    DC, FC = cfg.DC, cfg.FC
    j = l // 2
    cols = [_fm(inp["norm_mix"][l]), _fm(inp["norm_ffn"][l]), _fm(inp["b_mod"][l])]
    cw = np.asarray(inp["ffn_conv_w"][l], np.float32)
    cols += [_fm(cw[0]), _fm(cw[1]), _fm(cw[2]), _fm(inp["ffn_conv_b"][l])]
    QC = cfg.QR // 128; KC2 = cfg.KVR // 128
    z1 = np.zeros((128, 1), np.float32)
    if l % 2 == 0:
        dup = lambda v: np.concatenate([np.asarray(v, np.float32)] * 2)[:, None]
        cols += [_fm(inp["mla_g_dq"][j]), _fm(inp["mla_g_dkv"][j]), np.asarray(inp["mla_g_q_nope"][j], np.float32)[:, None],
                 dup(inp["mla_g_q_pe"][j]), dup(inp["mla_g_k_pe"][j]), np.asarray(inp["mla_g_k_nope"][j], np.float32)[:, None], z1, z1]
    else:
        cols += [np.zeros((128, QC), np.float32), np.zeros((128, KC2), np.float32), z1, z1, z1, z1,
                 np.asarray(inp["gqa_g_q"][j], np.float32)[:, None], np.asarray(inp["gqa_g_k"][j], np.float32)[:, None]]
    return np.concatenate(cols, axis=1).astype(np.float32)


def prep_inputs(cfg, inp):
    NR, TL, D, C = cfg.NR, cfg.TL, cfg.D, cfg.C
    L = cfg.depth
    f = lambda a: np.ascontiguousarray(np.asarray(a, np.float32))
    vecs = np.stack([pack_vecs(cfg, inp, l) for l in range(L)])
    RgT = rot_matrix_T(128)
    Rm = rot_matrix_T(64)
    RmT = np.zeros((128, 128), np.float32); RmT[:64, :64] = Rm; RmT[64:, 64:] = Rm
    bones = np.zeros((128, 128), np.float32); bones[:64, :64] = 1; bones[64:, 64:] = 1
    shared = dict(RgT=RgT, RmT=RmT, bones=bones, vecs=vecs, w_mod=f(inp["w_mod"]),
                  mla_w_dq=f(inp["mla_w_dq"]), mla_w_uq=f(inp["mla_w_uq"]), mla_w_dkv=f(inp["mla_w_dkv"]),
                  mla_w_ukv=f(inp["mla_w_ukv"]), mla_w_o=f(inp["mla_w_o"]),
                  gqa_w_q=f(inp["gqa_w_q"]), gqa_w_kv=f(inp["gqa_w_kv"]), gqa_w_o=f(inp["gqa_w_o"]),
                  ffn_w_up=f(inp["ffn_w_up"]), ffn_w_down=f(inp["ffn_w_down"]))
    x = np.asarray(inp["x"], np.float32); ctx = np.asarray(inp["ctx"], np.float32)
    c = np.asarray(inp["c"], np.float32); c_ctx = np.asarray(inp["c_ctx"], np.float32)
    maps = []
    for core in range(8):
        b = core // NR; r = core % NR
        t0 = r * TL
        m = dict(shared)
        m["xT"] = np.ascontiguousarray(x[b, t0:t0 + TL].T)
        m["ctxT"] = np.ascontiguousarray(ctx[b].T)
        cv = np.stack([_fm(c[b]), _fm(c_ctx)], axis=-1)
        m["cvec"] = np.ascontiguousarray(cv.reshape(128, -1))
        cg, sg = rope_tables(cfg, t0, TL, 128)
        cm, sm = rope_tables(cfg, t0, TL, 64)
        m["cosg"] = cg; m["sing"] = sg
        m["cosm"] = np.concatenate([cm, cm], 0); m["sinm"] = np.concatenate([sm, sm], 0)
        s = np.zeros((128, 2 * NR), np.float32)
        if r > 0:
            s[:, r - 1] = 1.0
        if r < NR - 1:
            s[:, NR + r + 1] = 1.0
        m["sel"] = s
        maps.append(m)
    return maps


_NC_CACHE = {}


def run(cfg, inp, debug=False):
    key = (cfg.D, cfg.S, cfg.C, cfg.FF, cfg.depth)
    if key not in _NC_CACHE:
        _NC_CACHE[key] = build(cfg, debug)
    nc = _NC_CACHE[key]
    maps = prep_inputs(cfg, inp)
    res = run_bass_kernel_spmd(nc, maps, core_ids=list(range(8)))
    out = np.zeros((cfg.B, cfg.S, cfg.D), np.float32)
    for core in range(8):
        b = core // cfg.NR; r = core % cfg.NR
        out[b, r * cfg.TL:(r + 1) * cfg.TL] = res.results[core]["yT"].T
    return out


def kernel(**inputs):
    return run(Cfg(), inputs)
```
